# Optimizing a Trainium2 kernel written in Bass

```python
import jax, jax.numpy as jnp
from jax import lax
import numpy as np


D_MODEL = 1024
BATCH = 16
SEQ = 2048
DEPTH = 1

GRID_W = 64
CTX_LEN = 256
HG_DK = 128
HG_HEADS = (D_MODEL // 2) // HG_DK
HG_DV = (D_MODEL // 2) // HG_HEADS
HG_WIDTH = HG_HEADS * HG_DV
RET_HEADS = 4
RET_DK = (D_MODEL // 2) // RET_HEADS
RET_DV = RET_DK
RET_WIDTH = RET_HEADS * RET_DV
MIX_WIDTH = HG_WIDTH + RET_WIDTH
CHUNK = 64
D_FF = ((8 * D_MODEL // 3 + 127) // 128) * 128
CONV_W = 3
ROPE_THETA = 10000.0
EPS = 1e-6
IN_SIZES = (HG_HEADS * HG_DK, HG_WIDTH, HG_HEADS * HG_DK, HG_HEADS * HG_DK, HG_WIDTH,
            RET_HEADS * RET_DK, RET_HEADS * RET_DK, RET_WIDTH, RET_WIDTH)
IN_WIDTH = sum(IN_SIZES)
IN_OFFSETS = tuple(sum(IN_SIZES[:i + 1]) for i in range(len(IN_SIZES) - 1))

kernel_name = 'hymba_hgrn2_retention_convffn_dit_block'


def _rmsnorm(t, g):
    tf = t.astype(jnp.float32)
    y = tf * lax.rsqrt(jnp.mean(tf * tf, axis=-1, keepdims=True) + EPS)
    return (y * g.astype(jnp.float32)).astype(t.dtype)


def _heads(t, n_heads):
    b, L, w = t.shape
    return t.reshape(b, L, n_heads, w // n_heads).transpose(0, 2, 1, 3)


def _merge(t):
    b, h, L, d = t.shape
    return t.transpose(0, 2, 1, 3).reshape(b, L, h * d)


def _flip(t):
    return jnp.flip(t, axis=2)


def _rope_2d(t, rows, cols):
    half = t.shape[-1] // 2
    quarter = half // 2
    freqs = ROPE_THETA ** (-jnp.arange(quarter, dtype=jnp.float32) / quarter)

    def rot(u, pos):
        ang = pos[:, None] * freqs[None, :]
        cos, sin = jnp.cos(ang), jnp.sin(ang)
        u1, u2 = u[..., :quarter], u[..., quarter:]
        return jnp.concatenate([u1 * cos - u2 * sin, u1 * sin + u2 * cos], axis=-1)

    return jnp.concatenate([rot(t[..., :half], rows), rot(t[..., half:], cols)], axis=-1)


def _to_chunks(t):
    b, h, L, d = t.shape
    return jnp.moveaxis(t.reshape(b, h, L // CHUNK, CHUNK, d), 2, 0)


def _from_chunks(t):
    n, b, h, c, d = t.shape
    return jnp.moveaxis(t, 0, 2).reshape(b, h, n * c, d)


def _hgrn2_chunk_scan(q, k, v, logf, s0):
    mask = jnp.tril(jnp.ones((CHUNK, CHUNK), dtype=bool))

    def step(s, inp):
        qc, kc, vc, lfc = inp
        b = jnp.cumsum(lfc, axis=2)
        diff = b[:, :, :, None, :] - b[:, :, None, :, :]
        decay = jnp.where(mask[:, :, None], jnp.exp(jnp.minimum(diff, 0.0)), 0.0)
        attn = jnp.einsum('bhtsk,bhsk->bhts', qc[:, :, :, None, :] * decay, kc)
        o = (jnp.einsum('bhtk,bhkv->bhtv', qc * jnp.exp(b), s)
             + jnp.einsum('bhts,bhsv->bhtv', attn, vc))
        b_last = b[:, :, -1:, :]
        s_new = (jnp.exp(b_last[:, :, 0, :])[..., None] * s
                 + jnp.einsum('bhsk,bhsv->bhkv', kc * jnp.exp(b_last - b), vc))
        return s_new, o

    s_fin, o = lax.scan(step, s0, (_to_chunks(q), _to_chunks(k), _to_chunks(v), _to_chunks(logf)))
    return _from_chunks(o), s_fin


def _hgrn2_final_state(k, v, logf):
    w = jnp.exp(lax.cumsum(logf, axis=2, reverse=True) - logf)
    return jnp.einsum('bhsk,bhsv->bhkv', k * w, v)


def _retention_chunk_scan(q, k, v, log_gamma, r0):
    pos = jnp.arange(CHUNK, dtype=jnp.float32)
    rel = pos[:, None] - pos[None, :]
    lg = log_gamma[:, None, None]
    intra = jnp.where(rel >= 0, jnp.exp(jnp.maximum(rel, 0.0) * lg), 0.0)
    q_dec = jnp.exp((pos + 1.0)[None, :] * log_gamma[:, None])
    k_dec = jnp.exp((CHUNK - 1.0 - pos)[None, :] * log_gamma[:, None])
    chunk_dec = jnp.exp(CHUNK * log_gamma)

    def step(r, inp):
        qc, kc, vc = inp
        scores = jnp.einsum('bhtk,bhsk->bhts', qc, kc) * intra
        o = (jnp.einsum('bhts,bhsv->bhtv', scores, vc)
             + jnp.einsum('bhtk,bhkv->bhtv', qc * q_dec[:, :, None], r))
        r_new = (chunk_dec[:, None, None] * r
                 + jnp.einsum('bhsk,bhsv->bhkv', kc * k_dec[:, :, None], vc))
        return r_new, o

    r_fin, o = lax.scan(step, r0, (_to_chunks(q), _to_chunks(k), _to_chunks(v)))
    return _from_chunks(o), r_fin


def _retention_final_state(k, v, log_gamma):
    L = k.shape[2]
    w = jnp.exp((L - 1.0 - jnp.arange(L, dtype=jnp.float32))[None, :] * log_gamma[:, None])
    return jnp.einsum('bhsk,bhsv->bhkv', k * w[:, :, None], v)


def _mixer_features(h, w_in_l, lb, pos):
    p = (h @ w_in_l).astype(jnp.float32)
    hq, hi, hff, hfb, hg, rq, rk, rv, rg = jnp.split(p, IN_OFFSETS, axis=-1)
    hq = _heads(hq, HG_HEADS)
    hv = _heads(hi, HG_HEADS)
    dirs = []
    for z, lbd in ((hff, lb[0]), (hfb, lb[1])):
        f = lbd + (1.0 - lbd) * jax.nn.sigmoid(_heads(z, HG_HEADS))
        dirs.append((1.0 - f, jnp.log(f)))
    rq = _heads(rq, RET_HEADS)
    rk = _heads(rk, RET_HEADS) * (RET_DK ** -0.5)
    rv = _heads(rv, RET_HEADS)
    if pos is not None:
        rq = _rope_2d(rq, pos[0], pos[1])
        rk = _rope_2d(rk, pos[0], pos[1])
    return (hq, hv, dirs[0], dirs[1], hg, rq, rk, rv, rg)


def _context_states(feats, log_gamma):
    hq, hv, (kf, lff), (kb, lfb), hg, rq, rk, rv, rg = feats
    s_f = _hgrn2_final_state(kf, hv, lff)
    s_b = _hgrn2_final_state(_flip(kb), _flip(hv), _flip(lfb))
    r_f = _retention_final_state(rk, rv, log_gamma[0])
    r_b = _retention_final_state(_flip(rk), _flip(rv), log_gamma[1])
    return (s_f, s_b, r_f, r_b)


def _zero_states(bsz):
    zh = jnp.zeros((bsz, HG_HEADS, HG_DK, HG_DV), jnp.float32)
    zr = jnp.zeros((bsz, RET_HEADS, RET_DK, RET_DV), jnp.float32)
    return (zh, zh, zr, zr)


def _bidir_mix(feats, states, log_gamma, hg_norm_g, ret_norm_g):
    hq, hv, (kf, lff), (kb, lfb), hg, rq, rk, rv, rg = feats
    s_f, s_b, r_f, r_b = states
    o_f, sf_out = _hgrn2_chunk_scan(hq, kf, hv, lff, s_f)
    o_b, sb_out = _hgrn2_chunk_scan(_flip(hq), _flip(kb), _flip(hv), _flip(lfb), s_b)
    hg_o = (o_f + _flip(o_b)) * jax.nn.sigmoid(_heads(hg, HG_HEADS))
    hg_o = _rmsnorm(hg_o, hg_norm_g.reshape(HG_HEADS, 1, HG_DV))
    y_f, rf_out = _retention_chunk_scan(rq, rk, rv, log_gamma[0], r_f)
    y_b, rb_out = _retention_chunk_scan(_flip(rq), _flip(rk), _flip(rv), log_gamma[1], r_b)
    ret_o = _rmsnorm(y_f + _flip(y_b), ret_norm_g.reshape(RET_HEADS, 1, RET_DV))
    ret_o = ret_o * jax.nn.silu(_heads(rg, RET_HEADS))
    mixed = jnp.concatenate([_merge(hg_o), _merge(ret_o)], axis=-1)
    return mixed, (sf_out, sb_out, rf_out, rb_out)


def _conv_ffn(h, w_up_l, conv_w_l, conv_b_l, w_down_l):
    gate, up = jnp.split(h @ w_up_l, 2, axis=-1)
    L = gate.shape[1]
    pad = CONV_W // 2
    gp = jnp.pad(gate, ((0, 0), (pad, pad), (0, 0)))
    gate = sum(gp[:, j:j + L] * conv_w_l[j] for j in range(CONV_W)) + conv_b_l
    return (jax.nn.silu(gate) * up) @ w_down_l


def setup_inputs(seed: int = 0) -> dict:
    key = jax.random.key(seed)
    ks = jax.random.split(key, 19)
    f32 = jnp.float32

    def nrm(k, shape, s):
        return jax.random.normal(k, shape, f32) * s

    x = nrm(ks[0], (BATCH, SEQ, D_MODEL), 1.0)
    c = nrm(ks[1], (BATCH, D_MODEL), 1.0)
    ctx = nrm(ks[2], (BATCH, CTX_LEN, D_MODEL), 1.0)
    c_ctx = nrm(ks[3], (D_MODEL,), 1.0)
    w_mod = nrm(ks[4], (DEPTH, D_MODEL, 6 * D_MODEL), 0.5 * D_MODEL ** -0.5)
    b_mod = nrm(ks[5], (DEPTH, 6 * D_MODEL), 0.02)
    norm1_g = 1.0 + nrm(ks[6], (DEPTH, D_MODEL), 0.02)
    w_in = nrm(ks[7], (DEPTH, D_MODEL, IN_WIDTH), D_MODEL ** -0.5)
    hgrn_lb = nrm(ks[8], (2, DEPTH + 1, HG_HEADS * HG_DK), 0.1)
    hgrn_norm_g = 1.0 + nrm(ks[9], (DEPTH, HG_WIDTH), 0.02)
    base = jnp.log(2.0 ** (5.0 + jnp.arange(RET_HEADS, dtype=f32)) - 1.0)
    ret_decay = base + nrm(ks[10], (DEPTH, 2, RET_HEADS), 0.01)
    ret_norm_g = 1.0 + nrm(ks[11], (DEPTH, RET_WIDTH), 0.02)
    w_out = nrm(ks[12], (DEPTH, MIX_WIDTH, D_MODEL), MIX_WIDTH ** -0.5)
    norm2_g = 1.0 + nrm(ks[13], (DEPTH, D_MODEL), 0.02)
    w_up = nrm(ks[14], (DEPTH, D_MODEL, 2 * D_FF), D_MODEL ** -0.5)
    conv_w = nrm(ks[15], (DEPTH, CONV_W, D_FF), CONV_W ** -0.5)
    conv_b = nrm(ks[16], (DEPTH, D_FF), 0.01)
    w_down = nrm(ks[17], (DEPTH, D_FF, D_MODEL), D_FF ** -0.5)
    final_g = 1.0 + nrm(ks[18], (D_MODEL,), 0.02)
    return {'x': x, 'c': c, 'ctx': ctx, 'c_ctx': c_ctx, 'w_mod': w_mod, 'b_mod': b_mod,
            'norm1_g': norm1_g, 'w_in': w_in, 'hgrn_lb': hgrn_lb, 'hgrn_norm_g': hgrn_norm_g,
            'ret_decay': ret_decay, 'ret_norm_g': ret_norm_g, 'w_out': w_out, 'norm2_g': norm2_g,
            'w_up': w_up, 'conv_w': conv_w, 'conv_b': conv_b, 'w_down': w_down, 'final_g': final_g}


def reference(x, c, ctx, c_ctx, w_mod, b_mod, norm1_g, w_in, hgrn_lb, hgrn_norm_g,
              ret_decay, ret_norm_g, w_out, norm2_g, w_up, conv_w, conv_b, w_down, final_g):
    f32 = jnp.float32
    seq_len = x.shape[1]
    ROWS = seq_len // GRID_W
    rows = jnp.repeat(jnp.arange(ROWS, dtype=f32), GRID_W)
    cols = jnp.tile(jnp.arange(GRID_W, dtype=f32), ROWS)
    lb_all = jnp.cumsum(jax.nn.softmax(hgrn_lb.astype(f32), axis=1), axis=1)
    for layer in range(DEPTH):
        last = layer == DEPTH - 1
        mod_x = (jax.nn.silu(c) @ w_mod[layer] + b_mod[layer])[:, None, :]
        mod_c = jax.nn.silu(c_ctx) @ w_mod[layer] + b_mod[layer]
        sh1, sc1, g1, sh2, sc2, g2 = jnp.split(mod_x, 6, axis=-1)
        csh1, csc1, cg1, csh2, csc2, cg2 = jnp.split(mod_c, 6, axis=-1)
        lb = lb_all[:, layer].reshape(2, HG_HEADS, 1, HG_DK)
        log_gamma = jax.nn.log_sigmoid(ret_decay[layer].astype(f32))

        hx = _rmsnorm(x, norm1_g[layer]) * (1.0 + sc1) + sh1
        hc = _rmsnorm(ctx, norm1_g[layer]) * (1.0 + csc1) + csh1
        feats_x = _mixer_features(hx, w_in[layer], lb, (rows, cols))
        feats_c = _mixer_features(hc, w_in[layer], lb, None)
        if last:
            ctx_states = _context_states(feats_c, log_gamma)
        else:
            mix_c, ctx_states = _bidir_mix(feats_c, _zero_states(ctx.shape[0]), log_gamma,
                                           hgrn_norm_g[layer], ret_norm_g[layer])
        mix_x, _ = _bidir_mix(feats_x, ctx_states, log_gamma, hgrn_norm_g[layer], ret_norm_g[layer])
        x = x + g1 * (mix_x.astype(x.dtype) @ w_out[layer])

        hx2 = _rmsnorm(x, norm2_g[layer]) * (1.0 + sc2) + sh2
        x = x + g2 * _conv_ffn(hx2, w_up[layer], conv_w[layer], conv_b[layer], w_down[layer])

        if not last:
            ctx = ctx + cg1 * (mix_c.astype(ctx.dtype) @ w_out[layer])
            hc2 = _rmsnorm(ctx, norm2_g[layer]) * (1.0 + csc2) + csh2
            ctx = ctx + cg2 * _conv_ffn(hc2, w_up[layer], conv_w[layer], conv_b[layer], w_down[layer])
    return _rmsnorm(x, final_g)
```

```python
import contextlib
import numpy as np
import ml_dtypes
import concourse.bass as bass
import concourse.mybir as mybir
from concourse.bass_utils import run_bass_kernel_spmd

F32 = mybir.dt.float32
BF16 = mybir.dt.bfloat16
AF = mybir.ActivationFunctionType
ALU = mybir.AluOpType
AX = mybir.AxisListType

D = 1024
L = 2048
LC = 256
TOK = L + LC
NT = TOK // 128
NX = L // 128
DFF = 2816
NFC = DFF // 128
INW = 4608
EPS = 1e-6
BLK = 256
NCORES = 8


class Buf:
    __slots__ = ("name", "w", "r", "excl")

    def __init__(self, name, excl=False):
        self.name = name
        self.w = None
        self.r = []
        self.excl = excl


class Op:
    __slots__ = ("eng", "fn", "deps", "sig", "needs_sig", "is_dma", "idx", "key")


class Sched:
    COMPUTE = ("pe", "act", "dve", "pool")

    def __init__(self, nc):
        self.nc = nc
        self.ops = []
        self.dma_count = {}
        self.last_dma = {}

    def add(self, eng, fn, reads=(), writes=(), dma=None):
        op = Op()
        op.eng = eng
        op.fn = fn
        op.idx = len(self.ops)
        op.is_dma = dma is not None
        op.needs_sig = op.is_dma
        op.key = dma
        writes = list(writes) + [b for b in reads if b.excl]
        reads = [b for b in reads if not b.excl]
        deps = {}
        for b in reads:
            if b.w is not None:
                deps[b.w] = True
        for b in writes:
            if b.w is not None:
                deps.setdefault(b.w, False)
            for r in b.r:
                deps.setdefault(r, False)
        op.deps = []
        for j, raw in deps.items():
            if j == op.idx:
                continue
            o = self.ops[j]
            if o.eng == eng and not o.is_dma and not op.is_dma:
                if eng == "pe":
                    continue
            op.deps.append(j)
            o.needs_sig = True
        for b in reads:
            b.r.append(op.idx)
        for b in writes:
            b.w = op.idx
            b.r = []
        if op.is_dma:
            prev = self.last_dma.get(dma)
            if prev is not None and prev not in op.deps:
                op.deps.append(prev)
            self.last_dma[dma] = op.idx
            c = self.dma_count.get(dma, 0) + 16
            self.dma_count[dma] = c
            op.sig = (dma, c)
        else:
            op.sig = None
        self.ops.append(op)
        return op.idx

    def seal_group(self, key, since=0):
        if key not in self.dma_count:
            return
        tot = self.dma_count[key]
        for op in self.ops[since:]:
            if op.is_dma and op.key == key:
                op.sig = (key, tot)

    def emit(self, final_wait_ops=()):
        nc = self.nc
        with contextlib.ExitStack() as st:
            sems = {}
            for e in self.COMPUTE:
                sems[e] = st.enter_context(nc.semaphore("s_" + e))
            for k in self.dma_count:
                sems[k] = st.enter_context(nc.semaphore("d_" + str(k)))
            cnt = {e: 0 for e in self.COMPUTE}
            for op in self.ops:
                if not op.is_dma and op.needs_sig:
                    cnt[op.eng] += 1
                    op.sig = (op.eng, cnt[op.eng])
            block = st.enter_context(nc.Block())
            ops = self.ops

            def run(engname, e):
                waited = {}
                for op in ops:
                    if op.eng != engname:
                        continue
                    for j in op.deps:
                        k, v = ops[j].sig
                        if waited.get(k, 0) < v:
                            e.wait_ge(sems[k], v)
                            waited[k] = v
                    ins = op.fn(e)
                    if op.needs_sig:
                        ins.then_inc(sems[op.sig[0]], 16 if op.is_dma else 1)
                if engname == "sp":
                    for j in final_wait_ops:
                        k, v = ops[j].sig
                        if waited.get(k, 0) < v:
                            e.wait_ge(sems[k], v)
                            waited[k] = v

            @block.tensor
            def _(e):
                run("pe", e)

            @block.scalar
            def _(e):
                run("act", e)

            @block.vector
            def _(e):
                run("dve", e)

            @block.gpsimd
            def _(e):
                run("pool", e)

            @block.sync
            def _(e):
                run("sp", e)


class Arena:
    def __init__(self, big, limit):
        self.big = big
        self.off = 0
        self.limit = limit
        self.hi = 0

    def _view(self, ap, shape):
        if len(shape) == 2:
            return ap
        if len(shape) == 3:
            return ap.rearrange("p (a b) -> p a b", a=shape[1])
        return ap.rearrange("p (a b c) -> p a b c", a=shape[1], b=shape[2])

    def alloc(self, shape, dt):
        n = int(np.prod(shape[1:]))
        nb = n * (4 if dt == F32 else 2)
        nb = (nb + 3) // 4 * 4
        o = self.off
        self.off += nb
        self.hi = max(self.hi, self.off)
        assert self.off <= self.limit, ("SBUF arena overflow", self.off)
        ap = self.big[:, o // 4:(o + nb) // 4]
        if dt != F32:
            ap = ap.bitcast(BF16)[:, 0:n]
        return self._view(ap, shape)


def build_program(debug=False, stop=None):
    nc = bass.Bass("TRN2", target_bir_lowering=False)

    def din(name, shape, dt=F32):
        return nc.dram_tensor(name, list(shape), dt, kind="ExternalInput").ap()

    x_d = din("x", [2, L, D])
    ctx_d = din("ctx", [2, LC, D])
    cT_d = din("cT", [128, 24])
    wmod_d = din("w_mod", [D, 6 * D])
    bmodP_d = din("bmodP", [128, 48])
    n1g_d = din("n1g", [128, 8])
    n2g_d = din("n2g", [128, 8])
    win_d = din("w_in", [D, INW])
    lbP_d = din("lbP", [128, 16])
    hgn_d = din("hgn", [1, 512])
    rd_d = din("rd", [1, 8])
    rtn_d = din("rtn", [1, 512])
    wout_d = din("w_out", [D, D])
    wup_d = din("w_up", [D, 2 * DFF])
    cwP_d = din("cwP", [128, 66])
    cbP_d = din("cbP", [128, 22])
    wdn_d = din("w_down", [DFF, D])
    fgn_d = din("fgn", [1, D])
    ident_d = din("ident", [128, 128])
    pm_d = din("pm", [128, 128])
    mf_d = din("mf", [128, 128])
    mb_d = din("mb", [128, 128])
    cos_d = din("cos", [128, L])
    sin_d = din("sin", [128, L])
    posf_d = din("posf", [128, 128])
    posb_d = din("posb", [128, 128])
    out_d = nc.dram_tensor("out", [2, L, D], F32, kind="ExternalOutput").ap()
    x1_d = nc.dram_tensor("x1s", [2, L, D], F32,
                          kind="ExternalOutput" if debug else "Internal").ap()

    st = contextlib.ExitStack()
    with st:
        LIMIT = 212000
        big = st.enter_context(nc.sbuf_tensor("big", [128, LIMIT // 4], F32))
        banks = [st.enter_context(nc.psum_tensor("bank%d" % i, [128, 512], F32)) for i in range(8)]
        bankB = [Buf("bank%d" % i, excl=True) for i in range(8)]
        PT = banks[7][:].bitcast(BF16)
        PTB = bankB[7]
        S = Sched(nc)
        A = Arena(big, LIMIT)
        dbg_d = nc.dram_tensor("dbg", [128, 16384], F32, kind="ExternalOutput").ap() if debug else None
        stage = {"n": 0, "off": 0, "stopped": False, "names": []}
        _add = S.add

        def gated_add(*a, **k):
            if stage["stopped"]:
                return 0
            if isinstance(stop, int) and len(S.ops) >= stop:
                stage["stopped"] = True
                return 0
            return _add(*a, **k)
        S.add = gated_add

        def dump(name, ap2d, bufs):
            n = ap2d.shape[1]
            o = stage["off"]
            stage["off"] += n
            stage["names"].append((name, o, n))
            _add("pool", lambda e: e.dma_start(out=dbg_d[:, o:o + n], in_=ap2d), list(bufs), [], dma="dbg%d" % len(stage["names"]))

        def mark(name, dumps=()):
            if stage["stopped"]:
                return
            if stop is not None and name == stop:
                for nm, ap, bufs in dumps():
                    dump(nm, ap, bufs)
                stage["stopped"] = True

        rot = [0]

        def nb():
            i = rot[0]
            rot[0] = (i + 1) % 7
            return banks[i], bankB[i]

        def act(out, in_, func, reads, writes, **kw):
            return S.add("act", lambda e: e.activation(out=out, in_=in_, func=func, **kw), reads, writes)

        def ts(eng, out, in0, s1, s2, op0, op1, reads, writes):
            if s2 is None:
                return S.add(eng, lambda e: e.tensor_scalar(out=out, in0=in0, scalar1=s1, scalar2=None, op0=op0),
                             reads, writes)
            return S.add(eng, lambda e: e.tensor_scalar(out=out, in0=in0, scalar1=s1, scalar2=s2, op0=op0, op1=op1),
                         reads, writes)

        def tt(eng, out, in0, in1, op, reads, writes):
            return S.add(eng, lambda e: e.tensor_tensor(out=out, in0=in0, in1=in1, op=op), reads, writes)

        def stt(out, in0, sc, in1, op0, op1, reads, writes):
            return S.add("dve", lambda e: e.scalar_tensor_tensor(out=out, in0=in0, scalar=sc, in1=in1, op0=op0, op1=op1),
                         reads, writes)

        def cp(eng, out, in_, reads, writes):
            if eng == "act":
                return act(out, in_, AF.Copy, reads, writes)
            return S.add(eng, lambda e: e.tensor_copy(out=out, in_=in_), reads, writes)

        def dma(eng, out, in_, key, reads, writes):
            return S.add(eng, lambda e: e.dma_start(out=out, in_=in_), reads, writes, dma=key)

        def mmg(out, pairs, reads, writes):
            pairs = list(pairs)

            def fn(e):
                n = len(pairs)
                ins = None
                for i, (l, r) in enumerate(pairs):
                    ins = e.matmul(out, lhsT=l, rhs=r, start=(i == 0), stop=(i == n - 1))
                return ins
            return S.add("pe", fn, reads, writes)

        def mm_multi(groups, reads, writes):
            groups = [(o, list(p)) for o, p in groups]

            def fn(e):
                ins = None
                for out, pairs in groups:
                    n = len(pairs)
                    for i, (l, r) in enumerate(pairs):
                        ins = e.matmul(out, lhsT=l, rhs=r, start=(i == 0), stop=(i == n - 1))
                return ins
            return S.add("pe", fn, reads, writes)

        def transposes(items, reads, writes):
            items = list(items)

            def fn(e):
                ins = None
                for o, i_ in items:
                    ins = e.transpose(out=o, in_=i_, identity=identb)
                return ins
            return S.add("pe", fn, reads + [Bc], writes)

        identb = A.alloc([128, 128], BF16)
        pmb = A.alloc([128, 128], BF16)
        mfb = A.alloc([128, 128], BF16)
        mbb = A.alloc([128, 128], BF16)
        identf = A.alloc([128, 128], F32)
        onesf = A.alloc([128, 128], F32)
        cT = A.alloc([128, 24], F32)
        cs = A.alloc([128, 8, 3], BF16)
        bmodP = A.alloc([128, 6, 8], F32)
        n1g = A.alloc([128, 8], F32)
        n2g = A.alloc([128, 8], F32)
        modP = A.alloc([128, 6, 8, 3], F32)
        lbP = A.alloc([128, 2, 2, 4], F32)
        lbv = A.alloc([128, 2, 4], F32)
        oml = A.alloc([128, 2, 4], F32)
        rdt = A.alloc([128, 8], F32)
        lg = A.alloc([128, 8], F32)
        nlg = A.alloc([128, 8], F32)
        g64 = A.alloc([128, 8], F32)
        g128 = A.alloc([128, 8], F32)
        cwP = A.alloc([128, 22, 3], F32)
        cbP = A.alloc([128, 22], F32)
        sm = A.alloc([128, 64], F32)
        Bc = Buf("consts")
        Bmod = Buf("modP")
        Bsm = Buf("sm")
        base_off = A.off

        cosb = A.alloc([128, L], BF16)
        sinb = A.alloc([128, L], BF16)
        posf = A.alloc([128, 128], F32)
        posb = A.alloc([128, 128], F32)
        RM = A.alloc([128, NT, 128], BF16)
        RT = A.alloc([128, 4, 128], F32)
        hgn = A.alloc([128, 512], F32)
        rtn = A.alloc([128, 512], F32)
        hxT = A.alloc([128, 8, TOK], BF16)
        mixT = A.alloc([128, 8, L], BF16)
        wfm = [A.alloc([128, 8, 384], BF16) for _ in range(2)]
        wtm = [A.alloc([128, 8, 256], BF16) for _ in range(2)]
        qT = A.alloc([128, L], F32)
        ar1 = A.off
        FA = A.alloc([128, TOK], F32)
        FB = A.alloc([128, TOK], F32)
        FC = A.alloc([128, TOK], F32)
        ar2 = A.off
        KdT = A.alloc([128, TOK], BF16)
        Kd = A.alloc([128, NT, 128], BF16)
        US = A.alloc([128, NT, 128], F32)
        ar2e = A.off
        QdT = [A.alloc([128, L], BF16) for _ in range(2)]
        S16 = [A.alloc([128, NX, 128], BF16) for _ in range(2)]
        AT = [A.alloc([128, NX, 128], BF16) for _ in range(2)]
        V = A.alloc([128, NT, 128], BF16)
        SG = A.alloc([128, NX, 128], BF16)
        tmpo = A.alloc([128, 4, 128], F32)
        tmpq = A.alloc([128, 4, 128], F32)
        mtok = A.alloc([128, 4, 128], BF16)
        tab = A.alloc([128, 2, 6, NT], F32)
        mixer_hi = A.off
        A.off = ar1
        xt = [A.alloc([128, D], F32) for _ in range(2)]
        xs = [A.alloc([128, D], BF16) for _ in range(2)]
        x1t = A.alloc([128, D], F32)
        gB1 = A.alloc([128, D], F32)
        diag = A.alloc([128, 8, 128], F32)
        assert A.off <= ar2
        A.off = ar2
        woutb = A.alloc([128, 8, D], BF16)
        assert A.off <= ar2e
        A.off = mixer_hi

        BFA, BFB, BFC = Buf("FA"), Buf("FB"), Buf("FC")
        BKdT, BKd, BUS = Buf("KdT"), Buf("Kd"), Buf("US")
        BhxT, BmixT, BqT = Buf("hxT"), Buf("mixT"), Buf("qT")
        Bwfm = [Buf("wfm0"), Buf("wfm1")]
        Bwtm = [Buf("wtm0"), Buf("wtm1")]
        BQd = [Buf("Qd0"), Buf("Qd1")]
        BS16 = [Buf("S160"), Buf("S161")]
        BAT = [Buf("AT0"), Buf("AT1")]
        BV, BSG = Buf("V"), Buf("SG")
        Btmpo, Btmpq, Bmtok, Btab = Buf("tmpo"), Buf("tmpq"), Buf("mtok"), Buf("tab")
        BRT = Buf("RT")
        Bxt = [BFA, BFA]
        Bxs = [BFB, BFB]
        Bx1t = BFB
        BgB1 = BFC
        Bdiag = BFC
        Bwout = [BKdT, BKd, BUS]
        mixer_bufs0 = [BFA, BFB, BFC, BKdT, BKd, BUS, BhxT, BmixT, BqT] + Bwfm + Bwtm + BQd + BS16 + BAT + \
                     [BV, BSG, Btmpo, Btmpq, Bmtok, Btab, BRT, Bc]
        Bxt = [Buf("xt0"), Buf("xt1")]
        Bxs = [Buf("xs0"), Buf("xs1")]
        Bx1t = Buf("x1t")
        BgB1 = Buf("gB1")
        Bdiag = Buf("diag")
        allF = [BFA, BFB, BFC]
        scratchB = Bxt + Bxs + [Bx1t, BgB1, Bdiag]

        def phase_barrier():
            S.add("dve", lambda e: e.memset(sm[:, 61:62], 0.0), [], allF + scratchB + [Bsm])

        ckn = [0]

        def CKf(q):
            ckn[0] += 1
            return "c%s%d" % (q, ckn[0] % 5)
        for dst, src in ((identb[:], ident_d), (pmb[:], pm_d), (mfb[:], mf_d), (mbb[:], mb_d),
                         (cosb[:], cos_d), (sinb[:], sin_d)):
            dma("pool", dst, src, CKf("p"), [], [Bc])
        for dst, src in ((identf[:], ident_d), (cT[:], cT_d), (bmodP[:], bmodP_d.rearrange("p (g j) -> p g j", g=6)),
                         (n1g[:], n1g_d), (n2g[:], n2g_d),
                         (lbP[:], lbP_d.rearrange("p (d i h) -> p d i h", d=2, i=2)),
                         (rdt[:], rd_d.partition_broadcast(128)),
                         (cwP[:], cwP_d.rearrange("p (f j) -> p f j", j=3)), (cbP[:], cbP_d),
                         (posf[:], posf_d), (posb[:], posb_d),
                         (hgn[:], hgn_d.partition_broadcast(128)), (rtn[:], rtn_d.partition_broadcast(128))):
            dma("sp", dst, src, CKf("s"), [], [Bc])
        S.add("dve", lambda e: e.memset(onesf[:], 1.0), [], [Bc])
        S.add("dve", lambda e: e.memset(RM[:], 1.0), [], [Bc])
        S.add("dve", lambda e: e.memset(RM[:, :, 0:1], 0.0), [], [Bc])
        act(cs[:], cT[:].rearrange("p (k j) -> p k j", j=3), AF.Silu, [Bc], [Bmod])
        tt("dve", lbv[:], lbP[:, :, 0, :], lbP[:, :, 1, :], ALU.subtract, [Bc], [Bmod])
        act(lbv[:], lbv[:], AF.Sigmoid, [Bmod], [Bmod])
        ts("dve", oml[:], lbv[:], -1.0, 1.0, ALU.mult, ALU.add, [Bmod], [Bmod])
        act(lg[:], rdt[:], AF.Sigmoid, [Bc], [Bmod])
        act(lg[:], lg[:], AF.Ln, [Bmod], [Bmod])
        ts("dve", nlg[:], lg[:], -1.0, None, ALU.mult, None, [Bmod], [Bmod])
        act(g64[:], lg[:], AF.Exp, [Bmod], [Bmod], scale=64.0)
        act(g128[:], lg[:], AF.Exp, [Bmod], [Bmod], scale=128.0)

        A.off = ar1
        wmv = [A.alloc([128, 8, 1024], BF16)]
        A.off = ar2
        wmv.append(A.alloc([128, 8, 1024], BF16))
        A.off = mixer_hi
        Bwm = [allF, [BKdT, BKd, BUS]]
        wmod_v = wmod_d.rearrange("(k p) n -> p k n", p=128)
        for g in range(6):
            sl = g % 2
            dma("pool", wmv[sl][:], wmod_v[:, :, g * 1024:(g + 1) * 1024], "wm%d" % sl, [], Bwm[sl])
            bk, bb = nb()
            groups = []
            for j in range(8):
                groups.append((bk[:, j * 4:j * 4 + 3],
                               [(wmv[sl][:, k, j * 128:(j + 1) * 128], cs[:, k, :]) for k in range(8)]))
            mm_multi(groups, Bwm[sl] + [Bmod], [bb])
            tt("dve", modP[:, g], bk[:, 0:32].rearrange("p (j c) -> p j c", c=4)[:, :, 0:3],
               bmodP[:, g, :].unsqueeze(2).to_broadcast([128, 8, 3]), ALU.add, [bb, Bc], [Bmod])
        for g, ng in ((1, n1g), (4, n2g)):
            ts("dve", modP[:, g], modP[:, g], 1.0, None, ALU.add, None, [Bmod], [Bmod])
            tt("dve", modP[:, g], modP[:, g], ng[:].unsqueeze(2).to_broadcast([128, 8, 3]), ALU.mult,
               [Bmod, Bc], [Bmod])

        mark("setup", lambda: [("modP", modP[:].rearrange("p a b c -> p (a b c)"), [Bmod]), ("lbv", lbv[:].rearrange("p a b -> p (a b)"), [Bmod]),
                               ("lg", lg[:], [Bmod]), ("g64", g64[:], [Bmod])])
        def norm_T(src_ap, src_key_bufs, slot, gA, gB_, col, dstT, dstB, c0, extra_reads=(), from_dram=True,
                   keep=None):
            dma("sp", xt[slot][:], src_ap, "xt%d" % slot, list(src_key_bufs), [Bxt[slot]])
            S.add("dve", lambda e: e.memset(sm[:, slot:slot + 1], 0.0), [], [Bsm])
            act(xs[slot][:], xt[slot][:], AF.Square, [Bxt[slot]], [Bxs[slot], Bsm], accum_out=sm[:, slot:slot + 1])
            ts("dve", sm[:, 2 + slot:3 + slot], sm[:, slot:slot + 1], 1.0 / D, EPS, ALU.mult, ALU.add, [Bsm], [Bsm])
            act(sm[:, 4 + slot:5 + slot], sm[:, 2 + slot:3 + slot], AF.Ln, [Bsm], [Bsm])
            act(sm[:, 6 + slot:7 + slot], sm[:, 4 + slot:5 + slot], AF.Exp, [Bsm], [Bsm], scale=-0.5)
            ts("pool" if slot else "dve", xs[slot][:], xt[slot][:], sm[:, 6 + slot:7 + slot], None, ALU.mult, None,
               [Bxt[slot], Bsm], [Bxs[slot]])
            transposes([(PT[:, j * 128:(j + 1) * 128], xs[slot][:, j * 128:(j + 1) * 128]) for j in range(8)],
                       [Bxs[slot]], [PTB])
            tmp = diag
            tt("dve", tmp[:], PT.rearrange("p (j t) -> p j t", j=8),
               modP[:, gA, :, col:col + 1].to_broadcast([128, 8, 128]), ALU.mult, [PTB, Bmod], [Bdiag])
            tt("pool", dstT[:, :, c0:c0 + 128], tmp[:],
               modP[:, gB_, :, col:col + 1].to_broadcast([128, 8, 128]), ALU.add, [Bdiag, Bmod] + list(extra_reads),
               [dstB])

        def fm_proj(wv, wB, c, t0, n):
            bk, bb = nb()
            mmg(bk[:, 0:n], [(wv[:, k, c * 128:(c + 1) * 128], hxT[:, k, t0:t0 + n]) for k in range(8)],
                [wB, BhxT], [bb])
            return bk, bb

        def gla_dir(d):
            a1 = tab[:, d, 0, :]
            a2 = tab[:, d, 1, :]
            Dc = tab[:, d, 2, :]
            for c0 in range(0, NT, 8):
                n = min(8, NT - c0)
                transposes([(PT[:, a * 128:(a + 1) * 128], KdT[:, (c0 + a) * 128:(c0 + a + 1) * 128]) for a in range(n)],
                           [BKdT], [PTB])
                cp("act", Kd[:].rearrange("p a k -> p (a k)")[:, c0 * 128:(c0 + n) * 128], PT[:, 0:n * 128], [PTB], [BKd])
            skip = NT - 1 if d == 0 else 2
            for c0 in range(0, NT, 4):
                cl = [c for c in range(c0, min(c0 + 4, NT))]
                bk, bb = nb()
                mm_multi([(bk[:, (c - c0) * 128:(c - c0 + 1) * 128], [(Kd[:, c, :], V[:, c, :])]) for c in cl],
                         [BKd, BV], [bb])
                n = len(cl)
                tt("dve", US[:, c0:c0 + n, :], bk[:, 0:n * 128].rearrange("p (a v) -> p a v", a=n),
                   a1[:, c0:c0 + n].unsqueeze(2).to_broadcast([128, n, 128]), ALU.mult, [bb, Btab], [BUS])
            order = list(range(NT)) if d == 0 else [1, 0] + list(range(NT - 1, 1, -1))
            for j in range(1, NT - 1):
                c, pc = order[j], order[j - 1]
                stt(US[:, c, :], US[:, pc, :], Dc[:, c:c + 1], US[:, c, :], ALU.mult, ALU.add, [BUS, Btab], [BUS])
            if d == 0:
                tt("pool", S16[d][:], US[:, 1:NT - 1, :], a2[:, 2:NT].unsqueeze(2).to_broadcast([128, NX, 128]),
                   ALU.mult, [BUS, Btab], [BS16[d]])
            else:
                tt("pool", S16[d][:, 0:NX - 1, :], US[:, 3:NT, :],
                   a2[:, 2:NT - 1].unsqueeze(2).to_broadcast([128, NX - 1, 128]), ALU.mult, [BUS, Btab], [BS16[d]])
                tt("pool", S16[d][:, NX - 1, :], US[:, 0, :], a2[:, NT - 1:NT].to_broadcast([128, 128]),
                   ALU.mult, [BUS, Btab], [BS16[d]])
            mask = mfb if d == 0 else mbb
            for x0 in range(0, NX, 4):
                bk, bb = nb()
                mm_multi([(bk[:, a * 128:(a + 1) * 128],
                           [(KdT[:, (x0 + a + 2) * 128:(x0 + a + 3) * 128], QdT[d][:, (x0 + a) * 128:(x0 + a + 1) * 128])])
                          for a in range(4)], [BKdT, BQd[d]], [bb])
                tt("dve", AT[d][:, x0:x0 + 4, :], bk[:].rearrange("p (a t) -> p a t", a=4),
                   mask[:].unsqueeze(1).to_broadcast([128, 4, 128]), ALU.mult, [bb, Bc], [BAT[d]])

        def gla_out(hd, is_ret, h):
            gain = rtn if is_ret else hgn
            for x0 in range(0, NX, 4):
                bk, bb = nb()
                groups = []
                for a in range(4):
                    xi = x0 + a
                    pairs = []
                    for d in range(2):
                        pairs.append((AT[d][:, xi, :], V[:, xi + 2, :]))
                        pairs.append((QdT[d][:, xi * 128:(xi + 1) * 128], S16[d][:, xi, :]))
                    groups.append((bk[:, a * 128:(a + 1) * 128], pairs))
                mm_multi(groups, BAT + BQd + BS16 + [BV], [bb])
                o3 = bk[:].rearrange("p (a v) -> p a v", a=4)
                if is_ret:
                    cp("act", tmpo[:].rearrange("p a v -> p (a v)"), bk[:], [bb], [Btmpo])
                else:
                    tt("dve", tmpo[:], o3, SG[:, x0:x0 + 4, :], ALU.mult, [bb, BSG], [Btmpo])
                tt("pool", tmpq[:], tmpo[:], tmpo[:], ALU.mult, [Btmpo], [Btmpq])
                S.add("dve", lambda e: e.reduce_sum(out=sm[:, 8:12], in_=tmpq[:], axis=AX.X), [Btmpq], [Bsm])
                ts("dve", sm[:, 12:16], sm[:, 8:12], 1.0 / 128, EPS, ALU.mult, ALU.add, [Bsm], [Bsm])
                act(sm[:, 16:20], sm[:, 12:16], AF.Ln, [Bsm], [Bsm])
                act(sm[:, 20:24], sm[:, 16:20], AF.Exp, [Bsm], [Bsm], scale=-0.5)
                tt("dve", tmpo[:], tmpo[:], sm[:, 20:24].unsqueeze(2).to_broadcast([128, 4, 128]), ALU.mult,
                   [Btmpo, Bsm], [Btmpo])
                gb = gain[:, h * 128:(h + 1) * 128].unsqueeze(1).to_broadcast([128, 4, 128])
                if is_ret:
                    tt("pool", tmpq[:], tmpo[:], gb, ALU.mult, [Btmpo, Bc], [Btmpq])
                    tt("pool", mtok[:], tmpq[:], SG[:, x0:x0 + 4, :], ALU.mult, [Btmpq, BSG], [Bmtok])
                else:
                    tt("pool", mtok[:], tmpo[:], gb, ALU.mult, [Btmpo, Bc], [Bmtok])
                transposes([(PT[:, a * 128:(a + 1) * 128], mtok[:, a, :]) for a in range(4)], [Bmtok], [PTB])
                cp("act", mixT[:, hd, x0 * 128:(x0 + 4) * 128], PT[:, 0:512], [PTB], [BmixT])

        def tm_proj(sl, is_ret):
            for i0 in range(0, NT, 2):
                bk, bb = nb()
                n = 128 if i0 < 2 else 256
                mm_multi([(bk[:, a * 256:a * 256 + n],
                           [(hxT[:, k, (i0 + a) * 128:(i0 + a + 1) * 128], wtm[sl][:, k, 0:n]) for k in range(8)])
                          for a in range(2)], [BhxT, Bwtm[sl]], [bb])
                b3 = bk[:].rearrange("p (a c) -> p a c", a=2)
                cp("dve", V[:, i0:i0 + 2, :], b3[:, :, 0:128], [bb], [BV])
                if i0 >= 2:
                    for a in range(2):
                        act(SG[:, i0 - 2 + a, :], bk[:, a * 256 + 128:a * 256 + 256], AF.Silu if is_ret else AF.Sigmoid,
                            [bb], [BSG])

        def load_head_weights(sl, fm_cols, tm_cols):
            since = len(S.ops)
            for i, c in enumerate(fm_cols):
                dma("pool", wfm[sl][:, :, i * 128:(i + 1) * 128], win_v[:, :, c:c + 128], "wf%d_%d" % (sl, i), [], [Bwfm[sl]])
            for i, c in enumerate(tm_cols):
                dma("pool", wtm[sl][:, :, i * 128:(i + 1) * 128], win_v[:, :, c:c + 128], "wt%d_%d" % (sl, i), [], [Bwtm[sl]])

        win_v = win_d.rearrange("(k p) n -> p k n", p=128)
        FA3 = FA[:].rearrange("p (c t) -> p c t", t=128)
        FB3 = FB[:].rearrange("p (c t) -> p c t", t=128)
        FC3 = FC[:].rearrange("p (c t) -> p c t", t=128)
        TOKBLK = [(0, 256)] + [(256 + i * 512, 512) for i in range(4)]
        XBLK = [(256 + i * 512, 512) for i in range(4)]

        def hgrn_head(b, h, sl):
            for (t0, n) in XBLK:
                bk, bb = fm_proj(wfm[sl], Bwfm[sl], 0, t0, n)
                cp("act", qT[:, t0 - 256:t0 - 256 + n], bk[:, 0:n], [bb], [BqT])
            tm_proj(sl, False)
            for d in range(2):
                a1, a2, Dc, mid, tot, tmp = (tab[:, d, i, :] for i in range(6))
                for (t0, n) in TOKBLK:
                    bk, bb = fm_proj(wfm[sl], Bwfm[sl], 1 + d, t0, n)
                    act(FA[:, t0:t0 + n], bk[:, 0:n], AF.Sigmoid, [bb], [BFA])
                ts("dve", FA[:], FA[:], oml[:, d, h:h + 1], lbv[:, d, h:h + 1], ALU.mult, ALU.add, [BFA, Bmod], [BFA])
                act(FB[:], FA[:], AF.Ln, [BFA], [BFB])
                ts("pool", FA[:], FA[:], -1.0, 1.0, ALU.mult, ALU.add, [BFA], [BFA])
                S.add("dve", lambda e: e.tensor_tensor_scan(out=FC[:], data0=RM[:].rearrange("p c t -> p (c t)"),
                                                              data1=FB[:], initial=0.0, op0=ALU.mult, op1=ALU.add),
                      [BFB, Bc], [BFC])
                cp("dve", tot, FC3[:, :, 127], [BFC], [Btab])
                act(Dc, tot, AF.Exp, [Btab], [Btab])
                if d == 0:
                    cp("dve", mid, FC3[:, :, 63], [BFC], [Btab])
                    act(a2, mid, AF.Exp, [Btab], [Btab])
                    tt("dve", tmp, tot, mid, ALU.subtract, [Btab], [Btab])
                    act(a1, tmp, AF.Exp, [Btab], [Btab])
                else:
                    tt("dve", FC[:], FC[:], FB[:], ALU.subtract, [BFC, BFB], [BFC])
                    cp("dve", mid, FC3[:, :, 64], [BFC], [Btab])
                    act(a1, mid, AF.Exp, [Btab], [Btab])
                    tt("dve", tmp, tot, mid, ALU.subtract, [Btab], [Btab])
                    act(a2, tmp, AF.Exp, [Btab], [Btab])
                tt("dve", FC3, FC3, mid.unsqueeze(2).to_broadcast([128, NT, 128]), ALU.subtract, [BFC, Btab], [BFC])
                sq, sk = (1.0, -1.0) if d == 0 else (-1.0, 1.0)
                act(FB[:, 256:TOK], FC[:, 256:TOK], AF.Exp, [BFC], [BFB], scale=sq)
                tt("pool", QdT[d][:], qT[:], FB[:, 256:TOK], ALU.mult, [BqT, BFB], [BQd[d]])
                act(FB[:], FC[:], AF.Exp, [BFC], [BFB], scale=sk)
                tt("pool", KdT[:], FA[:], FB[:], ALU.mult, [BFA, BFB], [BKdT])
                gla_dir(d)
            gla_out(h, False, h)

        def ret_head(b, h, sl):
            lnsc = float(np.log(128.0 ** -0.5))
            act(RT[:, 0, :], posf[:], AF.Exp, [Bc, Bmod], [BRT], scale=lg[:, h:h + 1])
            act(RT[:, 1, :], posf[:], AF.Exp, [Bc, Bmod], [BRT], scale=nlg[:, h:h + 1])
            act(RT[:, 2, :], posb[:], AF.Exp, [Bc, Bmod], [BRT], scale=lg[:, 4 + h:5 + h])
            act(RT[:, 3, :], posb[:], AF.Exp, [Bc, Bmod], [BRT], scale=nlg[:, 4 + h:5 + h])
            for kd in (1, 3):
                ts("dve", RT[:, kd, :], RT[:, kd, :], 128.0 ** -0.5, None, ALU.mult, None, [BRT], [BRT])
            for d in range(2):
                for i, src in ((0, g64), (1, g64), (2, g128)):
                    cp("dve", tab[:, d, i, :], src[:, d * 4 + h:d * 4 + h + 1].to_broadcast([128, NT]), [Bmod], [Btab])
            tm_proj(sl, True)
            for which in range(2):
                dst, dB = (qT, BqT) if which == 0 else (FA, BFA)
                blks = XBLK if which == 0 else TOKBLK
                for (t0, n) in blks:
                    o0 = t0 - 256 if which == 0 else t0
                    bk, bb = fm_proj(wfm[sl], Bwfm[sl], which, t0, n)
                    if t0 < 256:
                        cp("act", dst[:, o0:o0 + n], bk[:, 0:n], [bb], [dB])
                        continue
                    xo = t0 - 256
                    cp("act", FB[:, 0:n].bitcast(BF16)[:, 0:n], bk[:, 0:n], [bb], [BFB])
                    tt("dve", FC[:, 0:n], bk[:, 0:n], cosb[:, xo:xo + n], ALU.mult, [bb, Bc], [BFC])
                    bk2, bb2 = nb()
                    mmg(bk2[:, 0:n], [(pmb[:], FB[:, 0:n].bitcast(BF16)[:, 0:n])], [BFB, Bc], [bb2])
                    tt("dve", FC[:, 512:512 + n], bk2[:, 0:n], sinb[:, xo:xo + n], ALU.mult, [bb2, Bc], [BFC])
                    tt("pool", dst[:, o0:o0 + n], FC[:, 0:n], FC[:, 512:512 + n], ALU.add, [BFC], [dB])
            for d in range(2):
                tt("pool", QdT[d][:].rearrange("p (c t) -> p c t", t=128), qT[:].rearrange("p (c t) -> p c t", t=128),
                   RT[:, 2 * d, :].unsqueeze(1).to_broadcast([128, NX, 128]), ALU.mult, [BqT, BRT], [BQd[d]])
                tt("pool", KdT[:].rearrange("p (c t) -> p c t", t=128), FA3,
                   RT[:, 2 * d + 1, :].unsqueeze(1).to_broadcast([128, NT, 128]), ALU.mult, [BFA, BRT], [BKdT])
                gla_dir(d)
            gla_out(4 + h, True, h)

        HG_COLS = lambda h: ([h * 128, 1024 + h * 128, 1536 + h * 128], [512 + h * 128, 2048 + h * 128])
        RT_COLS = lambda h: ([2560 + h * 128, 3072 + h * 128], [3584 + h * 128, 4096 + h * 128])
        wout_v = wout_d.rearrange("(k p) n -> p k n", p=128)
        Bx1d = [[Buf("x1d%d_%d" % (b, i)) for i in range(NX)] for b in range(2)]

        heads = [(False, h) for h in range(4)] + [(True, h) for h in range(4)]
        for b in range(2):
            phase_barrier()
            for i in range(NT):
                slot = i % 2
                if i < 2:
                    src = ctx_d[b, i * 128:(i + 1) * 128, :]
                    col = 2
                else:
                    src = x_d[b, (i - 2) * 128:(i - 1) * 128, :]
                    col = b
                norm_T(src, [], slot, 1, 0, col, hxT, BhxT, i * 128)
            mark("P1b%d" % b, lambda: [("hxT0", hxT[:, 0, :], [BhxT]), ("hxT7", hxT[:, 7, :], [BhxT])])
            phase_barrier()
            for hi, (is_ret, h) in enumerate(heads):
                sl = hi % 2
                fm_cols, tm_cols = RT_COLS(h) if is_ret else HG_COLS(h)
                load_head_weights(sl, fm_cols, tm_cols)
                if is_ret:
                    ret_head(b, h, sl)
                else:
                    hgrn_head(b, h, sl)
                mark("head%d_%d" % (b, hi), lambda: [("mixT", mixT[:, hi, :], [BmixT]), ("QdT0", QdT[0][:], [BQd[0]]),
                                                     ("QdT1", QdT[1][:], [BQd[1]]), ("KdT", KdT[:], [BKdT]),
                                                     ("V", V[:].rearrange("p a b -> p (a b)"), [BV]),
                                                     ("tab", tab[:].rearrange("p a b c -> p (a b c)"), [Btab])])
            phase_barrier()
            dma("pool", woutb[:], wout_v, "wout", [], Bwout)
            for j in range(8):
                ts("dve", diag[:, j, :], identf[:], modP[:, 2, j, b:b + 1], None, ALU.mult, None, [Bc, Bmod], [Bdiag])
            for n in range(2):
                bk, bb = nb()
                mmg(bk[:], [(onesf[:], diag[:, 4 * n:4 * n + 4, :].rearrange("p j q -> p (j q)"))], [Bdiag, Bc], [bb])
                cp("act", gB1[:, n * 512:(n + 1) * 512], bk[:], [bb], [BgB1])
            for xi in range(NX):
                slot = xi % 2
                dma("sp", xt[slot][:], x_d[b, xi * 128:(xi + 1) * 128, :], "xt%d" % slot, [], [Bxt[slot]])
                for n in range(2):
                    bk, bb = nb()
                    mmg(bk[:], [(mixT[:, hd, xi * 128:(xi + 1) * 128], woutb[:, hd, n * 512:(n + 1) * 512])
                                for hd in range(8)], [BmixT] + Bwout, [bb])
                    tt("dve", x1t[:, n * 512:(n + 1) * 512], bk[:], gB1[:, n * 512:(n + 1) * 512], ALU.mult,
                       [bb, BgB1], [Bx1t])
                tt("pool", x1t[:], x1t[:], xt[slot][:], ALU.add, [Bx1t, Bxt[slot]], [Bx1t])
                dma("sp", x1_d[b, xi * 128:(xi + 1) * 128, :], x1t[:], "x1st", [Bx1t], [Bx1d[b][xi]])

        A.off = base_off
        fgn = A.alloc([128, D], F32)
        gB2 = A.alloc([128, D], F32)
        wupb = A.alloc([128, 8, 2 * DFF], BF16)
        wdnb = A.alloc([128, NFC, D], BF16)
        fxt = [A.alloc([128, D], F32) for _ in range(2)]
        fxs = A.alloc([128, D], BF16)
        h2T = [A.alloc([128, 8, BLK + 2], BF16) for _ in range(3)]
        hT = A.alloc([128, NFC, BLK], BF16)
        NEW = 3
        gbufs = [A.alloc([128, BLK + 2], F32) for _ in range(NEW)]
        t1s = [A.alloc([128, BLK], F32) for _ in range(NEW)]
        t2s = [A.alloc([128, BLK], F32) for _ in range(NEW)]
        hbs = [A.alloc([128, BLK], F32) for _ in range(NEW)]
        fdiag = A.alloc([128, 8, 128], F32)
        x2t = [A.alloc([128, D], F32) for _ in range(2)]
        outt = A.alloc([128, D], F32)
        Bfgn, BgB2, Bwup, Bwdn = Buf("fgn"), Buf("gB2"), Buf("wup"), Buf("wdn")
        Bfxt = [Buf("fxt0"), Buf("fxt1")]
        Bfxs = Buf("fxs")
        Bh2T = [Buf("h2T%d" % i) for i in range(3)]
        BhT, Bfdiag = Buf("hT"), Buf("fdiag")
        Bgbufs = [Buf("gbuf%d" % i) for i in range(NEW)]
        Bt1s = [Buf("t1%d" % i) for i in range(NEW)]
        Bt2s = [Buf("t2%d" % i) for i in range(NEW)]
        Bhbs = [Buf("hb%d" % i) for i in range(NEW)]
        build_program.ffn_end = None
        Bx2t = [Buf("x2t0"), Buf("x2t1")]
        Boutt = Buf("outt")
        ffn_bufs = [Bfgn, BgB2, Bwup, Bwdn, Bfxs, BhT, Bfdiag, Boutt] + Bfxt + Bh2T + Bx2t + Bgbufs + Bt1s + Bt2s + Bhbs
        build_program.ffn_end = A.off
        S.add("dve", lambda e: e.memset(sm[:, 60:61], 0.0), [], mixer_bufs0 + scratchB + ffn_bufs + [Bsm])

        wup_v = wup_d.rearrange("(k p) n -> p k n", p=128)
        wdn_v = wdn_d.rearrange("(f p) n -> p f n", p=128)
        UG = [(g * 512, min(512, DFF - g * 512)) for g in range(6)]
        Bwup_g = [Buf("wupg%d" % g) for g in range(6)]
        Bwdn_g = [Buf("wdng%d" % g) for g in range(4)]
        for g, (c0, n) in enumerate(UG):
            dma("pool", wupb[:, :, c0:c0 + n], wup_v[:, :, c0:c0 + n], "wupa%d" % g, [Bwup], [Bwup_g[g]])
            dma("pool", wupb[:, :, DFF + c0:DFF + c0 + n], wup_v[:, :, DFF + c0:DFF + c0 + n], "wupb%d" % g,
                [Bwup], [Bwup_g[g]])
        for gi, f0 in enumerate(range(0, NFC, 6)):
            f1 = min(NFC, f0 + 6)
            dma("pool", wdnb[:, f0:f1, :], wdn_v[:, f0:f1, :], "wdn%d" % f0, [Bwdn], [Bwdn_g[gi]])
        dma("sp", fgn[:], fgn_d.partition_broadcast(128), "fgn", [], [Bfgn])

        NB = L // BLK
        out_ops = []

        def ffn_A(b, j):
            g = b * NB + j
            sl = g % 3
            for a in range(BLK // 128):
                xi = j * (BLK // 128) + a
                s2 = (g * 2 + a) % 2
                dma("sp", fxt[s2][:], x1_d[b, xi * 128:(xi + 1) * 128, :], "fxt%d" % s2, [Bx1d[b][xi]], [Bfxt[s2]])
                S.add("dve", lambda e: e.memset(sm[:, 30:31], 0.0), [], [Bsm])
                act(fxs[:], fxt[s2][:], AF.Square, [Bfxt[s2]], [Bfxs, Bsm], accum_out=sm[:, 30:31])
                ts("dve", sm[:, 31:32], sm[:, 30:31], 1.0 / D, EPS, ALU.mult, ALU.add, [Bsm], [Bsm])
                act(sm[:, 32:33], sm[:, 31:32], AF.Ln, [Bsm], [Bsm])
                act(sm[:, 33:34], sm[:, 32:33], AF.Exp, [Bsm], [Bsm], scale=-0.5)
                ts("dve", fxs[:], fxt[s2][:], sm[:, 33:34], None, ALU.mult, None, [Bfxt[s2], Bsm], [Bfxs])
                transposes([(PT[:, jj * 128:(jj + 1) * 128], fxs[:, jj * 128:(jj + 1) * 128]) for jj in range(8)],
                           [Bfxs], [PTB])
                tt("dve", fdiag[:], PT.rearrange("p (j t) -> p j t", j=8),
                   modP[:, 4, :, b:b + 1].to_broadcast([128, 8, 128]), ALU.mult, [PTB, Bmod], [Bfdiag])
                tt("pool", h2T[sl][:, :, 1 + a * 128:1 + (a + 1) * 128], fdiag[:],
                   modP[:, 3, :, b:b + 1].to_broadcast([128, 8, 128]), ALU.add, [Bfdiag, Bmod], [Bh2T[sl]])
            if j == 0:
                S.add("pool", lambda e: e.memset(h2T[sl][:, :, 0:1], 0.0), [], [Bh2T[sl]])
            else:
                sp_ = (g - 1) % 3
                cp("pool", h2T[sl][:, :, 0:1], h2T[sp_][:, :, BLK:BLK + 1], [Bh2T[sp_]], [Bh2T[sl]])
                cp("pool", h2T[sp_][:, :, BLK + 1:BLK + 2], h2T[sl][:, :, 1:2], [Bh2T[sl]], [Bh2T[sp_]])
            if j == NB - 1:
                S.add("pool", lambda e: e.memset(h2T[sl][:, :, BLK + 1:BLK + 2], 0.0), [], [Bh2T[sl]])

        def ffn_F(b, j):
            g = b * NB + j
            sl = g % 3
            if j == 0:
                for jj in range(8):
                    ts("dve", fdiag[:, jj, :], identf[:], modP[:, 5, jj, b:b + 1], None, ALU.mult, None,
                       [Bc, Bmod], [Bfdiag])
                for n in range(2):
                    bk, bb = nb()
                    mmg(bk[:], [(onesf[:], fdiag[:, 4 * n:4 * n + 4, :].rearrange("p j q -> p (j q)"))],
                        [Bfdiag, Bc], [bb])
                    cp("act", gB2[:, n * 512:(n + 1) * 512], bk[:], [bb], [BgB2])
            for fc in range(NFC):
                ei = fc % NEW
                gbuf, t1, t2, hb = gbufs[ei], t1s[ei], t2s[ei], hbs[ei]
                Bgbuf, Bt1, Bt2, Bhb = Bgbufs[ei], Bt1s[ei], Bt2s[ei], Bhbs[ei]
                bg, bbg = nb()
                mmg(bg[:, 0:BLK + 2], [(wupb[:, k, fc * 128:(fc + 1) * 128], h2T[sl][:, k, :]) for k in range(8)],
                    [Bwup_g[fc // 4], Bh2T[sl]], [bbg])
                bu, bbu = nb()
                mmg(bu[:, 0:BLK], [(wupb[:, k, DFF + fc * 128:DFF + (fc + 1) * 128], h2T[sl][:, k, 1:BLK + 1])
                                   for k in range(8)], [Bwup_g[fc // 4], Bh2T[sl]], [bbu])
                cp("act", gbuf[:], bg[:, 0:BLK + 2], [bbg], [Bgbuf])
                ts("pool", t1[:], gbuf[:, 1:BLK + 1], cwP[:, fc, 1:2], cbP[:, fc:fc + 1], ALU.mult, ALU.add,
                   [Bgbuf, Bc], [Bt1])
                stt(t2[:], gbuf[:, 0:BLK], cwP[:, fc, 0:1], t1[:], ALU.mult, ALU.add, [Bgbuf, Bt1, Bc], [Bt2])
                stt(t1[:], gbuf[:, 2:BLK + 2], cwP[:, fc, 2:3], t2[:], ALU.mult, ALU.add, [Bgbuf, Bt2, Bc], [Bt1])
                act(hb[:], t1[:], AF.Silu, [Bt1], [Bhb])
                tt("dve", hT[:, fc, :], hb[:], bu[:, 0:BLK], ALU.mult, [Bhb, bbu], [BhT])
            for a in range(BLK // 128):
                xi = j * (BLK // 128) + a
                s2 = (g * 2 + a) % 2
                dma("sp", x2t[s2][:], x1_d[b, xi * 128:(xi + 1) * 128, :], "x2t%d" % s2, [Bx1d[b][xi]], [Bx2t[s2]])
                for n in range(2):
                    bk, bb = nb()
                    mmg(bk[:], [(hT[:, fc, a * 128:(a + 1) * 128], wdnb[:, fc, n * 512:(n + 1) * 512])
                                for fc in range(NFC)], [BhT] + Bwdn_g, [bb])
                    tt("dve", outt[:, n * 512:(n + 1) * 512], bk[:], gB2[:, n * 512:(n + 1) * 512], ALU.mult,
                       [bb, BgB2], [Boutt])
                tt("pool", x2t[s2][:], x2t[s2][:], outt[:], ALU.add, [Bx2t[s2], Boutt], [Bx2t[s2]])
                S.add("dve", lambda e: e.memset(sm[:, 40:41], 0.0), [], [Bsm])
                act(outt[:], x2t[s2][:], AF.Square, [Bx2t[s2]], [Boutt, Bsm], accum_out=sm[:, 40:41])
                ts("dve", sm[:, 41:42], sm[:, 40:41], 1.0 / D, EPS, ALU.mult, ALU.add, [Bsm], [Bsm])
                act(sm[:, 42:43], sm[:, 41:42], AF.Ln, [Bsm], [Bsm])
                act(sm[:, 43:44], sm[:, 42:43], AF.Exp, [Bsm], [Bsm], scale=-0.5)
                stt(outt[:], x2t[s2][:], sm[:, 43:44], fgn[:], ALU.mult, ALU.mult, [Bx2t[s2], Bsm, Bfgn], [Boutt])
                out_ops.append(dma("sp", out_d[b, xi * 128:(xi + 1) * 128, :], outt[:], "outst", [Boutt], []))

        seq = [(b, j) for b in range(2) for j in range(NB)]
        ffn_A(*seq[0])
        for i, (b, j) in enumerate(seq):
            if i + 1 < len(seq):
                ffn_A(*seq[i + 1])
            ffn_F(b, j)

        fw = out_ops[-4:] + out_ops[:1]
        if stage["stopped"]:
            last = {}
            for i, o in enumerate(S.ops):
                if o.is_dma:
                    last[o.key] = i
            fw = list(last.values())
        S.emit(final_wait_ops=fw)
        build_program.stats = (len(S.ops), A.hi)
        build_program.dbg_names = stage["names"]
    return nc


def _consts():
    k = np.arange(128)
    ident = np.eye(128, dtype=np.float32)
    swap = np.where((k % 64) < 32, k + 32, k - 32)
    pm = np.zeros((128, 128), np.float32)
    pm[swap, k] = 1.0
    s = np.arange(128)[:, None]
    t = np.arange(128)[None, :]
    mf = (s <= t).astype(np.float32)
    mb = (s >= t).astype(np.float32)
    tt_ = np.arange(L, dtype=np.float32)
    rows = np.floor(tt_ / 64.0)
    cols = tt_ - rows * 64.0
    quarter = 32
    freqs = (10000.0 ** (-np.arange(quarter, dtype=np.float32) / quarter)).astype(np.float32)
    cos = np.zeros((128, L), np.float32)
    sin = np.zeros((128, L), np.float32)
    for kk in range(128):
        pos = rows if kk < 64 else cols
        i = kk % 32
        ang = (pos * freqs[i]).astype(np.float32)
        cos[kk] = np.cos(ang)
        sin[kk] = -np.sin(ang) if (kk % 64) < 32 else np.sin(ang)
    posf = np.tile((np.arange(128, dtype=np.float32) - 63.0)[None, :], (128, 1))
    posb = np.tile((64.0 - np.arange(128, dtype=np.float32))[None, :], (128, 1))
    return dict(ident=ident, pm=pm, mf=mf, mb=mb, cos=cos, sin=sin, posf=posf, posb=posb)


def make_in_maps(x, c, ctx, c_ctx, w_mod, b_mod, norm1_g, w_in, hgrn_lb, hgrn_norm_g, ret_decay, ret_norm_g,
                 w_out, norm2_g, w_up, conv_w, conv_b, w_down, final_g):
    f = lambda a: np.ascontiguousarray(np.asarray(a, dtype=np.float32))
    cst = _consts()
    pl = lambda v: f(np.asarray(v).reshape(-1, 128).T)
    shared = dict(
        w_mod=f(w_mod[0]), bmodP=pl(b_mod[0]), n1g=pl(norm1_g[0]), n2g=pl(norm2_g[0]), w_in=f(w_in[0]),
        lbP=f(np.asarray(hgrn_lb).reshape(2, 2, 4, 128).transpose(3, 0, 1, 2).reshape(128, 16)),
        hgn=f(hgrn_norm_g[0]).reshape(1, 512), rd=f(ret_decay[0]).reshape(1, 8), rtn=f(ret_norm_g[0]).reshape(1, 512),
        w_out=f(w_out[0]), w_up=f(w_up[0]),
        cwP=f(np.asarray(conv_w[0]).reshape(3, NFC, 128).transpose(2, 1, 0).reshape(128, 66)),
        cbP=pl(conv_b[0]), w_down=f(w_down[0]), fgn=f(final_g).reshape(1, D), **cst)
    maps = []
    for core in range(NCORES):
        cc = np.stack([np.asarray(c[2 * core]), np.asarray(c[2 * core + 1]), np.asarray(c_ctx)], axis=0)
        cT = f(cc.reshape(3, 8, 128).transpose(2, 1, 0).reshape(128, 24))
        m = dict(shared)
        m.update(x=f(x[2 * core:2 * core + 2]), ctx=f(ctx[2 * core:2 * core + 2]), cT=cT)
        maps.append(m)
    return maps


_NC_CACHE = {}


def kernel(**inputs):
    if "nc" not in _NC_CACHE:
        _NC_CACHE["nc"] = build_program()
    nc = _NC_CACHE["nc"]
    in_maps = make_in_maps(**inputs)
    res = run_bass_kernel_spmd(nc, in_maps, core_ids=list(range(NCORES)))
    out = np.concatenate([np.asarray(r["out"]) for r in res.results], axis=0)
    return out.astype(np.float32)
```

```python
import contextlib
import numpy as np
import ml_dtypes
import concourse.bass as bass
import concourse.mybir as mybir
from concourse.bass_utils import run_bass_kernel_spmd

F32 = mybir.dt.float32
BF16 = mybir.dt.bfloat16
AF = mybir.ActivationFunctionType
ALU = mybir.AluOpType
AX = mybir.AxisListType

D = 1024
L = 2048
LC = 256
TOK = L + LC
NT = TOK // 128
NX = L // 128
DFF = 2816
NFC = DFF // 128
INW = 4608
EPS = 1e-6
BLK = 256
NCORES = 8
REORDER = True


class Buf:
    __slots__ = ("name", "w", "r", "excl")

    def __init__(self, name, excl=False):
        self.name = name
        self.w = None
        self.r = []
        self.excl = excl


class Op:
    __slots__ = ("eng", "fn", "deps", "sig", "needs_sig", "is_dma", "idx", "key", "cost", "xfer", "t0", "t1")


class Sched:
    COMPUTE = ("pe", "act", "dve", "pool")

    def __init__(self, nc):
        self.nc = nc
        self.ops = []
        self.dma_count = {}
        self.last_dma = {}

    def add(self, eng, fn, reads=(), writes=(), dma=None, cost=300.0, xfer=0.0):
        op = Op()
        op.cost = float(cost)
        op.xfer = float(xfer)
        op.eng = eng
        op.fn = fn
        op.idx = len(self.ops)
        op.is_dma = dma is not None
        op.needs_sig = op.is_dma
        op.key = dma
        writes = list(writes) + [b for b in reads if b.excl]
        reads = [b for b in reads if not b.excl]
        deps = {}
        for b in reads:
            if b.w is not None:
                deps[b.w] = True
        for b in writes:
            if b.w is not None:
                deps.setdefault(b.w, False)
            for r in b.r:
                deps.setdefault(r, False)
        op.deps = []
        for j, raw in deps.items():
            if j == op.idx:
                continue
            op.deps.append(j)
        for b in reads:
            b.r.append(op.idx)
        for b in writes:
            b.w = op.idx
            b.r = []
        if op.is_dma:
            prev = self.last_dma.get(dma)
            if prev is not None and prev not in op.deps:
                op.deps.append(prev)
            self.last_dma[dma] = op.idx
            c = self.dma_count.get(dma, 0) + 16
            self.dma_count[dma] = c
            op.sig = (dma, c)
        else:
            op.sig = None
        self.ops.append(op)
        return op.idx

    def seal_group(self, key, since=0):
        if key not in self.dma_count:
            return
        tot = self.dma_count[key]
        for op in self.ops[since:]:
            if op.is_dma and op.key == key:
                op.sig = (key, tot)

    def reorder(self):
        import heapq
        ops = self.ops
        n = len(ops)
        users = [[] for _ in range(n)]
        ndep = [0] * n
        for op in ops:
            ds = set(op.deps)
            op.deps = sorted(ds)
            ndep[op.idx] = len(op.deps)
            for j in op.deps:
                users[j].append(op.idx)
        engs = ("pe", "act", "dve", "pool", "sp")
        free = {e: 0.0 for e in engs}
        byready = {e: [] for e in engs}
        now = {e: [] for e in engs}
        ready_t = [0.0] * n
        fin = [0.0] * n
        dma_free = [0.0]
        for op in ops:
            if ndep[op.idx] == 0:
                heapq.heappush(byready[op.eng], (0.0, op.idx))
        order = []
        LAT = 200.0
        while len(order) < n:
            best = None
            for e in engs:
                br, nw = byready[e], now[e]
                while br and br[0][0] <= free[e]:
                    heapq.heappush(nw, heapq.heappop(br)[1])
                if nw:
                    est = free[e]
                elif br:
                    est = br[0][0]
                else:
                    continue
                if best is None or est < best[0]:
                    best = (est, e)
            est, e = best
            if now[e]:
                i = heapq.heappop(now[e])
            else:
                i = heapq.heappop(byready[e])[1]
            op = ops[i]
            op.t0 = est
            free[e] = est + op.cost
            if op.is_dma:
                st = max(est + op.cost, dma_free[0])
                fin[i] = st + op.xfer
                dma_free[0] = st + op.xfer * 0.6
            else:
                fin[i] = est + op.cost
            op.t1 = fin[i]
            order.append(i)
            for u in users[i]:
                ready_t[u] = max(ready_t[u], fin[i] + LAT)
                ndep[u] -= 1
                if ndep[u] == 0:
                    heapq.heappush(byready[ops[u].eng], (ready_t[u], u))
        newidx = {old: new for new, old in enumerate(order)}
        newops = [ops[i] for i in order]
        for op in newops:
            op.deps = [newidx[j] for j in op.deps]
            op.idx = newidx[op.idx]
        self.ops = newops
        self.est_total = max(fin) if fin else 0.0
        return newidx

    def emit(self, final_wait_ops=()):
        nc = self.nc
        for op in self.ops:
            kept = []
            for j in op.deps:
                o = self.ops[j]
                if o.eng == op.eng and op.eng == "pe" and not o.is_dma and not op.is_dma:
                    continue
                kept.append(j)
                o.needs_sig = True
            op.deps = kept
        with contextlib.ExitStack() as st:
            sems = {}
            for e in self.COMPUTE:
                sems[e] = st.enter_context(nc.semaphore("s_" + e))
            for k in self.dma_count:
                sems[k] = st.enter_context(nc.semaphore("d_" + str(k)))
            cnt = {e: 0 for e in self.COMPUTE}
            for op in self.ops:
                if not op.is_dma and op.needs_sig:
                    cnt[op.eng] += 1
                    op.sig = (op.eng, cnt[op.eng])
            block = st.enter_context(nc.Block())
            ops = self.ops

            def run(engname, e):
                waited = {}
                for op in ops:
                    if op.eng != engname:
                        continue
                    for j in op.deps:
                        k, v = ops[j].sig
                        if waited.get(k, 0) < v:
                            e.wait_ge(sems[k], v)
                            waited[k] = v
                    ins = op.fn(e)
                    if op.needs_sig:
                        ins.then_inc(sems[op.sig[0]], 16 if op.is_dma else 1)
                if engname == "sp":
                    for j in final_wait_ops:
                        k, v = ops[j].sig
                        if waited.get(k, 0) < v:
                            e.wait_ge(sems[k], v)
                            waited[k] = v

            @block.tensor
            def _(e):
                run("pe", e)

            @block.scalar
            def _(e):
                run("act", e)

            @block.vector
            def _(e):
                run("dve", e)

            @block.gpsimd
            def _(e):
                run("pool", e)

            @block.sync
            def _(e):
                run("sp", e)


class Arena:
    def __init__(self, big, limit):
        self.big = big
        self.off = 0
        self.limit = limit
        self.hi = 0

    def _view(self, ap, shape):
        if len(shape) == 2:
            return ap
        if len(shape) == 3:
            return ap.rearrange("p (a b) -> p a b", a=shape[1])
        return ap.rearrange("p (a b c) -> p a b c", a=shape[1], b=shape[2])

    def alloc(self, shape, dt):
        n = int(np.prod(shape[1:]))
        nb = n * (4 if dt == F32 else 2)
        nb = (nb + 3) // 4 * 4
        o = self.off
        self.off += nb
        self.hi = max(self.hi, self.off)
        assert self.off <= self.limit, ("SBUF arena overflow", self.off)
        ap = self.big[:, o // 4:(o + nb) // 4]
        if dt != F32:
            ap = ap.bitcast(BF16)[:, 0:n]
        return self._view(ap, shape)


def build_program(debug=False, stop=None):
    nc = bass.Bass("TRN2", target_bir_lowering=False)

    def din(name, shape, dt=F32):
        return nc.dram_tensor(name, list(shape), dt, kind="ExternalInput").ap()

    x_d = din("x", [2, L, D])
    ctx_d = din("ctx", [2, LC, D])
    cT_d = din("cT", [128, 24])
    wmod_d = din("w_mod", [D, 6 * D])
    bmodP_d = din("bmodP", [128, 48])
    n1g_d = din("n1g", [128, 8])
    n2g_d = din("n2g", [128, 8])
    win_d = din("w_in", [D, INW])
    lbP_d = din("lbP", [128, 16])
    hgn_d = din("hgn", [1, 512])
    rd_d = din("rd", [1, 8])
    rtn_d = din("rtn", [1, 512])
    wout_d = din("w_out", [D, D])
    wup_d = din("w_up", [D, 2 * DFF])
    cwP_d = din("cwP", [128, 66])
    cbP_d = din("cbP", [128, 22])
    wdn_d = din("w_down", [DFF, D])
    fgn_d = din("fgn", [1, D])
    ident_d = din("ident", [128, 128])
    pm_d = din("pm", [128, 128])
    mf_d = din("mf", [128, 128])
    mb_d = din("mb", [128, 128])
    cos_d = din("cos", [128, L])
    sin_d = din("sin", [128, L])
    posf_d = din("posf", [128, 128])
    posb_d = din("posb", [128, 128])
    out_d = nc.dram_tensor("out", [2, L, D], F32, kind="ExternalOutput").ap()
    x1_d = nc.dram_tensor("x1s", [2, L, D], F32,
                          kind="ExternalOutput" if debug else "Internal").ap()

    st = contextlib.ExitStack()
    with st:
        LIMIT = 212000
        big = st.enter_context(nc.sbuf_tensor("big", [128, LIMIT // 4], F32))
        banks = [st.enter_context(nc.psum_tensor("bank%d" % i, [128, 512], F32)) for i in range(8)]
        bankB = [Buf("bank%d" % i, excl=True) for i in range(8)]
        PT = banks[7][:].bitcast(BF16)
        PTB = bankB[7]
        S = Sched(nc)
        A = Arena(big, LIMIT)
        dbg_d = nc.dram_tensor("dbg", [128, 16384], F32, kind="ExternalOutput").ap() if debug else None
        stage = {"n": 0, "off": 0, "stopped": False, "names": []}
        _add = S.add

        def gated_add(*a, **k):
            if stage["stopped"]:
                return 0
            if isinstance(stop, int) and len(S.ops) >= stop:
                stage["stopped"] = True
                return 0
            return _add(*a, **k)
        S.add = gated_add

        def dump(name, ap2d, bufs):
            n = ap2d.shape[1]
            o = stage["off"]
            stage["off"] += n
            stage["names"].append((name, o, n))
            _add("pool", lambda e: e.dma_start(out=dbg_d[:, o:o + n], in_=ap2d), list(bufs), [], dma="dbg%d" % len(stage["names"]))

        def mark(name, dumps=()):
            if stage["stopped"]:
                return
            if stop is not None and name == stop:
                for nm, ap, bufs in dumps():
                    dump(nm, ap, bufs)
                stage["stopped"] = True

        rot = [0]

        def nb():
            i = rot[0]
            rot[0] = (i + 1) % 7
            return banks[i], bankB[i]

        def nel(ap):
            n = 1
            for d in ap.shape[1:]:
                n *= int(d)
            return n

        def ecost(eng, n):
            if eng == "act":
                return n * 0.83 + 300.0
            if eng == "dve":
                return n * 1.04 + 170.0
            return n * 1.7 + 350.0

        def act(out, in_, func, reads, writes, **kw):
            return S.add("act", lambda e: e.activation(out=out, in_=in_, func=func, **kw), reads, writes,
                         cost=ecost("act", nel(out)))

        def ts(eng, out, in0, s1, s2, op0, op1, reads, writes):
            c = ecost(eng, nel(out))
            if s2 is None:
                return S.add(eng, lambda e: e.tensor_scalar(out=out, in0=in0, scalar1=s1, scalar2=None, op0=op0),
                             reads, writes, cost=c)
            return S.add(eng, lambda e: e.tensor_scalar(out=out, in0=in0, scalar1=s1, scalar2=s2, op0=op0, op1=op1),
                         reads, writes, cost=c)

        def tt(eng, out, in0, in1, op, reads, writes):
            return S.add(eng, lambda e: e.tensor_tensor(out=out, in0=in0, in1=in1, op=op), reads, writes,
                         cost=ecost(eng, nel(out)))

        def stt(out, in0, sc, in1, op0, op1, reads, writes):
            return S.add("dve", lambda e: e.scalar_tensor_tensor(out=out, in0=in0, scalar=sc, in1=in1, op0=op0, op1=op1),
                         reads, writes, cost=ecost("dve", nel(out)))

        def cp(eng, out, in_, reads, writes):
            if eng == "act":
                return act(out, in_, AF.Copy, reads, writes)
            return S.add(eng, lambda e: e.tensor_copy(out=out, in_=in_), reads, writes, cost=ecost(eng, nel(out)))

        def dma(eng, out, in_, key, reads, writes):
            nbytes = 128 * nel(out) * 4
            if eng == "pool":
                return S.add(eng, lambda e: e.dma_start(out=out, in_=in_), reads, writes, dma=key,
                             cost=1500.0, xfer=nbytes / 130.0 + 2000.0)
            return S.add(eng, lambda e: e.dma_start(out=out, in_=in_), reads, writes, dma=key,
                         cost=250.0, xfer=nbytes / 200.0 + 2000.0)

        def mmcost(pairs):
            c = 0.0
            for l, r in pairs:
                c += max(64, nel(r)) / 2.2 + 35.0
            return c

        def mmg(out, pairs, reads, writes):
            pairs = list(pairs)

            def fn(e):
                n = len(pairs)
                ins = None
                for i, (l, r) in enumerate(pairs):
                    ins = e.matmul(out, lhsT=l, rhs=r, start=(i == 0), stop=(i == n - 1))
                return ins
            return S.add("pe", fn, reads, writes, cost=mmcost(pairs))

        def mm_multi(groups, reads, writes):
            groups = [(o, list(p)) for o, p in groups]

            def fn(e):
                ins = None
                for out, pairs in groups:
                    n = len(pairs)
                    for i, (l, r) in enumerate(pairs):
                        ins = e.matmul(out, lhsT=l, rhs=r, start=(i == 0), stop=(i == n - 1))
                return ins
            return S.add("pe", fn, reads, writes, cost=sum(mmcost(p) for _, p in groups))

        def transposes(items, reads, writes):
            items = list(items)

            def fn(e):
                ins = None
                for o, i_ in items:
                    ins = e.transpose(out=o, in_=i_, identity=identb)
                return ins
            return S.add("pe", fn, reads + [Bc], writes, cost=110.0 * len(items))

        identb = A.alloc([128, 128], BF16)
        pmb = A.alloc([128, 128], BF16)
        mfb = A.alloc([128, 128], BF16)
        mbb = A.alloc([128, 128], BF16)
        identf = A.alloc([128, 128], F32)
        onesf = A.alloc([128, 128], F32)
        cT = A.alloc([128, 24], F32)
        cs = A.alloc([128, 8, 3], BF16)
        bmodP = A.alloc([128, 6, 8], F32)
        n1g = A.alloc([128, 8], F32)
        n2g = A.alloc([128, 8], F32)
        modP = A.alloc([128, 6, 8, 3], F32)
        lbP = A.alloc([128, 2, 2, 4], F32)
        lbv = A.alloc([128, 2, 4], F32)
        oml = A.alloc([128, 2, 4], F32)
        rdt = A.alloc([128, 8], F32)
        lg = A.alloc([128, 8], F32)
        nlg = A.alloc([128, 8], F32)
        g64 = A.alloc([128, 8], F32)
        g128 = A.alloc([128, 8], F32)
        cwP = A.alloc([128, 22, 3], F32)
        cbP = A.alloc([128, 22], F32)
        sm = A.alloc([128, 64], F32)
        Bc = Buf("consts")
        Bmod = Buf("modP")
        Bsm = Buf("sm")
        base_off = A.off

        cosb = A.alloc([128, L], BF16)
        sinb = A.alloc([128, L], BF16)
        posf = A.alloc([128, 128], F32)
        posb = A.alloc([128, 128], F32)
        RM = A.alloc([128, NT, 128], BF16)
        RT = A.alloc([128, 4, 128], F32)
        hgn = A.alloc([128, 512], F32)
        rtn = A.alloc([128, 512], F32)
        hxT = A.alloc([128, 8, TOK], BF16)
        mixT = A.alloc([128, 8, L], BF16)
        wfm = [A.alloc([128, 8, 384], BF16) for _ in range(2)]
        wtm = [A.alloc([128, 8, 256], BF16) for _ in range(2)]
        qT = A.alloc([128, L], F32)
        ar1 = A.off
        FA = A.alloc([128, TOK], F32)
        FB = A.alloc([128, TOK], F32)
        FC = A.alloc([128, TOK], F32)
        ar2 = A.off
        KdT = A.alloc([128, TOK], BF16)
        Kd = A.alloc([128, NT, 128], BF16)
        US = A.alloc([128, NT, 128], F32)
        ar2e = A.off
        QdT = [A.alloc([128, L], BF16) for _ in range(2)]
        S16 = [A.alloc([128, NX, 128], BF16) for _ in range(2)]
        AT = [A.alloc([128, NX, 128], BF16) for _ in range(2)]
        V = A.alloc([128, NT, 128], BF16)
        SG = A.alloc([128, NX, 128], BF16)
        tmpo = A.alloc([128, 4, 128], F32)
        tmpq = A.alloc([128, 4, 128], F32)
        mtok = A.alloc([128, 4, 128], BF16)
        tab = A.alloc([128, 2, 6, NT], F32)
        mixer_hi = A.off
        A.off = ar1
        xt = [A.alloc([128, D], F32) for _ in range(2)]
        xs = [A.alloc([128, D], BF16) for _ in range(2)]
        x1t = A.alloc([128, D], F32)
        gB1 = A.alloc([128, D], F32)
        diag = A.alloc([128, 8, 128], F32)
        assert A.off <= ar2
        A.off = ar2
        woutb = A.alloc([128, 8, D], BF16)
        assert A.off <= ar2e
        A.off = mixer_hi

        BFA, BFB, BFC = Buf("FA"), Buf("FB"), Buf("FC")
        BKdT, BKd, BUS = Buf("KdT"), Buf("Kd"), Buf("US")
        BhxT, BmixT, BqT = Buf("hxT"), Buf("mixT"), Buf("qT")
        Bwfm = [Buf("wfm0"), Buf("wfm1")]
        Bwtm = [Buf("wtm0"), Buf("wtm1")]
        BQd = [Buf("Qd0"), Buf("Qd1")]
        BS16 = [Buf("S160"), Buf("S161")]
        BAT = [Buf("AT0"), Buf("AT1")]
        BV, BSG = Buf("V"), Buf("SG")
        Btmpo, Btmpq, Bmtok, Btab = Buf("tmpo"), Buf("tmpq"), Buf("mtok"), Buf("tab")
        BRT = Buf("RT")
        Bxt = [BFA, BFA]
        Bxs = [BFB, BFB]
        Bx1t = BFB
        BgB1 = BFC
        Bdiag = BFC
        Bwout = [BKdT, BKd, BUS]
        mixer_bufs0 = [BFA, BFB, BFC, BKdT, BKd, BUS, BhxT, BmixT, BqT] + Bwfm + Bwtm + BQd + BS16 + BAT + \
                     [BV, BSG, Btmpo, Btmpq, Bmtok, Btab, BRT, Bc]
        Bxt = [Buf("xt0"), Buf("xt1")]
        Bxs = [Buf("xs0"), Buf("xs1")]
        Bx1t = Buf("x1t")
        BgB1 = Buf("gB1")
        Bdiag = Buf("diag")
        allF = [BFA, BFB, BFC]
        scratchB = Bxt + Bxs + [Bx1t, BgB1, Bdiag]

        def phase_barrier():
            S.add("dve", lambda e: e.memset(sm[:, 61:62], 0.0), [], allF + scratchB + [Bsm])

        ckn = [0]

        def CKf(q):
            ckn[0] += 1
            return "c%s%d" % (q, ckn[0] % 5)
        for dst, src in ((identb[:], ident_d), (pmb[:], pm_d), (mfb[:], mf_d), (mbb[:], mb_d),
                         (cosb[:], cos_d), (sinb[:], sin_d)):
            dma("pool", dst, src, CKf("p"), [], [Bc])
        for dst, src in ((identf[:], ident_d), (cT[:], cT_d), (bmodP[:], bmodP_d.rearrange("p (g j) -> p g j", g=6)),
                         (n1g[:], n1g_d), (n2g[:], n2g_d),
                         (lbP[:], lbP_d.rearrange("p (d i h) -> p d i h", d=2, i=2)),
                         (rdt[:], rd_d.partition_broadcast(128)),
                         (cwP[:], cwP_d.rearrange("p (f j) -> p f j", j=3)), (cbP[:], cbP_d),
                         (posf[:], posf_d), (posb[:], posb_d),
                         (hgn[:], hgn_d.partition_broadcast(128)), (rtn[:], rtn_d.partition_broadcast(128))):
            dma("sp", dst, src, CKf("s"), [], [Bc])
        S.add("dve", lambda e: e.memset(onesf[:], 1.0), [], [Bc])
        S.add("dve", lambda e: e.memset(RM[:], 1.0), [], [Bc])
        S.add("dve", lambda e: e.memset(RM[:, :, 0:1], 0.0), [], [Bc])
        act(cs[:], cT[:].rearrange("p (k j) -> p k j", j=3), AF.Silu, [Bc], [Bmod])
        tt("dve", lbv[:], lbP[:, :, 0, :], lbP[:, :, 1, :], ALU.subtract, [Bc], [Bmod])
        act(lbv[:], lbv[:], AF.Sigmoid, [Bmod], [Bmod])
        ts("dve", oml[:], lbv[:], -1.0, 1.0, ALU.mult, ALU.add, [Bmod], [Bmod])
        act(lg[:], rdt[:], AF.Sigmoid, [Bc], [Bmod])
        act(lg[:], lg[:], AF.Ln, [Bmod], [Bmod])
        ts("dve", nlg[:], lg[:], -1.0, None, ALU.mult, None, [Bmod], [Bmod])
        act(g64[:], lg[:], AF.Exp, [Bmod], [Bmod], scale=64.0)
        act(g128[:], lg[:], AF.Exp, [Bmod], [Bmod], scale=128.0)

        A.off = ar1
        wmv = [A.alloc([128, 8, 1024], BF16)]
        A.off = ar2
        wmv.append(A.alloc([128, 8, 1024], BF16))
        A.off = mixer_hi
        Bwm = [allF, [BKdT, BKd, BUS]]
        wmod_v = wmod_d.rearrange("(k p) n -> p k n", p=128)
        for g in range(6):
            sl = g % 2
            dma("pool", wmv[sl][:], wmod_v[:, :, g * 1024:(g + 1) * 1024], "wm%d" % sl, [], Bwm[sl])
            bk, bb = nb()
            groups = []
            for j in range(8):
                groups.append((bk[:, j * 4:j * 4 + 3],
                               [(wmv[sl][:, k, j * 128:(j + 1) * 128], cs[:, k, :]) for k in range(8)]))
            mm_multi(groups, Bwm[sl] + [Bmod], [bb])
            tt("dve", modP[:, g], bk[:, 0:32].rearrange("p (j c) -> p j c", c=4)[:, :, 0:3],
               bmodP[:, g, :].unsqueeze(2).to_broadcast([128, 8, 3]), ALU.add, [bb, Bc], [Bmod])
        for g, ng in ((1, n1g), (4, n2g)):
            ts("dve", modP[:, g], modP[:, g], 1.0, None, ALU.add, None, [Bmod], [Bmod])
            tt("dve", modP[:, g], modP[:, g], ng[:].unsqueeze(2).to_broadcast([128, 8, 3]), ALU.mult,
               [Bmod, Bc], [Bmod])

        mark("setup", lambda: [("modP", modP[:].rearrange("p a b c -> p (a b c)"), [Bmod]), ("lbv", lbv[:].rearrange("p a b -> p (a b)"), [Bmod]),
                               ("lg", lg[:], [Bmod]), ("g64", g64[:], [Bmod])])
        def norm_T(src_ap, src_key_bufs, slot, gA, gB_, col, dstT, dstB, c0, extra_reads=(), from_dram=True,
                   keep=None):
            dma("sp", xt[slot][:], src_ap, "xt%d" % slot, list(src_key_bufs), [Bxt[slot]])
            S.add("dve", lambda e: e.memset(sm[:, slot:slot + 1], 0.0), [], [Bsm])
            act(xs[slot][:], xt[slot][:], AF.Square, [Bxt[slot]], [Bxs[slot], Bsm], accum_out=sm[:, slot:slot + 1])
            ts("dve", sm[:, 2 + slot:3 + slot], sm[:, slot:slot + 1], 1.0 / D, EPS, ALU.mult, ALU.add, [Bsm], [Bsm])
            act(sm[:, 4 + slot:5 + slot], sm[:, 2 + slot:3 + slot], AF.Ln, [Bsm], [Bsm])
            act(sm[:, 6 + slot:7 + slot], sm[:, 4 + slot:5 + slot], AF.Exp, [Bsm], [Bsm], scale=-0.5)
            ts("pool" if slot else "dve", xs[slot][:], xt[slot][:], sm[:, 6 + slot:7 + slot], None, ALU.mult, None,
               [Bxt[slot], Bsm], [Bxs[slot]])
            transposes([(PT[:, j * 128:(j + 1) * 128], xs[slot][:, j * 128:(j + 1) * 128]) for j in range(8)],
                       [Bxs[slot]], [PTB])
            tmp = diag
            tt("dve", tmp[:], PT.rearrange("p (j t) -> p j t", j=8),
               modP[:, gA, :, col:col + 1].to_broadcast([128, 8, 128]), ALU.mult, [PTB, Bmod], [Bdiag])
            tt("pool", dstT[:, :, c0:c0 + 128], tmp[:],
               modP[:, gB_, :, col:col + 1].to_broadcast([128, 8, 128]), ALU.add, [Bdiag, Bmod] + list(extra_reads),
               [dstB])

        def fm_proj(wv, wB, c, t0, n):
            bk, bb = nb()
            mmg(bk[:, 0:n], [(wv[:, k, c * 128:(c + 1) * 128], hxT[:, k, t0:t0 + n]) for k in range(8)],
                [wB, BhxT], [bb])
            return bk, bb

        def gla_dir(d):
            a1 = tab[:, d, 0, :]
            a2 = tab[:, d, 1, :]
            Dc = tab[:, d, 2, :]
            for c0 in range(0, NT, 8):
                n = min(8, NT - c0)
                transposes([(PT[:, a * 128:(a + 1) * 128], KdT[:, (c0 + a) * 128:(c0 + a + 1) * 128]) for a in range(n)],
                           [BKdT], [PTB])
                cp("act", Kd[:].rearrange("p a k -> p (a k)")[:, c0 * 128:(c0 + n) * 128], PT[:, 0:n * 128], [PTB], [BKd])
            skip = NT - 1 if d == 0 else 2
            for c0 in range(0, NT, 4):
                cl = [c for c in range(c0, min(c0 + 4, NT))]
                bk, bb = nb()
                mm_multi([(bk[:, (c - c0) * 128:(c - c0 + 1) * 128], [(Kd[:, c, :], V[:, c, :])]) for c in cl],
                         [BKd, BV], [bb])
                n = len(cl)
                tt("dve", US[:, c0:c0 + n, :], bk[:, 0:n * 128].rearrange("p (a v) -> p a v", a=n),
                   a1[:, c0:c0 + n].unsqueeze(2).to_broadcast([128, n, 128]), ALU.mult, [bb, Btab], [BUS])
            order = list(range(NT)) if d == 0 else [1, 0] + list(range(NT - 1, 1, -1))
            for j in range(1, NT - 1):
                c, pc = order[j], order[j - 1]
                stt(US[:, c, :], US[:, pc, :], Dc[:, c:c + 1], US[:, c, :], ALU.mult, ALU.add, [BUS, Btab], [BUS])
            if d == 0:
                tt("pool", S16[d][:], US[:, 1:NT - 1, :], a2[:, 2:NT].unsqueeze(2).to_broadcast([128, NX, 128]),
                   ALU.mult, [BUS, Btab], [BS16[d]])
            else:
                tt("pool", S16[d][:, 0:NX - 1, :], US[:, 3:NT, :],
                   a2[:, 2:NT - 1].unsqueeze(2).to_broadcast([128, NX - 1, 128]), ALU.mult, [BUS, Btab], [BS16[d]])
                tt("pool", S16[d][:, NX - 1, :], US[:, 0, :], a2[:, NT - 1:NT].to_broadcast([128, 128]),
                   ALU.mult, [BUS, Btab], [BS16[d]])
            mask = mfb if d == 0 else mbb
            for x0 in range(0, NX, 4):
                bk, bb = nb()
                mm_multi([(bk[:, a * 128:(a + 1) * 128],
                           [(KdT[:, (x0 + a + 2) * 128:(x0 + a + 3) * 128], QdT[d][:, (x0 + a) * 128:(x0 + a + 1) * 128])])
                          for a in range(4)], [BKdT, BQd[d]], [bb])
                tt("dve", AT[d][:, x0:x0 + 4, :], bk[:].rearrange("p (a t) -> p a t", a=4),
                   mask[:].unsqueeze(1).to_broadcast([128, 4, 128]), ALU.mult, [bb, Bc], [BAT[d]])

        def gla_out(hd, is_ret, h):
            gain = rtn if is_ret else hgn
            for x0 in range(0, NX, 4):
                bk, bb = nb()
                groups = []
                for a in range(4):
                    xi = x0 + a
                    pairs = []
                    for d in range(2):
                        pairs.append((AT[d][:, xi, :], V[:, xi + 2, :]))
                        pairs.append((QdT[d][:, xi * 128:(xi + 1) * 128], S16[d][:, xi, :]))
                    groups.append((bk[:, a * 128:(a + 1) * 128], pairs))
                mm_multi(groups, BAT + BQd + BS16 + [BV], [bb])
                o3 = bk[:].rearrange("p (a v) -> p a v", a=4)
                if is_ret:
                    cp("act", tmpo[:].rearrange("p a v -> p (a v)"), bk[:], [bb], [Btmpo])
                else:
                    tt("dve", tmpo[:], o3, SG[:, x0:x0 + 4, :], ALU.mult, [bb, BSG], [Btmpo])
                tt("pool", tmpq[:], tmpo[:], tmpo[:], ALU.mult, [Btmpo], [Btmpq])
                S.add("dve", lambda e: e.reduce_sum(out=sm[:, 8:12], in_=tmpq[:], axis=AX.X), [Btmpq], [Bsm], cost=700.0)
                ts("dve", sm[:, 12:16], sm[:, 8:12], 1.0 / 128, EPS, ALU.mult, ALU.add, [Bsm], [Bsm])
                act(sm[:, 16:20], sm[:, 12:16], AF.Ln, [Bsm], [Bsm])
                act(sm[:, 20:24], sm[:, 16:20], AF.Exp, [Bsm], [Bsm], scale=-0.5)
                tt("dve", tmpo[:], tmpo[:], sm[:, 20:24].unsqueeze(2).to_broadcast([128, 4, 128]), ALU.mult,
                   [Btmpo, Bsm], [Btmpo])
                gb = gain[:, h * 128:(h + 1) * 128].unsqueeze(1).to_broadcast([128, 4, 128])
                if is_ret:
                    tt("pool", tmpq[:], tmpo[:], gb, ALU.mult, [Btmpo, Bc], [Btmpq])
                    tt("pool", mtok[:], tmpq[:], SG[:, x0:x0 + 4, :], ALU.mult, [Btmpq, BSG], [Bmtok])
                else:
                    tt("pool", mtok[:], tmpo[:], gb, ALU.mult, [Btmpo, Bc], [Bmtok])
                transposes([(PT[:, a * 128:(a + 1) * 128], mtok[:, a, :]) for a in range(4)], [Bmtok], [PTB])
                cp("act", mixT[:, hd, x0 * 128:(x0 + 4) * 128], PT[:, 0:512], [PTB], [BmixT])

        def tm_proj(sl, is_ret):
            for i0 in range(0, NT, 2):
                bk, bb = nb()
                n = 128 if i0 < 2 else 256
                mm_multi([(bk[:, a * 256:a * 256 + n],
                           [(hxT[:, k, (i0 + a) * 128:(i0 + a + 1) * 128], wtm[sl][:, k, 0:n]) for k in range(8)])
                          for a in range(2)], [BhxT, Bwtm[sl]], [bb])
                b3 = bk[:].rearrange("p (a c) -> p a c", a=2)
                cp("dve", V[:, i0:i0 + 2, :], b3[:, :, 0:128], [bb], [BV])
                if i0 >= 2:
                    for a in range(2):
                        act(SG[:, i0 - 2 + a, :], bk[:, a * 256 + 128:a * 256 + 256], AF.Silu if is_ret else AF.Sigmoid,
                            [bb], [BSG])

        def load_head_weights(sl, fm_cols, tm_cols):
            since = len(S.ops)
            for i, c in enumerate(fm_cols):
                dma("pool", wfm[sl][:, :, i * 128:(i + 1) * 128], win_v[:, :, c:c + 128], "wf%d_%d" % (sl, i), [], [Bwfm[sl]])
            for i, c in enumerate(tm_cols):
                dma("pool", wtm[sl][:, :, i * 128:(i + 1) * 128], win_v[:, :, c:c + 128], "wt%d_%d" % (sl, i), [], [Bwtm[sl]])

        win_v = win_d.rearrange("(k p) n -> p k n", p=128)
        FA3 = FA[:].rearrange("p (c t) -> p c t", t=128)
        FB3 = FB[:].rearrange("p (c t) -> p c t", t=128)
        FC3 = FC[:].rearrange("p (c t) -> p c t", t=128)
        TOKBLK = [(0, 256)] + [(256 + i * 512, 512) for i in range(4)]
        XBLK = [(256 + i * 512, 512) for i in range(4)]

        def hgrn_head(b, h, sl):
            for (t0, n) in XBLK:
                bk, bb = fm_proj(wfm[sl], Bwfm[sl], 0, t0, n)
                cp("act", qT[:, t0 - 256:t0 - 256 + n], bk[:, 0:n], [bb], [BqT])
            tm_proj(sl, False)
            for d in range(2):
                a1, a2, Dc, mid, tot, tmp = (tab[:, d, i, :] for i in range(6))
                for (t0, n) in TOKBLK:
                    bk, bb = fm_proj(wfm[sl], Bwfm[sl], 1 + d, t0, n)
                    act(FA[:, t0:t0 + n], bk[:, 0:n], AF.Sigmoid, [bb], [BFA])
                ts("dve", FA[:], FA[:], oml[:, d, h:h + 1], lbv[:, d, h:h + 1], ALU.mult, ALU.add, [BFA, Bmod], [BFA])
                act(FB[:], FA[:], AF.Ln, [BFA], [BFB])
                ts("pool", FA[:], FA[:], -1.0, 1.0, ALU.mult, ALU.add, [BFA], [BFA])
                S.add("dve", lambda e: e.tensor_tensor_scan(out=FC[:], data0=RM[:].rearrange("p c t -> p (c t)"),
                                                              data1=FB[:], initial=0.0, op0=ALU.mult, op1=ALU.add),
                      [BFB, Bc], [BFC], cost=5000.0)
                cp("dve", tot, FC3[:, :, 127], [BFC], [Btab])
                act(Dc, tot, AF.Exp, [Btab], [Btab])
                if d == 0:
                    cp("dve", mid, FC3[:, :, 63], [BFC], [Btab])
                    act(a2, mid, AF.Exp, [Btab], [Btab])
                    tt("dve", tmp, tot, mid, ALU.subtract, [Btab], [Btab])
                    act(a1, tmp, AF.Exp, [Btab], [Btab])
                else:
                    tt("dve", FC[:], FC[:], FB[:], ALU.subtract, [BFC, BFB], [BFC])
                    cp("dve", mid, FC3[:, :, 64], [BFC], [Btab])
                    act(a1, mid, AF.Exp, [Btab], [Btab])
                    tt("dve", tmp, tot, mid, ALU.subtract, [Btab], [Btab])
                    act(a2, tmp, AF.Exp, [Btab], [Btab])
                tt("dve", FC3, FC3, mid.unsqueeze(2).to_broadcast([128, NT, 128]), ALU.subtract, [BFC, Btab], [BFC])
                sq, sk = (1.0, -1.0) if d == 0 else (-1.0, 1.0)
                act(FB[:, 256:TOK], FC[:, 256:TOK], AF.Exp, [BFC], [BFB], scale=sq)
                tt("pool", QdT[d][:], qT[:], FB[:, 256:TOK], ALU.mult, [BqT, BFB], [BQd[d]])
                act(FB[:], FC[:], AF.Exp, [BFC], [BFB], scale=sk)
                tt("pool", KdT[:], FA[:], FB[:], ALU.mult, [BFA, BFB], [BKdT])
                gla_dir(d)
            gla_out(h, False, h)

        def ret_head(b, h, sl):
            lnsc = float(np.log(128.0 ** -0.5))
            act(RT[:, 0, :], posf[:], AF.Exp, [Bc, Bmod], [BRT], scale=lg[:, h:h + 1])
            act(RT[:, 1, :], posf[:], AF.Exp, [Bc, Bmod], [BRT], scale=nlg[:, h:h + 1])
            act(RT[:, 2, :], posb[:], AF.Exp, [Bc, Bmod], [BRT], scale=lg[:, 4 + h:5 + h])
            act(RT[:, 3, :], posb[:], AF.Exp, [Bc, Bmod], [BRT], scale=nlg[:, 4 + h:5 + h])
            for kd in (1, 3):
                ts("dve", RT[:, kd, :], RT[:, kd, :], 128.0 ** -0.5, None, ALU.mult, None, [BRT], [BRT])
            for d in range(2):
                for i, src in ((0, g64), (1, g64), (2, g128)):
                    cp("dve", tab[:, d, i, :], src[:, d * 4 + h:d * 4 + h + 1].to_broadcast([128, NT]), [Bmod], [Btab])
            tm_proj(sl, True)
            for which in range(2):
                dst, dB = (qT, BqT) if which == 0 else (FA, BFA)
                blks = XBLK if which == 0 else TOKBLK
                for (t0, n) in blks:
                    o0 = t0 - 256 if which == 0 else t0
                    bk, bb = fm_proj(wfm[sl], Bwfm[sl], which, t0, n)
                    if t0 < 256:
                        cp("act", dst[:, o0:o0 + n], bk[:, 0:n], [bb], [dB])
                        continue
                    xo = t0 - 256
                    cp("act", FB[:, 0:n].bitcast(BF16)[:, 0:n], bk[:, 0:n], [bb], [BFB])
                    tt("dve", FC[:, 0:n], bk[:, 0:n], cosb[:, xo:xo + n], ALU.mult, [bb, Bc], [BFC])
                    bk2, bb2 = nb()
                    mmg(bk2[:, 0:n], [(pmb[:], FB[:, 0:n].bitcast(BF16)[:, 0:n])], [BFB, Bc], [bb2])
                    tt("dve", FC[:, 512:512 + n], bk2[:, 0:n], sinb[:, xo:xo + n], ALU.mult, [bb2, Bc], [BFC])
                    tt("pool", dst[:, o0:o0 + n], FC[:, 0:n], FC[:, 512:512 + n], ALU.add, [BFC], [dB])
            for d in range(2):
                tt("pool", QdT[d][:].rearrange("p (c t) -> p c t", t=128), qT[:].rearrange("p (c t) -> p c t", t=128),
                   RT[:, 2 * d, :].unsqueeze(1).to_broadcast([128, NX, 128]), ALU.mult, [BqT, BRT], [BQd[d]])
                tt("pool", KdT[:].rearrange("p (c t) -> p c t", t=128), FA3,
                   RT[:, 2 * d + 1, :].unsqueeze(1).to_broadcast([128, NT, 128]), ALU.mult, [BFA, BRT], [BKdT])
                gla_dir(d)
            gla_out(4 + h, True, h)

        HG_COLS = lambda h: ([h * 128, 1024 + h * 128, 1536 + h * 128], [512 + h * 128, 2048 + h * 128])
        RT_COLS = lambda h: ([2560 + h * 128, 3072 + h * 128], [3584 + h * 128, 4096 + h * 128])
        wout_v = wout_d.rearrange("(k p) n -> p k n", p=128)
        Bx1d = [[Buf("x1d%d_%d" % (b, i)) for i in range(NX)] for b in range(2)]

        heads = [(False, h) for h in range(4)] + [(True, h) for h in range(4)]
        for b in range(2):
            phase_barrier()
            for i in range(NT):
                slot = i % 2
                if i < 2:
                    src = ctx_d[b, i * 128:(i + 1) * 128, :]
                    col = 2
                else:
                    src = x_d[b, (i - 2) * 128:(i - 1) * 128, :]
                    col = b
                norm_T(src, [], slot, 1, 0, col, hxT, BhxT, i * 128)
            mark("P1b%d" % b, lambda: [("hxT0", hxT[:, 0, :], [BhxT]), ("hxT7", hxT[:, 7, :], [BhxT])])
            phase_barrier()
            for hi, (is_ret, h) in enumerate(heads):
                sl = hi % 2
                fm_cols, tm_cols = RT_COLS(h) if is_ret else HG_COLS(h)
                load_head_weights(sl, fm_cols, tm_cols)
                if is_ret:
                    ret_head(b, h, sl)
                else:
                    hgrn_head(b, h, sl)
                mark("head%d_%d" % (b, hi), lambda: [("mixT", mixT[:, hi, :], [BmixT]), ("QdT0", QdT[0][:], [BQd[0]]),
                                                     ("QdT1", QdT[1][:], [BQd[1]]), ("KdT", KdT[:], [BKdT]),
                                                     ("V", V[:].rearrange("p a b -> p (a b)"), [BV]),
                                                     ("tab", tab[:].rearrange("p a b c -> p (a b c)"), [Btab])])
            phase_barrier()
            dma("pool", woutb[:], wout_v, "wout", [], Bwout)
            for j in range(8):
                ts("dve", diag[:, j, :], identf[:], modP[:, 2, j, b:b + 1], None, ALU.mult, None, [Bc, Bmod], [Bdiag])
            for n in range(2):
                bk, bb = nb()
                mmg(bk[:], [(onesf[:], diag[:, 4 * n:4 * n + 4, :].rearrange("p j q -> p (j q)"))], [Bdiag, Bc], [bb])
                cp("act", gB1[:, n * 512:(n + 1) * 512], bk[:], [bb], [BgB1])
            for xi in range(NX):
                slot = xi % 2
                dma("sp", xt[slot][:], x_d[b, xi * 128:(xi + 1) * 128, :], "xt%d" % slot, [], [Bxt[slot]])
                for n in range(2):
                    bk, bb = nb()
                    mmg(bk[:], [(mixT[:, hd, xi * 128:(xi + 1) * 128], woutb[:, hd, n * 512:(n + 1) * 512])
                                for hd in range(8)], [BmixT] + Bwout, [bb])
                    tt("dve", x1t[:, n * 512:(n + 1) * 512], bk[:], gB1[:, n * 512:(n + 1) * 512], ALU.mult,
                       [bb, BgB1], [Bx1t])
                tt("pool", x1t[:], x1t[:], xt[slot][:], ALU.add, [Bx1t, Bxt[slot]], [Bx1t])
                dma("sp", x1_d[b, xi * 128:(xi + 1) * 128, :], x1t[:], "x1st", [Bx1t], [Bx1d[b][xi]])

        A.off = base_off
        fgn = A.alloc([128, D], F32)
        gB2 = A.alloc([128, D], F32)
        wupb = A.alloc([128, 8, 2 * DFF], BF16)
        wdnb = A.alloc([128, NFC, D], BF16)
        fxt = [A.alloc([128, D], F32) for _ in range(2)]
        fxs = A.alloc([128, D], BF16)
        h2T = [A.alloc([128, 8, BLK + 2], BF16) for _ in range(3)]
        hT = A.alloc([128, NFC, BLK], BF16)
        NEW = 3
        gbufs = [A.alloc([128, BLK + 2], F32) for _ in range(NEW)]
        t1s = [A.alloc([128, BLK], F32) for _ in range(NEW)]
        t2s = [A.alloc([128, BLK], F32) for _ in range(NEW)]
        hbs = [A.alloc([128, BLK], F32) for _ in range(NEW)]
        fdiag = A.alloc([128, 8, 128], F32)
        x2t = [A.alloc([128, D], F32) for _ in range(2)]
        outt = A.alloc([128, D], F32)
        Bfgn, BgB2, Bwup, Bwdn = Buf("fgn"), Buf("gB2"), Buf("wup"), Buf("wdn")
        Bfxt = [Buf("fxt0"), Buf("fxt1")]
        Bfxs = Buf("fxs")
        Bh2T = [Buf("h2T%d" % i) for i in range(3)]
        BhT, Bfdiag = Buf("hT"), Buf("fdiag")
        Bgbufs = [Buf("gbuf%d" % i) for i in range(NEW)]
        Bt1s = [Buf("t1%d" % i) for i in range(NEW)]
        Bt2s = [Buf("t2%d" % i) for i in range(NEW)]
        Bhbs = [Buf("hb%d" % i) for i in range(NEW)]
        build_program.ffn_end = None
        Bx2t = [Buf("x2t0"), Buf("x2t1")]
        Boutt = Buf("outt")
        ffn_bufs = [Bfgn, BgB2, Bwup, Bwdn, Bfxs, BhT, Bfdiag, Boutt] + Bfxt + Bh2T + Bx2t + Bgbufs + Bt1s + Bt2s + Bhbs
        build_program.ffn_end = A.off
        S.add("dve", lambda e: e.memset(sm[:, 60:61], 0.0), [], mixer_bufs0 + scratchB + ffn_bufs + [Bsm])

        wup_v = wup_d.rearrange("(k p) n -> p k n", p=128)
        wdn_v = wdn_d.rearrange("(f p) n -> p f n", p=128)
        UG = [(g * 512, min(512, DFF - g * 512)) for g in range(6)]
        Bwup_g = [Buf("wupg%d" % g) for g in range(6)]
        Bwdn_g = [Buf("wdng%d" % g) for g in range(4)]
        for g, (c0, n) in enumerate(UG):
            dma("pool", wupb[:, :, c0:c0 + n], wup_v[:, :, c0:c0 + n], "wupa%d" % g, [Bwup], [Bwup_g[g]])
            dma("pool", wupb[:, :, DFF + c0:DFF + c0 + n], wup_v[:, :, DFF + c0:DFF + c0 + n], "wupb%d" % g,
                [Bwup], [Bwup_g[g]])
        for gi, f0 in enumerate(range(0, NFC, 6)):
            f1 = min(NFC, f0 + 6)
            dma("pool", wdnb[:, f0:f1, :], wdn_v[:, f0:f1, :], "wdn%d" % f0, [Bwdn], [Bwdn_g[gi]])
        dma("sp", fgn[:], fgn_d.partition_broadcast(128), "fgn", [], [Bfgn])

        NB = L // BLK
        out_ops = []

        def ffn_A(b, j):
            g = b * NB + j
            sl = g % 3
            for a in range(BLK // 128):
                xi = j * (BLK // 128) + a
                s2 = (g * 2 + a) % 2
                dma("sp", fxt[s2][:], x1_d[b, xi * 128:(xi + 1) * 128, :], "fxt%d" % s2, [Bx1d[b][xi]], [Bfxt[s2]])
                S.add("dve", lambda e: e.memset(sm[:, 30:31], 0.0), [], [Bsm])
                act(fxs[:], fxt[s2][:], AF.Square, [Bfxt[s2]], [Bfxs, Bsm], accum_out=sm[:, 30:31])
                ts("dve", sm[:, 31:32], sm[:, 30:31], 1.0 / D, EPS, ALU.mult, ALU.add, [Bsm], [Bsm])
                act(sm[:, 32:33], sm[:, 31:32], AF.Ln, [Bsm], [Bsm])
                act(sm[:, 33:34], sm[:, 32:33], AF.Exp, [Bsm], [Bsm], scale=-0.5)
                ts("dve", fxs[:], fxt[s2][:], sm[:, 33:34], None, ALU.mult, None, [Bfxt[s2], Bsm], [Bfxs])
                transposes([(PT[:, jj * 128:(jj + 1) * 128], fxs[:, jj * 128:(jj + 1) * 128]) for jj in range(8)],
                           [Bfxs], [PTB])
                tt("dve", fdiag[:], PT.rearrange("p (j t) -> p j t", j=8),
                   modP[:, 4, :, b:b + 1].to_broadcast([128, 8, 128]), ALU.mult, [PTB, Bmod], [Bfdiag])
                tt("pool", h2T[sl][:, :, 1 + a * 128:1 + (a + 1) * 128], fdiag[:],
                   modP[:, 3, :, b:b + 1].to_broadcast([128, 8, 128]), ALU.add, [Bfdiag, Bmod], [Bh2T[sl]])
            if j == 0:
                S.add("pool", lambda e: e.memset(h2T[sl][:, :, 0:1], 0.0), [], [Bh2T[sl]])
            else:
                sp_ = (g - 1) % 3
                cp("pool", h2T[sl][:, :, 0:1], h2T[sp_][:, :, BLK:BLK + 1], [Bh2T[sp_]], [Bh2T[sl]])
                cp("pool", h2T[sp_][:, :, BLK + 1:BLK + 2], h2T[sl][:, :, 1:2], [Bh2T[sl]], [Bh2T[sp_]])
            if j == NB - 1:
                S.add("pool", lambda e: e.memset(h2T[sl][:, :, BLK + 1:BLK + 2], 0.0), [], [Bh2T[sl]])

        def ffn_F(b, j):
            g = b * NB + j
            sl = g % 3
            if j == 0:
                for jj in range(8):
                    ts("dve", fdiag[:, jj, :], identf[:], modP[:, 5, jj, b:b + 1], None, ALU.mult, None,
                       [Bc, Bmod], [Bfdiag])
                for n in range(2):
                    bk, bb = nb()
                    mmg(bk[:], [(onesf[:], fdiag[:, 4 * n:4 * n + 4, :].rearrange("p j q -> p (j q)"))],
                        [Bfdiag, Bc], [bb])
                    cp("act", gB2[:, n * 512:(n + 1) * 512], bk[:], [bb], [BgB2])
            for fc in range(NFC):
                ei = fc % NEW
                gbuf, t1, t2, hb = gbufs[ei], t1s[ei], t2s[ei], hbs[ei]
                Bgbuf, Bt1, Bt2, Bhb = Bgbufs[ei], Bt1s[ei], Bt2s[ei], Bhbs[ei]
                bg, bbg = nb()
                mmg(bg[:, 0:BLK + 2], [(wupb[:, k, fc * 128:(fc + 1) * 128], h2T[sl][:, k, :]) for k in range(8)],
                    [Bwup_g[fc // 4], Bh2T[sl]], [bbg])
                bu, bbu = nb()
                mmg(bu[:, 0:BLK], [(wupb[:, k, DFF + fc * 128:DFF + (fc + 1) * 128], h2T[sl][:, k, 1:BLK + 1])
                                   for k in range(8)], [Bwup_g[fc // 4], Bh2T[sl]], [bbu])
                cp("act", gbuf[:], bg[:, 0:BLK + 2], [bbg], [Bgbuf])
                ts("pool", t1[:], gbuf[:, 1:BLK + 1], cwP[:, fc, 1:2], cbP[:, fc:fc + 1], ALU.mult, ALU.add,
                   [Bgbuf, Bc], [Bt1])
                stt(t2[:], gbuf[:, 0:BLK], cwP[:, fc, 0:1], t1[:], ALU.mult, ALU.add, [Bgbuf, Bt1, Bc], [Bt2])
                stt(t1[:], gbuf[:, 2:BLK + 2], cwP[:, fc, 2:3], t2[:], ALU.mult, ALU.add, [Bgbuf, Bt2, Bc], [Bt1])
                act(hb[:], t1[:], AF.Silu, [Bt1], [Bhb])
                tt("dve", hT[:, fc, :], hb[:], bu[:, 0:BLK], ALU.mult, [Bhb, bbu], [BhT])
            for a in range(BLK // 128):
                xi = j * (BLK // 128) + a
                s2 = (g * 2 + a) % 2
                dma("sp", x2t[s2][:], x1_d[b, xi * 128:(xi + 1) * 128, :], "x2t%d" % s2, [Bx1d[b][xi]], [Bx2t[s2]])
                for n in range(2):
                    bk, bb = nb()
                    mmg(bk[:], [(hT[:, fc, a * 128:(a + 1) * 128], wdnb[:, fc, n * 512:(n + 1) * 512])
                                for fc in range(NFC)], [BhT] + Bwdn_g, [bb])
                    tt("dve", outt[:, n * 512:(n + 1) * 512], bk[:], gB2[:, n * 512:(n + 1) * 512], ALU.mult,
                       [bb, BgB2], [Boutt])
                tt("pool", x2t[s2][:], x2t[s2][:], outt[:], ALU.add, [Bx2t[s2], Boutt], [Bx2t[s2]])
                S.add("dve", lambda e: e.memset(sm[:, 40:41], 0.0), [], [Bsm])
                act(outt[:], x2t[s2][:], AF.Square, [Bx2t[s2]], [Boutt, Bsm], accum_out=sm[:, 40:41])
                ts("dve", sm[:, 41:42], sm[:, 40:41], 1.0 / D, EPS, ALU.mult, ALU.add, [Bsm], [Bsm])
                act(sm[:, 42:43], sm[:, 41:42], AF.Ln, [Bsm], [Bsm])
                act(sm[:, 43:44], sm[:, 42:43], AF.Exp, [Bsm], [Bsm], scale=-0.5)
                stt(outt[:], x2t[s2][:], sm[:, 43:44], fgn[:], ALU.mult, ALU.mult, [Bx2t[s2], Bsm, Bfgn], [Boutt])
                out_ops.append(dma("sp", out_d[b, xi * 128:(xi + 1) * 128, :], outt[:], "outst", [Boutt], []))

        seq = [(b, j) for b in range(2) for j in range(NB)]
        ffn_A(*seq[0])
        for i, (b, j) in enumerate(seq):
            if i + 1 < len(seq):
                ffn_A(*seq[i + 1])
            ffn_F(b, j)

        fw = out_ops[-4:] + out_ops[:1]
        if stage["stopped"]:
            last = {}
            for i, o in enumerate(S.ops):
                if o.is_dma:
                    last[o.key] = i
            fw = list(last.values())
        if REORDER:
            remap = S.reorder()
            fw = [remap[i] for i in fw]
        S.emit(final_wait_ops=fw)
        build_program.stats = (len(S.ops), A.hi, getattr(S, "est_total", None))
        build_program.dbg_names = stage["names"]
    return nc


def _consts():
    k = np.arange(128)
    ident = np.eye(128, dtype=np.float32)
    swap = np.where((k % 64) < 32, k + 32, k - 32)
    pm = np.zeros((128, 128), np.float32)
    pm[swap, k] = 1.0
    s = np.arange(128)[:, None]
    t = np.arange(128)[None, :]
    mf = (s <= t).astype(np.float32)
    mb = (s >= t).astype(np.float32)
    tt_ = np.arange(L, dtype=np.float32)
    rows = np.floor(tt_ / 64.0)
    cols = tt_ - rows * 64.0
    quarter = 32
    freqs = (10000.0 ** (-np.arange(quarter, dtype=np.float32) / quarter)).astype(np.float32)
    cos = np.zeros((128, L), np.float32)
    sin = np.zeros((128, L), np.float32)
    for kk in range(128):
        pos = rows if kk < 64 else cols
        i = kk % 32
        ang = (pos * freqs[i]).astype(np.float32)
        cos[kk] = np.cos(ang)
        sin[kk] = -np.sin(ang) if (kk % 64) < 32 else np.sin(ang)
    posf = np.tile((np.arange(128, dtype=np.float32) - 63.0)[None, :], (128, 1))
    posb = np.tile((64.0 - np.arange(128, dtype=np.float32))[None, :], (128, 1))
    return dict(ident=ident, pm=pm, mf=mf, mb=mb, cos=cos, sin=sin, posf=posf, posb=posb)


def make_in_maps(x, c, ctx, c_ctx, w_mod, b_mod, norm1_g, w_in, hgrn_lb, hgrn_norm_g, ret_decay, ret_norm_g,
                 w_out, norm2_g, w_up, conv_w, conv_b, w_down, final_g):
    f = lambda a: np.ascontiguousarray(np.asarray(a, dtype=np.float32))
    cst = _consts()
    pl = lambda v: f(np.asarray(v).reshape(-1, 128).T)
    shared = dict(
        w_mod=f(w_mod[0]), bmodP=pl(b_mod[0]), n1g=pl(norm1_g[0]), n2g=pl(norm2_g[0]), w_in=f(w_in[0]),
        lbP=f(np.asarray(hgrn_lb).reshape(2, 2, 4, 128).transpose(3, 0, 1, 2).reshape(128, 16)),
        hgn=f(hgrn_norm_g[0]).reshape(1, 512), rd=f(ret_decay[0]).reshape(1, 8), rtn=f(ret_norm_g[0]).reshape(1, 512),
        w_out=f(w_out[0]), w_up=f(w_up[0]),
        cwP=f(np.asarray(conv_w[0]).reshape(3, NFC, 128).transpose(2, 1, 0).reshape(128, 66)),
        cbP=pl(conv_b[0]), w_down=f(w_down[0]), fgn=f(final_g).reshape(1, D), **cst)
    maps = []
    for core in range(NCORES):
        cc = np.stack([np.asarray(c[2 * core]), np.asarray(c[2 * core + 1]), np.asarray(c_ctx)], axis=0)
        cT = f(cc.reshape(3, 8, 128).transpose(2, 1, 0).reshape(128, 24))
        m = dict(shared)
        m.update(x=f(x[2 * core:2 * core + 2]), ctx=f(ctx[2 * core:2 * core + 2]), cT=cT)
        maps.append(m)
    return maps


_NC_CACHE = {}


def kernel(**inputs):
    if "nc" not in _NC_CACHE:
        _NC_CACHE["nc"] = build_program()
    nc = _NC_CACHE["nc"]
    in_maps = make_in_maps(**inputs)
    res = run_bass_kernel_spmd(nc, in_maps, core_ids=list(range(NCORES)))
    out = np.concatenate([np.asarray(r["out"]) for r in res.results], axis=0)
    return out.astype(np.float32)
```

```python
import contextlib
import numpy as np
import ml_dtypes
import concourse.bass as bass
import concourse.mybir as mybir
from concourse.bass_utils import run_bass_kernel_spmd

F32 = mybir.dt.float32
BF16 = mybir.dt.bfloat16
AF = mybir.ActivationFunctionType
ALU = mybir.AluOpType
AX = mybir.AxisListType

D = 1024
L = 2048
LC = 256
TOK = L + LC
NT = TOK // 128
NX = L // 128
DFF = 2816
NFC = DFF // 128
INW = 4608
EPS = 1e-6
BLK = 256
NCORES = 8
REORDER = True


class Buf:
    __slots__ = ("name", "w", "r", "excl")

    def __init__(self, name, excl=False):
        self.name = name
        self.w = None
        self.r = []
        self.excl = excl


class Op:
    __slots__ = ("eng", "fn", "deps", "sig", "needs_sig", "is_dma", "idx", "key", "cost", "xfer", "t0", "t1")


class Sched:
    COMPUTE = ("pe", "act", "dve", "pool")

    def __init__(self, nc):
        self.nc = nc
        self.ops = []
        self.dma_count = {}
        self.last_dma = {}

    def add(self, eng, fn, reads=(), writes=(), dma=None, cost=300.0, xfer=0.0):
        op = Op()
        op.cost = float(cost)
        op.xfer = float(xfer)
        op.eng = eng
        op.fn = fn
        op.idx = len(self.ops)
        op.is_dma = dma is not None
        op.needs_sig = op.is_dma
        op.key = dma
        writes = list(writes) + [b for b in reads if b.excl]
        reads = [b for b in reads if not b.excl]
        deps = {}
        for b in reads:
            if b.w is not None:
                deps[b.w] = True
        for b in writes:
            if b.w is not None:
                deps.setdefault(b.w, False)
            for r in b.r:
                deps.setdefault(r, False)
        op.deps = []
        for j, raw in deps.items():
            if j == op.idx:
                continue
            op.deps.append(j)
        for b in reads:
            b.r.append(op.idx)
        for b in writes:
            b.w = op.idx
            b.r = []
        if op.is_dma:
            prev = self.last_dma.get(dma)
            if prev is not None and prev not in op.deps:
                op.deps.append(prev)
            self.last_dma[dma] = op.idx
            c = self.dma_count.get(dma, 0) + 16
            self.dma_count[dma] = c
            op.sig = (dma, c)
        else:
            op.sig = None
        self.ops.append(op)
        return op.idx

    def seal_group(self, key, since=0):
        if key not in self.dma_count:
            return
        tot = self.dma_count[key]
        for op in self.ops[since:]:
            if op.is_dma and op.key == key:
                op.sig = (key, tot)

    def reorder(self):
        import heapq
        ops = self.ops
        n = len(ops)
        users = [[] for _ in range(n)]
        ndep = [0] * n
        for op in ops:
            ds = set(op.deps)
            op.deps = sorted(ds)
            ndep[op.idx] = len(op.deps)
            for j in op.deps:
                users[j].append(op.idx)
        engs = ("pe", "act", "dve", "pool", "sp")
        free = {e: 0.0 for e in engs}
        byready = {e: [] for e in engs}
        now = {e: [] for e in engs}
        ready_t = [0.0] * n
        fin = [0.0] * n
        dma_free = [0.0]
        for op in ops:
            if ndep[op.idx] == 0:
                heapq.heappush(byready[op.eng], (0.0, op.idx))
        order = []
        LAT = 200.0
        while len(order) < n:
            best = None
            for e in engs:
                br, nw = byready[e], now[e]
                while br and br[0][0] <= free[e]:
                    heapq.heappush(nw, heapq.heappop(br)[1])
                if nw:
                    est = free[e]
                elif br:
                    est = br[0][0]
                else:
                    continue
                if best is None or est < best[0]:
                    best = (est, e)
            est, e = best
            if now[e]:
                i = heapq.heappop(now[e])
            else:
                i = heapq.heappop(byready[e])[1]
            op = ops[i]
            op.t0 = est
            free[e] = est + op.cost
            if op.is_dma:
                st = max(est + op.cost, dma_free[0])
                fin[i] = st + op.xfer
                dma_free[0] = st + op.xfer * 0.6
            else:
                fin[i] = est + op.cost
            op.t1 = fin[i]
            order.append(i)
            for u in users[i]:
                ready_t[u] = max(ready_t[u], fin[i] + LAT)
                ndep[u] -= 1
                if ndep[u] == 0:
                    heapq.heappush(byready[ops[u].eng], (ready_t[u], u))
        newidx = {old: new for new, old in enumerate(order)}
        newops = [ops[i] for i in order]
        for op in newops:
            op.deps = [newidx[j] for j in op.deps]
            op.idx = newidx[op.idx]
        self.ops = newops
        self.est_total = max(fin) if fin else 0.0
        return newidx

    def emit(self, final_wait_ops=()):
        nc = self.nc
        for op in self.ops:
            kept = []
            for j in op.deps:
                o = self.ops[j]
                if o.eng == op.eng and op.eng == "pe" and not o.is_dma and not op.is_dma:
                    continue
                kept.append(j)
                o.needs_sig = True
            op.deps = kept
        with contextlib.ExitStack() as st:
            sems = {}
            for e in self.COMPUTE:
                sems[e] = st.enter_context(nc.semaphore("s_" + e))
            for k in self.dma_count:
                sems[k] = st.enter_context(nc.semaphore("d_" + str(k)))
            cnt = {e: 0 for e in self.COMPUTE}
            for op in self.ops:
                if not op.is_dma and op.needs_sig:
                    cnt[op.eng] += 1
                    op.sig = (op.eng, cnt[op.eng])
            block = st.enter_context(nc.Block())
            ops = self.ops

            def run(engname, e):
                waited = {}
                for op in ops:
                    if op.eng != engname:
                        continue
                    for j in op.deps:
                        k, v = ops[j].sig
                        if waited.get(k, 0) < v:
                            e.wait_ge(sems[k], v)
                            waited[k] = v
                    ins = op.fn(e)
                    if op.needs_sig:
                        ins.then_inc(sems[op.sig[0]], 16 if op.is_dma else 1)
                if engname == "sp":
                    for j in final_wait_ops:
                        k, v = ops[j].sig
                        if waited.get(k, 0) < v:
                            e.wait_ge(sems[k], v)
                            waited[k] = v

            @block.tensor
            def _(e):
                run("pe", e)

            @block.scalar
            def _(e):
                run("act", e)

            @block.vector
            def _(e):
                run("dve", e)

            @block.gpsimd
            def _(e):
                run("pool", e)

            @block.sync
            def _(e):
                run("sp", e)


class Arena:
    def __init__(self, big, limit):
        self.big = big
        self.off = 0
        self.limit = limit
        self.hi = 0

    def _view(self, ap, shape):
        if len(shape) == 2:
            return ap
        if len(shape) == 3:
            return ap.rearrange("p (a b) -> p a b", a=shape[1])
        return ap.rearrange("p (a b c) -> p a b c", a=shape[1], b=shape[2])

    def alloc(self, shape, dt):
        n = int(np.prod(shape[1:]))
        nb = n * (4 if dt == F32 else 2)
        nb = (nb + 3) // 4 * 4
        o = self.off
        self.off += nb
        self.hi = max(self.hi, self.off)
        assert self.off <= self.limit, ("SBUF arena overflow", self.off)
        ap = self.big[:, o // 4:(o + nb) // 4]
        if dt != F32:
            ap = ap.bitcast(BF16)[:, 0:n]
        return self._view(ap, shape)


def build_program(debug=False, stop=None):
    nc = bass.Bass("TRN2", target_bir_lowering=False)

    def din(name, shape, dt=F32):
        return nc.dram_tensor(name, list(shape), dt, kind="ExternalInput").ap()

    x_d = din("x", [2, L, D])
    ctx_d = din("ctx", [2, LC, D])
    cT_d = din("cT", [128, 24])
    wmod_d = din("w_mod", [D, 6 * D])
    bmodP_d = din("bmodP", [128, 48])
    n1g_d = din("n1g", [128, 8])
    n2g_d = din("n2g", [128, 8])
    win_d = din("w_in", [D, INW])
    lbP_d = din("lbP", [128, 16])
    hgn_d = din("hgn", [1, 512])
    rd_d = din("rd", [1, 8])
    rtn_d = din("rtn", [1, 512])
    wout_d = din("w_out", [D, D])
    wup_d = din("w_up", [D, 2 * DFF])
    cwP_d = din("cwP", [128, 66])
    cbP_d = din("cbP", [128, 22])
    wdn_d = din("w_down", [DFF, D])
    fgn_d = din("fgn", [1, D])
    ident_d = din("ident", [128, 128])
    pm_d = din("pm", [128, 128])
    mf_d = din("mf", [128, 128])
    mb_d = din("mb", [128, 128])
    cos_d = din("cos", [128, L])
    sin_d = din("sin", [128, L])
    posf_d = din("posf", [128, 128])
    posb_d = din("posb", [128, 128])
    out_d = nc.dram_tensor("out", [2, L, D], F32, kind="ExternalOutput").ap()
    x1_d = nc.dram_tensor("x1s", [2, L, D], F32,
                          kind="ExternalOutput" if debug else "Internal").ap()

    st = contextlib.ExitStack()
    with st:
        LIMIT = 212000
        big = st.enter_context(nc.sbuf_tensor("big", [128, LIMIT // 4], F32))
        banks = [st.enter_context(nc.psum_tensor("bank%d" % i, [128, 512], F32)) for i in range(8)]
        bankB = [Buf("bank%d" % i, excl=True) for i in range(8)]
        PT = banks[7][:].bitcast(BF16)
        PTB = bankB[7]
        S = Sched(nc)
        A = Arena(big, LIMIT)
        dbg_d = nc.dram_tensor("dbg", [128, 16384], F32, kind="ExternalOutput").ap() if debug else None
        stage = {"n": 0, "off": 0, "stopped": False, "names": []}
        _add = S.add

        def gated_add(*a, **k):
            if stage["stopped"]:
                return 0
            if isinstance(stop, int) and len(S.ops) >= stop:
                stage["stopped"] = True
                return 0
            return _add(*a, **k)
        S.add = gated_add

        def dump(name, ap2d, bufs):
            n = ap2d.shape[1]
            o = stage["off"]
            stage["off"] += n
            stage["names"].append((name, o, n))
            _add("pool", lambda e: e.dma_start(out=dbg_d[:, o:o + n], in_=ap2d), list(bufs), [], dma="dbg%d" % len(stage["names"]))

        def mark(name, dumps=()):
            if stage["stopped"]:
                return
            if stop is not None and name == stop:
                for nm, ap, bufs in dumps():
                    dump(nm, ap, bufs)
                stage["stopped"] = True

        rot = [0]

        def nb():
            i = rot[0]
            rot[0] = (i + 1) % 7
            return banks[i], bankB[i]

        def nel(ap):
            n = 1
            for d in ap.shape[1:]:
                n *= int(d)
            return n

        def ecost(eng, n):
            if eng == "act":
                return n * 0.83 + 300.0
            if eng == "dve":
                return n * 1.04 + 170.0
            return n * 1.7 + 350.0

        def act(out, in_, func, reads, writes, **kw):
            return S.add("act", lambda e: e.activation(out=out, in_=in_, func=func, **kw), reads, writes,
                         cost=ecost("act", nel(out)))

        def ts(eng, out, in0, s1, s2, op0, op1, reads, writes):
            c = ecost(eng, nel(out))
            if s2 is None:
                return S.add(eng, lambda e: e.tensor_scalar(out=out, in0=in0, scalar1=s1, scalar2=None, op0=op0),
                             reads, writes, cost=c)
            return S.add(eng, lambda e: e.tensor_scalar(out=out, in0=in0, scalar1=s1, scalar2=s2, op0=op0, op1=op1),
                         reads, writes, cost=c)

        def tt(eng, out, in0, in1, op, reads, writes):
            return S.add(eng, lambda e: e.tensor_tensor(out=out, in0=in0, in1=in1, op=op), reads, writes,
                         cost=ecost(eng, nel(out)))

        def stt(out, in0, sc, in1, op0, op1, reads, writes):
            return S.add("dve", lambda e: e.scalar_tensor_tensor(out=out, in0=in0, scalar=sc, in1=in1, op0=op0, op1=op1),
                         reads, writes, cost=ecost("dve", nel(out)))

        def cp(eng, out, in_, reads, writes):
            if eng == "act":
                return act(out, in_, AF.Copy, reads, writes)
            return S.add(eng, lambda e: e.tensor_copy(out=out, in_=in_), reads, writes, cost=ecost(eng, nel(out)))

        def dma(eng, out, in_, key, reads, writes):
            nbytes = 128 * nel(out) * 4
            if eng == "pool":
                return S.add(eng, lambda e: e.dma_start(out=out, in_=in_), reads, writes, dma=key,
                             cost=1500.0, xfer=nbytes / 130.0 + 2000.0)
            return S.add(eng, lambda e: e.dma_start(out=out, in_=in_), reads, writes, dma=key,
                         cost=250.0, xfer=nbytes / 200.0 + 2000.0)

        def mmcost(pairs):
            c = 0.0
            for l, r in pairs:
                c += max(64, nel(r)) / 2.2 + 35.0
            return c

        def mmg(out, pairs, reads, writes):
            pairs = list(pairs)

            def fn(e):
                n = len(pairs)
                ins = None
                for i, (l, r) in enumerate(pairs):
                    ins = e.matmul(out, lhsT=l, rhs=r, start=(i == 0), stop=(i == n - 1))
                return ins
            return S.add("pe", fn, reads, writes, cost=mmcost(pairs))

        def mm_multi(groups, reads, writes):
            groups = [(o, list(p)) for o, p in groups]

            def fn(e):
                ins = None
                for out, pairs in groups:
                    n = len(pairs)
                    for i, (l, r) in enumerate(pairs):
                        ins = e.matmul(out, lhsT=l, rhs=r, start=(i == 0), stop=(i == n - 1))
                return ins
            return S.add("pe", fn, reads, writes, cost=sum(mmcost(p) for _, p in groups))

        def transposes(items, reads, writes):
            items = list(items)

            def fn(e):
                ins = None
                for o, i_ in items:
                    ins = e.transpose(out=o, in_=i_, identity=identb)
                return ins
            return S.add("pe", fn, reads + [Bc], writes, cost=110.0 * len(items))

        identb = A.alloc([128, 128], BF16)
        pmb = A.alloc([128, 128], BF16)
        mfb = A.alloc([128, 128], BF16)
        mbb = A.alloc([128, 128], BF16)
        identf = A.alloc([128, 128], F32)
        onesf = A.alloc([128, 128], F32)
        cT = A.alloc([128, 24], F32)
        cs = A.alloc([128, 8, 3], BF16)
        bmodP = A.alloc([128, 6, 8], F32)
        n1g = A.alloc([128, 8], F32)
        n2g = A.alloc([128, 8], F32)
        modP = A.alloc([128, 6, 8, 3], F32)
        lbP = A.alloc([128, 2, 2, 4], F32)
        lbv = A.alloc([128, 2, 4], F32)
        oml = A.alloc([128, 2, 4], F32)
        rdt = A.alloc([128, 8], F32)
        lg = A.alloc([128, 8], F32)
        nlg = A.alloc([128, 8], F32)
        g64 = A.alloc([128, 8], F32)
        g128 = A.alloc([128, 8], F32)
        cwP = A.alloc([128, 22, 3], F32)
        cbP = A.alloc([128, 22], F32)
        sm = A.alloc([128, 64], F32)
        Bc = Buf("consts")
        Bmod = Buf("modP")
        Bmg = [Buf("modg%d" % g) for g in range(6)]
        Bsm = Buf("sm")
        base_off = A.off

        cosb = A.alloc([128, L], BF16)
        sinb = A.alloc([128, L], BF16)
        posf = A.alloc([128, 128], F32)
        posb = A.alloc([128, 128], F32)
        RM = A.alloc([128, NT, 128], BF16)
        RT = A.alloc([128, 4, 128], F32)
        hgn = A.alloc([128, 512], F32)
        rtn = A.alloc([128, 512], F32)
        hxT = A.alloc([128, 8, TOK], BF16)
        mixT = A.alloc([128, 8, L], BF16)
        wfm = [A.alloc([128, 8, 384], BF16) for _ in range(2)]
        wtm = [A.alloc([128, 8, 256], BF16) for _ in range(2)]
        qT = A.alloc([128, L], F32)
        ar1 = A.off
        FA = A.alloc([128, TOK], F32)
        FB = A.alloc([128, TOK], F32)
        FC = A.alloc([128, TOK], F32)
        ar2 = A.off
        KdT = A.alloc([128, TOK], BF16)
        Kd = A.alloc([128, NT, 128], BF16)
        US = A.alloc([128, NT, 128], F32)
        ar2e = A.off
        QdT = [A.alloc([128, L], BF16) for _ in range(2)]
        S16 = [A.alloc([128, NX, 128], BF16) for _ in range(2)]
        AT = [A.alloc([128, NX, 128], BF16) for _ in range(2)]
        V = A.alloc([128, NT, 128], BF16)
        SG = A.alloc([128, NX, 128], BF16)
        tmpo = A.alloc([128, 4, 128], F32)
        tmpq = A.alloc([128, 4, 128], F32)
        mtok = A.alloc([128, 4, 128], BF16)
        tab = A.alloc([128, 2, 6, NT], F32)
        mixer_hi = A.off
        A.off = ar1
        xt = [A.alloc([128, D], F32) for _ in range(2)]
        xs = [A.alloc([128, D], BF16) for _ in range(2)]
        x1t = A.alloc([128, D], F32)
        gB1 = A.alloc([128, D], F32)
        diag = A.alloc([128, 8, 128], F32)
        assert A.off <= ar2
        A.off = ar2
        woutb = A.alloc([128, 8, D], BF16)
        assert A.off <= ar2e
        A.off = mixer_hi

        BFA, BFB, BFC = Buf("FA"), Buf("FB"), Buf("FC")
        BKdT, BKd, BUS = Buf("KdT"), Buf("Kd"), Buf("US")
        BhxT, BmixT, BqT = Buf("hxT"), Buf("mixT"), Buf("qT")
        Bwfm = [Buf("wfm0"), Buf("wfm1")]
        Bwtm = [Buf("wtm0"), Buf("wtm1")]
        BQd = [Buf("Qd0"), Buf("Qd1")]
        BS16 = [Buf("S160"), Buf("S161")]
        BAT = [Buf("AT0"), Buf("AT1")]
        BV, BSG = Buf("V"), Buf("SG")
        Btmpo, Btmpq, Bmtok, Btab = Buf("tmpo"), Buf("tmpq"), Buf("mtok"), Buf("tab")
        BRT = Buf("RT")
        Bxt = [BFA, BFA]
        Bxs = [BFB, BFB]
        Bx1t = BFB
        BgB1 = BFC
        Bdiag = BFC
        Bwout = [BKdT, BKd, BUS]
        mixer_bufs0 = [BFA, BFB, BFC, BKdT, BKd, BUS, BhxT, BmixT, BqT] + Bwfm + Bwtm + BQd + BS16 + BAT + \
                     [BV, BSG, Btmpo, Btmpq, Bmtok, Btab, BRT, Bc]
        Bxt = [Buf("xt0"), Buf("xt1")]
        Bxs = [Buf("xs0"), Buf("xs1")]
        Bx1t = Buf("x1t")
        BgB1 = Buf("gB1")
        Bdiag = Buf("diag")
        allF = [BFA, BFB, BFC]
        scratchB = Bxt + Bxs + [Bx1t, BgB1, Bdiag]

        def phase_barrier():
            S.add("dve", lambda e: e.memset(sm[:, 61:62], 0.0), [], allF + scratchB + [Bsm])

        ckn = [0]

        def CKf(q):
            ckn[0] += 1
            return "c%s%d" % (q, ckn[0] % 5)
        for dst, src in ((identb[:], ident_d), (pmb[:], pm_d), (mfb[:], mf_d), (mbb[:], mb_d),
                         (cosb[:], cos_d), (sinb[:], sin_d)):
            dma("pool", dst, src, CKf("p"), [], [Bc])
        for dst, src in ((identf[:], ident_d), (cT[:], cT_d), (bmodP[:], bmodP_d.rearrange("p (g j) -> p g j", g=6)),
                         (n1g[:], n1g_d), (n2g[:], n2g_d),
                         (lbP[:], lbP_d.rearrange("p (d i h) -> p d i h", d=2, i=2)),
                         (rdt[:], rd_d.partition_broadcast(128)),
                         (cwP[:], cwP_d.rearrange("p (f j) -> p f j", j=3)), (cbP[:], cbP_d),
                         (posf[:], posf_d), (posb[:], posb_d),
                         (hgn[:], hgn_d.partition_broadcast(128)), (rtn[:], rtn_d.partition_broadcast(128))):
            dma("sp", dst, src, CKf("s"), [], [Bc])
        S.add("dve", lambda e: e.memset(onesf[:], 1.0), [], [Bc])
        S.add("dve", lambda e: e.memset(RM[:], 1.0), [], [Bc])
        S.add("dve", lambda e: e.memset(RM[:, :, 0:1], 0.0), [], [Bc])
        act(cs[:], cT[:].rearrange("p (k j) -> p k j", j=3), AF.Silu, [Bc], [Bmod])
        tt("dve", lbv[:], lbP[:, :, 0, :], lbP[:, :, 1, :], ALU.subtract, [Bc], [Bmod])
        act(lbv[:], lbv[:], AF.Sigmoid, [Bmod], [Bmod])
        ts("dve", oml[:], lbv[:], -1.0, 1.0, ALU.mult, ALU.add, [Bmod], [Bmod])
        act(lg[:], rdt[:], AF.Sigmoid, [Bc], [Bmod])
        act(lg[:], lg[:], AF.Ln, [Bmod], [Bmod])
        ts("dve", nlg[:], lg[:], -1.0, None, ALU.mult, None, [Bmod], [Bmod])
        act(g64[:], lg[:], AF.Exp, [Bmod], [Bmod], scale=64.0)
        act(g128[:], lg[:], AF.Exp, [Bmod], [Bmod], scale=128.0)

        A.off = ar1
        wmv = [A.alloc([128, 8, 1024], BF16)]
        A.off = ar2
        wmv.append(A.alloc([128, 8, 1024], BF16))
        A.off = mixer_hi
        Bwm = [allF, [BKdT, BKd, BUS]]
        wmod_v = wmod_d.rearrange("(k p) n -> p k n", p=128)
        for gi, g in enumerate((1, 0, 2, 3, 4, 5)):
            sl = 0 if gi == 0 else 1
            dma("pool", wmv[sl][:], wmod_v[:, :, g * 1024:(g + 1) * 1024], "wm%d" % sl, [], Bwm[sl])
            bk, bb = nb()
            groups = []
            for j in range(8):
                groups.append((bk[:, j * 4:j * 4 + 3],
                               [(wmv[sl][:, k, j * 128:(j + 1) * 128], cs[:, k, :]) for k in range(8)]))
            mm_multi(groups, Bwm[sl] + [Bmod], [bb])
            tt("dve", modP[:, g], bk[:, 0:32].rearrange("p (j c) -> p j c", c=4)[:, :, 0:3],
               bmodP[:, g, :].unsqueeze(2).to_broadcast([128, 8, 3]), ALU.add, [bb, Bc], [Bmg[g]])
            if g in (1, 4):
                ng = n1g if g == 1 else n2g
                ts("dve", modP[:, g], modP[:, g], 1.0, None, ALU.add, None, [Bmg[g]], [Bmg[g]])
                tt("dve", modP[:, g], modP[:, g], ng[:].unsqueeze(2).to_broadcast([128, 8, 3]), ALU.mult,
                   [Bmg[g], Bc], [Bmg[g]])
        mark("setup", lambda: [("modP", modP[:].rearrange("p a b c -> p (a b c)"), Bmg), ("lbv", lbv[:].rearrange("p a b -> p (a b)"), [Bmod]),
                               ("lg", lg[:], [Bmod]), ("g64", g64[:], [Bmod])])
        def norm_T(src_ap, src_key_bufs, slot, gA, gB_, col, dstT, dstB, c0, extra_reads=(), from_dram=True,
                   keep=None):
            dma("sp", xt[slot][:], src_ap, "xt%d" % slot, list(src_key_bufs), [Bxt[slot]])
            S.add("dve", lambda e: e.memset(sm[:, slot:slot + 1], 0.0), [], [Bsm])
            act(xs[slot][:], xt[slot][:], AF.Square, [Bxt[slot]], [Bxs[slot], Bsm], accum_out=sm[:, slot:slot + 1])
            ts("dve", sm[:, 2 + slot:3 + slot], sm[:, slot:slot + 1], 1.0 / D, EPS, ALU.mult, ALU.add, [Bsm], [Bsm])
            act(sm[:, 4 + slot:5 + slot], sm[:, 2 + slot:3 + slot], AF.Ln, [Bsm], [Bsm])
            act(sm[:, 6 + slot:7 + slot], sm[:, 4 + slot:5 + slot], AF.Exp, [Bsm], [Bsm], scale=-0.5)
            ts("pool" if slot else "dve", xs[slot][:], xt[slot][:], sm[:, 6 + slot:7 + slot], None, ALU.mult, None,
               [Bxt[slot], Bsm], [Bxs[slot]])
            transposes([(PT[:, j * 128:(j + 1) * 128], xs[slot][:, j * 128:(j + 1) * 128]) for j in range(8)],
                       [Bxs[slot]], [PTB])
            tmp = diag
            tt("dve", tmp[:], PT.rearrange("p (j t) -> p j t", j=8),
               modP[:, gA, :, col:col + 1].to_broadcast([128, 8, 128]), ALU.mult, [PTB, Bmg[gA]], [Bdiag])
            tt("pool", dstT[:, :, c0:c0 + 128], tmp[:],
               modP[:, gB_, :, col:col + 1].to_broadcast([128, 8, 128]), ALU.add, [Bdiag, Bmg[gB_]] + list(extra_reads),
               [dstB])

        def fm_proj(wv, wB, c, t0, n):
            bk, bb = nb()
            mmg(bk[:, 0:n], [(wv[:, k, c * 128:(c + 1) * 128], hxT[:, k, t0:t0 + n]) for k in range(8)],
                [wB, BhxT], [bb])
            return bk, bb

        def gla_dir(d):
            a1 = tab[:, d, 0, :]
            a2 = tab[:, d, 1, :]
            Dc = tab[:, d, 2, :]
            for c0 in range(0, NT, 8):
                n = min(8, NT - c0)
                transposes([(PT[:, a * 128:(a + 1) * 128], KdT[:, (c0 + a) * 128:(c0 + a + 1) * 128]) for a in range(n)],
                           [BKdT], [PTB])
                cp("act", Kd[:].rearrange("p a k -> p (a k)")[:, c0 * 128:(c0 + n) * 128], PT[:, 0:n * 128], [PTB], [BKd])
            skip = NT - 1 if d == 0 else 2
            for c0 in range(0, NT, 4):
                cl = [c for c in range(c0, min(c0 + 4, NT))]
                bk, bb = nb()
                mm_multi([(bk[:, (c - c0) * 128:(c - c0 + 1) * 128], [(Kd[:, c, :], V[:, c, :])]) for c in cl],
                         [BKd, BV], [bb])
                n = len(cl)
                tt("dve", US[:, c0:c0 + n, :], bk[:, 0:n * 128].rearrange("p (a v) -> p a v", a=n),
                   a1[:, c0:c0 + n].unsqueeze(2).to_broadcast([128, n, 128]), ALU.mult, [bb, Btab], [BUS])
            order = list(range(NT)) if d == 0 else [1, 0] + list(range(NT - 1, 1, -1))
            for j in range(1, NT - 1):
                c, pc = order[j], order[j - 1]
                stt(US[:, c, :], US[:, pc, :], Dc[:, c:c + 1], US[:, c, :], ALU.mult, ALU.add, [BUS, Btab], [BUS])
            if d == 0:
                tt("pool", S16[d][:], US[:, 1:NT - 1, :], a2[:, 2:NT].unsqueeze(2).to_broadcast([128, NX, 128]),
                   ALU.mult, [BUS, Btab], [BS16[d]])
            else:
                tt("pool", S16[d][:, 0:NX - 1, :], US[:, 3:NT, :],
                   a2[:, 2:NT - 1].unsqueeze(2).to_broadcast([128, NX - 1, 128]), ALU.mult, [BUS, Btab], [BS16[d]])
                tt("pool", S16[d][:, NX - 1, :], US[:, 0, :], a2[:, NT - 1:NT].to_broadcast([128, 128]),
                   ALU.mult, [BUS, Btab], [BS16[d]])
            mask = mfb if d == 0 else mbb
            for x0 in range(0, NX, 4):
                bk, bb = nb()
                mm_multi([(bk[:, a * 128:(a + 1) * 128],
                           [(KdT[:, (x0 + a + 2) * 128:(x0 + a + 3) * 128], QdT[d][:, (x0 + a) * 128:(x0 + a + 1) * 128])])
                          for a in range(4)], [BKdT, BQd[d]], [bb])
                tt("dve", AT[d][:, x0:x0 + 4, :], bk[:].rearrange("p (a t) -> p a t", a=4),
                   mask[:].unsqueeze(1).to_broadcast([128, 4, 128]), ALU.mult, [bb, Bc], [BAT[d]])

        def gla_out(hd, is_ret, h):
            gain = rtn if is_ret else hgn
            for x0 in range(0, NX, 4):
                bk, bb = nb()
                groups = []
                for a in range(4):
                    xi = x0 + a
                    pairs = []
                    for d in range(2):
                        pairs.append((AT[d][:, xi, :], V[:, xi + 2, :]))
                        pairs.append((QdT[d][:, xi * 128:(xi + 1) * 128], S16[d][:, xi, :]))
                    groups.append((bk[:, a * 128:(a + 1) * 128], pairs))
                mm_multi(groups, BAT + BQd + BS16 + [BV], [bb])
                o3 = bk[:].rearrange("p (a v) -> p a v", a=4)
                if is_ret:
                    cp("act", tmpo[:].rearrange("p a v -> p (a v)"), bk[:], [bb], [Btmpo])
                else:
                    tt("dve", tmpo[:], o3, SG[:, x0:x0 + 4, :], ALU.mult, [bb, BSG], [Btmpo])
                tt("pool", tmpq[:], tmpo[:], tmpo[:], ALU.mult, [Btmpo], [Btmpq])
                S.add("dve", lambda e: e.reduce_sum(out=sm[:, 8:12], in_=tmpq[:], axis=AX.X), [Btmpq], [Bsm], cost=700.0)
                ts("dve", sm[:, 12:16], sm[:, 8:12], 1.0 / 128, EPS, ALU.mult, ALU.add, [Bsm], [Bsm])
                act(sm[:, 16:20], sm[:, 12:16], AF.Ln, [Bsm], [Bsm])
                act(sm[:, 20:24], sm[:, 16:20], AF.Exp, [Bsm], [Bsm], scale=-0.5)
                tt("dve", tmpo[:], tmpo[:], sm[:, 20:24].unsqueeze(2).to_broadcast([128, 4, 128]), ALU.mult,
                   [Btmpo, Bsm], [Btmpo])
                gb = gain[:, h * 128:(h + 1) * 128].unsqueeze(1).to_broadcast([128, 4, 128])
                if is_ret:
                    tt("pool", tmpq[:], tmpo[:], gb, ALU.mult, [Btmpo, Bc], [Btmpq])
                    tt("pool", mtok[:], tmpq[:], SG[:, x0:x0 + 4, :], ALU.mult, [Btmpq, BSG], [Bmtok])
                else:
                    tt("pool", mtok[:], tmpo[:], gb, ALU.mult, [Btmpo, Bc], [Bmtok])
                transposes([(PT[:, a * 128:(a + 1) * 128], mtok[:, a, :]) for a in range(4)], [Bmtok], [PTB])
                cp("act", mixT[:, hd, x0 * 128:(x0 + 4) * 128], PT[:, 0:512], [PTB], [BmixT])

        def tm_proj(sl, is_ret):
            for i0 in range(0, NT, 2):
                bk, bb = nb()
                n = 128 if i0 < 2 else 256
                mm_multi([(bk[:, a * 256:a * 256 + n],
                           [(hxT[:, k, (i0 + a) * 128:(i0 + a + 1) * 128], wtm[sl][:, k, 0:n]) for k in range(8)])
                          for a in range(2)], [BhxT, Bwtm[sl]], [bb])
                b3 = bk[:].rearrange("p (a c) -> p a c", a=2)
                cp("dve", V[:, i0:i0 + 2, :], b3[:, :, 0:128], [bb], [BV])
                if i0 >= 2:
                    for a in range(2):
                        act(SG[:, i0 - 2 + a, :], bk[:, a * 256 + 128:a * 256 + 256], AF.Silu if is_ret else AF.Sigmoid,
                            [bb], [BSG])

        def load_head_weights(sl, fm_cols, tm_cols):
            since = len(S.ops)
            for i, c in enumerate(fm_cols):
                dma("pool", wfm[sl][:, :, i * 128:(i + 1) * 128], win_v[:, :, c:c + 128], "wf%d_%d" % (sl, i), [], [Bwfm[sl]])
            for i, c in enumerate(tm_cols):
                dma("pool", wtm[sl][:, :, i * 128:(i + 1) * 128], win_v[:, :, c:c + 128], "wt%d_%d" % (sl, i), [], [Bwtm[sl]])

        win_v = win_d.rearrange("(k p) n -> p k n", p=128)
        FA3 = FA[:].rearrange("p (c t) -> p c t", t=128)
        FB3 = FB[:].rearrange("p (c t) -> p c t", t=128)
        FC3 = FC[:].rearrange("p (c t) -> p c t", t=128)
        TOKBLK = [(0, 256)] + [(256 + i * 512, 512) for i in range(4)]
        XBLK = [(256 + i * 512, 512) for i in range(4)]

        def hgrn_head(b, h, sl):
            for (t0, n) in XBLK:
                bk, bb = fm_proj(wfm[sl], Bwfm[sl], 0, t0, n)
                cp("act", qT[:, t0 - 256:t0 - 256 + n], bk[:, 0:n], [bb], [BqT])
            tm_proj(sl, False)
            for d in range(2):
                a1, a2, Dc, mid, tot, tmp = (tab[:, d, i, :] for i in range(6))
                for (t0, n) in TOKBLK:
                    bk, bb = fm_proj(wfm[sl], Bwfm[sl], 1 + d, t0, n)
                    act(FA[:, t0:t0 + n], bk[:, 0:n], AF.Sigmoid, [bb], [BFA])
                ts("dve", FA[:], FA[:], oml[:, d, h:h + 1], lbv[:, d, h:h + 1], ALU.mult, ALU.add, [BFA, Bmod], [BFA])
                act(FB[:], FA[:], AF.Ln, [BFA], [BFB])
                ts("pool", FA[:], FA[:], -1.0, 1.0, ALU.mult, ALU.add, [BFA], [BFA])
                S.add("dve", lambda e: e.tensor_tensor_scan(out=FC[:], data0=RM[:].rearrange("p c t -> p (c t)"),
                                                              data1=FB[:], initial=0.0, op0=ALU.mult, op1=ALU.add),
                      [BFB, Bc], [BFC], cost=5000.0)
                cp("dve", tot, FC3[:, :, 127], [BFC], [Btab])
                act(Dc, tot, AF.Exp, [Btab], [Btab])
                if d == 0:
                    cp("dve", mid, FC3[:, :, 63], [BFC], [Btab])
                    act(a2, mid, AF.Exp, [Btab], [Btab])
                    tt("dve", tmp, tot, mid, ALU.subtract, [Btab], [Btab])
                    act(a1, tmp, AF.Exp, [Btab], [Btab])
                else:
                    tt("dve", FC[:], FC[:], FB[:], ALU.subtract, [BFC, BFB], [BFC])
                    cp("dve", mid, FC3[:, :, 64], [BFC], [Btab])
                    act(a1, mid, AF.Exp, [Btab], [Btab])
                    tt("dve", tmp, tot, mid, ALU.subtract, [Btab], [Btab])
                    act(a2, tmp, AF.Exp, [Btab], [Btab])
                tt("dve", FC3, FC3, mid.unsqueeze(2).to_broadcast([128, NT, 128]), ALU.subtract, [BFC, Btab], [BFC])
                sq, sk = (1.0, -1.0) if d == 0 else (-1.0, 1.0)
                act(FB[:, 256:TOK], FC[:, 256:TOK], AF.Exp, [BFC], [BFB], scale=sq)
                tt("pool", QdT[d][:], qT[:], FB[:, 256:TOK], ALU.mult, [BqT, BFB], [BQd[d]])
                act(FB[:], FC[:], AF.Exp, [BFC], [BFB], scale=sk)
                tt("pool", KdT[:], FA[:], FB[:], ALU.mult, [BFA, BFB], [BKdT])
                gla_dir(d)
            gla_out(h, False, h)

        def ret_head(b, h, sl):
            lnsc = float(np.log(128.0 ** -0.5))
            act(RT[:, 0, :], posf[:], AF.Exp, [Bc, Bmod], [BRT], scale=lg[:, h:h + 1])
            act(RT[:, 1, :], posf[:], AF.Exp, [Bc, Bmod], [BRT], scale=nlg[:, h:h + 1])
            act(RT[:, 2, :], posb[:], AF.Exp, [Bc, Bmod], [BRT], scale=lg[:, 4 + h:5 + h])
            act(RT[:, 3, :], posb[:], AF.Exp, [Bc, Bmod], [BRT], scale=nlg[:, 4 + h:5 + h])
            for kd in (1, 3):
                ts("dve", RT[:, kd, :], RT[:, kd, :], 128.0 ** -0.5, None, ALU.mult, None, [BRT], [BRT])
            for d in range(2):
                for i, src in ((0, g64), (1, g64), (2, g128)):
                    cp("dve", tab[:, d, i, :], src[:, d * 4 + h:d * 4 + h + 1].to_broadcast([128, NT]), [Bmod], [Btab])
            tm_proj(sl, True)
            for which in range(2):
                dst, dB = (qT, BqT) if which == 0 else (FA, BFA)
                blks = XBLK if which == 0 else TOKBLK
                for (t0, n) in blks:
                    o0 = t0 - 256 if which == 0 else t0
                    bk, bb = fm_proj(wfm[sl], Bwfm[sl], which, t0, n)
                    if t0 < 256:
                        cp("act", dst[:, o0:o0 + n], bk[:, 0:n], [bb], [dB])
                        continue
                    xo = t0 - 256
                    cp("act", FB[:, 0:n].bitcast(BF16)[:, 0:n], bk[:, 0:n], [bb], [BFB])
                    tt("dve", FC[:, 0:n], bk[:, 0:n], cosb[:, xo:xo + n], ALU.mult, [bb, Bc], [BFC])
                    bk2, bb2 = nb()
                    mmg(bk2[:, 0:n], [(pmb[:], FB[:, 0:n].bitcast(BF16)[:, 0:n])], [BFB, Bc], [bb2])
                    tt("dve", FC[:, 512:512 + n], bk2[:, 0:n], sinb[:, xo:xo + n], ALU.mult, [bb2, Bc], [BFC])
                    tt("pool", dst[:, o0:o0 + n], FC[:, 0:n], FC[:, 512:512 + n], ALU.add, [BFC], [dB])
            for d in range(2):
                tt("pool", QdT[d][:].rearrange("p (c t) -> p c t", t=128), qT[:].rearrange("p (c t) -> p c t", t=128),
                   RT[:, 2 * d, :].unsqueeze(1).to_broadcast([128, NX, 128]), ALU.mult, [BqT, BRT], [BQd[d]])
                tt("pool", KdT[:].rearrange("p (c t) -> p c t", t=128), FA3,
                   RT[:, 2 * d + 1, :].unsqueeze(1).to_broadcast([128, NT, 128]), ALU.mult, [BFA, BRT], [BKdT])
                gla_dir(d)
            gla_out(4 + h, True, h)

        HG_COLS = lambda h: ([h * 128, 1024 + h * 128, 1536 + h * 128], [512 + h * 128, 2048 + h * 128])
        RT_COLS = lambda h: ([2560 + h * 128, 3072 + h * 128], [3584 + h * 128, 4096 + h * 128])
        wout_v = wout_d.rearrange("(k p) n -> p k n", p=128)
        Bx1d = [[Buf("x1d%d_%d" % (b, i)) for i in range(NX)] for b in range(2)]

        heads = [(False, h) for h in range(4)] + [(True, h) for h in range(4)]
        for b in range(2):
            phase_barrier()
            for i in range(NT):
                slot = i % 2
                if i < 2:
                    src = ctx_d[b, i * 128:(i + 1) * 128, :]
                    col = 2
                else:
                    src = x_d[b, (i - 2) * 128:(i - 1) * 128, :]
                    col = b
                norm_T(src, [], slot, 1, 0, col, hxT, BhxT, i * 128)
            mark("P1b%d" % b, lambda: [("hxT0", hxT[:, 0, :], [BhxT]), ("hxT7", hxT[:, 7, :], [BhxT])])
            phase_barrier()
            for hi, (is_ret, h) in enumerate(heads):
                sl = hi % 2
                fm_cols, tm_cols = RT_COLS(h) if is_ret else HG_COLS(h)
                load_head_weights(sl, fm_cols, tm_cols)
                if is_ret:
                    ret_head(b, h, sl)
                else:
                    hgrn_head(b, h, sl)
                mark("head%d_%d" % (b, hi), lambda: [("mixT", mixT[:, hi, :], [BmixT]), ("QdT0", QdT[0][:], [BQd[0]]),
                                                     ("QdT1", QdT[1][:], [BQd[1]]), ("KdT", KdT[:], [BKdT]),
                                                     ("V", V[:].rearrange("p a b -> p (a b)"), [BV]),
                                                     ("tab", tab[:].rearrange("p a b c -> p (a b c)"), [Btab])])
            phase_barrier()
            dma("pool", woutb[:], wout_v, "wout", [], Bwout)
            for j in range(8):
                ts("dve", diag[:, j, :], identf[:], modP[:, 2, j, b:b + 1], None, ALU.mult, None, [Bc, Bmg[2]], [Bdiag])
            for n in range(2):
                bk, bb = nb()
                mmg(bk[:], [(onesf[:], diag[:, 4 * n:4 * n + 4, :].rearrange("p j q -> p (j q)"))], [Bdiag, Bc], [bb])
                cp("act", gB1[:, n * 512:(n + 1) * 512], bk[:], [bb], [BgB1])
            x1ts = [x1t, diag[:].rearrange("p a b -> p (a b)")]
            Bx1ts = [Bx1t, Bdiag]
            for xi in range(NX):
                slot = xi % 2
                xo, Bxo = x1ts[slot], Bx1ts[slot]
                dma("sp", xt[slot][:], x_d[b, xi * 128:(xi + 1) * 128, :], "xt%d" % slot, [], [Bxt[slot]])
                for n in range(2):
                    bk, bb = nb()
                    mmg(bk[:], [(mixT[:, hd, xi * 128:(xi + 1) * 128], woutb[:, hd, n * 512:(n + 1) * 512])
                                for hd in range(8)], [BmixT] + Bwout, [bb])
                    tt("dve", xo[:, n * 512:(n + 1) * 512], bk[:], gB1[:, n * 512:(n + 1) * 512], ALU.mult,
                       [bb, BgB1], [Bxo])
                tt("pool", xo[:], xo[:], xt[slot][:], ALU.add, [Bxo, Bxt[slot]], [Bxo])
                dma("sp", x1_d[b, xi * 128:(xi + 1) * 128, :], xo[:], "x1st%d" % slot, [Bxo], [Bx1d[b][xi]])

        A.off = base_off
        fgn = A.alloc([128, D], F32)
        gB2 = A.alloc([128, D], F32)
        wupb = A.alloc([128, 8, 2 * DFF], BF16)
        wdnb = A.alloc([128, NFC, D], BF16)
        fxt = [A.alloc([128, D], F32) for _ in range(2)]
        fxs = A.alloc([128, D], BF16)
        h2T = [A.alloc([128, 8, BLK + 2], BF16) for _ in range(3)]
        hT = A.alloc([128, NFC, BLK], BF16)
        NEW = 3
        gbufs = [A.alloc([128, BLK + 2], F32) for _ in range(NEW)]
        t1s = [A.alloc([128, BLK], F32) for _ in range(NEW)]
        t2s = [A.alloc([128, BLK], F32) for _ in range(NEW)]
        hbs = [A.alloc([128, BLK], F32) for _ in range(NEW)]
        fdiag = A.alloc([128, 8, 128], F32)
        x2t = [A.alloc([128, D], F32) for _ in range(2)]
        outt = A.alloc([128, D], F32)
        Bfgn, BgB2, Bwup, Bwdn = Buf("fgn"), Buf("gB2"), Buf("wup"), Buf("wdn")
        Bfxt = [Buf("fxt0"), Buf("fxt1")]
        Bfxs = Buf("fxs")
        Bh2T = [Buf("h2T%d" % i) for i in range(3)]
        BhT, Bfdiag = Buf("hT"), Buf("fdiag")
        Bgbufs = [Buf("gbuf%d" % i) for i in range(NEW)]
        Bt1s = [Buf("t1%d" % i) for i in range(NEW)]
        Bt2s = [Buf("t2%d" % i) for i in range(NEW)]
        Bhbs = [Buf("hb%d" % i) for i in range(NEW)]
        build_program.ffn_end = None
        Bx2t = [Buf("x2t0"), Buf("x2t1")]
        Boutt = Buf("outt")
        ffn_bufs = [Bfgn, BgB2, Bwup, Bwdn, Bfxs, BhT, Bfdiag, Boutt] + Bfxt + Bh2T + Bx2t + Bgbufs + Bt1s + Bt2s + Bhbs
        build_program.ffn_end = A.off
        S.add("dve", lambda e: e.memset(sm[:, 60:61], 0.0), [], mixer_bufs0 + scratchB + ffn_bufs + [Bsm])

        wup_v = wup_d.rearrange("(k p) n -> p k n", p=128)
        wdn_v = wdn_d.rearrange("(f p) n -> p f n", p=128)
        UG = [(g * 512, min(512, DFF - g * 512)) for g in range(6)]
        Bwup_g = [Buf("wupg%d" % g) for g in range(6)]
        Bwdn_g = [Buf("wdng%d" % g) for g in range(4)]
        for g, (c0, n) in enumerate(UG):
            dma("pool", wupb[:, :, c0:c0 + n], wup_v[:, :, c0:c0 + n], "wupa%d" % g, [Bwup], [Bwup_g[g]])
            dma("pool", wupb[:, :, DFF + c0:DFF + c0 + n], wup_v[:, :, DFF + c0:DFF + c0 + n], "wupb%d" % g,
                [Bwup], [Bwup_g[g]])
        for gi, f0 in enumerate(range(0, NFC, 6)):
            f1 = min(NFC, f0 + 6)
            dma("pool", wdnb[:, f0:f1, :], wdn_v[:, f0:f1, :], "wdn%d" % f0, [Bwdn], [Bwdn_g[gi]])
        dma("sp", fgn[:], fgn_d.partition_broadcast(128), "fgn", [], [Bfgn])

        NB = L // BLK
        out_ops = []

        def ffn_A(b, j):
            g = b * NB + j
            sl = g % 3
            for a in range(BLK // 128):
                xi = j * (BLK // 128) + a
                s2 = (g * 2 + a) % 2
                dma("sp", fxt[s2][:], x1_d[b, xi * 128:(xi + 1) * 128, :], "fxt%d" % s2, [Bx1d[b][xi]], [Bfxt[s2]])
                S.add("dve", lambda e: e.memset(sm[:, 30:31], 0.0), [], [Bsm])
                act(fxs[:], fxt[s2][:], AF.Square, [Bfxt[s2]], [Bfxs, Bsm], accum_out=sm[:, 30:31])
                ts("dve", sm[:, 31:32], sm[:, 30:31], 1.0 / D, EPS, ALU.mult, ALU.add, [Bsm], [Bsm])
                act(sm[:, 32:33], sm[:, 31:32], AF.Ln, [Bsm], [Bsm])
                act(sm[:, 33:34], sm[:, 32:33], AF.Exp, [Bsm], [Bsm], scale=-0.5)
                ts("dve", fxs[:], fxt[s2][:], sm[:, 33:34], None, ALU.mult, None, [Bfxt[s2], Bsm], [Bfxs])
                transposes([(PT[:, jj * 128:(jj + 1) * 128], fxs[:, jj * 128:(jj + 1) * 128]) for jj in range(8)],
                           [Bfxs], [PTB])
                tt("dve", fdiag[:], PT.rearrange("p (j t) -> p j t", j=8),
                   modP[:, 4, :, b:b + 1].to_broadcast([128, 8, 128]), ALU.mult, [PTB, Bmg[4]], [Bfdiag])
                tt("pool", h2T[sl][:, :, 1 + a * 128:1 + (a + 1) * 128], fdiag[:],
                   modP[:, 3, :, b:b + 1].to_broadcast([128, 8, 128]), ALU.add, [Bfdiag, Bmg[3]], [Bh2T[sl]])
            if j == 0:
                S.add("pool", lambda e: e.memset(h2T[sl][:, :, 0:1], 0.0), [], [Bh2T[sl]])
            else:
                sp_ = (g - 1) % 3
                cp("pool", h2T[sl][:, :, 0:1], h2T[sp_][:, :, BLK:BLK + 1], [Bh2T[sp_]], [Bh2T[sl]])
                cp("pool", h2T[sp_][:, :, BLK + 1:BLK + 2], h2T[sl][:, :, 1:2], [Bh2T[sl]], [Bh2T[sp_]])
            if j == NB - 1:
                S.add("pool", lambda e: e.memset(h2T[sl][:, :, BLK + 1:BLK + 2], 0.0), [], [Bh2T[sl]])

        def ffn_F(b, j):
            g = b * NB + j
            sl = g % 3
            if j == 0:
                for jj in range(8):
                    ts("dve", fdiag[:, jj, :], identf[:], modP[:, 5, jj, b:b + 1], None, ALU.mult, None,
                       [Bc, Bmg[5]], [Bfdiag])
                for n in range(2):
                    bk, bb = nb()
                    mmg(bk[:], [(onesf[:], fdiag[:, 4 * n:4 * n + 4, :].rearrange("p j q -> p (j q)"))],
                        [Bfdiag, Bc], [bb])
                    cp("act", gB2[:, n * 512:(n + 1) * 512], bk[:], [bb], [BgB2])
            for fc in range(NFC):
                ei = fc % NEW
                gbuf, t1, t2, hb = gbufs[ei], t1s[ei], t2s[ei], hbs[ei]
                Bgbuf, Bt1, Bt2, Bhb = Bgbufs[ei], Bt1s[ei], Bt2s[ei], Bhbs[ei]
                bg, bbg = nb()
                mmg(bg[:, 0:BLK + 2], [(wupb[:, k, fc * 128:(fc + 1) * 128], h2T[sl][:, k, :]) for k in range(8)],
                    [Bwup_g[fc // 4], Bh2T[sl]], [bbg])
                bu, bbu = nb()
                mmg(bu[:, 0:BLK], [(wupb[:, k, DFF + fc * 128:DFF + (fc + 1) * 128], h2T[sl][:, k, 1:BLK + 1])
                                   for k in range(8)], [Bwup_g[fc // 4], Bh2T[sl]], [bbu])
                cp("act", gbuf[:], bg[:, 0:BLK + 2], [bbg], [Bgbuf])
                ts("pool", t1[:], gbuf[:, 1:BLK + 1], cwP[:, fc, 1:2], cbP[:, fc:fc + 1], ALU.mult, ALU.add,
                   [Bgbuf, Bc], [Bt1])
                stt(t2[:], gbuf[:, 0:BLK], cwP[:, fc, 0:1], t1[:], ALU.mult, ALU.add, [Bgbuf, Bt1, Bc], [Bt2])
                stt(t1[:], gbuf[:, 2:BLK + 2], cwP[:, fc, 2:3], t2[:], ALU.mult, ALU.add, [Bgbuf, Bt2, Bc], [Bt1])
                act(hb[:], t1[:], AF.Silu, [Bt1], [Bhb])
                tt("dve", hT[:, fc, :], hb[:], bu[:, 0:BLK], ALU.mult, [Bhb, bbu], [BhT])
            for a in range(BLK // 128):
                xi = j * (BLK // 128) + a
                s2 = (g * 2 + a) % 2
                dma("sp", x2t[s2][:], x1_d[b, xi * 128:(xi + 1) * 128, :], "x2t%d" % s2, [Bx1d[b][xi]], [Bx2t[s2]])
                for n in range(2):
                    bk, bb = nb()
                    mmg(bk[:], [(hT[:, fc, a * 128:(a + 1) * 128], wdnb[:, fc, n * 512:(n + 1) * 512])
                                for fc in range(NFC)], [BhT] + Bwdn_g, [bb])
                    tt("dve", outt[:, n * 512:(n + 1) * 512], bk[:], gB2[:, n * 512:(n + 1) * 512], ALU.mult,
                       [bb, BgB2], [Boutt])
                tt("pool", x2t[s2][:], x2t[s2][:], outt[:], ALU.add, [Bx2t[s2], Boutt], [Bx2t[s2]])
                S.add("dve", lambda e: e.memset(sm[:, 40:41], 0.0), [], [Bsm])
                act(outt[:], x2t[s2][:], AF.Square, [Bx2t[s2]], [Boutt, Bsm], accum_out=sm[:, 40:41])
                ts("dve", sm[:, 41:42], sm[:, 40:41], 1.0 / D, EPS, ALU.mult, ALU.add, [Bsm], [Bsm])
                act(sm[:, 42:43], sm[:, 41:42], AF.Ln, [Bsm], [Bsm])
                act(sm[:, 43:44], sm[:, 42:43], AF.Exp, [Bsm], [Bsm], scale=-0.5)
                stt(outt[:], x2t[s2][:], sm[:, 43:44], fgn[:], ALU.mult, ALU.mult, [Bx2t[s2], Bsm, Bfgn], [Boutt])
                out_ops.append(dma("sp", out_d[b, xi * 128:(xi + 1) * 128, :], outt[:], "outst", [Boutt], []))

        seq = [(b, j) for b in range(2) for j in range(NB)]
        ffn_A(*seq[0])
        for i, (b, j) in enumerate(seq):
            if i + 1 < len(seq):
                ffn_A(*seq[i + 1])
            ffn_F(b, j)

        fw = out_ops[-4:] + out_ops[:1]
        if stage["stopped"]:
            last = {}
            for i, o in enumerate(S.ops):
                if o.is_dma:
                    last[o.key] = i
            fw = list(last.values())
        if REORDER:
            remap = S.reorder()
            fw = [remap[i] for i in fw]
        S.emit(final_wait_ops=fw)
        build_program.stats = (len(S.ops), A.hi, getattr(S, "est_total", None))
        build_program.dbg_names = stage["names"]
    return nc


def _consts():
    k = np.arange(128)
    ident = np.eye(128, dtype=np.float32)
    swap = np.where((k % 64) < 32, k + 32, k - 32)
    pm = np.zeros((128, 128), np.float32)
    pm[swap, k] = 1.0
    s = np.arange(128)[:, None]
    t = np.arange(128)[None, :]
    mf = (s <= t).astype(np.float32)
    mb = (s >= t).astype(np.float32)
    tt_ = np.arange(L, dtype=np.float32)
    rows = np.floor(tt_ / 64.0)
    cols = tt_ - rows * 64.0
    quarter = 32
    freqs = (10000.0 ** (-np.arange(quarter, dtype=np.float32) / quarter)).astype(np.float32)
    cos = np.zeros((128, L), np.float32)
    sin = np.zeros((128, L), np.float32)
    for kk in range(128):
        pos = rows if kk < 64 else cols
        i = kk % 32
        ang = (pos * freqs[i]).astype(np.float32)
        cos[kk] = np.cos(ang)
        sin[kk] = -np.sin(ang) if (kk % 64) < 32 else np.sin(ang)
    posf = np.tile((np.arange(128, dtype=np.float32) - 63.0)[None, :], (128, 1))
    posb = np.tile((64.0 - np.arange(128, dtype=np.float32))[None, :], (128, 1))
    return dict(ident=ident, pm=pm, mf=mf, mb=mb, cos=cos, sin=sin, posf=posf, posb=posb)


def make_in_maps(x, c, ctx, c_ctx, w_mod, b_mod, norm1_g, w_in, hgrn_lb, hgrn_norm_g, ret_decay, ret_norm_g,
                 w_out, norm2_g, w_up, conv_w, conv_b, w_down, final_g):
    f = lambda a: np.ascontiguousarray(np.asarray(a, dtype=np.float32))
    cst = _consts()
    pl = lambda v: f(np.asarray(v).reshape(-1, 128).T)
    shared = dict(
        w_mod=f(w_mod[0]), bmodP=pl(b_mod[0]), n1g=pl(norm1_g[0]), n2g=pl(norm2_g[0]), w_in=f(w_in[0]),
        lbP=f(np.asarray(hgrn_lb).reshape(2, 2, 4, 128).transpose(3, 0, 1, 2).reshape(128, 16)),
        hgn=f(hgrn_norm_g[0]).reshape(1, 512), rd=f(ret_decay[0]).reshape(1, 8), rtn=f(ret_norm_g[0]).reshape(1, 512),
        w_out=f(w_out[0]), w_up=f(w_up[0]),
        cwP=f(np.asarray(conv_w[0]).reshape(3, NFC, 128).transpose(2, 1, 0).reshape(128, 66)),
        cbP=pl(conv_b[0]), w_down=f(w_down[0]), fgn=f(final_g).reshape(1, D), **cst)
    maps = []
    for core in range(NCORES):
        cc = np.stack([np.asarray(c[2 * core]), np.asarray(c[2 * core + 1]), np.asarray(c_ctx)], axis=0)
        cT = f(cc.reshape(3, 8, 128).transpose(2, 1, 0).reshape(128, 24))
        m = dict(shared)
        m.update(x=f(x[2 * core:2 * core + 2]), ctx=f(ctx[2 * core:2 * core + 2]), cT=cT)
        maps.append(m)
    return maps


_NC_CACHE = {}


def kernel(**inputs):
    if "nc" not in _NC_CACHE:
        _NC_CACHE["nc"] = build_program()
    nc = _NC_CACHE["nc"]
    in_maps = make_in_maps(**inputs)
    res = run_bass_kernel_spmd(nc, in_maps, core_ids=list(range(NCORES)))
    out = np.concatenate([np.asarray(r["out"]) for r in res.results], axis=0)
    return out.astype(np.float32)
```

```python
import contextlib
import numpy as np
import ml_dtypes
import concourse.bass as bass
import concourse.mybir as mybir
from concourse.bass_utils import run_bass_kernel_spmd

F32 = mybir.dt.float32
BF16 = mybir.dt.bfloat16
AF = mybir.ActivationFunctionType
ALU = mybir.AluOpType
AX = mybir.AxisListType

D = 1024
L = 2048
LC = 256
TOK = L + LC
NT = TOK // 128
NX = L // 128
DFF = 2816
NFC = DFF // 128
INW = 4608
EPS = 1e-6
BLK = 256
NCORES = 8
REORDER = True
PRIO = "idx"


class Buf:
    __slots__ = ("name", "w", "r", "excl")

    def __init__(self, name, excl=False):
        self.name = name
        self.w = None
        self.r = []
        self.excl = excl


class Op:
    __slots__ = ("eng", "fn", "deps", "sig", "needs_sig", "is_dma", "idx", "key", "cost", "xfer", "t0", "t1", "tag")


class Sched:
    COMPUTE = ("pe", "act", "dve", "pool")

    def __init__(self, nc):
        self.nc = nc
        self.ops = []
        self.dma_count = {}
        self.last_dma = {}

    def add(self, eng, fn, reads=(), writes=(), dma=None, cost=300.0, xfer=0.0):
        op = Op()
        op.cost = float(cost)
        op.xfer = float(xfer)
        op.tag = getattr(self, "tag", "")
        op.eng = eng
        op.fn = fn
        op.idx = len(self.ops)
        op.is_dma = dma is not None
        op.needs_sig = op.is_dma
        op.key = dma
        writes = list(writes) + [b for b in reads if b.excl]
        reads = [b for b in reads if not b.excl]
        deps = {}
        for b in reads:
            if b.w is not None:
                deps[b.w] = True
        for b in writes:
            if b.w is not None:
                deps.setdefault(b.w, False)
            for r in b.r:
                deps.setdefault(r, False)
        op.deps = []
        for j, raw in deps.items():
            if j == op.idx:
                continue
            op.deps.append(j)
        for b in reads:
            b.r.append(op.idx)
        for b in writes:
            b.w = op.idx
            b.r = []
        if op.is_dma:
            prev = self.last_dma.get(dma)
            if prev is not None and prev not in op.deps:
                op.deps.append(prev)
            self.last_dma[dma] = op.idx
            c = self.dma_count.get(dma, 0) + 16
            self.dma_count[dma] = c
            op.sig = (dma, c)
        else:
            op.sig = None
        self.ops.append(op)
        return op.idx

    def seal_group(self, key, since=0):
        if key not in self.dma_count:
            return
        tot = self.dma_count[key]
        for op in self.ops[since:]:
            if op.is_dma and op.key == key:
                op.sig = (key, tot)

    def reorder(self):
        import heapq
        ops = self.ops
        n = len(ops)
        users = [[] for _ in range(n)]
        ndep = [0] * n
        for op in ops:
            ds = set(op.deps)
            op.deps = sorted(ds)
            ndep[op.idx] = len(op.deps)
            for j in op.deps:
                users[j].append(op.idx)
        LAT = 200.0
        blevel = [0.0] * n
        for i in range(n - 1, -1, -1):
            op = ops[i]
            m = 0.0
            for u in users[i]:
                if blevel[u] + LAT > m:
                    m = blevel[u] + LAT
            blevel[i] = op.cost + op.xfer + m
        if PRIO == "blevel":
            key = [-blevel[i] for i in range(n)]
        else:
            key = [float(i) for i in range(n)]
        engs = ("pe", "act", "dve", "pool", "sp")
        free = {e: 0.0 for e in engs}
        byready = {e: [] for e in engs}
        now = {e: [] for e in engs}
        ready_t = [0.0] * n
        fin = [0.0] * n
        dma_free = [0.0]
        for op in ops:
            if ndep[op.idx] == 0:
                heapq.heappush(byready[op.eng], (0.0, op.idx))
        order = []
        while len(order) < n:
            best = None
            for e in engs:
                br, nw = byready[e], now[e]
                while br and br[0][0] <= free[e]:
                    ii = heapq.heappop(br)[1]
                    heapq.heappush(nw, (key[ii], ii))
                if nw:
                    est = free[e]
                elif br:
                    est = br[0][0]
                else:
                    continue
                if best is None or est < best[0]:
                    best = (est, e)
            est, e = best
            if now[e]:
                i = heapq.heappop(now[e])[1]
            else:
                i = heapq.heappop(byready[e])[1]
            op = ops[i]
            op.t0 = est
            free[e] = est + op.cost
            if op.is_dma:
                st = max(est + op.cost, dma_free[0])
                fin[i] = st + op.xfer
                dma_free[0] = st + op.xfer * 0.6
            else:
                fin[i] = est + op.cost
            op.t1 = fin[i]
            order.append(i)
            for u in users[i]:
                ready_t[u] = max(ready_t[u], fin[i] + LAT)
                ndep[u] -= 1
                if ndep[u] == 0:
                    heapq.heappush(byready[ops[u].eng], (ready_t[u], u))
        newidx = {old: new for new, old in enumerate(order)}
        newops = [ops[i] for i in order]
        for op in newops:
            op.deps = [newidx[j] for j in op.deps]
            op.idx = newidx[op.idx]
        self.ops = newops
        self.est_total = max(fin) if fin else 0.0
        return newidx

    def emit(self, final_wait_ops=()):
        nc = self.nc
        for op in self.ops:
            kept = []
            for j in op.deps:
                o = self.ops[j]
                if o.eng == op.eng and op.eng == "pe" and not o.is_dma and not op.is_dma:
                    continue
                kept.append(j)
                o.needs_sig = True
            op.deps = kept
        with contextlib.ExitStack() as st:
            sems = {}
            for e in self.COMPUTE:
                sems[e] = st.enter_context(nc.semaphore("s_" + e))
            for k in self.dma_count:
                sems[k] = st.enter_context(nc.semaphore("d_" + str(k)))
            cnt = {e: 0 for e in self.COMPUTE}
            for op in self.ops:
                if not op.is_dma and op.needs_sig:
                    cnt[op.eng] += 1
                    op.sig = (op.eng, cnt[op.eng])
            block = st.enter_context(nc.Block())
            ops = self.ops

            def run(engname, e):
                waited = {}
                for op in ops:
                    if op.eng != engname:
                        continue
                    for j in op.deps:
                        k, v = ops[j].sig
                        if waited.get(k, 0) < v:
                            e.wait_ge(sems[k], v)
                            waited[k] = v
                    ins = op.fn(e)
                    if op.needs_sig:
                        ins.then_inc(sems[op.sig[0]], 16 if op.is_dma else 1)
                if engname == "sp":
                    for j in final_wait_ops:
                        k, v = ops[j].sig
                        if waited.get(k, 0) < v:
                            e.wait_ge(sems[k], v)
                            waited[k] = v

            @block.tensor
            def _(e):
                run("pe", e)

            @block.scalar
            def _(e):
                run("act", e)

            @block.vector
            def _(e):
                run("dve", e)

            @block.gpsimd
            def _(e):
                run("pool", e)

            @block.sync
            def _(e):
                run("sp", e)


class Arena:
    def __init__(self, big, limit):
        self.big = big
        self.off = 0
        self.limit = limit
        self.hi = 0

    def _view(self, ap, shape):
        if len(shape) == 2:
            return ap
        if len(shape) == 3:
            return ap.rearrange("p (a b) -> p a b", a=shape[1])
        return ap.rearrange("p (a b c) -> p a b c", a=shape[1], b=shape[2])

    def alloc(self, shape, dt):
        n = int(np.prod(shape[1:]))
        nb = n * (4 if dt == F32 else 2)
        nb = (nb + 3) // 4 * 4
        o = self.off
        self.off += nb
        self.hi = max(self.hi, self.off)
        assert self.off <= self.limit, ("SBUF arena overflow", self.off)
        ap = self.big[:, o // 4:(o + nb) // 4]
        if dt != F32:
            ap = ap.bitcast(BF16)[:, 0:n]
        return self._view(ap, shape)


def build_program(debug=False, stop=None):
    nc = bass.Bass("TRN2", target_bir_lowering=False)

    def din(name, shape, dt=F32):
        return nc.dram_tensor(name, list(shape), dt, kind="ExternalInput").ap()

    x_d = din("x", [2, L, D])
    ctx_d = din("ctx", [2, LC, D])
    cT_d = din("cT", [128, 24])
    wmod_d = din("w_mod", [D, 6 * D])
    bmodP_d = din("bmodP", [128, 48])
    n1g_d = din("n1g", [128, 8])
    n2g_d = din("n2g", [128, 8])
    win_d = din("w_in", [D, INW])
    lbP_d = din("lbP", [128, 16])
    hgn_d = din("hgn", [1, 512])
    rd_d = din("rd", [1, 8])
    rtn_d = din("rtn", [1, 512])
    wout_d = din("w_out", [D, D])
    wup_d = din("w_up", [D, 2 * DFF])
    cwP_d = din("cwP", [128, 66])
    cbP_d = din("cbP", [128, 22])
    wdn_d = din("w_down", [DFF, D])
    fgn_d = din("fgn", [1, D])
    ident_d = din("ident", [128, 128])
    pm_d = din("pm", [128, 128])
    mf_d = din("mf", [128, 128])
    mb_d = din("mb", [128, 128])
    cos_d = din("cos", [128, L])
    sin_d = din("sin", [128, L])
    posf_d = din("posf", [128, 128])
    posb_d = din("posb", [128, 128])
    out_d = nc.dram_tensor("out", [2, L, D], F32, kind="ExternalOutput").ap()
    x1_d = nc.dram_tensor("x1s", [2, L, D], F32,
                          kind="ExternalOutput" if debug else "Internal").ap()

    st = contextlib.ExitStack()
    with st:
        LIMIT = 212000
        big = st.enter_context(nc.sbuf_tensor("big", [128, LIMIT // 4], F32))
        banks = [st.enter_context(nc.psum_tensor("bank%d" % i, [128, 512], F32)) for i in range(8)]
        bankB = [Buf("bank%d" % i, excl=True) for i in range(8)]
        PT = banks[7][:].bitcast(BF16)
        PTB = bankB[7]
        S = Sched(nc)
        A = Arena(big, LIMIT)
        dbg_d = nc.dram_tensor("dbg", [128, 16384], F32, kind="ExternalOutput").ap() if debug else None
        stage = {"n": 0, "off": 0, "stopped": False, "names": []}
        _add = S.add

        def gated_add(*a, **k):
            if stage["stopped"]:
                return 0
            if isinstance(stop, int) and len(S.ops) >= stop:
                stage["stopped"] = True
                return 0
            return _add(*a, **k)
        S.add = gated_add

        def dump(name, ap2d, bufs):
            n = ap2d.shape[1]
            o = stage["off"]
            stage["off"] += n
            stage["names"].append((name, o, n))
            _add("pool", lambda e: e.dma_start(out=dbg_d[:, o:o + n], in_=ap2d), list(bufs), [], dma="dbg%d" % len(stage["names"]))

        def mark(name, dumps=()):
            if stage["stopped"]:
                return
            if stop is not None and name == stop:
                for nm, ap, bufs in dumps():
                    dump(nm, ap, bufs)
                stage["stopped"] = True

        rot = [0]

        def nb():
            i = rot[0]
            rot[0] = (i + 1) % 7
            return banks[i], bankB[i]

        def nel(ap):
            n = 1
            for d in ap.shape[1:]:
                n *= int(d)
            return n

        def ecost(eng, n):
            if eng == "act":
                return n * 0.83 + 300.0
            if eng == "dve":
                return n * 1.04 + 170.0
            return n * 1.7 + 350.0

        def act(out, in_, func, reads, writes, **kw):
            return S.add("act", lambda e: e.activation(out=out, in_=in_, func=func, **kw), reads, writes,
                         cost=ecost("act", nel(out)))

        def ts(eng, out, in0, s1, s2, op0, op1, reads, writes):
            c = ecost(eng, nel(out))
            if s2 is None:
                return S.add(eng, lambda e: e.tensor_scalar(out=out, in0=in0, scalar1=s1, scalar2=None, op0=op0),
                             reads, writes, cost=c)
            return S.add(eng, lambda e: e.tensor_scalar(out=out, in0=in0, scalar1=s1, scalar2=s2, op0=op0, op1=op1),
                         reads, writes, cost=c)

        def tt(eng, out, in0, in1, op, reads, writes):
            return S.add(eng, lambda e: e.tensor_tensor(out=out, in0=in0, in1=in1, op=op), reads, writes,
                         cost=ecost(eng, nel(out)))

        def stt(out, in0, sc, in1, op0, op1, reads, writes):
            return S.add("dve", lambda e: e.scalar_tensor_tensor(out=out, in0=in0, scalar=sc, in1=in1, op0=op0, op1=op1),
                         reads, writes, cost=ecost("dve", nel(out)))

        def cp(eng, out, in_, reads, writes):
            if eng == "act":
                return act(out, in_, AF.Copy, reads, writes)
            return S.add(eng, lambda e: e.tensor_copy(out=out, in_=in_), reads, writes, cost=ecost(eng, nel(out)))

        def dma(eng, out, in_, key, reads, writes):
            nbytes = 128 * nel(out) * 4
            if eng == "pool":
                return S.add(eng, lambda e: e.dma_start(out=out, in_=in_), reads, writes, dma=key,
                             cost=1500.0, xfer=nbytes / 130.0 + 2000.0)
            return S.add(eng, lambda e: e.dma_start(out=out, in_=in_), reads, writes, dma=key,
                         cost=250.0, xfer=nbytes / 200.0 + 2000.0)

        def mmcost(pairs):
            c = 0.0
            for l, r in pairs:
                c += max(64, nel(r)) / 2.2 + 35.0
            return c

        def mmg(out, pairs, reads, writes):
            pairs = list(pairs)

            def fn(e):
                n = len(pairs)
                ins = None
                for i, (l, r) in enumerate(pairs):
                    ins = e.matmul(out, lhsT=l, rhs=r, start=(i == 0), stop=(i == n - 1))
                return ins
            return S.add("pe", fn, reads, writes, cost=mmcost(pairs))

        def mm_multi(groups, reads, writes):
            groups = [(o, list(p)) for o, p in groups]

            def fn(e):
                ins = None
                for out, pairs in groups:
                    n = len(pairs)
                    for i, (l, r) in enumerate(pairs):
                        ins = e.matmul(out, lhsT=l, rhs=r, start=(i == 0), stop=(i == n - 1))
                return ins
            return S.add("pe", fn, reads, writes, cost=sum(mmcost(p) for _, p in groups))

        def transposes(items, reads, writes):
            items = list(items)

            def fn(e):
                ins = None
                for o, i_ in items:
                    ins = e.transpose(out=o, in_=i_, identity=identb)
                return ins
            return S.add("pe", fn, reads + [Bc], writes, cost=110.0 * len(items))

        identb = A.alloc([128, 128], BF16)
        pmb = A.alloc([128, 128], BF16)
        mfb = A.alloc([128, 128], BF16)
        mbb = A.alloc([128, 128], BF16)
        identf = A.alloc([128, 128], F32)
        onesf = A.alloc([128, 128], F32)
        cT = A.alloc([128, 24], F32)
        cs = A.alloc([128, 8, 3], BF16)
        bmodP = A.alloc([128, 6, 8], F32)
        n1g = A.alloc([128, 8], F32)
        n2g = A.alloc([128, 8], F32)
        modP = A.alloc([128, 6, 8, 3], F32)
        lbP = A.alloc([128, 2, 2, 4], F32)
        lbv = A.alloc([128, 2, 4], F32)
        oml = A.alloc([128, 2, 4], F32)
        rdt = A.alloc([128, 8], F32)
        lg = A.alloc([128, 8], F32)
        nlg = A.alloc([128, 8], F32)
        g64 = A.alloc([128, 8], F32)
        g128 = A.alloc([128, 8], F32)
        cwP = A.alloc([128, 22, 3], F32)
        cbP = A.alloc([128, 22], F32)
        sm = A.alloc([128, 64], F32)
        Bc = Buf("consts")
        Bmod = Buf("modP")
        Bmg = [Buf("modg%d" % g) for g in range(6)]
        Bsm = Buf("sm")
        base_off = A.off

        cosb = A.alloc([128, L], BF16)
        sinb = A.alloc([128, L], BF16)
        posf = A.alloc([128, 128], F32)
        posb = A.alloc([128, 128], F32)
        RM = A.alloc([128, NT, 128], BF16)
        RT = A.alloc([128, 4, 128], F32)
        hgn = A.alloc([128, 512], F32)
        rtn = A.alloc([128, 512], F32)
        hxT = A.alloc([128, 8, TOK], BF16)
        mixT = A.alloc([128, 8, L], BF16)
        wfm = [A.alloc([128, 8, 384], BF16) for _ in range(2)]
        wtm = [A.alloc([128, 8, 256], BF16) for _ in range(2)]
        qT = A.alloc([128, L], F32)
        ar1 = A.off
        FA = A.alloc([128, TOK], F32)
        FB = A.alloc([128, TOK], F32)
        FC = A.alloc([128, TOK], F32)
        ar2 = A.off
        KdT = A.alloc([128, TOK], BF16)
        Kd = A.alloc([128, NT, 128], BF16)
        US = A.alloc([128, NT, 128], F32)
        ar2e = A.off
        QdT = [A.alloc([128, L], BF16) for _ in range(2)]
        S16 = [A.alloc([128, NX, 128], BF16) for _ in range(2)]
        AT = [A.alloc([128, NX, 128], BF16) for _ in range(2)]
        V = A.alloc([128, NT, 128], BF16)
        SG = A.alloc([128, NX, 128], BF16)
        tmpo = A.alloc([128, 4, 128], F32)
        tmpq = A.alloc([128, 4, 128], F32)
        mtok = A.alloc([128, 4, 128], BF16)
        tab = A.alloc([128, 2, 6, NT], F32)
        mixer_hi = A.off
        A.off = ar1
        xt = [A.alloc([128, D], F32) for _ in range(2)]
        xs = [A.alloc([128, D], BF16) for _ in range(2)]
        x1t = A.alloc([128, D], F32)
        gB1 = A.alloc([128, D], F32)
        diag = A.alloc([128, 8, 128], F32)
        assert A.off <= ar2
        A.off = ar2
        woutb = A.alloc([128, 8, D], BF16)
        assert A.off <= ar2e
        A.off = mixer_hi

        BFA, BFB, BFC = Buf("FA"), Buf("FB"), Buf("FC")
        BKdT, BKd, BUS = Buf("KdT"), Buf("Kd"), Buf("US")
        BhxT, BmixT, BqT = Buf("hxT"), Buf("mixT"), Buf("qT")
        Bwfm = [Buf("wfm0"), Buf("wfm1")]
        Bwtm = [Buf("wtm0"), Buf("wtm1")]
        BQd = [Buf("Qd0"), Buf("Qd1")]
        BS16 = [Buf("S160"), Buf("S161")]
        BAT = [Buf("AT0"), Buf("AT1")]
        BV, BSG = Buf("V"), Buf("SG")
        Btmpo, Btmpq, Bmtok, Btab = Buf("tmpo"), Buf("tmpq"), Buf("mtok"), Buf("tab")
        BRT = Buf("RT")
        Bxt = [BFA, BFA]
        Bxs = [BFB, BFB]
        Bx1t = BFB
        BgB1 = BFC
        Bdiag = BFC
        Bwout = [BKdT, BKd, BUS]
        mixer_bufs0 = [BFA, BFB, BFC, BKdT, BKd, BUS, BhxT, BmixT, BqT] + Bwfm + Bwtm + BQd + BS16 + BAT + \
                     [BV, BSG, Btmpo, Btmpq, Bmtok, Btab, BRT, Bc]
        Bxt = [Buf("xt0"), Buf("xt1")]
        Bxs = [Buf("xs0"), Buf("xs1")]
        Bx1t = Buf("x1t")
        BgB1 = Buf("gB1")
        Bdiag = Buf("diag")
        allF = [BFA, BFB, BFC]
        scratchB = Bxt + Bxs + [Bx1t, BgB1, Bdiag]

        Bbar = Buf("bar")

        def phase_barrier():
            S.add("dve", lambda e: e.memset(sm[:, 61:62], 0.0), [], allF + scratchB + [Bbar])

        ckn = [0]

        def CKf(q):
            ckn[0] += 1
            return "c%s%d" % (q, ckn[0] % 5)
        for dst, src in ((identb[:], ident_d), (pmb[:], pm_d), (mfb[:], mf_d), (mbb[:], mb_d),
                         (cosb[:], cos_d), (sinb[:], sin_d)):
            dma("pool", dst, src, CKf("p"), [], [Bc])
        for dst, src in ((identf[:], ident_d), (cT[:], cT_d), (bmodP[:], bmodP_d.rearrange("p (g j) -> p g j", g=6)),
                         (n1g[:], n1g_d), (n2g[:], n2g_d),
                         (lbP[:], lbP_d.rearrange("p (d i h) -> p d i h", d=2, i=2)),
                         (rdt[:], rd_d.partition_broadcast(128)),
                         (cwP[:], cwP_d.rearrange("p (f j) -> p f j", j=3)), (cbP[:], cbP_d),
                         (posf[:], posf_d), (posb[:], posb_d),
                         (hgn[:], hgn_d.partition_broadcast(128)), (rtn[:], rtn_d.partition_broadcast(128))):
            dma("sp", dst, src, CKf("s"), [], [Bc])
        S.add("dve", lambda e: e.memset(onesf[:], 1.0), [], [Bc])
        S.add("dve", lambda e: e.memset(RM[:], 1.0), [], [Bc])
        S.add("dve", lambda e: e.memset(RM[:, :, 0:1], 0.0), [], [Bc])
        act(cs[:], cT[:].rearrange("p (k j) -> p k j", j=3), AF.Silu, [Bc], [Bmod])
        tt("dve", lbv[:], lbP[:, :, 0, :], lbP[:, :, 1, :], ALU.subtract, [Bc], [Bmod])
        act(lbv[:], lbv[:], AF.Sigmoid, [Bmod], [Bmod])
        ts("dve", oml[:], lbv[:], -1.0, 1.0, ALU.mult, ALU.add, [Bmod], [Bmod])
        act(lg[:], rdt[:], AF.Sigmoid, [Bc], [Bmod])
        act(lg[:], lg[:], AF.Ln, [Bmod], [Bmod])
        ts("dve", nlg[:], lg[:], -1.0, None, ALU.mult, None, [Bmod], [Bmod])
        act(g64[:], lg[:], AF.Exp, [Bmod], [Bmod], scale=64.0)
        act(g128[:], lg[:], AF.Exp, [Bmod], [Bmod], scale=128.0)

        A.off = ar1
        wmv = [A.alloc([128, 8, 1024], BF16)]
        A.off = ar2
        wmv.append(A.alloc([128, 8, 1024], BF16))
        A.off = mixer_hi
        Bwm = [allF, [BKdT, BKd, BUS]]
        wmod_v = wmod_d.rearrange("(k p) n -> p k n", p=128)
        for gi, g in enumerate((1, 0, 2, 3, 4, 5)):
            sl = 0 if gi == 0 else 1
            dma("pool", wmv[sl][:], wmod_v[:, :, g * 1024:(g + 1) * 1024], "wm%d" % sl, [], Bwm[sl])
            bk, bb = nb()
            groups = []
            for j in range(8):
                groups.append((bk[:, j * 4:j * 4 + 3],
                               [(wmv[sl][:, k, j * 128:(j + 1) * 128], cs[:, k, :]) for k in range(8)]))
            mm_multi(groups, Bwm[sl] + [Bmod], [bb])
            tt("dve", modP[:, g], bk[:, 0:32].rearrange("p (j c) -> p j c", c=4)[:, :, 0:3],
               bmodP[:, g, :].unsqueeze(2).to_broadcast([128, 8, 3]), ALU.add, [bb, Bc], [Bmg[g]])
            if g in (1, 4):
                ng = n1g if g == 1 else n2g
                ts("dve", modP[:, g], modP[:, g], 1.0, None, ALU.add, None, [Bmg[g]], [Bmg[g]])
                tt("dve", modP[:, g], modP[:, g], ng[:].unsqueeze(2).to_broadcast([128, 8, 3]), ALU.mult,
                   [Bmg[g], Bc], [Bmg[g]])
        mark("setup", lambda: [("modP", modP[:].rearrange("p a b c -> p (a b c)"), Bmg), ("lbv", lbv[:].rearrange("p a b -> p (a b)"), [Bmod]),
                               ("lg", lg[:], [Bmod]), ("g64", g64[:], [Bmod])])
        xt1 = [QdT[i][:].bitcast(F32) for i in range(2)]
        xs1 = [S16[i][:].rearrange("p a b -> p (a b)")[:, 0:D] for i in range(2)]
        tmp1 = AT[0][:].rearrange("p a b -> p (a b)").bitcast(F32).rearrange("p (j t) -> p j t", j=8)
        Bsm1 = [Buf("sm_p1_0"), Buf("sm_p1_1")]

        def norm_T(src_ap, src_key_bufs, slot, gA, gB_, col, dstT, dstB, c0, extra_reads=(), from_dram=True,
                   keep=None):
            bx, bs, bsm = BQd[slot], BS16[slot], Bsm1[slot]
            dma("sp", xt1[slot], src_ap, "p1x%d" % slot, list(src_key_bufs), [bx])
            S.add("dve", lambda e: e.memset(sm[:, slot:slot + 1], 0.0), [], [bsm])
            act(xs1[slot], xt1[slot], AF.Square, [bx], [bs, bsm], accum_out=sm[:, slot:slot + 1])
            ts("dve", sm[:, 2 + slot:3 + slot], sm[:, slot:slot + 1], 1.0 / D, EPS, ALU.mult, ALU.add, [bsm], [bsm])
            act(sm[:, 4 + slot:5 + slot], sm[:, 2 + slot:3 + slot], AF.Ln, [bsm], [bsm])
            act(sm[:, 6 + slot:7 + slot], sm[:, 4 + slot:5 + slot], AF.Exp, [bsm], [bsm], scale=-0.5)
            act(xs1[slot], xt1[slot], AF.Copy, [bx, bsm], [bs], scale=sm[:, 6 + slot:7 + slot])
            transposes([(PT[:, j * 128:(j + 1) * 128], xs1[slot][:, j * 128:(j + 1) * 128]) for j in range(8)],
                       [bs], [PTB])
            tt("dve", tmp1, PT.rearrange("p (j t) -> p j t", j=8),
               modP[:, gA, :, col:col + 1].to_broadcast([128, 8, 128]), ALU.mult, [PTB, Bmg[gA]], [BAT[0]])
            tt("pool", dstT[:, :, c0:c0 + 128], tmp1,
               modP[:, gB_, :, col:col + 1].to_broadcast([128, 8, 128]), ALU.add, [BAT[0], Bmg[gB_]] + list(extra_reads),
               [dstB])

        def fm_proj(wv, wB, c, t0, n):
            bk, bb = nb()
            mmg(bk[:, 0:n], [(wv[:, k, c * 128:(c + 1) * 128], hxT[:, k, t0:t0 + n]) for k in range(8)],
                [wB, BhxT], [bb])
            return bk, bb

        def gla_dir(d):
            a1 = tab[:, d, 0, :]
            a2 = tab[:, d, 1, :]
            Dc = tab[:, d, 2, :]
            for c0 in range(0, NT, 8):
                n = min(8, NT - c0)
                transposes([(PT[:, a * 128:(a + 1) * 128], KdT[:, (c0 + a) * 128:(c0 + a + 1) * 128]) for a in range(n)],
                           [BKdT], [PTB])
                cp("act", Kd[:].rearrange("p a k -> p (a k)")[:, c0 * 128:(c0 + n) * 128], PT[:, 0:n * 128], [PTB], [BKd])
            skip = NT - 1 if d == 0 else 2
            for c0 in range(0, NT, 4):
                cl = [c for c in range(c0, min(c0 + 4, NT))]
                bk, bb = nb()
                mm_multi([(bk[:, (c - c0) * 128:(c - c0 + 1) * 128], [(Kd[:, c, :], V[:, c, :])]) for c in cl],
                         [BKd, BV], [bb])
                n = len(cl)
                tt("dve", US[:, c0:c0 + n, :], bk[:, 0:n * 128].rearrange("p (a v) -> p a v", a=n),
                   a1[:, c0:c0 + n].unsqueeze(2).to_broadcast([128, n, 128]), ALU.mult, [bb, Btab], [BUS])
            order = list(range(NT)) if d == 0 else [1, 0] + list(range(NT - 1, 1, -1))
            for j in range(1, NT - 1):
                c, pc = order[j], order[j - 1]
                stt(US[:, c, :], US[:, pc, :], Dc[:, c:c + 1], US[:, c, :], ALU.mult, ALU.add, [BUS, Btab], [BUS])
            if d == 0:
                tt("pool", S16[d][:], US[:, 1:NT - 1, :], a2[:, 2:NT].unsqueeze(2).to_broadcast([128, NX, 128]),
                   ALU.mult, [BUS, Btab], [BS16[d]])
            else:
                tt("pool", S16[d][:, 0:NX - 1, :], US[:, 3:NT, :],
                   a2[:, 2:NT - 1].unsqueeze(2).to_broadcast([128, NX - 1, 128]), ALU.mult, [BUS, Btab], [BS16[d]])
                tt("pool", S16[d][:, NX - 1, :], US[:, 0, :], a2[:, NT - 1:NT].to_broadcast([128, 128]),
                   ALU.mult, [BUS, Btab], [BS16[d]])
            mask = mfb if d == 0 else mbb
            for x0 in range(0, NX, 4):
                bk, bb = nb()
                mm_multi([(bk[:, a * 128:(a + 1) * 128],
                           [(KdT[:, (x0 + a + 2) * 128:(x0 + a + 3) * 128], QdT[d][:, (x0 + a) * 128:(x0 + a + 1) * 128])])
                          for a in range(4)], [BKdT, BQd[d]], [bb])
                tt("dve", AT[d][:, x0:x0 + 4, :], bk[:].rearrange("p (a t) -> p a t", a=4),
                   mask[:].unsqueeze(1).to_broadcast([128, 4, 128]), ALU.mult, [bb, Bc], [BAT[d]])

        Bsm_go = Buf("sm_go")
        Bsm_fa, Bsm_ff = Buf("sm_fa"), Buf("sm_ff")

        def gla_out(hd, is_ret, h):
            Bsm = Bsm_go
            gain = rtn if is_ret else hgn
            for x0 in range(0, NX, 4):
                bk, bb = nb()
                groups = []
                for a in range(4):
                    xi = x0 + a
                    pairs = []
                    for d in range(2):
                        pairs.append((AT[d][:, xi, :], V[:, xi + 2, :]))
                        pairs.append((QdT[d][:, xi * 128:(xi + 1) * 128], S16[d][:, xi, :]))
                    groups.append((bk[:, a * 128:(a + 1) * 128], pairs))
                mm_multi(groups, BAT + BQd + BS16 + [BV], [bb])
                o3 = bk[:].rearrange("p (a v) -> p a v", a=4)
                if is_ret:
                    cp("act", tmpo[:].rearrange("p a v -> p (a v)"), bk[:], [bb], [Btmpo])
                else:
                    tt("dve", tmpo[:], o3, SG[:, x0:x0 + 4, :], ALU.mult, [bb, BSG], [Btmpo])
                tt("pool", tmpq[:], tmpo[:], tmpo[:], ALU.mult, [Btmpo], [Btmpq])
                S.add("dve", lambda e: e.reduce_sum(out=sm[:, 8:12], in_=tmpq[:], axis=AX.X), [Btmpq], [Bsm], cost=700.0)
                ts("dve", sm[:, 12:16], sm[:, 8:12], 1.0 / 128, EPS, ALU.mult, ALU.add, [Bsm], [Bsm])
                act(sm[:, 16:20], sm[:, 12:16], AF.Ln, [Bsm], [Bsm])
                act(sm[:, 20:24], sm[:, 16:20], AF.Exp, [Bsm], [Bsm], scale=-0.5)
                tt("dve", tmpo[:], tmpo[:], sm[:, 20:24].unsqueeze(2).to_broadcast([128, 4, 128]), ALU.mult,
                   [Btmpo, Bsm], [Btmpo])
                gb = gain[:, h * 128:(h + 1) * 128].unsqueeze(1).to_broadcast([128, 4, 128])
                if is_ret:
                    tt("pool", tmpq[:], tmpo[:], gb, ALU.mult, [Btmpo, Bc], [Btmpq])
                    tt("pool", mtok[:], tmpq[:], SG[:, x0:x0 + 4, :], ALU.mult, [Btmpq, BSG], [Bmtok])
                else:
                    tt("pool", mtok[:], tmpo[:], gb, ALU.mult, [Btmpo, Bc], [Bmtok])
                transposes([(PT[:, a * 128:(a + 1) * 128], mtok[:, a, :]) for a in range(4)], [Bmtok], [PTB])
                cp("act", mixT[:, hd, x0 * 128:(x0 + 4) * 128], PT[:, 0:512], [PTB], [BmixT])

        def tm_proj(sl, is_ret):
            for i0 in range(0, NT, 2):
                bk, bb = nb()
                n = 128 if i0 < 2 else 256
                mm_multi([(bk[:, a * 256:a * 256 + n],
                           [(hxT[:, k, (i0 + a) * 128:(i0 + a + 1) * 128], wtm[sl][:, k, 0:n]) for k in range(8)])
                          for a in range(2)], [BhxT, Bwtm[sl]], [bb])
                b3 = bk[:].rearrange("p (a c) -> p a c", a=2)
                cp("dve", V[:, i0:i0 + 2, :], b3[:, :, 0:128], [bb], [BV])
                if i0 >= 2:
                    for a in range(2):
                        act(SG[:, i0 - 2 + a, :], bk[:, a * 256 + 128:a * 256 + 256], AF.Silu if is_ret else AF.Sigmoid,
                            [bb], [BSG])

        def load_head_weights(sl, fm_cols, tm_cols):
            since = len(S.ops)
            for i, c in enumerate(fm_cols):
                dma("pool", wfm[sl][:, :, i * 128:(i + 1) * 128], win_v[:, :, c:c + 128], "wf%d_%d" % (sl, i), [], [Bwfm[sl]])
            for i, c in enumerate(tm_cols):
                dma("pool", wtm[sl][:, :, i * 128:(i + 1) * 128], win_v[:, :, c:c + 128], "wt%d_%d" % (sl, i), [], [Bwtm[sl]])

        win_v = win_d.rearrange("(k p) n -> p k n", p=128)
        FA3 = FA[:].rearrange("p (c t) -> p c t", t=128)
        FB3 = FB[:].rearrange("p (c t) -> p c t", t=128)
        FC3 = FC[:].rearrange("p (c t) -> p c t", t=128)
        TOKBLK = [(0, 256)] + [(256 + i * 512, 512) for i in range(4)]
        XBLK = [(256 + i * 512, 512) for i in range(4)]

        def hgrn_head(b, h, sl):
            for (t0, n) in XBLK:
                bk, bb = fm_proj(wfm[sl], Bwfm[sl], 0, t0, n)
                cp("act", qT[:, t0 - 256:t0 - 256 + n], bk[:, 0:n], [bb], [BqT])
            tm_proj(sl, False)
            for d in range(2):
                a1, a2, Dc, mid, tot, tmp = (tab[:, d, i, :] for i in range(6))
                for (t0, n) in TOKBLK:
                    bk, bb = fm_proj(wfm[sl], Bwfm[sl], 1 + d, t0, n)
                    act(FA[:, t0:t0 + n], bk[:, 0:n], AF.Sigmoid, [bb], [BFA])
                ts("dve", FA[:], FA[:], oml[:, d, h:h + 1], lbv[:, d, h:h + 1], ALU.mult, ALU.add, [BFA, Bmod], [BFA])
                act(FB[:], FA[:], AF.Ln, [BFA], [BFB])
                ts("pool", FA[:], FA[:], -1.0, 1.0, ALU.mult, ALU.add, [BFA], [BFA])
                S.add("dve", lambda e: e.tensor_tensor_scan(out=FC[:], data0=RM[:].rearrange("p c t -> p (c t)"),
                                                              data1=FB[:], initial=0.0, op0=ALU.mult, op1=ALU.add),
                      [BFB, Bc], [BFC], cost=5000.0)
                cp("dve", tot, FC3[:, :, 127], [BFC], [Btab])
                act(Dc, tot, AF.Exp, [Btab], [Btab])
                if d == 0:
                    cp("dve", mid, FC3[:, :, 63], [BFC], [Btab])
                    act(a2, mid, AF.Exp, [Btab], [Btab])
                    tt("dve", tmp, tot, mid, ALU.subtract, [Btab], [Btab])
                    act(a1, tmp, AF.Exp, [Btab], [Btab])
                else:
                    tt("dve", FC[:], FC[:], FB[:], ALU.subtract, [BFC, BFB], [BFC])
                    cp("dve", mid, FC3[:, :, 64], [BFC], [Btab])
                    act(a1, mid, AF.Exp, [Btab], [Btab])
                    tt("dve", tmp, tot, mid, ALU.subtract, [Btab], [Btab])
                    act(a2, tmp, AF.Exp, [Btab], [Btab])
                tt("dve", FC3, FC3, mid.unsqueeze(2).to_broadcast([128, NT, 128]), ALU.subtract, [BFC, Btab], [BFC])
                sq, sk = (1.0, -1.0) if d == 0 else (-1.0, 1.0)
                act(FB[:, 256:TOK], FC[:, 256:TOK], AF.Exp, [BFC], [BFB], scale=sq)
                tt("pool", QdT[d][:], qT[:], FB[:, 256:TOK], ALU.mult, [BqT, BFB], [BQd[d]])
                act(FB[:], FC[:], AF.Exp, [BFC], [BFB], scale=sk)
                tt("pool", KdT[:], FA[:], FB[:], ALU.mult, [BFA, BFB], [BKdT])
                gla_dir(d)
            gla_out(h, False, h)

        def ret_head(b, h, sl):
            lnsc = float(np.log(128.0 ** -0.5))
            act(RT[:, 0, :], posf[:], AF.Exp, [Bc, Bmod], [BRT], scale=lg[:, h:h + 1])
            act(RT[:, 1, :], posf[:], AF.Exp, [Bc, Bmod], [BRT], scale=nlg[:, h:h + 1])
            act(RT[:, 2, :], posb[:], AF.Exp, [Bc, Bmod], [BRT], scale=lg[:, 4 + h:5 + h])
            act(RT[:, 3, :], posb[:], AF.Exp, [Bc, Bmod], [BRT], scale=nlg[:, 4 + h:5 + h])
            for kd in (1, 3):
                ts("dve", RT[:, kd, :], RT[:, kd, :], 128.0 ** -0.5, None, ALU.mult, None, [BRT], [BRT])
            for d in range(2):
                for i, src in ((0, g64), (1, g64), (2, g128)):
                    cp("dve", tab[:, d, i, :], src[:, d * 4 + h:d * 4 + h + 1].to_broadcast([128, NT]), [Bmod], [Btab])
            tm_proj(sl, True)
            for which in range(2):
                dst, dB = (qT, BqT) if which == 0 else (FA, BFA)
                blks = XBLK if which == 0 else TOKBLK
                for (t0, n) in blks:
                    o0 = t0 - 256 if which == 0 else t0
                    bk, bb = fm_proj(wfm[sl], Bwfm[sl], which, t0, n)
                    if t0 < 256:
                        cp("act", dst[:, o0:o0 + n], bk[:, 0:n], [bb], [dB])
                        continue
                    xo = t0 - 256
                    cp("act", FB[:, 0:n].bitcast(BF16)[:, 0:n], bk[:, 0:n], [bb], [BFB])
                    tt("dve", FC[:, 0:n], bk[:, 0:n], cosb[:, xo:xo + n], ALU.mult, [bb, Bc], [BFC])
                    bk2, bb2 = nb()
                    mmg(bk2[:, 0:n], [(pmb[:], FB[:, 0:n].bitcast(BF16)[:, 0:n])], [BFB, Bc], [bb2])
                    tt("dve", FC[:, 512:512 + n], bk2[:, 0:n], sinb[:, xo:xo + n], ALU.mult, [bb2, Bc], [BFC])
                    tt("pool", dst[:, o0:o0 + n], FC[:, 0:n], FC[:, 512:512 + n], ALU.add, [BFC], [dB])
            for d in range(2):
                tt("pool", QdT[d][:].rearrange("p (c t) -> p c t", t=128), qT[:].rearrange("p (c t) -> p c t", t=128),
                   RT[:, 2 * d, :].unsqueeze(1).to_broadcast([128, NX, 128]), ALU.mult, [BqT, BRT], [BQd[d]])
                tt("pool", KdT[:].rearrange("p (c t) -> p c t", t=128), FA3,
                   RT[:, 2 * d + 1, :].unsqueeze(1).to_broadcast([128, NT, 128]), ALU.mult, [BFA, BRT], [BKdT])
                gla_dir(d)
            gla_out(4 + h, True, h)

        HG_COLS = lambda h: ([h * 128, 1024 + h * 128, 1536 + h * 128], [512 + h * 128, 2048 + h * 128])
        RT_COLS = lambda h: ([2560 + h * 128, 3072 + h * 128], [3584 + h * 128, 4096 + h * 128])
        wout_v = wout_d.rearrange("(k p) n -> p k n", p=128)
        Bx1d = [[Buf("x1d%d_%d" % (b, i)) for i in range(NX)] for b in range(2)]

        heads = [(False, h) for h in range(4)] + [(True, h) for h in range(4)]
        for b in range(2):
            S.tag = "b%d_P1" % b
            for i in range(NT):
                slot = i % 2
                if i < 2:
                    src = ctx_d[b, i * 128:(i + 1) * 128, :]
                    col = 2
                else:
                    src = x_d[b, (i - 2) * 128:(i - 1) * 128, :]
                    col = b
                norm_T(src, [], slot, 1, 0, col, hxT, BhxT, i * 128)
            mark("P1b%d" % b, lambda: [("hxT0", hxT[:, 0, :], [BhxT]), ("hxT7", hxT[:, 7, :], [BhxT])])
            phase_barrier()
            for hi, (is_ret, h) in enumerate(heads):
                sl = hi % 2
                S.tag = "b%d_h%d" % (b, hi)
                fm_cols, tm_cols = RT_COLS(h) if is_ret else HG_COLS(h)
                load_head_weights(sl, fm_cols, tm_cols)
                if is_ret:
                    ret_head(b, h, sl)
                else:
                    hgrn_head(b, h, sl)
                mark("head%d_%d" % (b, hi), lambda: [("mixT", mixT[:, hi, :], [BmixT]), ("QdT0", QdT[0][:], [BQd[0]]),
                                                     ("QdT1", QdT[1][:], [BQd[1]]), ("KdT", KdT[:], [BKdT]),
                                                     ("V", V[:].rearrange("p a b -> p (a b)"), [BV]),
                                                     ("tab", tab[:].rearrange("p a b c -> p (a b c)"), [Btab])])
            S.tag = "b%d_P3" % b
            phase_barrier()
            dma("pool", woutb[:], wout_v, "wout", [], Bwout)
            for j in range(8):
                ts("dve", diag[:, j, :], identf[:], modP[:, 2, j, b:b + 1], None, ALU.mult, None, [Bc, Bmg[2]], [Bdiag])
            for n in range(2):
                bk, bb = nb()
                mmg(bk[:], [(onesf[:], diag[:, 4 * n:4 * n + 4, :].rearrange("p j q -> p (j q)"))], [Bdiag, Bc], [bb])
                cp("act", gB1[:, n * 512:(n + 1) * 512], bk[:], [bb], [BgB1])
            x1ts = [x1t, diag[:].rearrange("p a b -> p (a b)")]
            Bx1ts = [Bx1t, Bdiag]
            for xi in range(NX):
                slot = xi % 2
                xo, Bxo = x1ts[slot], Bx1ts[slot]
                dma("sp", xt[slot][:], x_d[b, xi * 128:(xi + 1) * 128, :], "xt%d" % slot, [], [Bxt[slot]])
                for n in range(2):
                    bk, bb = nb()
                    mmg(bk[:], [(mixT[:, hd, xi * 128:(xi + 1) * 128], woutb[:, hd, n * 512:(n + 1) * 512])
                                for hd in range(8)], [BmixT] + Bwout, [bb])
                    tt("dve", xo[:, n * 512:(n + 1) * 512], bk[:], gB1[:, n * 512:(n + 1) * 512], ALU.mult,
                       [bb, BgB1], [Bxo])
                tt("pool", xo[:], xo[:], xt[slot][:], ALU.add, [Bxo, Bxt[slot]], [Bxo])
                dma("sp", x1_d[b, xi * 128:(xi + 1) * 128, :], xo[:], "x1st%d" % slot, [Bxo], [Bx1d[b][xi]])

        S.tag = "ffn"
        A.off = base_off
        fgn = A.alloc([128, D], F32)
        gB2 = A.alloc([128, D], F32)
        wupb = A.alloc([128, 8, 2 * DFF], BF16)
        wdnb = A.alloc([128, NFC, D], BF16)
        fxt = [A.alloc([128, D], F32) for _ in range(2)]
        fxs = A.alloc([128, D], BF16)
        h2T = [A.alloc([128, 8, BLK + 2], BF16) for _ in range(3)]
        hT = A.alloc([128, NFC, BLK], BF16)
        NEW = 3
        gbufs = [A.alloc([128, BLK + 2], F32) for _ in range(NEW)]
        t1s = [A.alloc([128, BLK], F32) for _ in range(NEW)]
        t2s = [A.alloc([128, BLK], F32) for _ in range(NEW)]
        hbs = [A.alloc([128, BLK], F32) for _ in range(NEW)]
        fdiag = A.alloc([128, 8, 128], F32)
        x2t = [A.alloc([128, D], F32) for _ in range(2)]
        outt = A.alloc([128, D], F32)
        Bfgn, BgB2, Bwup, Bwdn = Buf("fgn"), Buf("gB2"), Buf("wup"), Buf("wdn")
        Bfxt = [Buf("fxt0"), Buf("fxt1")]
        Bfxs = Buf("fxs")
        Bh2T = [Buf("h2T%d" % i) for i in range(3)]
        BhT, Bfdiag = Buf("hT"), Buf("fdiag")
        Bgbufs = [Buf("gbuf%d" % i) for i in range(NEW)]
        Bt1s = [Buf("t1%d" % i) for i in range(NEW)]
        Bt2s = [Buf("t2%d" % i) for i in range(NEW)]
        Bhbs = [Buf("hb%d" % i) for i in range(NEW)]
        build_program.ffn_end = None
        Bx2t = [Buf("x2t0"), Buf("x2t1")]
        Boutt = Buf("outt")
        ffn_bufs = [Bfgn, BgB2, Bwup, Bwdn, Bfxs, BhT, Bfdiag, Boutt] + Bfxt + Bh2T + Bx2t + Bgbufs + Bt1s + Bt2s + Bhbs
        build_program.ffn_end = A.off
        S.add("dve", lambda e: e.memset(sm[:, 60:61], 0.0), [], mixer_bufs0 + scratchB + ffn_bufs + [Bsm, Bsm_go, Bsm_fa, Bsm_ff] + Bsm1)

        wup_v = wup_d.rearrange("(k p) n -> p k n", p=128)
        wdn_v = wdn_d.rearrange("(f p) n -> p f n", p=128)
        UG = [(g * 512, min(512, DFF - g * 512)) for g in range(6)]
        Bwup_g = [Buf("wupg%d" % g) for g in range(6)]
        Bwdn_g = [Buf("wdng%d" % g) for g in range(4)]
        for g, (c0, n) in enumerate(UG):
            dma("pool", wupb[:, :, c0:c0 + n], wup_v[:, :, c0:c0 + n], "wupa%d" % g, [Bwup], [Bwup_g[g]])
            dma("pool", wupb[:, :, DFF + c0:DFF + c0 + n], wup_v[:, :, DFF + c0:DFF + c0 + n], "wupb%d" % g,
                [Bwup], [Bwup_g[g]])
        for gi, f0 in enumerate(range(0, NFC, 6)):
            f1 = min(NFC, f0 + 6)
            dma("pool", wdnb[:, f0:f1, :], wdn_v[:, f0:f1, :], "wdn%d" % f0, [Bwdn], [Bwdn_g[gi]])
        dma("sp", fgn[:], fgn_d.partition_broadcast(128), "fgn", [], [Bfgn])

        NB = L // BLK
        out_ops = []

        def ffn_A(b, j):
            Bsm = Bsm_fa
            g = b * NB + j
            sl = g % 3
            for a in range(BLK // 128):
                xi = j * (BLK // 128) + a
                s2 = (g * 2 + a) % 2
                dma("sp", fxt[s2][:], x1_d[b, xi * 128:(xi + 1) * 128, :], "fxt%d" % s2, [Bx1d[b][xi]], [Bfxt[s2]])
                S.add("dve", lambda e: e.memset(sm[:, 30:31], 0.0), [], [Bsm])
                act(fxs[:], fxt[s2][:], AF.Square, [Bfxt[s2]], [Bfxs, Bsm], accum_out=sm[:, 30:31])
                ts("dve", sm[:, 31:32], sm[:, 30:31], 1.0 / D, EPS, ALU.mult, ALU.add, [Bsm], [Bsm])
                act(sm[:, 32:33], sm[:, 31:32], AF.Ln, [Bsm], [Bsm])
                act(sm[:, 33:34], sm[:, 32:33], AF.Exp, [Bsm], [Bsm], scale=-0.5)
                ts("dve", fxs[:], fxt[s2][:], sm[:, 33:34], None, ALU.mult, None, [Bfxt[s2], Bsm], [Bfxs])
                transposes([(PT[:, jj * 128:(jj + 1) * 128], fxs[:, jj * 128:(jj + 1) * 128]) for jj in range(8)],
                           [Bfxs], [PTB])
                tt("dve", fdiag[:], PT.rearrange("p (j t) -> p j t", j=8),
                   modP[:, 4, :, b:b + 1].to_broadcast([128, 8, 128]), ALU.mult, [PTB, Bmg[4]], [Bfdiag])
                tt("pool", h2T[sl][:, :, 1 + a * 128:1 + (a + 1) * 128], fdiag[:],
                   modP[:, 3, :, b:b + 1].to_broadcast([128, 8, 128]), ALU.add, [Bfdiag, Bmg[3]], [Bh2T[sl]])
            if j == 0:
                S.add("pool", lambda e: e.memset(h2T[sl][:, :, 0:1], 0.0), [], [Bh2T[sl]])
            else:
                sp_ = (g - 1) % 3
                cp("pool", h2T[sl][:, :, 0:1], h2T[sp_][:, :, BLK:BLK + 1], [Bh2T[sp_]], [Bh2T[sl]])
                cp("pool", h2T[sp_][:, :, BLK + 1:BLK + 2], h2T[sl][:, :, 1:2], [Bh2T[sl]], [Bh2T[sp_]])
            if j == NB - 1:
                S.add("pool", lambda e: e.memset(h2T[sl][:, :, BLK + 1:BLK + 2], 0.0), [], [Bh2T[sl]])

        def ffn_F(b, j):
            Bsm = Bsm_ff
            g = b * NB + j
            sl = g % 3
            if j == 0:
                for jj in range(8):
                    ts("dve", fdiag[:, jj, :], identf[:], modP[:, 5, jj, b:b + 1], None, ALU.mult, None,
                       [Bc, Bmg[5]], [Bfdiag])
                for n in range(2):
                    bk, bb = nb()
                    mmg(bk[:], [(onesf[:], fdiag[:, 4 * n:4 * n + 4, :].rearrange("p j q -> p (j q)"))],
                        [Bfdiag, Bc], [bb])
                    cp("act", gB2[:, n * 512:(n + 1) * 512], bk[:], [bb], [BgB2])
            for fc in range(NFC):
                ei = fc % NEW
                gbuf, t1, t2, hb = gbufs[ei], t1s[ei], t2s[ei], hbs[ei]
                Bgbuf, Bt1, Bt2, Bhb = Bgbufs[ei], Bt1s[ei], Bt2s[ei], Bhbs[ei]
                bg, bbg = nb()
                mmg(bg[:, 0:BLK + 2], [(wupb[:, k, fc * 128:(fc + 1) * 128], h2T[sl][:, k, :]) for k in range(8)],
                    [Bwup_g[fc // 4], Bh2T[sl]], [bbg])
                bu, bbu = nb()
                mmg(bu[:, 0:BLK], [(wupb[:, k, DFF + fc * 128:DFF + (fc + 1) * 128], h2T[sl][:, k, 1:BLK + 1])
                                   for k in range(8)], [Bwup_g[fc // 4], Bh2T[sl]], [bbu])
                cp("act", gbuf[:], bg[:, 0:BLK + 2], [bbg], [Bgbuf])
                ts("pool", t1[:], gbuf[:, 1:BLK + 1], cwP[:, fc, 1:2], cbP[:, fc:fc + 1], ALU.mult, ALU.add,
                   [Bgbuf, Bc], [Bt1])
                stt(t2[:], gbuf[:, 0:BLK], cwP[:, fc, 0:1], t1[:], ALU.mult, ALU.add, [Bgbuf, Bt1, Bc], [Bt2])
                stt(t1[:], gbuf[:, 2:BLK + 2], cwP[:, fc, 2:3], t2[:], ALU.mult, ALU.add, [Bgbuf, Bt2, Bc], [Bt1])
                act(hb[:], t1[:], AF.Silu, [Bt1], [Bhb])
                tt("dve", hT[:, fc, :], hb[:], bu[:, 0:BLK], ALU.mult, [Bhb, bbu], [BhT])
            for a in range(BLK // 128):
                xi = j * (BLK // 128) + a
                s2 = (g * 2 + a) % 2
                dma("sp", x2t[s2][:], x1_d[b, xi * 128:(xi + 1) * 128, :], "x2t%d" % s2, [Bx1d[b][xi]], [Bx2t[s2]])
                for n in range(2):
                    bk, bb = nb()
                    mmg(bk[:], [(hT[:, fc, a * 128:(a + 1) * 128], wdnb[:, fc, n * 512:(n + 1) * 512])
                                for fc in range(NFC)], [BhT] + Bwdn_g, [bb])
                    tt("dve", outt[:, n * 512:(n + 1) * 512], bk[:], gB2[:, n * 512:(n + 1) * 512], ALU.mult,
                       [bb, BgB2], [Boutt])
                tt("pool", x2t[s2][:], x2t[s2][:], outt[:], ALU.add, [Bx2t[s2], Boutt], [Bx2t[s2]])
                S.add("dve", lambda e: e.memset(sm[:, 40:41], 0.0), [], [Bsm])
                act(outt[:], x2t[s2][:], AF.Square, [Bx2t[s2]], [Boutt, Bsm], accum_out=sm[:, 40:41])
                ts("dve", sm[:, 41:42], sm[:, 40:41], 1.0 / D, EPS, ALU.mult, ALU.add, [Bsm], [Bsm])
                act(sm[:, 42:43], sm[:, 41:42], AF.Ln, [Bsm], [Bsm])
                act(sm[:, 43:44], sm[:, 42:43], AF.Exp, [Bsm], [Bsm], scale=-0.5)
                stt(outt[:], x2t[s2][:], sm[:, 43:44], fgn[:], ALU.mult, ALU.mult, [Bx2t[s2], Bsm, Bfgn], [Boutt])
                out_ops.append(dma("sp", out_d[b, xi * 128:(xi + 1) * 128, :], outt[:], "outst", [Boutt], []))

        seq = [(b, j) for b in range(2) for j in range(NB)]
        ffn_A(*seq[0])
        for i, (b, j) in enumerate(seq):
            if i + 1 < len(seq):
                ffn_A(*seq[i + 1])
            ffn_F(b, j)

        fw = out_ops[-4:] + out_ops[:1]
        if stage["stopped"]:
            last = {}
            for i, o in enumerate(S.ops):
                if o.is_dma:
                    last[o.key] = i
            fw = list(last.values())
        if REORDER:
            remap = S.reorder()
            fw = [remap[i] for i in fw]
        S.emit(final_wait_ops=fw)
        build_program.stats = (len(S.ops), A.hi, getattr(S, "est_total", None))
        busy = {}
        for o in S.ops:
            busy[o.eng] = busy.get(o.eng, 0.0) + o.cost
        build_program.busy = busy
        tags = {}
        for o in S.ops:
            if getattr(o, "t1", None) is not None:
                a = tags.setdefault(o.tag, [1e18, 0.0])
                a[0] = min(a[0], o.t0)
                a[1] = max(a[1], o.t1)
        build_program.tags = tags
        build_program.dbg_names = stage["names"]
    return nc


def _consts():
    k = np.arange(128)
    ident = np.eye(128, dtype=np.float32)
    swap = np.where((k % 64) < 32, k + 32, k - 32)
    pm = np.zeros((128, 128), np.float32)
    pm[swap, k] = 1.0
    s = np.arange(128)[:, None]
    t = np.arange(128)[None, :]
    mf = (s <= t).astype(np.float32)
    mb = (s >= t).astype(np.float32)
    tt_ = np.arange(L, dtype=np.float32)
    rows = np.floor(tt_ / 64.0)
    cols = tt_ - rows * 64.0
    quarter = 32
    freqs = (10000.0 ** (-np.arange(quarter, dtype=np.float32) / quarter)).astype(np.float32)
    cos = np.zeros((128, L), np.float32)
    sin = np.zeros((128, L), np.float32)
    for kk in range(128):
        pos = rows if kk < 64 else cols
        i = kk % 32
        ang = (pos * freqs[i]).astype(np.float32)
        cos[kk] = np.cos(ang)
        sin[kk] = -np.sin(ang) if (kk % 64) < 32 else np.sin(ang)
    posf = np.tile((np.arange(128, dtype=np.float32) - 63.0)[None, :], (128, 1))
    posb = np.tile((64.0 - np.arange(128, dtype=np.float32))[None, :], (128, 1))
    return dict(ident=ident, pm=pm, mf=mf, mb=mb, cos=cos, sin=sin, posf=posf, posb=posb)


def make_in_maps(x, c, ctx, c_ctx, w_mod, b_mod, norm1_g, w_in, hgrn_lb, hgrn_norm_g, ret_decay, ret_norm_g,
                 w_out, norm2_g, w_up, conv_w, conv_b, w_down, final_g):
    f = lambda a: np.ascontiguousarray(np.asarray(a, dtype=np.float32))
    cst = _consts()
    pl = lambda v: f(np.asarray(v).reshape(-1, 128).T)
    shared = dict(
        w_mod=f(w_mod[0]), bmodP=pl(b_mod[0]), n1g=pl(norm1_g[0]), n2g=pl(norm2_g[0]), w_in=f(w_in[0]),
        lbP=f(np.asarray(hgrn_lb).reshape(2, 2, 4, 128).transpose(3, 0, 1, 2).reshape(128, 16)),
        hgn=f(hgrn_norm_g[0]).reshape(1, 512), rd=f(ret_decay[0]).reshape(1, 8), rtn=f(ret_norm_g[0]).reshape(1, 512),
        w_out=f(w_out[0]), w_up=f(w_up[0]),
        cwP=f(np.asarray(conv_w[0]).reshape(3, NFC, 128).transpose(2, 1, 0).reshape(128, 66)),
        cbP=pl(conv_b[0]), w_down=f(w_down[0]), fgn=f(final_g).reshape(1, D), **cst)
    maps = []
    for core in range(NCORES):
        cc = np.stack([np.asarray(c[2 * core]), np.asarray(c[2 * core + 1]), np.asarray(c_ctx)], axis=0)
        cT = f(cc.reshape(3, 8, 128).transpose(2, 1, 0).reshape(128, 24))
        m = dict(shared)
        m.update(x=f(x[2 * core:2 * core + 2]), ctx=f(ctx[2 * core:2 * core + 2]), cT=cT)
        maps.append(m)
    return maps


_NC_CACHE = {}


def kernel(**inputs):
    if "nc" not in _NC_CACHE:
        _NC_CACHE["nc"] = build_program()
    nc = _NC_CACHE["nc"]
    in_maps = make_in_maps(**inputs)
    res = run_bass_kernel_spmd(nc, in_maps, core_ids=list(range(NCORES)))
    out = np.concatenate([np.asarray(r["out"]) for r in res.results], axis=0)
    return out.astype(np.float32)
```

```python
import contextlib
import numpy as np
import ml_dtypes
import concourse.bass as bass
import concourse.mybir as mybir
from concourse.bass_utils import run_bass_kernel_spmd

F32 = mybir.dt.float32
BF16 = mybir.dt.bfloat16
AF = mybir.ActivationFunctionType
ALU = mybir.AluOpType
AX = mybir.AxisListType

D = 1024
L = 2048
LC = 256
TOK = L + LC
NT = TOK // 128
NX = L // 128
DFF = 2816
NFC = DFF // 128
INW = 4608
EPS = 1e-6
BLK = 256
NCORES = 8
REORDER = True
PRIO = "idx"


class Buf:
    __slots__ = ("name", "w", "r", "excl")

    def __init__(self, name, excl=False):
        self.name = name
        self.w = None
        self.r = []
        self.excl = excl


class Op:
    __slots__ = ("eng", "fn", "deps", "sig", "needs_sig", "is_dma", "idx", "key", "cost", "xfer", "t0", "t1", "tag")


class Sched:
    COMPUTE = ("pe", "act", "dve", "pool")

    def __init__(self, nc):
        self.nc = nc
        self.ops = []
        self.dma_count = {}
        self.last_dma = {}

    def add(self, eng, fn, reads=(), writes=(), dma=None, cost=300.0, xfer=0.0):
        op = Op()
        op.cost = float(cost)
        op.xfer = float(xfer)
        op.tag = getattr(self, "tag", "")
        op.eng = eng
        op.fn = fn
        op.idx = len(self.ops)
        op.is_dma = dma is not None
        op.needs_sig = op.is_dma
        op.key = dma
        writes = list(writes) + [b for b in reads if b.excl]
        reads = [b for b in reads if not b.excl]
        deps = {}
        for b in reads:
            if b.w is not None:
                deps[b.w] = True
        for b in writes:
            if b.w is not None:
                deps.setdefault(b.w, False)
            for r in b.r:
                deps.setdefault(r, False)
        op.deps = []
        for j, raw in deps.items():
            if j == op.idx:
                continue
            op.deps.append(j)
        for b in reads:
            b.r.append(op.idx)
        for b in writes:
            b.w = op.idx
            b.r = []
        if op.is_dma:
            prev = self.last_dma.get(dma)
            if prev is not None and prev not in op.deps:
                op.deps.append(prev)
            self.last_dma[dma] = op.idx
            c = self.dma_count.get(dma, 0) + 16
            self.dma_count[dma] = c
            op.sig = (dma, c)
        else:
            op.sig = None
        self.ops.append(op)
        return op.idx

    def seal_group(self, key, since=0):
        if key not in self.dma_count:
            return
        tot = self.dma_count[key]
        for op in self.ops[since:]:
            if op.is_dma and op.key == key:
                op.sig = (key, tot)

    def reorder(self):
        import heapq
        ops = self.ops
        n = len(ops)
        users = [[] for _ in range(n)]
        ndep = [0] * n
        for op in ops:
            ds = set(op.deps)
            op.deps = sorted(ds)
            ndep[op.idx] = len(op.deps)
            for j in op.deps:
                users[j].append(op.idx)
        LAT = 200.0
        blevel = [0.0] * n
        for i in range(n - 1, -1, -1):
            op = ops[i]
            m = 0.0
            for u in users[i]:
                if blevel[u] + LAT > m:
                    m = blevel[u] + LAT
            blevel[i] = op.cost + op.xfer + m
        if PRIO == "blevel":
            key = [-blevel[i] for i in range(n)]
        else:
            key = [float(i) for i in range(n)]
        engs = ("pe", "act", "dve", "pool", "sp")
        free = {e: 0.0 for e in engs}
        byready = {e: [] for e in engs}
        now = {e: [] for e in engs}
        ready_t = [0.0] * n
        fin = [0.0] * n
        dma_free = [0.0]
        for op in ops:
            if ndep[op.idx] == 0:
                heapq.heappush(byready[op.eng], (0.0, op.idx))
        order = []
        while len(order) < n:
            best = None
            for e in engs:
                br, nw = byready[e], now[e]
                while br and br[0][0] <= free[e]:
                    ii = heapq.heappop(br)[1]
                    heapq.heappush(nw, (key[ii], ii))
                if nw:
                    est = free[e]
                elif br:
                    est = br[0][0]
                else:
                    continue
                if best is None or est < best[0]:
                    best = (est, e)
            est, e = best
            if now[e]:
                i = heapq.heappop(now[e])[1]
            else:
                i = heapq.heappop(byready[e])[1]
            op = ops[i]
            op.t0 = est
            free[e] = est + op.cost
            if op.is_dma:
                st = max(est + op.cost, dma_free[0])
                fin[i] = st + op.xfer
                dma_free[0] = st + op.xfer * 0.6
            else:
                fin[i] = est + op.cost
            op.t1 = fin[i]
            order.append(i)
            for u in users[i]:
                ready_t[u] = max(ready_t[u], fin[i] + LAT)
                ndep[u] -= 1
                if ndep[u] == 0:
                    heapq.heappush(byready[ops[u].eng], (ready_t[u], u))
        newidx = {old: new for new, old in enumerate(order)}
        newops = [ops[i] for i in order]
        for op in newops:
            op.deps = [newidx[j] for j in op.deps]
            op.idx = newidx[op.idx]
        self.ops = newops
        self.est_total = max(fin) if fin else 0.0
        return newidx

    def emit(self, final_wait_ops=()):
        nc = self.nc
        for op in self.ops:
            kept = []
            for j in op.deps:
                o = self.ops[j]
                if o.eng == op.eng and op.eng == "pe" and not o.is_dma and not op.is_dma:
                    continue
                kept.append(j)
                o.needs_sig = True
            op.deps = kept
        with contextlib.ExitStack() as st:
            sems = {}
            for e in self.COMPUTE:
                sems[e] = st.enter_context(nc.semaphore("s_" + e))
            for k in self.dma_count:
                sems[k] = st.enter_context(nc.semaphore("d_" + str(k)))
            cnt = {e: 0 for e in self.COMPUTE}
            for op in self.ops:
                if not op.is_dma and op.needs_sig:
                    cnt[op.eng] += 1
                    op.sig = (op.eng, cnt[op.eng])
            block = st.enter_context(nc.Block())
            ops = self.ops

            def run(engname, e):
                waited = {}
                for op in ops:
                    if op.eng != engname:
                        continue
                    for j in op.deps:
                        k, v = ops[j].sig
                        if waited.get(k, 0) < v:
                            e.wait_ge(sems[k], v)
                            waited[k] = v
                    ins = op.fn(e)
                    if op.needs_sig:
                        ins.then_inc(sems[op.sig[0]], 16 if op.is_dma else 1)
                if engname == "sp":
                    for j in final_wait_ops:
                        k, v = ops[j].sig
                        if waited.get(k, 0) < v:
                            e.wait_ge(sems[k], v)
                            waited[k] = v

            @block.tensor
            def _(e):
                run("pe", e)

            @block.scalar
            def _(e):
                run("act", e)

            @block.vector
            def _(e):
                run("dve", e)

            @block.gpsimd
            def _(e):
                run("pool", e)

            @block.sync
            def _(e):
                run("sp", e)


class Arena:
    def __init__(self, big, limit):
        self.big = big
        self.off = 0
        self.limit = limit
        self.hi = 0

    def _view(self, ap, shape):
        if len(shape) == 2:
            return ap
        if len(shape) == 3:
            return ap.rearrange("p (a b) -> p a b", a=shape[1])
        return ap.rearrange("p (a b c) -> p a b c", a=shape[1], b=shape[2])

    def alloc(self, shape, dt):
        n = int(np.prod(shape[1:]))
        nb = n * (4 if dt == F32 else 2)
        nb = (nb + 3) // 4 * 4
        o = self.off
        self.off += nb
        self.hi = max(self.hi, self.off)
        assert self.off <= self.limit, ("SBUF arena overflow", self.off)
        ap = self.big[:, o // 4:(o + nb) // 4]
        if dt != F32:
            ap = ap.bitcast(BF16)[:, 0:n]
        return self._view(ap, shape)


def build_program(debug=False, stop=None):
    nc = bass.Bass("TRN2", target_bir_lowering=False)

    def din(name, shape, dt=F32):
        return nc.dram_tensor(name, list(shape), dt, kind="ExternalInput").ap()

    x_d = din("x", [2, L, D])
    ctx_d = din("ctx", [2, LC, D])
    cT_d = din("cT", [128, 24])
    wmod_d = din("w_mod", [D, 6 * D])
    bmodP_d = din("bmodP", [128, 48])
    n1g_d = din("n1g", [128, 8])
    n2g_d = din("n2g", [128, 8])
    win_d = din("w_in", [D, INW])
    lbP_d = din("lbP", [128, 16])
    hgn_d = din("hgn", [1, 512])
    rd_d = din("rd", [1, 8])
    rtn_d = din("rtn", [1, 512])
    wout_d = din("w_out", [D, D])
    wup_d = din("w_up", [D, 2 * DFF])
    cwP_d = din("cwP", [128, 66])
    cbP_d = din("cbP", [128, 22])
    wdn_d = din("w_down", [DFF, D])
    fgn_d = din("fgn", [1, D])
    ident_d = din("ident", [128, 128])
    pm_d = din("pm", [128, 128])
    mf_d = din("mf", [128, 128])
    mb_d = din("mb", [128, 128])
    cos_d = din("cos", [128, L])
    sin_d = din("sin", [128, L])
    posf_d = din("posf", [128, 128])
    posb_d = din("posb", [128, 128])
    out_d = nc.dram_tensor("out", [2, L, D], F32, kind="ExternalOutput").ap()
    x1_d = nc.dram_tensor("x1s", [2, L, D], F32,
                          kind="ExternalOutput" if debug else "Internal").ap()

    st = contextlib.ExitStack()
    with st:
        LIMIT = 212000
        big = st.enter_context(nc.sbuf_tensor("big", [128, LIMIT // 4], F32))
        banks = [st.enter_context(nc.psum_tensor("bank%d" % i, [128, 512], F32)) for i in range(8)]
        bankB = [Buf("bank%d" % i, excl=True) for i in range(8)]
        PT = banks[7][:].bitcast(BF16)
        PTB = bankB[7]
        S = Sched(nc)
        A = Arena(big, LIMIT)
        dbg_d = nc.dram_tensor("dbg", [128, 16384], F32, kind="ExternalOutput").ap() if debug else None
        stage = {"n": 0, "off": 0, "stopped": False, "names": []}
        _add = S.add

        def gated_add(*a, **k):
            if stage["stopped"]:
                return 0
            if isinstance(stop, int) and len(S.ops) >= stop:
                stage["stopped"] = True
                return 0
            return _add(*a, **k)
        S.add = gated_add

        def dump(name, ap2d, bufs):
            n = ap2d.shape[1]
            o = stage["off"]
            stage["off"] += n
            stage["names"].append((name, o, n))
            _add("pool", lambda e: e.dma_start(out=dbg_d[:, o:o + n], in_=ap2d), list(bufs), [], dma="dbg%d" % len(stage["names"]))

        def mark(name, dumps=()):
            if stage["stopped"]:
                return
            if stop is not None and name == stop:
                for nm, ap, bufs in dumps():
                    dump(nm, ap, bufs)
                stage["stopped"] = True

        rot = [0]

        def nb():
            i = rot[0]
            rot[0] = (i + 1) % 7
            return banks[i], bankB[i]

        def nel(ap):
            n = 1
            for d in ap.shape[1:]:
                n *= int(d)
            return n

        def ecost(eng, n):
            if eng == "act":
                return n * 0.83 + 300.0
            if eng == "dve":
                return n * 1.04 + 170.0
            return n * 1.7 + 350.0

        def act(out, in_, func, reads, writes, **kw):
            return S.add("act", lambda e: e.activation(out=out, in_=in_, func=func, **kw), reads, writes,
                         cost=ecost("act", nel(out)))

        def ts(eng, out, in0, s1, s2, op0, op1, reads, writes):
            c = ecost(eng, nel(out))
            if s2 is None:
                return S.add(eng, lambda e: e.tensor_scalar(out=out, in0=in0, scalar1=s1, scalar2=None, op0=op0),
                             reads, writes, cost=c)
            return S.add(eng, lambda e: e.tensor_scalar(out=out, in0=in0, scalar1=s1, scalar2=s2, op0=op0, op1=op1),
                         reads, writes, cost=c)

        def tt(eng, out, in0, in1, op, reads, writes):
            return S.add(eng, lambda e: e.tensor_tensor(out=out, in0=in0, in1=in1, op=op), reads, writes,
                         cost=ecost(eng, nel(out)))

        def stt(out, in0, sc, in1, op0, op1, reads, writes):
            return S.add("dve", lambda e: e.scalar_tensor_tensor(out=out, in0=in0, scalar=sc, in1=in1, op0=op0, op1=op1),
                         reads, writes, cost=ecost("dve", nel(out)))

        def cp(eng, out, in_, reads, writes):
            if eng == "act":
                return act(out, in_, AF.Copy, reads, writes)
            return S.add(eng, lambda e: e.tensor_copy(out=out, in_=in_), reads, writes, cost=ecost(eng, nel(out)))

        def dma(eng, out, in_, key, reads, writes):
            nbytes = 128 * nel(out) * 4
            if eng == "pool":
                return S.add(eng, lambda e: e.dma_start(out=out, in_=in_), reads, writes, dma=key,
                             cost=1500.0, xfer=nbytes / 130.0 + 2000.0)
            return S.add(eng, lambda e: e.dma_start(out=out, in_=in_), reads, writes, dma=key,
                         cost=250.0, xfer=nbytes / 200.0 + 2000.0)

        def mmcost(pairs):
            c = 0.0
            for l, r in pairs:
                c += max(64, nel(r)) / 2.2 + 35.0
            return c

        def mmg(out, pairs, reads, writes):
            pairs = list(pairs)

            def fn(e):
                n = len(pairs)
                ins = None
                for i, (l, r) in enumerate(pairs):
                    ins = e.matmul(out, lhsT=l, rhs=r, start=(i == 0), stop=(i == n - 1))
                return ins
            return S.add("pe", fn, reads, writes, cost=mmcost(pairs))

        def mm_multi(groups, reads, writes):
            groups = [(o, list(p)) for o, p in groups]

            def fn(e):
                ins = None
                for out, pairs in groups:
                    n = len(pairs)
                    for i, (l, r) in enumerate(pairs):
                        ins = e.matmul(out, lhsT=l, rhs=r, start=(i == 0), stop=(i == n - 1))
                return ins
            return S.add("pe", fn, reads, writes, cost=sum(mmcost(p) for _, p in groups))

        def transposes(items, reads, writes):
            items = list(items)

            def fn(e):
                ins = None
                for o, i_ in items:
                    ins = e.transpose(out=o, in_=i_, identity=identb)
                return ins
            return S.add("pe", fn, reads + [Bc], writes, cost=110.0 * len(items))

        identb = A.alloc([128, 128], BF16)
        pmb = A.alloc([128, 128], BF16)
        mfb = A.alloc([128, 128], BF16)
        mbb = A.alloc([128, 128], BF16)
        identf = A.alloc([128, 128], F32)
        onesf = A.alloc([128, 128], F32)
        cT = A.alloc([128, 24], F32)
        cs = A.alloc([128, 8, 3], BF16)
        bmodP = A.alloc([128, 6, 8], F32)
        n1g = A.alloc([128, 8], F32)
        n2g = A.alloc([128, 8], F32)
        modP = A.alloc([128, 6, 8, 3], F32)
        lbP = A.alloc([128, 2, 2, 4], F32)
        lbv = A.alloc([128, 2, 4], F32)
        oml = A.alloc([128, 2, 4], F32)
        rdt = A.alloc([128, 8], F32)
        lg = A.alloc([128, 8], F32)
        nlg = A.alloc([128, 8], F32)
        g64 = A.alloc([128, 8], F32)
        g128 = A.alloc([128, 8], F32)
        cwP = A.alloc([128, 22, 3], F32)
        cbP = A.alloc([128, 22], F32)
        sm = A.alloc([128, 64], F32)
        Bc = Buf("consts")
        Bmod = Buf("modP")
        Bmg = [Buf("modg%d" % g) for g in range(6)]
        Bsm = Buf("sm")
        base_off = A.off

        cosb = A.alloc([128, L], BF16)
        sinb = A.alloc([128, L], BF16)
        posf = A.alloc([128, 128], F32)
        posb = A.alloc([128, 128], F32)
        RM = A.alloc([128, NT, 128], BF16)
        RT = A.alloc([128, 4, 128], F32)
        hgn = A.alloc([128, 512], F32)
        rtn = A.alloc([128, 512], F32)
        hxT = A.alloc([128, 8, TOK], BF16)
        mixT = A.alloc([128, 8, L], BF16)
        wfm = [A.alloc([128, 8, 384], BF16) for _ in range(2)]
        wtm = [A.alloc([128, 8, 256], BF16) for _ in range(2)]
        qT = A.alloc([128, L], F32)
        ar1 = A.off
        FA = A.alloc([128, TOK], F32)
        FB = A.alloc([128, TOK], F32)
        FC = A.alloc([128, TOK], F32)
        ar2 = A.off
        KdT = A.alloc([128, TOK], BF16)
        Kd = A.alloc([128, NT, 128], BF16)
        US = A.alloc([128, NT, 128], F32)
        ar2e = A.off
        QdT = [A.alloc([128, L], BF16) for _ in range(2)]
        S16 = [A.alloc([128, NX, 128], BF16) for _ in range(2)]
        AT = [A.alloc([128, NX, 128], BF16) for _ in range(2)]
        V = A.alloc([128, NT, 128], BF16)
        SG = A.alloc([128, NX, 128], BF16)
        tmpo = A.alloc([128, 4, 128], F32)
        tmpq = A.alloc([128, 4, 128], F32)
        mtok = A.alloc([128, 4, 128], BF16)
        tab = A.alloc([128, 2, 6, NT], F32)
        mixer_hi = A.off
        A.off = ar1
        xt = [A.alloc([128, D], F32) for _ in range(2)]
        xs = [A.alloc([128, D], BF16) for _ in range(2)]
        x1t = A.alloc([128, D], F32)
        gB1 = A.alloc([128, D], F32)
        diag = A.alloc([128, 8, 128], F32)
        assert A.off <= ar2
        A.off = ar2
        woutb = A.alloc([128, 8, D], BF16)
        assert A.off <= ar2e
        A.off = mixer_hi

        BFA, BFB, BFC = Buf("FA"), Buf("FB"), Buf("FC")
        BKdT, BKd, BUS = Buf("KdT"), Buf("Kd"), Buf("US")
        BmixT, BqT = Buf("mixT"), Buf("qT")
        BhxTb = [Buf("hxT%d" % i) for i in range(5)]
        BhxT = BhxTb

        def hx_blk(tok):
            return BhxTb[0] if tok < 256 else BhxTb[1 + (tok - 256) // 512]
        Bwfm = [Buf("wfm0"), Buf("wfm1")]
        Bwtm = [Buf("wtm0"), Buf("wtm1")]
        BQd = [Buf("Qd0"), Buf("Qd1")]
        BS16 = [Buf("S160"), Buf("S161")]
        BAT = [Buf("AT0"), Buf("AT1")]
        BV, BSG = Buf("V"), Buf("SG")
        Btmpo, Btmpq, Bmtok, Btab = Buf("tmpo"), Buf("tmpq"), Buf("mtok"), Buf("tab")
        BRT = Buf("RT")
        Bxt = [BFA, BFA]
        Bxs = [BFB, BFB]
        Bx1t = BFB
        BgB1 = BFC
        Bdiag = BFC
        Bwout = [BKdT, BKd, BUS]
        mixer_bufs0 = [BFA, BFB, BFC, BKdT, BKd, BUS, BmixT, BqT] + BhxTb + Bwfm + Bwtm + BQd + BS16 + BAT + \
                     [BV, BSG, Btmpo, Btmpq, Bmtok, Btab, BRT, Bc]
        Bxt = [Buf("xt0"), Buf("xt1")]
        Bxs = [Buf("xs0"), Buf("xs1")]
        Bx1t = Buf("x1t")
        BgB1 = Buf("gB1")
        Bdiag = Buf("diag")
        allF = [BFA, BFB, BFC]
        scratchB = Bxt + Bxs + [Bx1t, BgB1, Bdiag]

        Bbar = Buf("bar")

        def phase_barrier():
            S.add("dve", lambda e: e.memset(sm[:, 61:62], 0.0), [], allF + scratchB + [Bbar])

        ckn = [0]

        def CKf(q):
            ckn[0] += 1
            return "c%s%d" % (q, ckn[0] % 5)
        for dst, src in ((identb[:], ident_d), (pmb[:], pm_d), (mfb[:], mf_d), (mbb[:], mb_d),
                         (cosb[:], cos_d), (sinb[:], sin_d)):
            dma("pool", dst, src, CKf("p"), [], [Bc])
        for dst, src in ((identf[:], ident_d), (cT[:], cT_d), (bmodP[:], bmodP_d.rearrange("p (g j) -> p g j", g=6)),
                         (n1g[:], n1g_d), (n2g[:], n2g_d),
                         (lbP[:], lbP_d.rearrange("p (d i h) -> p d i h", d=2, i=2)),
                         (rdt[:], rd_d.partition_broadcast(128)),
                         (cwP[:], cwP_d.rearrange("p (f j) -> p f j", j=3)), (cbP[:], cbP_d),
                         (posf[:], posf_d), (posb[:], posb_d),
                         (hgn[:], hgn_d.partition_broadcast(128)), (rtn[:], rtn_d.partition_broadcast(128))):
            dma("sp", dst, src, CKf("s"), [], [Bc])
        S.add("dve", lambda e: e.memset(onesf[:], 1.0), [], [Bc])
        S.add("dve", lambda e: e.memset(RM[:], 1.0), [], [Bc])
        S.add("dve", lambda e: e.memset(RM[:, :, 0:1], 0.0), [], [Bc])
        act(cs[:], cT[:].rearrange("p (k j) -> p k j", j=3), AF.Silu, [Bc], [Bmod])
        tt("dve", lbv[:], lbP[:, :, 0, :], lbP[:, :, 1, :], ALU.subtract, [Bc], [Bmod])
        act(lbv[:], lbv[:], AF.Sigmoid, [Bmod], [Bmod])
        ts("dve", oml[:], lbv[:], -1.0, 1.0, ALU.mult, ALU.add, [Bmod], [Bmod])
        act(lg[:], rdt[:], AF.Sigmoid, [Bc], [Bmod])
        act(lg[:], lg[:], AF.Ln, [Bmod], [Bmod])
        ts("dve", nlg[:], lg[:], -1.0, None, ALU.mult, None, [Bmod], [Bmod])
        act(g64[:], lg[:], AF.Exp, [Bmod], [Bmod], scale=64.0)
        act(g128[:], lg[:], AF.Exp, [Bmod], [Bmod], scale=128.0)

        A.off = ar1
        wmv = [A.alloc([128, 8, 1024], BF16)]
        A.off = ar2
        wmv.append(A.alloc([128, 8, 1024], BF16))
        A.off = mixer_hi
        Bwm = [allF, [BKdT, BKd, BUS]]
        wmod_v = wmod_d.rearrange("(k p) n -> p k n", p=128)
        def mod_group(g, wv, wB, key):
            dma("pool", wv[:], wmod_v[:, :, g * 1024:(g + 1) * 1024], key, [], wB)
            bk, bb = nb()
            groups = []
            for j in range(8):
                groups.append((bk[:, j * 4:j * 4 + 3],
                               [(wv[:, k, j * 128:(j + 1) * 128], cs[:, k, :]) for k in range(8)]))
            mm_multi(groups, wB + [Bmod], [bb])
            tt("dve", modP[:, g], bk[:, 0:32].rearrange("p (j c) -> p j c", c=4)[:, :, 0:3],
               bmodP[:, g, :].unsqueeze(2).to_broadcast([128, 8, 3]), ALU.add, [bb, Bc], [Bmg[g]])
            if g in (1, 4):
                ng = n1g if g == 1 else n2g
                ts("dve", modP[:, g], modP[:, g], 1.0, None, ALU.add, None, [Bmg[g]], [Bmg[g]])
                tt("dve", modP[:, g], modP[:, g], ng[:].unsqueeze(2).to_broadcast([128, 8, 3]), ALU.mult,
                   [Bmg[g], Bc], [Bmg[g]])

        mod_group(1, wmv[0], Bwm[0], "wm0")
        mod_group(0, wmv[1], Bwm[1], "wm1")
        mark("setup", lambda: [("modP", modP[:].rearrange("p a b c -> p (a b c)"), Bmg), ("lbv", lbv[:].rearrange("p a b -> p (a b)"), [Bmod]),
                               ("lg", lg[:], [Bmod]), ("g64", g64[:], [Bmod])])
        xt1 = [QdT[i][:].bitcast(F32) for i in range(2)]
        xs1 = [S16[i][:].rearrange("p a b -> p (a b)")[:, 0:D] for i in range(2)]
        tmp1 = AT[0][:].rearrange("p a b -> p (a b)").bitcast(F32).rearrange("p (j t) -> p j t", j=8)
        Bsm1 = [Buf("sm_p1_0"), Buf("sm_p1_1")]

        def norm_T(src_ap, src_key_bufs, slot, gA, gB_, col, dstT, dstB, c0, extra_reads=(), from_dram=True,
                   keep=None):
            bx, bs, bsm = BQd[slot], BS16[slot], Bsm1[slot]
            dma("sp", xt1[slot], src_ap, "p1x%d" % slot, list(src_key_bufs), [bx])
            S.add("dve", lambda e: e.memset(sm[:, slot:slot + 1], 0.0), [], [bsm])
            act(xs1[slot], xt1[slot], AF.Square, [bx], [bs, bsm], accum_out=sm[:, slot:slot + 1])
            ts("dve", sm[:, 2 + slot:3 + slot], sm[:, slot:slot + 1], 1.0 / D, EPS, ALU.mult, ALU.add, [bsm], [bsm])
            act(sm[:, 4 + slot:5 + slot], sm[:, 2 + slot:3 + slot], AF.Ln, [bsm], [bsm])
            act(sm[:, 6 + slot:7 + slot], sm[:, 4 + slot:5 + slot], AF.Exp, [bsm], [bsm], scale=-0.5)
            act(xs1[slot], xt1[slot], AF.Copy, [bx, bsm], [bs], scale=sm[:, 6 + slot:7 + slot])
            transposes([(PT[:, j * 128:(j + 1) * 128], xs1[slot][:, j * 128:(j + 1) * 128]) for j in range(8)],
                       [bs], [PTB])
            tt("dve", tmp1, PT.rearrange("p (j t) -> p j t", j=8),
               modP[:, gA, :, col:col + 1].to_broadcast([128, 8, 128]), ALU.mult, [PTB, Bmg[gA]], [BAT[0]])
            tt("pool", dstT[:, :, c0:c0 + 128], tmp1,
               modP[:, gB_, :, col:col + 1].to_broadcast([128, 8, 128]), ALU.add, [BAT[0], Bmg[gB_]] + list(extra_reads),
               [dstB])

        def fm_proj(wv, wB, c, t0, n):
            bk, bb = nb()
            mmg(bk[:, 0:n], [(wv[:, k, c * 128:(c + 1) * 128], hxT[:, k, t0:t0 + n]) for k in range(8)],
                [wB, hx_blk(t0)], [bb])
            return bk, bb

        def gla_dir(d):
            a1 = tab[:, d, 0, :]
            a2 = tab[:, d, 1, :]
            Dc = tab[:, d, 2, :]
            for c0 in range(0, NT, 8):
                n = min(8, NT - c0)
                transposes([(PT[:, a * 128:(a + 1) * 128], KdT[:, (c0 + a) * 128:(c0 + a + 1) * 128]) for a in range(n)],
                           [BKdT], [PTB])
                cp("act", Kd[:].rearrange("p a k -> p (a k)")[:, c0 * 128:(c0 + n) * 128], PT[:, 0:n * 128], [PTB], [BKd])
            skip = NT - 1 if d == 0 else 2
            for c0 in range(0, NT, 4):
                cl = [c for c in range(c0, min(c0 + 4, NT))]
                bk, bb = nb()
                mm_multi([(bk[:, (c - c0) * 128:(c - c0 + 1) * 128], [(Kd[:, c, :], V[:, c, :])]) for c in cl],
                         [BKd, BV], [bb])
                n = len(cl)
                tt("dve", US[:, c0:c0 + n, :], bk[:, 0:n * 128].rearrange("p (a v) -> p a v", a=n),
                   a1[:, c0:c0 + n].unsqueeze(2).to_broadcast([128, n, 128]), ALU.mult, [bb, Btab], [BUS])
            order = list(range(NT)) if d == 0 else [1, 0] + list(range(NT - 1, 1, -1))
            for j in range(1, NT - 1):
                c, pc = order[j], order[j - 1]
                stt(US[:, c, :], US[:, pc, :], Dc[:, c:c + 1], US[:, c, :], ALU.mult, ALU.add, [BUS, Btab], [BUS])
            if d == 0:
                tt("pool", S16[d][:], US[:, 1:NT - 1, :], a2[:, 2:NT].unsqueeze(2).to_broadcast([128, NX, 128]),
                   ALU.mult, [BUS, Btab], [BS16[d]])
            else:
                tt("pool", S16[d][:, 0:NX - 1, :], US[:, 3:NT, :],
                   a2[:, 2:NT - 1].unsqueeze(2).to_broadcast([128, NX - 1, 128]), ALU.mult, [BUS, Btab], [BS16[d]])
                tt("pool", S16[d][:, NX - 1, :], US[:, 0, :], a2[:, NT - 1:NT].to_broadcast([128, 128]),
                   ALU.mult, [BUS, Btab], [BS16[d]])
            mask = mfb if d == 0 else mbb
            for x0 in range(0, NX, 4):
                bk, bb = nb()
                mm_multi([(bk[:, a * 128:(a + 1) * 128],
                           [(KdT[:, (x0 + a + 2) * 128:(x0 + a + 3) * 128], QdT[d][:, (x0 + a) * 128:(x0 + a + 1) * 128])])
                          for a in range(4)], [BKdT, BQd[d]], [bb])
                tt("dve", AT[d][:, x0:x0 + 4, :], bk[:].rearrange("p (a t) -> p a t", a=4),
                   mask[:].unsqueeze(1).to_broadcast([128, 4, 128]), ALU.mult, [bb, Bc], [BAT[d]])

        Bsm_go = Buf("sm_go")
        Bsm_fa, Bsm_ff = Buf("sm_fa"), Buf("sm_ff")

        def gla_out(hd, is_ret, h):
            Bsm = Bsm_go
            gain = rtn if is_ret else hgn
            for x0 in range(0, NX, 4):
                bk, bb = nb()
                groups = []
                for a in range(4):
                    xi = x0 + a
                    pairs = []
                    for d in range(2):
                        pairs.append((AT[d][:, xi, :], V[:, xi + 2, :]))
                        pairs.append((QdT[d][:, xi * 128:(xi + 1) * 128], S16[d][:, xi, :]))
                    groups.append((bk[:, a * 128:(a + 1) * 128], pairs))
                mm_multi(groups, BAT + BQd + BS16 + [BV], [bb])
                o3 = bk[:].rearrange("p (a v) -> p a v", a=4)
                if is_ret:
                    cp("act", tmpo[:].rearrange("p a v -> p (a v)"), bk[:], [bb], [Btmpo])
                else:
                    tt("dve", tmpo[:], o3, SG[:, x0:x0 + 4, :], ALU.mult, [bb, BSG], [Btmpo])
                tt("pool", tmpq[:], tmpo[:], tmpo[:], ALU.mult, [Btmpo], [Btmpq])
                S.add("dve", lambda e: e.reduce_sum(out=sm[:, 8:12], in_=tmpq[:], axis=AX.X), [Btmpq], [Bsm], cost=700.0)
                ts("dve", sm[:, 12:16], sm[:, 8:12], 1.0 / 128, EPS, ALU.mult, ALU.add, [Bsm], [Bsm])
                act(sm[:, 16:20], sm[:, 12:16], AF.Ln, [Bsm], [Bsm])
                act(sm[:, 20:24], sm[:, 16:20], AF.Exp, [Bsm], [Bsm], scale=-0.5)
                tt("dve", tmpo[:], tmpo[:], sm[:, 20:24].unsqueeze(2).to_broadcast([128, 4, 128]), ALU.mult,
                   [Btmpo, Bsm], [Btmpo])
                gb = gain[:, h * 128:(h + 1) * 128].unsqueeze(1).to_broadcast([128, 4, 128])
                if is_ret:
                    tt("pool", tmpq[:], tmpo[:], gb, ALU.mult, [Btmpo, Bc], [Btmpq])
                    tt("pool", mtok[:], tmpq[:], SG[:, x0:x0 + 4, :], ALU.mult, [Btmpq, BSG], [Bmtok])
                else:
                    tt("pool", mtok[:], tmpo[:], gb, ALU.mult, [Btmpo, Bc], [Bmtok])
                transposes([(PT[:, a * 128:(a + 1) * 128], mtok[:, a, :]) for a in range(4)], [Bmtok], [PTB])
                cp("act", mixT[:, hd, x0 * 128:(x0 + 4) * 128], PT[:, 0:512], [PTB], [BmixT])

        def tm_proj(sl, is_ret):
            for i0 in range(0, NT, 2):
                bk, bb = nb()
                n = 128 if i0 < 2 else 256
                mm_multi([(bk[:, a * 256:a * 256 + n],
                           [(hxT[:, k, (i0 + a) * 128:(i0 + a + 1) * 128], wtm[sl][:, k, 0:n]) for k in range(8)])
                          for a in range(2)], [hx_blk(i0 * 128), Bwtm[sl]], [bb])
                b3 = bk[:].rearrange("p (a c) -> p a c", a=2)
                cp("dve", V[:, i0:i0 + 2, :], b3[:, :, 0:128], [bb], [BV])
                if i0 >= 2:
                    for a in range(2):
                        act(SG[:, i0 - 2 + a, :], bk[:, a * 256 + 128:a * 256 + 256], AF.Silu if is_ret else AF.Sigmoid,
                            [bb], [BSG])

        def load_head_weights(sl, fm_cols, tm_cols):
            since = len(S.ops)
            for i, c in enumerate(fm_cols):
                dma("pool", wfm[sl][:, :, i * 128:(i + 1) * 128], win_v[:, :, c:c + 128], "wf%d_%d" % (sl, i), [], [Bwfm[sl]])
            for i, c in enumerate(tm_cols):
                dma("pool", wtm[sl][:, :, i * 128:(i + 1) * 128], win_v[:, :, c:c + 128], "wt%d_%d" % (sl, i), [], [Bwtm[sl]])

        win_v = win_d.rearrange("(k p) n -> p k n", p=128)
        FA3 = FA[:].rearrange("p (c t) -> p c t", t=128)
        FB3 = FB[:].rearrange("p (c t) -> p c t", t=128)
        FC3 = FC[:].rearrange("p (c t) -> p c t", t=128)
        TOKBLK = [(0, 256)] + [(256 + i * 512, 512) for i in range(4)]
        XBLK = [(256 + i * 512, 512) for i in range(4)]

        def hgrn_head(b, h, sl):
            for (t0, n) in XBLK:
                bk, bb = fm_proj(wfm[sl], Bwfm[sl], 0, t0, n)
                cp("act", qT[:, t0 - 256:t0 - 256 + n], bk[:, 0:n], [bb], [BqT])
            tm_proj(sl, False)
            for d in range(2):
                a1, a2, Dc, mid, tot, tmp = (tab[:, d, i, :] for i in range(6))
                for (t0, n) in TOKBLK:
                    bk, bb = fm_proj(wfm[sl], Bwfm[sl], 1 + d, t0, n)
                    act(FA[:, t0:t0 + n], bk[:, 0:n], AF.Sigmoid, [bb], [BFA])
                ts("dve", FA[:], FA[:], oml[:, d, h:h + 1], lbv[:, d, h:h + 1], ALU.mult, ALU.add, [BFA, Bmod], [BFA])
                act(FB[:], FA[:], AF.Ln, [BFA], [BFB])
                ts("pool", FA[:], FA[:], -1.0, 1.0, ALU.mult, ALU.add, [BFA], [BFA])
                S.add("dve", lambda e: e.tensor_tensor_scan(out=FC[:], data0=RM[:].rearrange("p c t -> p (c t)"),
                                                              data1=FB[:], initial=0.0, op0=ALU.mult, op1=ALU.add),
                      [BFB, Bc], [BFC], cost=5000.0)
                cp("dve", tot, FC3[:, :, 127], [BFC], [Btab])
                act(Dc, tot, AF.Exp, [Btab], [Btab])
                if d == 0:
                    cp("dve", mid, FC3[:, :, 63], [BFC], [Btab])
                    act(a2, mid, AF.Exp, [Btab], [Btab])
                    tt("dve", tmp, tot, mid, ALU.subtract, [Btab], [Btab])
                    act(a1, tmp, AF.Exp, [Btab], [Btab])
                else:
                    tt("dve", FC[:], FC[:], FB[:], ALU.subtract, [BFC, BFB], [BFC])
                    cp("dve", mid, FC3[:, :, 64], [BFC], [Btab])
                    act(a1, mid, AF.Exp, [Btab], [Btab])
                    tt("dve", tmp, tot, mid, ALU.subtract, [Btab], [Btab])
                    act(a2, tmp, AF.Exp, [Btab], [Btab])
                tt("dve", FC3, FC3, mid.unsqueeze(2).to_broadcast([128, NT, 128]), ALU.subtract, [BFC, Btab], [BFC])
                sq, sk = (1.0, -1.0) if d == 0 else (-1.0, 1.0)
                act(FB[:, 256:TOK], FC[:, 256:TOK], AF.Exp, [BFC], [BFB], scale=sq)
                tt("pool", QdT[d][:], qT[:], FB[:, 256:TOK], ALU.mult, [BqT, BFB], [BQd[d]])
                act(FB[:], FC[:], AF.Exp, [BFC], [BFB], scale=sk)
                tt("pool", KdT[:], FA[:], FB[:], ALU.mult, [BFA, BFB], [BKdT])
                gla_dir(d)
            gla_out(h, False, h)

        def ret_head(b, h, sl):
            lnsc = float(np.log(128.0 ** -0.5))
            act(RT[:, 0, :], posf[:], AF.Exp, [Bc, Bmod], [BRT], scale=lg[:, h:h + 1])
            act(RT[:, 1, :], posf[:], AF.Exp, [Bc, Bmod], [BRT], scale=nlg[:, h:h + 1])
            act(RT[:, 2, :], posb[:], AF.Exp, [Bc, Bmod], [BRT], scale=lg[:, 4 + h:5 + h])
            act(RT[:, 3, :], posb[:], AF.Exp, [Bc, Bmod], [BRT], scale=nlg[:, 4 + h:5 + h])
            for kd in (1, 3):
                ts("dve", RT[:, kd, :], RT[:, kd, :], 128.0 ** -0.5, None, ALU.mult, None, [BRT], [BRT])
            for d in range(2):
                for i, src in ((0, g64), (1, g64), (2, g128)):
                    cp("dve", tab[:, d, i, :], src[:, d * 4 + h:d * 4 + h + 1].to_broadcast([128, NT]), [Bmod], [Btab])
            tm_proj(sl, True)
            for which in range(2):
                dst, dB = (qT, BqT) if which == 0 else (FA, BFA)
                blks = XBLK if which == 0 else TOKBLK
                for (t0, n) in blks:
                    o0 = t0 - 256 if which == 0 else t0
                    bk, bb = fm_proj(wfm[sl], Bwfm[sl], which, t0, n)
                    if t0 < 256:
                        cp("act", dst[:, o0:o0 + n], bk[:, 0:n], [bb], [dB])
                        continue
                    xo = t0 - 256
                    cp("act", FB[:, 0:n].bitcast(BF16)[:, 0:n], bk[:, 0:n], [bb], [BFB])
                    tt("dve", FC[:, 0:n], bk[:, 0:n], cosb[:, xo:xo + n], ALU.mult, [bb, Bc], [BFC])
                    bk2, bb2 = nb()
                    mmg(bk2[:, 0:n], [(pmb[:], FB[:, 0:n].bitcast(BF16)[:, 0:n])], [BFB, Bc], [bb2])
                    tt("dve", FC[:, 512:512 + n], bk2[:, 0:n], sinb[:, xo:xo + n], ALU.mult, [bb2, Bc], [BFC])
                    tt("pool", dst[:, o0:o0 + n], FC[:, 0:n], FC[:, 512:512 + n], ALU.add, [BFC], [dB])
            for d in range(2):
                tt("pool", QdT[d][:].rearrange("p (c t) -> p c t", t=128), qT[:].rearrange("p (c t) -> p c t", t=128),
                   RT[:, 2 * d, :].unsqueeze(1).to_broadcast([128, NX, 128]), ALU.mult, [BqT, BRT], [BQd[d]])
                tt("pool", KdT[:].rearrange("p (c t) -> p c t", t=128), FA3,
                   RT[:, 2 * d + 1, :].unsqueeze(1).to_broadcast([128, NT, 128]), ALU.mult, [BFA, BRT], [BKdT])
                gla_dir(d)
            gla_out(4 + h, True, h)

        HG_COLS = lambda h: ([h * 128, 1024 + h * 128, 1536 + h * 128], [512 + h * 128, 2048 + h * 128])
        RT_COLS = lambda h: ([2560 + h * 128, 3072 + h * 128], [3584 + h * 128, 4096 + h * 128])
        wout_v = wout_d.rearrange("(k p) n -> p k n", p=128)
        Bx1d = [[Buf("x1d%d_%d" % (b, i)) for i in range(NX)] for b in range(2)]

        heads = [(False, h) for h in range(4)] + [(True, h) for h in range(4)]
        load_head_weights(0, *HG_COLS(0))
        mixT2 = mixT[:].rearrange("p k n -> p (k n)")
        wm2 = [mixT2[:, i * 8192:(i + 1) * 8192].rearrange("p (k n) -> p k n", k=8) for i in range(2)]
        for gi, g in enumerate((2, 3, 4, 5)):
            mod_group(g, wm2[gi % 2], [BmixT], "wm2_%d" % (gi % 2))
        for b in range(2):
            S.tag = "b%d_P1" % b
            for i in range(NT):
                slot = i % 2
                if i < 2:
                    src = ctx_d[b, i * 128:(i + 1) * 128, :]
                    col = 2
                else:
                    src = x_d[b, (i - 2) * 128:(i - 1) * 128, :]
                    col = b
                norm_T(src, [], slot, 1, 0, col, hxT, hx_blk(i * 128), i * 128)
            mark("P1b%d" % b, lambda: [("hxT0", hxT[:, 0, :], BhxTb), ("hxT7", hxT[:, 7, :], BhxTb)])
            phase_barrier()
            for hi, (is_ret, h) in enumerate(heads):
                sl = hi % 2
                S.tag = "b%d_h%d" % (b, hi)
                fm_cols, tm_cols = RT_COLS(h) if is_ret else HG_COLS(h)
                if not (b == 0 and hi == 0):
                    load_head_weights(sl, fm_cols, tm_cols)
                if is_ret:
                    ret_head(b, h, sl)
                else:
                    hgrn_head(b, h, sl)
                mark("head%d_%d" % (b, hi), lambda: [("mixT", mixT[:, hi, :], [BmixT]), ("QdT0", QdT[0][:], [BQd[0]]),
                                                     ("QdT1", QdT[1][:], [BQd[1]]), ("KdT", KdT[:], [BKdT]),
                                                     ("V", V[:].rearrange("p a b -> p (a b)"), [BV]),
                                                     ("tab", tab[:].rearrange("p a b c -> p (a b c)"), [Btab])])
            S.tag = "b%d_P3" % b
            phase_barrier()
            dma("pool", woutb[:], wout_v, "wout", [], Bwout)
            for j in range(8):
                ts("dve", diag[:, j, :], identf[:], modP[:, 2, j, b:b + 1], None, ALU.mult, None, [Bc, Bmg[2]], [Bdiag])
            for n in range(2):
                bk, bb = nb()
                mmg(bk[:], [(onesf[:], diag[:, 4 * n:4 * n + 4, :].rearrange("p j q -> p (j q)"))], [Bdiag, Bc], [bb])
                cp("act", gB1[:, n * 512:(n + 1) * 512], bk[:], [bb], [BgB1])
            x1ts = [x1t, diag[:].rearrange("p a b -> p (a b)")]
            Bx1ts = [Bx1t, Bdiag]
            for xi in range(NX):
                slot = xi % 2
                xo, Bxo = x1ts[slot], Bx1ts[slot]
                dma("sp", xt[slot][:], x_d[b, xi * 128:(xi + 1) * 128, :], "xt%d" % slot, [], [Bxt[slot]])
                for n in range(2):
                    bk, bb = nb()
                    mmg(bk[:], [(mixT[:, hd, xi * 128:(xi + 1) * 128], woutb[:, hd, n * 512:(n + 1) * 512])
                                for hd in range(8)], [BmixT] + Bwout, [bb])
                    tt("dve", xo[:, n * 512:(n + 1) * 512], bk[:], gB1[:, n * 512:(n + 1) * 512], ALU.mult,
                       [bb, BgB1], [Bxo])
                tt("pool", xo[:], xo[:], xt[slot][:], ALU.add, [Bxo, Bxt[slot]], [Bxo])
                dma("sp", x1_d[b, xi * 128:(xi + 1) * 128, :], xo[:], "x1st%d" % slot, [Bxo], [Bx1d[b][xi]])

        S.tag = "ffn"
        A.off = base_off
        fgn = A.alloc([128, D], F32)
        gB2 = A.alloc([128, D], F32)
        wupb = A.alloc([128, 8, 2 * DFF], BF16)
        wdnb = A.alloc([128, NFC, D], BF16)
        fxt = [A.alloc([128, D], F32) for _ in range(2)]
        fxs = A.alloc([128, D], BF16)
        h2T = [A.alloc([128, 8, BLK + 2], BF16) for _ in range(3)]
        hT = A.alloc([128, NFC, BLK], BF16)
        NEW = 3
        gbufs = [A.alloc([128, BLK + 2], F32) for _ in range(NEW)]
        t1s = [A.alloc([128, BLK], F32) for _ in range(NEW)]
        t2s = [A.alloc([128, BLK], F32) for _ in range(NEW)]
        hbs = [A.alloc([128, BLK], F32) for _ in range(NEW)]
        fdiag = A.alloc([128, 8, 128], F32)
        x2t = [A.alloc([128, D], F32) for _ in range(2)]
        outt = A.alloc([128, D], F32)
        Bfgn, BgB2, Bwup, Bwdn = Buf("fgn"), Buf("gB2"), Buf("wup"), Buf("wdn")
        Bfxt = [Buf("fxt0"), Buf("fxt1")]
        Bfxs = Buf("fxs")
        Bh2T = [Buf("h2T%d" % i) for i in range(3)]
        BhT, Bfdiag = Buf("hT"), Buf("fdiag")
        Bgbufs = [Buf("gbuf%d" % i) for i in range(NEW)]
        Bt1s = [Buf("t1%d" % i) for i in range(NEW)]
        Bt2s = [Buf("t2%d" % i) for i in range(NEW)]
        Bhbs = [Buf("hb%d" % i) for i in range(NEW)]
        build_program.ffn_end = None
        Bx2t = [Buf("x2t0"), Buf("x2t1")]
        Boutt = Buf("outt")
        ffn_bufs = [Bfgn, BgB2, Bwup, Bwdn, Bfxs, BhT, Bfdiag, Boutt] + Bfxt + Bh2T + Bx2t + Bgbufs + Bt1s + Bt2s + Bhbs
        build_program.ffn_end = A.off
        S.add("dve", lambda e: e.memset(sm[:, 60:61], 0.0), [], mixer_bufs0 + scratchB + ffn_bufs + [Bsm, Bsm_go, Bsm_fa, Bsm_ff] + Bsm1)

        wup_v = wup_d.rearrange("(k p) n -> p k n", p=128)
        wdn_v = wdn_d.rearrange("(f p) n -> p f n", p=128)
        UG = [(g * 512, min(512, DFF - g * 512)) for g in range(6)]
        Bwup_g = [Buf("wupg%d" % g) for g in range(6)]
        Bwdn_g = [Buf("wdng%d" % g) for g in range(4)]
        for g, (c0, n) in enumerate(UG):
            dma("pool", wupb[:, :, c0:c0 + n], wup_v[:, :, c0:c0 + n], "wupa%d" % g, [Bwup], [Bwup_g[g]])
            dma("pool", wupb[:, :, DFF + c0:DFF + c0 + n], wup_v[:, :, DFF + c0:DFF + c0 + n], "wupb%d" % g,
                [Bwup], [Bwup_g[g]])
        for gi, f0 in enumerate(range(0, NFC, 6)):
            f1 = min(NFC, f0 + 6)
            dma("pool", wdnb[:, f0:f1, :], wdn_v[:, f0:f1, :], "wdn%d" % f0, [Bwdn], [Bwdn_g[gi]])
        dma("sp", fgn[:], fgn_d.partition_broadcast(128), "fgn", [], [Bfgn])

        NB = L // BLK
        out_ops = []

        def ffn_A(b, j):
            Bsm = Bsm_fa
            g = b * NB + j
            sl = g % 3
            for a in range(BLK // 128):
                xi = j * (BLK // 128) + a
                s2 = (g * 2 + a) % 2
                dma("sp", fxt[s2][:], x1_d[b, xi * 128:(xi + 1) * 128, :], "fxt%d" % s2, [Bx1d[b][xi]], [Bfxt[s2]])
                S.add("dve", lambda e: e.memset(sm[:, 30:31], 0.0), [], [Bsm])
                act(fxs[:], fxt[s2][:], AF.Square, [Bfxt[s2]], [Bfxs, Bsm], accum_out=sm[:, 30:31])
                ts("dve", sm[:, 31:32], sm[:, 30:31], 1.0 / D, EPS, ALU.mult, ALU.add, [Bsm], [Bsm])
                act(sm[:, 32:33], sm[:, 31:32], AF.Ln, [Bsm], [Bsm])
                act(sm[:, 33:34], sm[:, 32:33], AF.Exp, [Bsm], [Bsm], scale=-0.5)
                ts("dve", fxs[:], fxt[s2][:], sm[:, 33:34], None, ALU.mult, None, [Bfxt[s2], Bsm], [Bfxs])
                transposes([(PT[:, jj * 128:(jj + 1) * 128], fxs[:, jj * 128:(jj + 1) * 128]) for jj in range(8)],
                           [Bfxs], [PTB])
                tt("dve", fdiag[:], PT.rearrange("p (j t) -> p j t", j=8),
                   modP[:, 4, :, b:b + 1].to_broadcast([128, 8, 128]), ALU.mult, [PTB, Bmg[4]], [Bfdiag])
                tt("pool", h2T[sl][:, :, 1 + a * 128:1 + (a + 1) * 128], fdiag[:],
                   modP[:, 3, :, b:b + 1].to_broadcast([128, 8, 128]), ALU.add, [Bfdiag, Bmg[3]], [Bh2T[sl]])
            if j == 0:
                S.add("pool", lambda e: e.memset(h2T[sl][:, :, 0:1], 0.0), [], [Bh2T[sl]])
            else:
                sp_ = (g - 1) % 3
                cp("pool", h2T[sl][:, :, 0:1], h2T[sp_][:, :, BLK:BLK + 1], [Bh2T[sp_]], [Bh2T[sl]])
                cp("pool", h2T[sp_][:, :, BLK + 1:BLK + 2], h2T[sl][:, :, 1:2], [Bh2T[sl]], [Bh2T[sp_]])
            if j == NB - 1:
                S.add("pool", lambda e: e.memset(h2T[sl][:, :, BLK + 1:BLK + 2], 0.0), [], [Bh2T[sl]])

        def ffn_F(b, j):
            Bsm = Bsm_ff
            g = b * NB + j
            sl = g % 3
            if j == 0:
                for jj in range(8):
                    ts("dve", fdiag[:, jj, :], identf[:], modP[:, 5, jj, b:b + 1], None, ALU.mult, None,
                       [Bc, Bmg[5]], [Bfdiag])
                for n in range(2):
                    bk, bb = nb()
                    mmg(bk[:], [(onesf[:], fdiag[:, 4 * n:4 * n + 4, :].rearrange("p j q -> p (j q)"))],
                        [Bfdiag, Bc], [bb])
                    cp("act", gB2[:, n * 512:(n + 1) * 512], bk[:], [bb], [BgB2])
            for fc in range(NFC):
                ei = fc % NEW
                gbuf, t1, t2, hb = gbufs[ei], t1s[ei], t2s[ei], hbs[ei]
                Bgbuf, Bt1, Bt2, Bhb = Bgbufs[ei], Bt1s[ei], Bt2s[ei], Bhbs[ei]
                bg, bbg = nb()
                mmg(bg[:, 0:BLK + 2], [(wupb[:, k, fc * 128:(fc + 1) * 128], h2T[sl][:, k, :]) for k in range(8)],
                    [Bwup_g[fc // 4], Bh2T[sl]], [bbg])
                bu, bbu = nb()
                mmg(bu[:, 0:BLK], [(wupb[:, k, DFF + fc * 128:DFF + (fc + 1) * 128], h2T[sl][:, k, 1:BLK + 1])
                                   for k in range(8)], [Bwup_g[fc // 4], Bh2T[sl]], [bbu])
                cp("act", gbuf[:], bg[:, 0:BLK + 2], [bbg], [Bgbuf])
                ts("pool", t1[:], gbuf[:, 1:BLK + 1], cwP[:, fc, 1:2], cbP[:, fc:fc + 1], ALU.mult, ALU.add,
                   [Bgbuf, Bc], [Bt1])
                stt(t2[:], gbuf[:, 0:BLK], cwP[:, fc, 0:1], t1[:], ALU.mult, ALU.add, [Bgbuf, Bt1, Bc], [Bt2])
                stt(t1[:], gbuf[:, 2:BLK + 2], cwP[:, fc, 2:3], t2[:], ALU.mult, ALU.add, [Bgbuf, Bt2, Bc], [Bt1])
                act(hb[:], t1[:], AF.Silu, [Bt1], [Bhb])
                tt("dve", hT[:, fc, :], hb[:], bu[:, 0:BLK], ALU.mult, [Bhb, bbu], [BhT])
            for a in range(BLK // 128):
                xi = j * (BLK // 128) + a
                s2 = (g * 2 + a) % 2
                dma("sp", x2t[s2][:], x1_d[b, xi * 128:(xi + 1) * 128, :], "x2t%d" % s2, [Bx1d[b][xi]], [Bx2t[s2]])
                for n in range(2):
                    bk, bb = nb()
                    mmg(bk[:], [(hT[:, fc, a * 128:(a + 1) * 128], wdnb[:, fc, n * 512:(n + 1) * 512])
                                for fc in range(NFC)], [BhT] + Bwdn_g, [bb])
                    tt("dve", outt[:, n * 512:(n + 1) * 512], bk[:], gB2[:, n * 512:(n + 1) * 512], ALU.mult,
                       [bb, BgB2], [Boutt])
                tt("pool", x2t[s2][:], x2t[s2][:], outt[:], ALU.add, [Bx2t[s2], Boutt], [Bx2t[s2]])
                S.add("dve", lambda e: e.memset(sm[:, 40:41], 0.0), [], [Bsm])
                act(outt[:], x2t[s2][:], AF.Square, [Bx2t[s2]], [Boutt, Bsm], accum_out=sm[:, 40:41])
                ts("dve", sm[:, 41:42], sm[:, 40:41], 1.0 / D, EPS, ALU.mult, ALU.add, [Bsm], [Bsm])
                act(sm[:, 42:43], sm[:, 41:42], AF.Ln, [Bsm], [Bsm])
                act(sm[:, 43:44], sm[:, 42:43], AF.Exp, [Bsm], [Bsm], scale=-0.5)
                stt(outt[:], x2t[s2][:], sm[:, 43:44], fgn[:], ALU.mult, ALU.mult, [Bx2t[s2], Bsm, Bfgn], [Boutt])
                out_ops.append(dma("sp", out_d[b, xi * 128:(xi + 1) * 128, :], outt[:], "outst", [Boutt], []))

        seq = [(b, j) for b in range(2) for j in range(NB)]
        ffn_A(*seq[0])
        for i, (b, j) in enumerate(seq):
            if i + 1 < len(seq):
                ffn_A(*seq[i + 1])
            ffn_F(b, j)

        fw = out_ops[-4:] + out_ops[:1]
        if stage["stopped"]:
            last = {}
            for i, o in enumerate(S.ops):
                if o.is_dma:
                    last[o.key] = i
            fw = list(last.values())
        if REORDER:
            remap = S.reorder()
            fw = [remap[i] for i in fw]
        S.emit(final_wait_ops=fw)
        build_program.stats = (len(S.ops), A.hi, getattr(S, "est_total", None))
        busy = {}
        for o in S.ops:
            busy[o.eng] = busy.get(o.eng, 0.0) + o.cost
        build_program.busy = busy
        build_program.ops = S.ops
        tags = {}
        for o in S.ops:
            if getattr(o, "t1", None) is not None:
                a = tags.setdefault(o.tag, [1e18, 0.0])
                a[0] = min(a[0], o.t0)
                a[1] = max(a[1], o.t1)
        build_program.tags = tags
        build_program.dbg_names = stage["names"]
    return nc


def _consts():
    k = np.arange(128)
    ident = np.eye(128, dtype=np.float32)
    swap = np.where((k % 64) < 32, k + 32, k - 32)
    pm = np.zeros((128, 128), np.float32)
    pm[swap, k] = 1.0
    s = np.arange(128)[:, None]
    t = np.arange(128)[None, :]
    mf = (s <= t).astype(np.float32)
    mb = (s >= t).astype(np.float32)
    tt_ = np.arange(L, dtype=np.float32)
    rows = np.floor(tt_ / 64.0)
    cols = tt_ - rows * 64.0
    quarter = 32
    freqs = (10000.0 ** (-np.arange(quarter, dtype=np.float32) / quarter)).astype(np.float32)
    cos = np.zeros((128, L), np.float32)
    sin = np.zeros((128, L), np.float32)
    for kk in range(128):
        pos = rows if kk < 64 else cols
        i = kk % 32
        ang = (pos * freqs[i]).astype(np.float32)
        cos[kk] = np.cos(ang)
        sin[kk] = -np.sin(ang) if (kk % 64) < 32 else np.sin(ang)
    posf = np.tile((np.arange(128, dtype=np.float32) - 63.0)[None, :], (128, 1))
    posb = np.tile((64.0 - np.arange(128, dtype=np.float32))[None, :], (128, 1))
    return dict(ident=ident, pm=pm, mf=mf, mb=mb, cos=cos, sin=sin, posf=posf, posb=posb)


def make_in_maps(x, c, ctx, c_ctx, w_mod, b_mod, norm1_g, w_in, hgrn_lb, hgrn_norm_g, ret_decay, ret_norm_g,
                 w_out, norm2_g, w_up, conv_w, conv_b, w_down, final_g):
    f = lambda a: np.ascontiguousarray(np.asarray(a, dtype=np.float32))
    cst = _consts()
    pl = lambda v: f(np.asarray(v).reshape(-1, 128).T)
    shared = dict(
        w_mod=f(w_mod[0]), bmodP=pl(b_mod[0]), n1g=pl(norm1_g[0]), n2g=pl(norm2_g[0]), w_in=f(w_in[0]),
        lbP=f(np.asarray(hgrn_lb).reshape(2, 2, 4, 128).transpose(3, 0, 1, 2).reshape(128, 16)),
        hgn=f(hgrn_norm_g[0]).reshape(1, 512), rd=f(ret_decay[0]).reshape(1, 8), rtn=f(ret_norm_g[0]).reshape(1, 512),
        w_out=f(w_out[0]), w_up=f(w_up[0]),
        cwP=f(np.asarray(conv_w[0]).reshape(3, NFC, 128).transpose(2, 1, 0).reshape(128, 66)),
        cbP=pl(conv_b[0]), w_down=f(w_down[0]), fgn=f(final_g).reshape(1, D), **cst)
    maps = []
    for core in range(NCORES):
        cc = np.stack([np.asarray(c[2 * core]), np.asarray(c[2 * core + 1]), np.asarray(c_ctx)], axis=0)
        cT = f(cc.reshape(3, 8, 128).transpose(2, 1, 0).reshape(128, 24))
        m = dict(shared)
        m.update(x=f(x[2 * core:2 * core + 2]), ctx=f(ctx[2 * core:2 * core + 2]), cT=cT)
        maps.append(m)
    return maps


_NC_CACHE = {}


def kernel(**inputs):
    if "nc" not in _NC_CACHE:
        _NC_CACHE["nc"] = build_program()
    nc = _NC_CACHE["nc"]
    in_maps = make_in_maps(**inputs)
    res = run_bass_kernel_spmd(nc, in_maps, core_ids=list(range(NCORES)))
    out = np.concatenate([np.asarray(r["out"]) for r in res.results], axis=0)
    return out.astype(np.float32)
```

```python
import contextlib
import numpy as np
import ml_dtypes
import concourse.bass as bass
import concourse.mybir as mybir
from concourse.bass_utils import run_bass_kernel_spmd

F32 = mybir.dt.float32
BF16 = mybir.dt.bfloat16
AF = mybir.ActivationFunctionType
ALU = mybir.AluOpType
AX = mybir.AxisListType

D = 1024
L = 2048
LC = 256
TOK = L + LC
NT = TOK // 128
NX = L // 128
DFF = 2816
NFC = DFF // 128
INW = 4608
EPS = 1e-6
BLK = 256
NCORES = 8
REORDER = True
PRIO = "idx"


class Buf:
    __slots__ = ("name", "w", "r", "excl")

    def __init__(self, name, excl=False):
        self.name = name
        self.w = None
        self.r = []
        self.excl = excl


class Op:
    __slots__ = ("eng", "fn", "deps", "sig", "needs_sig", "is_dma", "idx", "key", "cost", "xfer", "t0", "t1", "tag")


class Sched:
    COMPUTE = ("pe", "act", "dve", "pool")

    def __init__(self, nc):
        self.nc = nc
        self.ops = []
        self.dma_count = {}
        self.last_dma = {}

    def add(self, eng, fn, reads=(), writes=(), dma=None, cost=300.0, xfer=0.0):
        op = Op()
        op.cost = float(cost)
        op.xfer = float(xfer)
        op.tag = getattr(self, "tag", "")
        op.eng = eng
        op.fn = fn
        op.idx = len(self.ops)
        op.is_dma = dma is not None
        op.needs_sig = op.is_dma
        op.key = dma
        writes = list(writes) + [b for b in reads if b.excl]
        reads = [b for b in reads if not b.excl]
        deps = {}
        for b in reads:
            if b.w is not None:
                deps[b.w] = True
        for b in writes:
            if b.w is not None:
                deps.setdefault(b.w, False)
            for r in b.r:
                deps.setdefault(r, False)
        op.deps = []
        for j, raw in deps.items():
            if j == op.idx:
                continue
            op.deps.append(j)
        for b in reads:
            b.r.append(op.idx)
        for b in writes:
            b.w = op.idx
            b.r = []
        if op.is_dma:
            prev = self.last_dma.get(dma)
            if prev is not None and prev not in op.deps:
                op.deps.append(prev)
            self.last_dma[dma] = op.idx
            c = self.dma_count.get(dma, 0) + 16
            self.dma_count[dma] = c
            op.sig = (dma, c)
        else:
            op.sig = None
        self.ops.append(op)
        return op.idx

    def seal_group(self, key, since=0):
        if key not in self.dma_count:
            return
        tot = self.dma_count[key]
        for op in self.ops[since:]:
            if op.is_dma and op.key == key:
                op.sig = (key, tot)

    def reorder(self):
        import heapq
        ops = self.ops
        n = len(ops)
        users = [[] for _ in range(n)]
        ndep = [0] * n
        for op in ops:
            ds = set(op.deps)
            op.deps = sorted(ds)
            ndep[op.idx] = len(op.deps)
            for j in op.deps:
                users[j].append(op.idx)
        LAT = 200.0
        blevel = [0.0] * n
        for i in range(n - 1, -1, -1):
            op = ops[i]
            m = 0.0
            for u in users[i]:
                if blevel[u] + LAT > m:
                    m = blevel[u] + LAT
            blevel[i] = op.cost + op.xfer + m
        if PRIO == "blevel":
            key = [-blevel[i] for i in range(n)]
        else:
            key = [float(i) for i in range(n)]
        engs = ("pe", "act", "dve", "pool", "sp")
        free = {e: 0.0 for e in engs}
        byready = {e: [] for e in engs}
        now = {e: [] for e in engs}
        ready_t = [0.0] * n
        fin = [0.0] * n
        dma_free = [0.0]
        for op in ops:
            if ndep[op.idx] == 0:
                heapq.heappush(byready[op.eng], (0.0, op.idx))
        order = []
        while len(order) < n:
            best = None
            for e in engs:
                br, nw = byready[e], now[e]
                while br and br[0][0] <= free[e]:
                    ii = heapq.heappop(br)[1]
                    heapq.heappush(nw, (key[ii], ii))
                if nw:
                    est = free[e]
                elif br:
                    est = br[0][0]
                else:
                    continue
                if best is None or est < best[0]:
                    best = (est, e)
            est, e = best
            if now[e]:
                i = heapq.heappop(now[e])[1]
            else:
                i = heapq.heappop(byready[e])[1]
            op = ops[i]
            op.t0 = est
            free[e] = est + op.cost
            if op.is_dma:
                st = max(est + op.cost, dma_free[0])
                fin[i] = st + op.xfer
                dma_free[0] = st + op.xfer * 0.6
            else:
                fin[i] = est + op.cost
            op.t1 = fin[i]
            order.append(i)
            for u in users[i]:
                ready_t[u] = max(ready_t[u], fin[i] + LAT)
                ndep[u] -= 1
                if ndep[u] == 0:
                    heapq.heappush(byready[ops[u].eng], (ready_t[u], u))
        newidx = {old: new for new, old in enumerate(order)}
        newops = [ops[i] for i in order]
        for op in newops:
            op.deps = [newidx[j] for j in op.deps]
            op.idx = newidx[op.idx]
        self.ops = newops
        self.est_total = max(fin) if fin else 0.0
        return newidx

    def emit(self, final_wait_ops=()):
        nc = self.nc
        for op in self.ops:
            kept = []
            for j in op.deps:
                o = self.ops[j]
                if o.eng == op.eng and op.eng == "pe" and not o.is_dma and not op.is_dma:
                    continue
                kept.append(j)
                o.needs_sig = True
            op.deps = kept
        with contextlib.ExitStack() as st:
            sems = {}
            for e in self.COMPUTE:
                sems[e] = st.enter_context(nc.semaphore("s_" + e))
            for k in self.dma_count:
                sems[k] = st.enter_context(nc.semaphore("d_" + str(k)))
            cnt = {e: 0 for e in self.COMPUTE}
            for op in self.ops:
                if not op.is_dma and op.needs_sig:
                    cnt[op.eng] += 1
                    op.sig = (op.eng, cnt[op.eng])
            block = st.enter_context(nc.Block())
            ops = self.ops

            def run(engname, e):
                waited = {}
                for op in ops:
                    if op.eng != engname:
                        continue
                    for j in op.deps:
                        k, v = ops[j].sig
                        if waited.get(k, 0) < v:
                            e.wait_ge(sems[k], v)
                            waited[k] = v
                    ins = op.fn(e)
                    if op.needs_sig:
                        ins.then_inc(sems[op.sig[0]], 16 if op.is_dma else 1)
                if engname == "sp":
                    for j in final_wait_ops:
                        k, v = ops[j].sig
                        if waited.get(k, 0) < v:
                            e.wait_ge(sems[k], v)
                            waited[k] = v

            @block.tensor
            def _(e):
                run("pe", e)

            @block.scalar
            def _(e):
                run("act", e)

            @block.vector
            def _(e):
                run("dve", e)

            @block.gpsimd
            def _(e):
                run("pool", e)

            @block.sync
            def _(e):
                run("sp", e)


class Arena:
    def __init__(self, big, limit):
        self.big = big
        self.off = 0
        self.limit = limit
        self.hi = 0

    def _view(self, ap, shape):
        if len(shape) == 2:
            return ap
        if len(shape) == 3:
            return ap.rearrange("p (a b) -> p a b", a=shape[1])
        return ap.rearrange("p (a b c) -> p a b c", a=shape[1], b=shape[2])

    def alloc(self, shape, dt):
        n = int(np.prod(shape[1:]))
        nb = n * (4 if dt == F32 else 2)
        nb = (nb + 3) // 4 * 4
        o = self.off
        self.off += nb
        self.hi = max(self.hi, self.off)
        assert self.off <= self.limit, ("SBUF arena overflow", self.off)
        ap = self.big[:, o // 4:(o + nb) // 4]
        if dt != F32:
            ap = ap.bitcast(BF16)[:, 0:n]
        return self._view(ap, shape)


def build_program(debug=False, stop=None):
    nc = bass.Bass("TRN2", target_bir_lowering=False)

    def din(name, shape, dt=F32):
        return nc.dram_tensor(name, list(shape), dt, kind="ExternalInput").ap()

    x_d = din("x", [2, L, D])
    ctx_d = din("ctx", [2, LC, D])
    cT_d = din("cT", [128, 24])
    wmod_d = din("w_mod", [D, 6 * D])
    bmodP_d = din("bmodP", [128, 48])
    n1g_d = din("n1g", [128, 8])
    n2g_d = din("n2g", [128, 8])
    win_d = din("w_in", [D, INW])
    lbP_d = din("lbP", [128, 16])
    hgn_d = din("hgn", [1, 512])
    rd_d = din("rd", [1, 8])
    rtn_d = din("rtn", [1, 512])
    wout_d = din("w_out", [D, D])
    wup_d = din("w_up", [D, 2 * DFF])
    cwP_d = din("cwP", [128, 66])
    cbP_d = din("cbP", [128, 22])
    wdn_d = din("w_down", [DFF, D])
    fgn_d = din("fgn", [1, D])
    ident_d = din("ident", [128, 128])
    pm_d = din("pm", [128, 128])
    mf_d = din("mf", [128, 128])
    mb_d = din("mb", [128, 128])
    cos_d = din("cos", [128, L])
    sin_d = din("sin", [128, L])
    posf_d = din("posf", [128, 128])
    posb_d = din("posb", [128, 128])
    out_d = nc.dram_tensor("out", [2, L, D], F32, kind="ExternalOutput").ap()
    x1_d = nc.dram_tensor("x1s", [2, L, D], F32,
                          kind="ExternalOutput" if debug else "Internal").ap()

    st = contextlib.ExitStack()
    with st:
        LIMIT = 212000
        big = st.enter_context(nc.sbuf_tensor("big", [128, LIMIT // 4], F32))
        banks = [st.enter_context(nc.psum_tensor("bank%d" % i, [128, 512], F32)) for i in range(8)]
        bankB = [Buf("bank%d" % i, excl=True) for i in range(8)]
        PT = banks[7][:].bitcast(BF16)
        PTB = bankB[7]
        S = Sched(nc)
        A = Arena(big, LIMIT)
        dbg_d = nc.dram_tensor("dbg", [128, 16384], F32, kind="ExternalOutput").ap() if debug else None
        stage = {"n": 0, "off": 0, "stopped": False, "names": []}
        _add = S.add

        def gated_add(*a, **k):
            if stage["stopped"]:
                return 0
            if isinstance(stop, int) and len(S.ops) >= stop:
                stage["stopped"] = True
                return 0
            return _add(*a, **k)
        S.add = gated_add

        def dump(name, ap2d, bufs):
            n = ap2d.shape[1]
            o = stage["off"]
            stage["off"] += n
            stage["names"].append((name, o, n))
            _add("pool", lambda e: e.dma_start(out=dbg_d[:, o:o + n], in_=ap2d), list(bufs), [], dma="dbg%d" % len(stage["names"]))

        def mark(name, dumps=()):
            if stage["stopped"]:
                return
            if stop is not None and name == stop:
                for nm, ap, bufs in dumps():
                    dump(nm, ap, bufs)
                stage["stopped"] = True

        rot = [0]

        def nb():
            i = rot[0]
            rot[0] = (i + 1) % 7
            return banks[i], bankB[i]

        def nel(ap):
            n = 1
            for d in ap.shape[1:]:
                n *= int(d)
            return n

        def ecost(eng, n):
            if eng == "act":
                return n * 0.83 + 300.0
            if eng == "dve":
                return n * 1.04 + 170.0
            return n * 1.7 + 350.0

        def act(out, in_, func, reads, writes, **kw):
            return S.add("act", lambda e: e.activation(out=out, in_=in_, func=func, **kw), reads, writes,
                         cost=ecost("act", nel(out)))

        def ts(eng, out, in0, s1, s2, op0, op1, reads, writes):
            c = ecost(eng, nel(out))
            if s2 is None:
                return S.add(eng, lambda e: e.tensor_scalar(out=out, in0=in0, scalar1=s1, scalar2=None, op0=op0),
                             reads, writes, cost=c)
            return S.add(eng, lambda e: e.tensor_scalar(out=out, in0=in0, scalar1=s1, scalar2=s2, op0=op0, op1=op1),
                         reads, writes, cost=c)

        def tt(eng, out, in0, in1, op, reads, writes):
            return S.add(eng, lambda e: e.tensor_tensor(out=out, in0=in0, in1=in1, op=op), reads, writes,
                         cost=ecost(eng, nel(out)))

        def stt(out, in0, sc, in1, op0, op1, reads, writes):
            return S.add("dve", lambda e: e.scalar_tensor_tensor(out=out, in0=in0, scalar=sc, in1=in1, op0=op0, op1=op1),
                         reads, writes, cost=ecost("dve", nel(out)))

        def cp(eng, out, in_, reads, writes):
            if eng == "act":
                return act(out, in_, AF.Copy, reads, writes)
            return S.add(eng, lambda e: e.tensor_copy(out=out, in_=in_), reads, writes, cost=ecost(eng, nel(out)))

        def dma(eng, out, in_, key, reads, writes):
            nbytes = 128 * nel(out) * 4
            if eng == "pool":
                return S.add(eng, lambda e: e.dma_start(out=out, in_=in_), reads, writes, dma=key,
                             cost=1500.0, xfer=nbytes / 130.0 + 2000.0)
            return S.add(eng, lambda e: e.dma_start(out=out, in_=in_), reads, writes, dma=key,
                         cost=250.0, xfer=nbytes / 200.0 + 2000.0)

        def mmcost(pairs):
            c = 0.0
            for l, r in pairs:
                c += max(64, nel(r)) / 2.2 + 35.0
            return c

        def mmg(out, pairs, reads, writes):
            pairs = list(pairs)

            def fn(e):
                n = len(pairs)
                ins = None
                for i, (l, r) in enumerate(pairs):
                    ins = e.matmul(out, lhsT=l, rhs=r, start=(i == 0), stop=(i == n - 1))
                return ins
            return S.add("pe", fn, reads, writes, cost=mmcost(pairs))

        def mm_multi(groups, reads, writes):
            groups = [(o, list(p)) for o, p in groups]

            def fn(e):
                ins = None
                for out, pairs in groups:
                    n = len(pairs)
                    for i, (l, r) in enumerate(pairs):
                        ins = e.matmul(out, lhsT=l, rhs=r, start=(i == 0), stop=(i == n - 1))
                return ins
            return S.add("pe", fn, reads, writes, cost=sum(mmcost(p) for _, p in groups))

        def transposes(items, reads, writes):
            items = list(items)

            def fn(e):
                ins = None
                for o, i_ in items:
                    ins = e.transpose(out=o, in_=i_, identity=identb)
                return ins
            return S.add("pe", fn, reads + [Bc], writes, cost=110.0 * len(items))

        identb = A.alloc([128, 128], BF16)
        pmb = A.alloc([128, 128], BF16)
        mfb = A.alloc([128, 128], BF16)
        mbb = A.alloc([128, 128], BF16)
        identf = A.alloc([128, 128], F32)
        onesf = A.alloc([128, 128], F32)
        cT = A.alloc([128, 24], F32)
        cs = A.alloc([128, 8, 3], BF16)
        bmodP = A.alloc([128, 6, 8], F32)
        n1g = A.alloc([128, 8], F32)
        n2g = A.alloc([128, 8], F32)
        modP = A.alloc([128, 6, 8, 3], F32)
        lbP = A.alloc([128, 2, 2, 4], F32)
        lbv = A.alloc([128, 2, 4], F32)
        oml = A.alloc([128, 2, 4], F32)
        rdt = A.alloc([128, 8], F32)
        lg = A.alloc([128, 8], F32)
        nlg = A.alloc([128, 8], F32)
        g64 = A.alloc([128, 8], F32)
        g128 = A.alloc([128, 8], F32)
        cwP = A.alloc([128, 22, 3], F32)
        cbP = A.alloc([128, 22], F32)
        sm = A.alloc([128, 64], F32)
        Bc = Buf("consts")
        Bmod = Buf("modP")
        Bmg = [Buf("modg%d" % g) for g in range(6)]
        Bsm = Buf("sm")
        base_off = A.off

        cosb = A.alloc([128, L], BF16)
        sinb = A.alloc([128, L], BF16)
        posf = A.alloc([128, 128], F32)
        posb = A.alloc([128, 128], F32)
        RM = A.alloc([128, NT, 128], BF16)
        RT = A.alloc([128, 4, 128], F32)
        hgn = A.alloc([128, 512], F32)
        rtn = A.alloc([128, 512], F32)
        hxT = A.alloc([128, 8, TOK], BF16)
        mixT = A.alloc([128, 8, L], BF16)
        wfm = [A.alloc([128, 8, 384], BF16) for _ in range(2)]
        wtm = [A.alloc([128, 8, 256], BF16) for _ in range(2)]
        qT = A.alloc([128, L], F32)
        ar1 = A.off
        FA = A.alloc([128, TOK], F32)
        FB = A.alloc([128, TOK], F32)
        FC = A.alloc([128, TOK], F32)
        ar2 = A.off
        KdT = A.alloc([128, TOK], BF16)
        Kd = A.alloc([128, NT, 128], BF16)
        US = A.alloc([128, NT, 128], F32)
        ar2e = A.off
        QdT = [A.alloc([128, L], BF16) for _ in range(2)]
        S16 = [A.alloc([128, NX, 128], BF16) for _ in range(2)]
        AT = [A.alloc([128, NX, 128], BF16) for _ in range(2)]
        V = A.alloc([128, NT, 128], BF16)
        SG = A.alloc([128, NX, 128], BF16)
        tmpo = A.alloc([128, 4, 128], F32)
        tmpq = A.alloc([128, 4, 128], F32)
        mtok = A.alloc([128, 4, 128], BF16)
        mtok2 = A.alloc([128, 4, 128], BF16)
        tab = A.alloc([128, 2, 6, NT], F32)
        mixer_hi = A.off
        A.off = ar1
        xt = [A.alloc([128, D], F32) for _ in range(2)]
        xs = [A.alloc([128, D], BF16) for _ in range(2)]
        x1t = A.alloc([128, D], F32)
        gB1 = A.alloc([128, D], F32)
        diag = A.alloc([128, 8, 128], F32)
        assert A.off <= ar2
        A.off = ar2
        woutb = A.alloc([128, 8, D], BF16)
        assert A.off <= ar2e
        A.off = mixer_hi

        BFA, BFB, BFC = Buf("FA"), Buf("FB"), Buf("FC")
        BKdT, BKd, BUS = Buf("KdT"), Buf("Kd"), Buf("US")
        BmixT, BqT = Buf("mixT"), Buf("qT")
        BhxTb = [Buf("hxT%d" % i) for i in range(5)]
        BhxT = BhxTb

        def hx_blk(tok):
            return BhxTb[0] if tok < 256 else BhxTb[1 + (tok - 256) // 512]
        Bwfm = [Buf("wfm0"), Buf("wfm1")]
        Bwtm = [Buf("wtm0"), Buf("wtm1")]
        BQd = [Buf("Qd0"), Buf("Qd1")]
        BS16 = [Buf("S160"), Buf("S161")]
        BAT = [Buf("AT0"), Buf("AT1")]
        BV, BSG = Buf("V"), Buf("SG")
        Btmpo, Btmpq, Bmtok, Btab = Buf("tmpo"), Buf("tmpq"), Buf("mtok"), Buf("tab")
        BRT = Buf("RT")
        Bxt = [BFA, BFA]
        Bxs = [BFB, BFB]
        Bx1t = BFB
        BgB1 = BFC
        Bdiag = BFC
        Bwout = [BKdT, BKd, BUS]
        mixer_bufs0 = [BFA, BFB, BFC, BKdT, BKd, BUS, BmixT, BqT] + BhxTb + Bwfm + Bwtm + BQd + BS16 + BAT + \
                     [BV, BSG, Btmpo, Btmpq, Bmtok, Btab, BRT, Bc]
        Bxt = [Buf("xt0"), Buf("xt1")]
        Bxs = [Buf("xs0"), Buf("xs1")]
        Bx1t = Buf("x1t")
        BgB1 = Buf("gB1")
        Bdiag = Buf("diag")
        allF = [BFA, BFB, BFC]
        scratchB = Bxt + Bxs + [Bx1t, BgB1, Bdiag]

        Bbar = Buf("bar")

        def phase_barrier():
            S.add("dve", lambda e: e.memset(sm[:, 61:62], 0.0), [], allF + scratchB + [Bbar])

        ckn = [0]

        def CKf(q):
            ckn[0] += 1
            return "c%s%d" % (q, ckn[0] % 5)
        for dst, src in ((identb[:], ident_d), (pmb[:], pm_d), (mfb[:], mf_d), (mbb[:], mb_d),
                         (cosb[:], cos_d), (sinb[:], sin_d)):
            dma("pool", dst, src, CKf("p"), [], [Bc])
        for dst, src in ((identf[:], ident_d), (cT[:], cT_d), (bmodP[:], bmodP_d.rearrange("p (g j) -> p g j", g=6)),
                         (n1g[:], n1g_d), (n2g[:], n2g_d),
                         (lbP[:], lbP_d.rearrange("p (d i h) -> p d i h", d=2, i=2)),
                         (rdt[:], rd_d.partition_broadcast(128)),
                         (cwP[:], cwP_d.rearrange("p (f j) -> p f j", j=3)), (cbP[:], cbP_d),
                         (posf[:], posf_d), (posb[:], posb_d),
                         (hgn[:], hgn_d.partition_broadcast(128)), (rtn[:], rtn_d.partition_broadcast(128))):
            dma("sp", dst, src, CKf("s"), [], [Bc])
        S.add("dve", lambda e: e.memset(onesf[:], 1.0), [], [Bc])
        S.add("dve", lambda e: e.memset(RM[:], 1.0), [], [Bc])
        S.add("dve", lambda e: e.memset(RM[:, :, 0:1], 0.0), [], [Bc])
        act(cs[:], cT[:].rearrange("p (k j) -> p k j", j=3), AF.Silu, [Bc], [Bmod])
        tt("dve", lbv[:], lbP[:, :, 0, :], lbP[:, :, 1, :], ALU.subtract, [Bc], [Bmod])
        act(lbv[:], lbv[:], AF.Sigmoid, [Bmod], [Bmod])
        ts("dve", oml[:], lbv[:], -1.0, 1.0, ALU.mult, ALU.add, [Bmod], [Bmod])
        act(lg[:], rdt[:], AF.Sigmoid, [Bc], [Bmod])
        act(lg[:], lg[:], AF.Ln, [Bmod], [Bmod])
        ts("dve", nlg[:], lg[:], -1.0, None, ALU.mult, None, [Bmod], [Bmod])
        act(g64[:], lg[:], AF.Exp, [Bmod], [Bmod], scale=64.0)
        act(g128[:], lg[:], AF.Exp, [Bmod], [Bmod], scale=128.0)

        A.off = ar1
        wmv = [A.alloc([128, 8, 1024], BF16)]
        A.off = ar2
        wmv.append(A.alloc([128, 8, 1024], BF16))
        A.off = mixer_hi
        Bwm = [allF, [BKdT, BKd, BUS]]
        wmod_v = wmod_d.rearrange("(k p) n -> p k n", p=128)
        def mod_group(g, wv, wB, key):
            dma("pool", wv[:], wmod_v[:, :, g * 1024:(g + 1) * 1024], key, [], wB)
            bk, bb = nb()
            groups = []
            for j in range(8):
                groups.append((bk[:, j * 4:j * 4 + 3],
                               [(wv[:, k, j * 128:(j + 1) * 128], cs[:, k, :]) for k in range(8)]))
            mm_multi(groups, wB + [Bmod], [bb])
            tt("dve", modP[:, g], bk[:, 0:32].rearrange("p (j c) -> p j c", c=4)[:, :, 0:3],
               bmodP[:, g, :].unsqueeze(2).to_broadcast([128, 8, 3]), ALU.add, [bb, Bc], [Bmg[g]])
            if g in (1, 4):
                ng = n1g if g == 1 else n2g
                ts("dve", modP[:, g], modP[:, g], 1.0, None, ALU.add, None, [Bmg[g]], [Bmg[g]])
                tt("dve", modP[:, g], modP[:, g], ng[:].unsqueeze(2).to_broadcast([128, 8, 3]), ALU.mult,
                   [Bmg[g], Bc], [Bmg[g]])

        mod_group(1, wmv[0], Bwm[0], "wm0")
        mod_group(0, wmv[1], Bwm[1], "wm1")
        mark("setup", lambda: [("modP", modP[:].rearrange("p a b c -> p (a b c)"), Bmg), ("lbv", lbv[:].rearrange("p a b -> p (a b)"), [Bmod]),
                               ("lg", lg[:], [Bmod]), ("g64", g64[:], [Bmod])])
        xt1 = [QdT[i][:].bitcast(F32) for i in range(2)]
        xs1 = [S16[i][:].rearrange("p a b -> p (a b)")[:, 0:D] for i in range(2)]
        tmp1 = AT[0][:].rearrange("p a b -> p (a b)").bitcast(F32).rearrange("p (j t) -> p j t", j=8)
        Bsm1 = [Buf("sm_p1_0"), Buf("sm_p1_1")]

        def norm_T(src_ap, src_key_bufs, slot, gA, gB_, col, dstT, dstB, c0, extra_reads=(), from_dram=True,
                   keep=None):
            bx, bs, bsm = BQd[slot], BS16[slot], Bsm1[slot]
            dma("sp", xt1[slot], src_ap, "p1x%d" % slot, list(src_key_bufs), [bx])
            S.add("dve", lambda e: e.memset(sm[:, slot:slot + 1], 0.0), [], [bsm])
            act(xs1[slot], xt1[slot], AF.Square, [bx], [bs, bsm], accum_out=sm[:, slot:slot + 1])
            ts("dve", sm[:, 2 + slot:3 + slot], sm[:, slot:slot + 1], 1.0 / D, EPS, ALU.mult, ALU.add, [bsm], [bsm])
            act(sm[:, 4 + slot:5 + slot], sm[:, 2 + slot:3 + slot], AF.Ln, [bsm], [bsm])
            act(sm[:, 6 + slot:7 + slot], sm[:, 4 + slot:5 + slot], AF.Exp, [bsm], [bsm], scale=-0.5)
            act(xs1[slot], xt1[slot], AF.Copy, [bx, bsm], [bs], scale=sm[:, 6 + slot:7 + slot])
            transposes([(PT[:, j * 128:(j + 1) * 128], xs1[slot][:, j * 128:(j + 1) * 128]) for j in range(8)],
                       [bs], [PTB])
            tt("dve", tmp1, PT.rearrange("p (j t) -> p j t", j=8),
               modP[:, gA, :, col:col + 1].to_broadcast([128, 8, 128]), ALU.mult, [PTB, Bmg[gA]], [BAT[0]])
            tt("pool", dstT[:, :, c0:c0 + 128], tmp1,
               modP[:, gB_, :, col:col + 1].to_broadcast([128, 8, 128]), ALU.add, [BAT[0], Bmg[gB_]] + list(extra_reads),
               [dstB])

        def fm_proj(wv, wB, c, t0, n):
            bk, bb = nb()
            mmg(bk[:, 0:n], [(wv[:, k, c * 128:(c + 1) * 128], hxT[:, k, t0:t0 + n]) for k in range(8)],
                [wB, hx_blk(t0)], [bb])
            return bk, bb

        def gla_dir(d):
            a1 = tab[:, d, 0, :]
            a2 = tab[:, d, 1, :]
            Dc = tab[:, d, 2, :]
            for c0 in range(0, NT, 8):
                n = min(8, NT - c0)
                transposes([(PT[:, a * 128:(a + 1) * 128], KdT[:, (c0 + a) * 128:(c0 + a + 1) * 128]) for a in range(n)],
                           [BKdT], [PTB])
                cp("act", Kd[:].rearrange("p a k -> p (a k)")[:, c0 * 128:(c0 + n) * 128], PT[:, 0:n * 128], [PTB], [BKd])
            skip = NT - 1 if d == 0 else 2
            for c0 in range(0, NT, 4):
                cl = [c for c in range(c0, min(c0 + 4, NT))]
                bk, bb = nb()
                mm_multi([(bk[:, (c - c0) * 128:(c - c0 + 1) * 128], [(Kd[:, c, :], V[:, c, :])]) for c in cl],
                         [BKd, BV], [bb])
                n = len(cl)
                tt("dve", US[:, c0:c0 + n, :], bk[:, 0:n * 128].rearrange("p (a v) -> p a v", a=n),
                   a1[:, c0:c0 + n].unsqueeze(2).to_broadcast([128, n, 128]), ALU.mult, [bb, Btab], [BUS])
            order = list(range(NT)) if d == 0 else [1, 0] + list(range(NT - 1, 1, -1))
            for j in range(1, NT - 1):
                c, pc = order[j], order[j - 1]
                stt(US[:, c, :], US[:, pc, :], Dc[:, c:c + 1], US[:, c, :], ALU.mult, ALU.add, [BUS, Btab], [BUS])
            if d == 0:
                tt("pool", S16[d][:], US[:, 1:NT - 1, :], a2[:, 2:NT].unsqueeze(2).to_broadcast([128, NX, 128]),
                   ALU.mult, [BUS, Btab], [BS16[d]])
            else:
                tt("pool", S16[d][:, 0:NX - 1, :], US[:, 3:NT, :],
                   a2[:, 2:NT - 1].unsqueeze(2).to_broadcast([128, NX - 1, 128]), ALU.mult, [BUS, Btab], [BS16[d]])
                tt("pool", S16[d][:, NX - 1, :], US[:, 0, :], a2[:, NT - 1:NT].to_broadcast([128, 128]),
                   ALU.mult, [BUS, Btab], [BS16[d]])
            mask = mfb if d == 0 else mbb
            for x0 in range(0, NX, 4):
                bk, bb = nb()
                mm_multi([(bk[:, a * 128:(a + 1) * 128],
                           [(KdT[:, (x0 + a + 2) * 128:(x0 + a + 3) * 128], QdT[d][:, (x0 + a) * 128:(x0 + a + 1) * 128])])
                          for a in range(4)], [BKdT, BQd[d]], [bb])
                tt("dve", AT[d][:, x0:x0 + 4, :], bk[:].rearrange("p (a t) -> p a t", a=4),
                   mask[:].unsqueeze(1).to_broadcast([128, 4, 128]), ALU.mult, [bb, Bc], [BAT[d]])

        Bsm_go = Buf("sm_go")
        Bsm_fa, Bsm_ff = Buf("sm_fa"), Buf("sm_ff")

        tmps = [tmpo, tmpq]
        Btmps = [Btmpo, Btmpq]
        mtoks = [mtok, mtok2]
        Bmtoks = [Bmtok, Buf("mtok2")]
        Bsm_gos = [Buf("sm_go0"), Buf("sm_go1")]

        def gla_out(hd, is_ret, h):
            gain = rtn if is_ret else hgn
            for x0 in range(0, NX, 4):
                g2 = (x0 // 4) % 2
                tb, Btb, mt, Bmt, Bsm = tmps[g2], Btmps[g2], mtoks[g2], Bmtoks[g2], Bsm_gos[g2]
                c0 = 8 if g2 == 0 else 44
                bk, bb = nb()
                groups = []
                for a in range(4):
                    xi = x0 + a
                    pairs = []
                    for d in range(2):
                        pairs.append((AT[d][:, xi, :], V[:, xi + 2, :]))
                        pairs.append((QdT[d][:, xi * 128:(xi + 1) * 128], S16[d][:, xi, :]))
                    groups.append((bk[:, a * 128:(a + 1) * 128], pairs))
                mm_multi(groups, BAT + BQd + BS16 + [BV], [bb])
                o3 = bk[:].rearrange("p (a v) -> p a v", a=4)
                if is_ret:
                    cp("act", tb[:].rearrange("p a v -> p (a v)"), bk[:], [bb], [Btb])
                else:
                    tt("dve", tb[:], o3, SG[:, x0:x0 + 4, :], ALU.mult, [bb, BSG], [Btb])
                S.add("dve", lambda e, c0=c0: e.memset(sm[:, c0:c0 + 4], 0.0), [], [Bsm], cost=200.0)
                for a in range(4):
                    act(mt[:, a, :], tb[:, a, :], AF.Square, [Btb], [Bmt, Bsm], accum_out=sm[:, c0 + a:c0 + a + 1])
                ts("dve", sm[:, c0 + 4:c0 + 8], sm[:, c0:c0 + 4], 1.0 / 128, EPS, ALU.mult, ALU.add, [Bsm], [Bsm])
                act(sm[:, c0 + 8:c0 + 12], sm[:, c0 + 4:c0 + 8], AF.Ln, [Bsm], [Bsm])
                act(sm[:, c0 + 12:c0 + 16], sm[:, c0 + 8:c0 + 12], AF.Exp, [Bsm], [Bsm], scale=-0.5)
                tt("dve", tb[:], tb[:], sm[:, c0 + 12:c0 + 16].unsqueeze(2).to_broadcast([128, 4, 128]), ALU.mult,
                   [Btb, Bsm], [Btb])
                gb = gain[:, h * 128:(h + 1) * 128].unsqueeze(1).to_broadcast([128, 4, 128])
                if is_ret:
                    tt("pool", tb[:], tb[:], gb, ALU.mult, [Btb, Bc], [Btb])
                    tt("pool", mt[:], tb[:], SG[:, x0:x0 + 4, :], ALU.mult, [Btb, BSG], [Bmt])
                else:
                    tt("pool", mt[:], tb[:], gb, ALU.mult, [Btb, Bc], [Bmt])
                transposes([(PT[:, a * 128:(a + 1) * 128], mt[:, a, :]) for a in range(4)], [Bmt], [PTB])
                cp("act", mixT[:, hd, x0 * 128:(x0 + 4) * 128], PT[:, 0:512], [PTB], [BmixT])

        def tm_proj(sl, is_ret):
            for i0 in range(0, NT, 2):
                bk, bb = nb()
                n = 128 if i0 < 2 else 256
                mm_multi([(bk[:, a * 256:a * 256 + n],
                           [(hxT[:, k, (i0 + a) * 128:(i0 + a + 1) * 128], wtm[sl][:, k, 0:n]) for k in range(8)])
                          for a in range(2)], [hx_blk(i0 * 128), Bwtm[sl]], [bb])
                b3 = bk[:].rearrange("p (a c) -> p a c", a=2)
                cp("dve", V[:, i0:i0 + 2, :], b3[:, :, 0:128], [bb], [BV])
                if i0 >= 2:
                    for a in range(2):
                        act(SG[:, i0 - 2 + a, :], bk[:, a * 256 + 128:a * 256 + 256], AF.Silu if is_ret else AF.Sigmoid,
                            [bb], [BSG])

        def load_head_weights(sl, fm_cols, tm_cols):
            since = len(S.ops)
            for i, c in enumerate(fm_cols):
                dma("pool", wfm[sl][:, :, i * 128:(i + 1) * 128], win_v[:, :, c:c + 128], "wf%d_%d" % (sl, i), [], [Bwfm[sl]])
            for i, c in enumerate(tm_cols):
                dma("pool", wtm[sl][:, :, i * 128:(i + 1) * 128], win_v[:, :, c:c + 128], "wt%d_%d" % (sl, i), [], [Bwtm[sl]])

        win_v = win_d.rearrange("(k p) n -> p k n", p=128)
        FA3 = FA[:].rearrange("p (c t) -> p c t", t=128)
        FB3 = FB[:].rearrange("p (c t) -> p c t", t=128)
        FC3 = FC[:].rearrange("p (c t) -> p c t", t=128)
        TOKBLK = [(0, 256)] + [(256 + i * 512, 512) for i in range(4)]
        XBLK = [(256 + i * 512, 512) for i in range(4)]

        def hgrn_head(b, h, sl):
            for (t0, n) in XBLK:
                bk, bb = fm_proj(wfm[sl], Bwfm[sl], 0, t0, n)
                cp("act", qT[:, t0 - 256:t0 - 256 + n], bk[:, 0:n], [bb], [BqT])
            tm_proj(sl, False)
            for d in range(2):
                a1, a2, Dc, mid, tot, tmp = (tab[:, d, i, :] for i in range(6))
                for (t0, n) in TOKBLK:
                    bk, bb = fm_proj(wfm[sl], Bwfm[sl], 1 + d, t0, n)
                    act(FA[:, t0:t0 + n], bk[:, 0:n], AF.Sigmoid, [bb], [BFA])
                ts("dve", FA[:], FA[:], oml[:, d, h:h + 1], lbv[:, d, h:h + 1], ALU.mult, ALU.add, [BFA, Bmod], [BFA])
                act(FB[:], FA[:], AF.Ln, [BFA], [BFB])
                ts("pool", FA[:], FA[:], -1.0, 1.0, ALU.mult, ALU.add, [BFA], [BFA])
                S.add("dve", lambda e: e.tensor_tensor_scan(out=FC[:], data0=RM[:].rearrange("p c t -> p (c t)"),
                                                              data1=FB[:], initial=0.0, op0=ALU.mult, op1=ALU.add),
                      [BFB, Bc], [BFC], cost=5000.0)
                cp("dve", tot, FC3[:, :, 127], [BFC], [Btab])
                act(Dc, tot, AF.Exp, [Btab], [Btab])
                if d == 0:
                    cp("dve", mid, FC3[:, :, 63], [BFC], [Btab])
                    act(a2, mid, AF.Exp, [Btab], [Btab])
                    tt("dve", tmp, tot, mid, ALU.subtract, [Btab], [Btab])
                    act(a1, tmp, AF.Exp, [Btab], [Btab])
                else:
                    tt("dve", FC[:], FC[:], FB[:], ALU.subtract, [BFC, BFB], [BFC])
                    cp("dve", mid, FC3[:, :, 64], [BFC], [Btab])
                    act(a1, mid, AF.Exp, [Btab], [Btab])
                    tt("dve", tmp, tot, mid, ALU.subtract, [Btab], [Btab])
                    act(a2, tmp, AF.Exp, [Btab], [Btab])
                tt("dve", FC3, FC3, mid.unsqueeze(2).to_broadcast([128, NT, 128]), ALU.subtract, [BFC, Btab], [BFC])
                sq, sk = (1.0, -1.0) if d == 0 else (-1.0, 1.0)
                act(FB[:, 256:TOK], FC[:, 256:TOK], AF.Exp, [BFC], [BFB], scale=sq)
                tt("pool", QdT[d][:], qT[:], FB[:, 256:TOK], ALU.mult, [BqT, BFB], [BQd[d]])
                act(FB[:], FC[:], AF.Exp, [BFC], [BFB], scale=sk)
                tt("pool", KdT[:], FA[:], FB[:], ALU.mult, [BFA, BFB], [BKdT])
                gla_dir(d)
            gla_out(h, False, h)

        def ret_head(b, h, sl):
            lnsc = float(np.log(128.0 ** -0.5))
            act(RT[:, 0, :], posf[:], AF.Exp, [Bc, Bmod], [BRT], scale=lg[:, h:h + 1])
            act(RT[:, 1, :], posf[:], AF.Exp, [Bc, Bmod], [BRT], scale=nlg[:, h:h + 1])
            act(RT[:, 2, :], posb[:], AF.Exp, [Bc, Bmod], [BRT], scale=lg[:, 4 + h:5 + h])
            act(RT[:, 3, :], posb[:], AF.Exp, [Bc, Bmod], [BRT], scale=nlg[:, 4 + h:5 + h])
            for kd in (1, 3):
                ts("dve", RT[:, kd, :], RT[:, kd, :], 128.0 ** -0.5, None, ALU.mult, None, [BRT], [BRT])
            for d in range(2):
                for i, src in ((0, g64), (1, g64), (2, g128)):
                    cp("dve", tab[:, d, i, :], src[:, d * 4 + h:d * 4 + h + 1].to_broadcast([128, NT]), [Bmod], [Btab])
            tm_proj(sl, True)
            for which in range(2):
                dst, dB = (qT, BqT) if which == 0 else (FA, BFA)
                blks = XBLK if which == 0 else TOKBLK
                for (t0, n) in blks:
                    o0 = t0 - 256 if which == 0 else t0
                    bk, bb = fm_proj(wfm[sl], Bwfm[sl], which, t0, n)
                    if t0 < 256:
                        cp("act", dst[:, o0:o0 + n], bk[:, 0:n], [bb], [dB])
                        continue
                    xo = t0 - 256
                    cp("act", FB[:, 0:n].bitcast(BF16)[:, 0:n], bk[:, 0:n], [bb], [BFB])
                    tt("dve", FC[:, 0:n], bk[:, 0:n], cosb[:, xo:xo + n], ALU.mult, [bb, Bc], [BFC])
                    bk2, bb2 = nb()
                    mmg(bk2[:, 0:n], [(pmb[:], FB[:, 0:n].bitcast(BF16)[:, 0:n])], [BFB, Bc], [bb2])
                    tt("dve", FC[:, 512:512 + n], bk2[:, 0:n], sinb[:, xo:xo + n], ALU.mult, [bb2, Bc], [BFC])
                    tt("pool", dst[:, o0:o0 + n], FC[:, 0:n], FC[:, 512:512 + n], ALU.add, [BFC], [dB])
            for d in range(2):
                tt("pool", QdT[d][:].rearrange("p (c t) -> p c t", t=128), qT[:].rearrange("p (c t) -> p c t", t=128),
                   RT[:, 2 * d, :].unsqueeze(1).to_broadcast([128, NX, 128]), ALU.mult, [BqT, BRT], [BQd[d]])
                tt("pool", KdT[:].rearrange("p (c t) -> p c t", t=128), FA3,
                   RT[:, 2 * d + 1, :].unsqueeze(1).to_broadcast([128, NT, 128]), ALU.mult, [BFA, BRT], [BKdT])
                gla_dir(d)
            gla_out(4 + h, True, h)

        HG_COLS = lambda h: ([h * 128, 1024 + h * 128, 1536 + h * 128], [512 + h * 128, 2048 + h * 128])
        RT_COLS = lambda h: ([2560 + h * 128, 3072 + h * 128], [3584 + h * 128, 4096 + h * 128])
        wout_v = wout_d.rearrange("(k p) n -> p k n", p=128)
        Bx1d = [[Buf("x1d%d_%d" % (b, i)) for i in range(NX)] for b in range(2)]

        heads = [(False, h) for h in range(4)] + [(True, h) for h in range(4)]
        load_head_weights(0, *HG_COLS(0))
        mixT2 = mixT[:].rearrange("p k n -> p (k n)")
        wm2 = [mixT2[:, i * 8192:(i + 1) * 8192].rearrange("p (k n) -> p k n", k=8) for i in range(2)]
        for gi, g in enumerate((2, 3, 4, 5)):
            mod_group(g, wm2[gi % 2], [BmixT], "wm2_%d" % (gi % 2))
        for b in range(2):
            S.tag = "b%d_P1" % b
            for i in range(NT):
                slot = i % 2
                if i < 2:
                    src = ctx_d[b, i * 128:(i + 1) * 128, :]
                    col = 2
                else:
                    src = x_d[b, (i - 2) * 128:(i - 1) * 128, :]
                    col = b
                norm_T(src, [], slot, 1, 0, col, hxT, hx_blk(i * 128), i * 128)
            mark("P1b%d" % b, lambda: [("hxT0", hxT[:, 0, :], BhxTb), ("hxT7", hxT[:, 7, :], BhxTb)])
            phase_barrier()
            for hi, (is_ret, h) in enumerate(heads):
                sl = hi % 2
                S.tag = "b%d_h%d" % (b, hi)
                fm_cols, tm_cols = RT_COLS(h) if is_ret else HG_COLS(h)
                if not (b == 0 and hi == 0):
                    load_head_weights(sl, fm_cols, tm_cols)
                if is_ret:
                    ret_head(b, h, sl)
                else:
                    hgrn_head(b, h, sl)
                mark("head%d_%d" % (b, hi), lambda: [("mixT", mixT[:, hi, :], [BmixT]), ("QdT0", QdT[0][:], [BQd[0]]),
                                                     ("QdT1", QdT[1][:], [BQd[1]]), ("KdT", KdT[:], [BKdT]),
                                                     ("V", V[:].rearrange("p a b -> p (a b)"), [BV]),
                                                     ("tab", tab[:].rearrange("p a b c -> p (a b c)"), [Btab])])
            S.tag = "b%d_P3" % b
            phase_barrier()
            dma("pool", woutb[:], wout_v, "wout", [], Bwout)
            for j in range(8):
                ts("dve", diag[:, j, :], identf[:], modP[:, 2, j, b:b + 1], None, ALU.mult, None, [Bc, Bmg[2]], [Bdiag])
            for n in range(2):
                bk, bb = nb()
                mmg(bk[:], [(onesf[:], diag[:, 4 * n:4 * n + 4, :].rearrange("p j q -> p (j q)"))], [Bdiag, Bc], [bb])
                cp("act", gB1[:, n * 512:(n + 1) * 512], bk[:], [bb], [BgB1])
            x1ts = [x1t, diag[:].rearrange("p a b -> p (a b)")]
            Bx1ts = [Bx1t, Bdiag]
            for xi in range(NX):
                slot = xi % 2
                xo, Bxo = x1ts[slot], Bx1ts[slot]
                dma("sp", xt[slot][:], x_d[b, xi * 128:(xi + 1) * 128, :], "xt%d" % slot, [], [Bxt[slot]])
                for n in range(2):
                    bk, bb = nb()
                    mmg(bk[:], [(mixT[:, hd, xi * 128:(xi + 1) * 128], woutb[:, hd, n * 512:(n + 1) * 512])
                                for hd in range(8)], [BmixT] + Bwout, [bb])
                    tt("dve", xo[:, n * 512:(n + 1) * 512], bk[:], gB1[:, n * 512:(n + 1) * 512], ALU.mult,
                       [bb, BgB1], [Bxo])
                tt("pool", xo[:], xo[:], xt[slot][:], ALU.add, [Bxo, Bxt[slot]], [Bxo])
                dma("sp", x1_d[b, xi * 128:(xi + 1) * 128, :], xo[:], "x1st%d" % slot, [Bxo], [Bx1d[b][xi]])

        S.tag = "ffn"
        A.off = base_off
        fgn = A.alloc([128, D], F32)
        gB2 = A.alloc([128, D], F32)
        wupb = A.alloc([128, 8, 2 * DFF], BF16)
        wdnb = A.alloc([128, NFC, D], BF16)
        fxt = [A.alloc([128, D], F32) for _ in range(2)]
        fxs = A.alloc([128, D], BF16)
        h2T = [A.alloc([128, 8, BLK + 2], BF16) for _ in range(3)]
        hT = A.alloc([128, NFC, BLK], BF16)
        NEW = 3
        gbufs = [A.alloc([128, BLK + 2], F32) for _ in range(NEW)]
        t1s = [A.alloc([128, BLK], F32) for _ in range(NEW)]
        t2s = [A.alloc([128, BLK], F32) for _ in range(NEW)]
        hbs = [A.alloc([128, BLK], F32) for _ in range(NEW)]
        fdiag = A.alloc([128, 8, 128], F32)
        x2t = [A.alloc([128, D], F32) for _ in range(2)]
        outt = A.alloc([128, D], F32)
        Bfgn, BgB2, Bwup, Bwdn = Buf("fgn"), Buf("gB2"), Buf("wup"), Buf("wdn")
        Bfxt = [Buf("fxt0"), Buf("fxt1")]
        Bfxs = Buf("fxs")
        Bh2T = [Buf("h2T%d" % i) for i in range(3)]
        BhT, Bfdiag = Buf("hT"), Buf("fdiag")
        Bgbufs = [Buf("gbuf%d" % i) for i in range(NEW)]
        Bt1s = [Buf("t1%d" % i) for i in range(NEW)]
        Bt2s = [Buf("t2%d" % i) for i in range(NEW)]
        Bhbs = [Buf("hb%d" % i) for i in range(NEW)]
        build_program.ffn_end = None
        Bx2t = [Buf("x2t0"), Buf("x2t1")]
        Boutt = Buf("outt")
        ffn_bufs = [Bfgn, BgB2, Bwup, Bwdn, Bfxs, BhT, Bfdiag, Boutt] + Bfxt + Bh2T + Bx2t + Bgbufs + Bt1s + Bt2s + Bhbs
        build_program.ffn_end = A.off
        S.add("dve", lambda e: e.memset(sm[:, 60:61], 0.0), [], mixer_bufs0 + scratchB + ffn_bufs + [Bsm, Bsm_go, Bsm_fa, Bsm_ff] + Bsm1 + Bsm_gos + Bmtoks)

        wup_v = wup_d.rearrange("(k p) n -> p k n", p=128)
        wdn_v = wdn_d.rearrange("(f p) n -> p f n", p=128)
        UG = [(g * 512, min(512, DFF - g * 512)) for g in range(6)]
        Bwup_g = [Buf("wupg%d" % g) for g in range(6)]
        Bwdn_g = [Buf("wdng%d" % g) for g in range(4)]
        for g, (c0, n) in enumerate(UG):
            dma("pool", wupb[:, :, c0:c0 + n], wup_v[:, :, c0:c0 + n], "wupa%d" % g, [Bwup], [Bwup_g[g]])
            dma("pool", wupb[:, :, DFF + c0:DFF + c0 + n], wup_v[:, :, DFF + c0:DFF + c0 + n], "wupb%d" % g,
                [Bwup], [Bwup_g[g]])
        for gi, f0 in enumerate(range(0, NFC, 6)):
            f1 = min(NFC, f0 + 6)
            dma("pool", wdnb[:, f0:f1, :], wdn_v[:, f0:f1, :], "wdn%d" % f0, [Bwdn], [Bwdn_g[gi]])
        dma("sp", fgn[:], fgn_d.partition_broadcast(128), "fgn", [], [Bfgn])

        NB = L // BLK
        out_ops = []

        def ffn_A(b, j):
            Bsm = Bsm_fa
            g = b * NB + j
            sl = g % 3
            for a in range(BLK // 128):
                xi = j * (BLK // 128) + a
                s2 = (g * 2 + a) % 2
                dma("sp", fxt[s2][:], x1_d[b, xi * 128:(xi + 1) * 128, :], "fxt%d" % s2, [Bx1d[b][xi]], [Bfxt[s2]])
                S.add("dve", lambda e: e.memset(sm[:, 30:31], 0.0), [], [Bsm])
                act(fxs[:], fxt[s2][:], AF.Square, [Bfxt[s2]], [Bfxs, Bsm], accum_out=sm[:, 30:31])
                ts("dve", sm[:, 31:32], sm[:, 30:31], 1.0 / D, EPS, ALU.mult, ALU.add, [Bsm], [Bsm])
                act(sm[:, 32:33], sm[:, 31:32], AF.Ln, [Bsm], [Bsm])
                act(sm[:, 33:34], sm[:, 32:33], AF.Exp, [Bsm], [Bsm], scale=-0.5)
                ts("dve", fxs[:], fxt[s2][:], sm[:, 33:34], None, ALU.mult, None, [Bfxt[s2], Bsm], [Bfxs])
                transposes([(PT[:, jj * 128:(jj + 1) * 128], fxs[:, jj * 128:(jj + 1) * 128]) for jj in range(8)],
                           [Bfxs], [PTB])
                tt("dve", fdiag[:], PT.rearrange("p (j t) -> p j t", j=8),
                   modP[:, 4, :, b:b + 1].to_broadcast([128, 8, 128]), ALU.mult, [PTB, Bmg[4]], [Bfdiag])
                tt("pool", h2T[sl][:, :, 1 + a * 128:1 + (a + 1) * 128], fdiag[:],
                   modP[:, 3, :, b:b + 1].to_broadcast([128, 8, 128]), ALU.add, [Bfdiag, Bmg[3]], [Bh2T[sl]])
            if j == 0:
                S.add("pool", lambda e: e.memset(h2T[sl][:, :, 0:1], 0.0), [], [Bh2T[sl]])
            else:
                sp_ = (g - 1) % 3
                cp("pool", h2T[sl][:, :, 0:1], h2T[sp_][:, :, BLK:BLK + 1], [Bh2T[sp_]], [Bh2T[sl]])
                cp("pool", h2T[sp_][:, :, BLK + 1:BLK + 2], h2T[sl][:, :, 1:2], [Bh2T[sl]], [Bh2T[sp_]])
            if j == NB - 1:
                S.add("pool", lambda e: e.memset(h2T[sl][:, :, BLK + 1:BLK + 2], 0.0), [], [Bh2T[sl]])

        def ffn_F(b, j):
            Bsm = Bsm_ff
            g = b * NB + j
            sl = g % 3
            if j == 0:
                for jj in range(8):
                    ts("dve", fdiag[:, jj, :], identf[:], modP[:, 5, jj, b:b + 1], None, ALU.mult, None,
                       [Bc, Bmg[5]], [Bfdiag])
                for n in range(2):
                    bk, bb = nb()
                    mmg(bk[:], [(onesf[:], fdiag[:, 4 * n:4 * n + 4, :].rearrange("p j q -> p (j q)"))],
                        [Bfdiag, Bc], [bb])
                    cp("act", gB2[:, n * 512:(n + 1) * 512], bk[:], [bb], [BgB2])
            for fc in range(NFC):
                ei = fc % NEW
                gbuf, t1, t2, hb = gbufs[ei], t1s[ei], t2s[ei], hbs[ei]
                Bgbuf, Bt1, Bt2, Bhb = Bgbufs[ei], Bt1s[ei], Bt2s[ei], Bhbs[ei]
                bg, bbg = nb()
                mmg(bg[:, 0:BLK + 2], [(wupb[:, k, fc * 128:(fc + 1) * 128], h2T[sl][:, k, :]) for k in range(8)],
                    [Bwup_g[fc // 4], Bh2T[sl]], [bbg])
                bu, bbu = nb()
                mmg(bu[:, 0:BLK], [(wupb[:, k, DFF + fc * 128:DFF + (fc + 1) * 128], h2T[sl][:, k, 1:BLK + 1])
                                   for k in range(8)], [Bwup_g[fc // 4], Bh2T[sl]], [bbu])
                cp("act", gbuf[:], bg[:, 0:BLK + 2], [bbg], [Bgbuf])
                ts("pool", t1[:], gbuf[:, 1:BLK + 1], cwP[:, fc, 1:2], cbP[:, fc:fc + 1], ALU.mult, ALU.add,
                   [Bgbuf, Bc], [Bt1])
                stt(t2[:], gbuf[:, 0:BLK], cwP[:, fc, 0:1], t1[:], ALU.mult, ALU.add, [Bgbuf, Bt1, Bc], [Bt2])
                stt(t1[:], gbuf[:, 2:BLK + 2], cwP[:, fc, 2:3], t2[:], ALU.mult, ALU.add, [Bgbuf, Bt2, Bc], [Bt1])
                act(hb[:], t1[:], AF.Silu, [Bt1], [Bhb])
                tt("dve", hT[:, fc, :], hb[:], bu[:, 0:BLK], ALU.mult, [Bhb, bbu], [BhT])
            for a in range(BLK // 128):
                xi = j * (BLK // 128) + a
                s2 = (g * 2 + a) % 2
                dma("sp", x2t[s2][:], x1_d[b, xi * 128:(xi + 1) * 128, :], "x2t%d" % s2, [Bx1d[b][xi]], [Bx2t[s2]])
                for n in range(2):
                    bk, bb = nb()
                    mmg(bk[:], [(hT[:, fc, a * 128:(a + 1) * 128], wdnb[:, fc, n * 512:(n + 1) * 512])
                                for fc in range(NFC)], [BhT] + Bwdn_g, [bb])
                    tt("dve", outt[:, n * 512:(n + 1) * 512], bk[:], gB2[:, n * 512:(n + 1) * 512], ALU.mult,
                       [bb, BgB2], [Boutt])
                tt("pool", x2t[s2][:], x2t[s2][:], outt[:], ALU.add, [Bx2t[s2], Boutt], [Bx2t[s2]])
                S.add("dve", lambda e: e.memset(sm[:, 40:41], 0.0), [], [Bsm])
                act(outt[:], x2t[s2][:], AF.Square, [Bx2t[s2]], [Boutt, Bsm], accum_out=sm[:, 40:41])
                ts("dve", sm[:, 41:42], sm[:, 40:41], 1.0 / D, EPS, ALU.mult, ALU.add, [Bsm], [Bsm])
                act(sm[:, 42:43], sm[:, 41:42], AF.Ln, [Bsm], [Bsm])
                act(sm[:, 43:44], sm[:, 42:43], AF.Exp, [Bsm], [Bsm], scale=-0.5)
                stt(outt[:], x2t[s2][:], sm[:, 43:44], fgn[:], ALU.mult, ALU.mult, [Bx2t[s2], Bsm, Bfgn], [Boutt])
                out_ops.append(dma("sp", out_d[b, xi * 128:(xi + 1) * 128, :], outt[:], "outst", [Boutt], []))

        seq = [(b, j) for b in range(2) for j in range(NB)]
        ffn_A(*seq[0])
        for i, (b, j) in enumerate(seq):
            if i + 1 < len(seq):
                ffn_A(*seq[i + 1])
            ffn_F(b, j)

        fw = out_ops[-4:] + out_ops[:1]
        if stage["stopped"]:
            last = {}
            for i, o in enumerate(S.ops):
                if o.is_dma:
                    last[o.key] = i
            fw = list(last.values())
        if REORDER:
            remap = S.reorder()
            fw = [remap[i] for i in fw]
        S.emit(final_wait_ops=fw)
        build_program.stats = (len(S.ops), A.hi, getattr(S, "est_total", None))
        busy = {}
        for o in S.ops:
            busy[o.eng] = busy.get(o.eng, 0.0) + o.cost
        build_program.busy = busy
        build_program.ops = S.ops
        tags = {}
        for o in S.ops:
            if getattr(o, "t1", None) is not None:
                a = tags.setdefault(o.tag, [1e18, 0.0])
                a[0] = min(a[0], o.t0)
                a[1] = max(a[1], o.t1)
        build_program.tags = tags
        build_program.dbg_names = stage["names"]
    return nc


def _consts():
    k = np.arange(128)
    ident = np.eye(128, dtype=np.float32)
    swap = np.where((k % 64) < 32, k + 32, k - 32)
    pm = np.zeros((128, 128), np.float32)
    pm[swap, k] = 1.0
    s = np.arange(128)[:, None]
    t = np.arange(128)[None, :]
    mf = (s <= t).astype(np.float32)
    mb = (s >= t).astype(np.float32)
    tt_ = np.arange(L, dtype=np.float32)
    rows = np.floor(tt_ / 64.0)
    cols = tt_ - rows * 64.0
    quarter = 32
    freqs = (10000.0 ** (-np.arange(quarter, dtype=np.float32) / quarter)).astype(np.float32)
    cos = np.zeros((128, L), np.float32)
    sin = np.zeros((128, L), np.float32)
    for kk in range(128):
        pos = rows if kk < 64 else cols
        i = kk % 32
        ang = (pos * freqs[i]).astype(np.float32)
        cos[kk] = np.cos(ang)
        sin[kk] = -np.sin(ang) if (kk % 64) < 32 else np.sin(ang)
    posf = np.tile((np.arange(128, dtype=np.float32) - 63.0)[None, :], (128, 1))
    posb = np.tile((64.0 - np.arange(128, dtype=np.float32))[None, :], (128, 1))
    return dict(ident=ident, pm=pm, mf=mf, mb=mb, cos=cos, sin=sin, posf=posf, posb=posb)


def make_in_maps(x, c, ctx, c_ctx, w_mod, b_mod, norm1_g, w_in, hgrn_lb, hgrn_norm_g, ret_decay, ret_norm_g,
                 w_out, norm2_g, w_up, conv_w, conv_b, w_down, final_g):
    f = lambda a: np.ascontiguousarray(np.asarray(a, dtype=np.float32))
    cst = _consts()
    pl = lambda v: f(np.asarray(v).reshape(-1, 128).T)
    shared = dict(
        w_mod=f(w_mod[0]), bmodP=pl(b_mod[0]), n1g=pl(norm1_g[0]), n2g=pl(norm2_g[0]), w_in=f(w_in[0]),
        lbP=f(np.asarray(hgrn_lb).reshape(2, 2, 4, 128).transpose(3, 0, 1, 2).reshape(128, 16)),
        hgn=f(hgrn_norm_g[0]).reshape(1, 512), rd=f(ret_decay[0]).reshape(1, 8), rtn=f(ret_norm_g[0]).reshape(1, 512),
        w_out=f(w_out[0]), w_up=f(w_up[0]),
        cwP=f(np.asarray(conv_w[0]).reshape(3, NFC, 128).transpose(2, 1, 0).reshape(128, 66)),
        cbP=pl(conv_b[0]), w_down=f(w_down[0]), fgn=f(final_g).reshape(1, D), **cst)
    maps = []
    for core in range(NCORES):
        cc = np.stack([np.asarray(c[2 * core]), np.asarray(c[2 * core + 1]), np.asarray(c_ctx)], axis=0)
        cT = f(cc.reshape(3, 8, 128).transpose(2, 1, 0).reshape(128, 24))
        m = dict(shared)
        m.update(x=f(x[2 * core:2 * core + 2]), ctx=f(ctx[2 * core:2 * core + 2]), cT=cT)
        maps.append(m)
    return maps


_NC_CACHE = {}


def kernel(**inputs):
    if "nc" not in _NC_CACHE:
        _NC_CACHE["nc"] = build_program()
    nc = _NC_CACHE["nc"]
    in_maps = make_in_maps(**inputs)
    res = run_bass_kernel_spmd(nc, in_maps, core_ids=list(range(NCORES)))
    out = np.concatenate([np.asarray(r["out"]) for r in res.results], axis=0)
    return out.astype(np.float32)
```

```python
import contextlib
import numpy as np
import ml_dtypes
import concourse.bass as bass
import concourse.mybir as mybir
from concourse.bass_utils import run_bass_kernel_spmd

F32 = mybir.dt.float32
BF16 = mybir.dt.bfloat16
AF = mybir.ActivationFunctionType
ALU = mybir.AluOpType
AX = mybir.AxisListType

D = 1024
L = 2048
LC = 256
TOK = L + LC
NT = TOK // 128
NX = L // 128
DFF = 2816
NFC = DFF // 128
INW = 4608
EPS = 1e-6
BLK = 256
NCORES = 8
REORDER = True
PRIO = "blevel"


class Buf:
    __slots__ = ("name", "w", "r", "excl")

    def __init__(self, name, excl=False):
        self.name = name
        self.w = None
        self.r = []
        self.excl = excl


class Op:
    __slots__ = ("eng", "fn", "deps", "sig", "needs_sig", "is_dma", "idx", "key", "cost", "xfer", "t0", "t1", "tag")


class Sched:
    COMPUTE = ("pe", "act", "dve", "pool")

    def __init__(self, nc):
        self.nc = nc
        self.ops = []
        self.dma_count = {}
        self.last_dma = {}

    def add(self, eng, fn, reads=(), writes=(), dma=None, cost=300.0, xfer=0.0):
        op = Op()
        op.cost = float(cost)
        op.xfer = float(xfer)
        op.tag = getattr(self, "tag", "")
        op.eng = eng
        op.fn = fn
        op.idx = len(self.ops)
        op.is_dma = dma is not None
        op.needs_sig = op.is_dma
        op.key = dma
        writes = list(writes) + [b for b in reads if b.excl]
        reads = [b for b in reads if not b.excl]
        deps = {}
        for b in reads:
            if b.w is not None:
                deps[b.w] = True
        for b in writes:
            if b.w is not None:
                deps.setdefault(b.w, False)
            for r in b.r:
                deps.setdefault(r, False)
        op.deps = []
        for j, raw in deps.items():
            if j == op.idx:
                continue
            op.deps.append(j)
        for b in reads:
            b.r.append(op.idx)
        for b in writes:
            b.w = op.idx
            b.r = []
        if op.is_dma:
            prev = self.last_dma.get(dma)
            if prev is not None and prev not in op.deps:
                op.deps.append(prev)
            self.last_dma[dma] = op.idx
            c = self.dma_count.get(dma, 0) + 16
            self.dma_count[dma] = c
            op.sig = (dma, c)
        else:
            op.sig = None
        self.ops.append(op)
        return op.idx

    def seal_group(self, key, since=0):
        if key not in self.dma_count:
            return
        tot = self.dma_count[key]
        for op in self.ops[since:]:
            if op.is_dma and op.key == key:
                op.sig = (key, tot)

    def reorder(self):
        import heapq
        ops = self.ops
        n = len(ops)
        users = [[] for _ in range(n)]
        ndep = [0] * n
        for op in ops:
            ds = set(op.deps)
            op.deps = sorted(ds)
            ndep[op.idx] = len(op.deps)
            for j in op.deps:
                users[j].append(op.idx)
        LAT = 200.0
        blevel = [0.0] * n
        for i in range(n - 1, -1, -1):
            op = ops[i]
            m = 0.0
            for u in users[i]:
                if blevel[u] + LAT > m:
                    m = blevel[u] + LAT
            blevel[i] = op.cost + op.xfer + m
        if PRIO == "blevel":
            key = [-blevel[i] for i in range(n)]
        else:
            key = [float(i) for i in range(n)]
        engs = ("pe", "act", "dve", "pool", "sp")
        free = {e: 0.0 for e in engs}
        byready = {e: [] for e in engs}
        now = {e: [] for e in engs}
        ready_t = [0.0] * n
        fin = [0.0] * n
        dma_free = [0.0]
        for op in ops:
            if ndep[op.idx] == 0:
                heapq.heappush(byready[op.eng], (0.0, op.idx))
        order = []
        while len(order) < n:
            best = None
            for e in engs:
                br, nw = byready[e], now[e]
                while br and br[0][0] <= free[e]:
                    ii = heapq.heappop(br)[1]
                    heapq.heappush(nw, (key[ii], ii))
                if nw:
                    est = free[e]
                elif br:
                    est = br[0][0]
                else:
                    continue
                if best is None or est < best[0]:
                    best = (est, e)
            est, e = best
            if now[e]:
                i = heapq.heappop(now[e])[1]
            else:
                i = heapq.heappop(byready[e])[1]
            op = ops[i]
            op.t0 = est
            free[e] = est + op.cost
            if op.is_dma:
                st = max(est + op.cost, dma_free[0])
                fin[i] = st + op.xfer
                dma_free[0] = st + op.xfer * 0.6
            else:
                fin[i] = est + op.cost
            op.t1 = fin[i]
            order.append(i)
            for u in users[i]:
                ready_t[u] = max(ready_t[u], fin[i] + LAT)
                ndep[u] -= 1
                if ndep[u] == 0:
                    heapq.heappush(byready[ops[u].eng], (ready_t[u], u))
        newidx = {old: new for new, old in enumerate(order)}
        newops = [ops[i] for i in order]
        for op in newops:
            op.deps = [newidx[j] for j in op.deps]
            op.idx = newidx[op.idx]
        self.ops = newops
        self.est_total = max(fin) if fin else 0.0
        return newidx

    def emit(self, final_wait_ops=()):
        nc = self.nc
        for op in self.ops:
            kept = []
            for j in op.deps:
                o = self.ops[j]
                if o.eng == op.eng and op.eng == "pe" and not o.is_dma and not op.is_dma:
                    continue
                kept.append(j)
                o.needs_sig = True
            op.deps = kept
        with contextlib.ExitStack() as st:
            sems = {}
            for e in self.COMPUTE:
                sems[e] = st.enter_context(nc.semaphore("s_" + e))
            for k in self.dma_count:
                sems[k] = st.enter_context(nc.semaphore("d_" + str(k)))
            cnt = {e: 0 for e in self.COMPUTE}
            for op in self.ops:
                if not op.is_dma and op.needs_sig:
                    cnt[op.eng] += 1
                    op.sig = (op.eng, cnt[op.eng])
            block = st.enter_context(nc.Block())
            ops = self.ops

            def run(engname, e):
                waited = {}
                for op in ops:
                    if op.eng != engname:
                        continue
                    for j in op.deps:
                        k, v = ops[j].sig
                        if waited.get(k, 0) < v:
                            e.wait_ge(sems[k], v)
                            waited[k] = v
                    ins = op.fn(e)
                    if op.needs_sig:
                        ins.then_inc(sems[op.sig[0]], 16 if op.is_dma else 1)
                if engname == "sp":
                    for j in final_wait_ops:
                        k, v = ops[j].sig
                        if waited.get(k, 0) < v:
                            e.wait_ge(sems[k], v)
                            waited[k] = v

            @block.tensor
            def _(e):
                run("pe", e)

            @block.scalar
            def _(e):
                run("act", e)

            @block.vector
            def _(e):
                run("dve", e)

            @block.gpsimd
            def _(e):
                run("pool", e)

            @block.sync
            def _(e):
                run("sp", e)


class Arena:
    def __init__(self, big, limit):
        self.big = big
        self.off = 0
        self.limit = limit
        self.hi = 0

    def _view(self, ap, shape):
        if len(shape) == 2:
            return ap
        if len(shape) == 3:
            return ap.rearrange("p (a b) -> p a b", a=shape[1])
        return ap.rearrange("p (a b c) -> p a b c", a=shape[1], b=shape[2])

    def alloc(self, shape, dt):
        n = int(np.prod(shape[1:]))
        nb = n * (4 if dt == F32 else 2)
        nb = (nb + 3) // 4 * 4
        o = self.off
        self.off += nb
        self.hi = max(self.hi, self.off)
        assert self.off <= self.limit, ("SBUF arena overflow", self.off)
        ap = self.big[:, o // 4:(o + nb) // 4]
        if dt != F32:
            ap = ap.bitcast(BF16)[:, 0:n]
        return self._view(ap, shape)


def build_program(debug=False, stop=None):
    nc = bass.Bass("TRN2", target_bir_lowering=False)

    def din(name, shape, dt=F32):
        return nc.dram_tensor(name, list(shape), dt, kind="ExternalInput").ap()

    x_d = din("x", [2, L, D])
    ctx_d = din("ctx", [2, LC, D])
    cT_d = din("cT", [128, 24])
    wmod_d = din("w_mod", [D, 6 * D])
    bmodP_d = din("bmodP", [128, 48])
    n1g_d = din("n1g", [128, 8])
    n2g_d = din("n2g", [128, 8])
    win_d = din("w_in", [D, INW])
    lbP_d = din("lbP", [128, 16])
    hgn_d = din("hgn", [1, 512])
    rd_d = din("rd", [1, 8])
    rtn_d = din("rtn", [1, 512])
    wout_d = din("w_out", [D, D])
    wup_d = din("w_up", [D, 2 * DFF])
    cwP_d = din("cwP", [128, 66])
    cbP_d = din("cbP", [128, 22])
    wdn_d = din("w_down", [DFF, D])
    fgn_d = din("fgn", [1, D])
    ident_d = din("ident", [128, 128])
    pm_d = din("pm", [128, 128])
    mf_d = din("mf", [128, 128])
    mb_d = din("mb", [128, 128])
    cos_d = din("cos", [128, L])
    sin_d = din("sin", [128, L])
    posf_d = din("posf", [128, 128])
    posb_d = din("posb", [128, 128])
    out_d = nc.dram_tensor("out", [2, L, D], F32, kind="ExternalOutput").ap()
    x1_d = nc.dram_tensor("x1s", [2, L, D], F32,
                          kind="ExternalOutput" if debug else "Internal").ap()

    st = contextlib.ExitStack()
    with st:
        LIMIT = 212000
        big = st.enter_context(nc.sbuf_tensor("big", [128, LIMIT // 4], F32))
        banks = [st.enter_context(nc.psum_tensor("bank%d" % i, [128, 512], F32)) for i in range(8)]
        bankB = [Buf("bank%d" % i, excl=True) for i in range(8)]
        PT = banks[7][:].bitcast(BF16)
        PTB = bankB[7]
        S = Sched(nc)
        A = Arena(big, LIMIT)
        dbg_d = nc.dram_tensor("dbg", [128, 16384], F32, kind="ExternalOutput").ap() if debug else None
        stage = {"n": 0, "off": 0, "stopped": False, "names": []}
        _add = S.add

        def gated_add(*a, **k):
            if stage["stopped"]:
                return 0
            if isinstance(stop, int) and len(S.ops) >= stop:
                stage["stopped"] = True
                return 0
            return _add(*a, **k)
        S.add = gated_add

        def dump(name, ap2d, bufs):
            n = ap2d.shape[1]
            o = stage["off"]
            stage["off"] += n
            stage["names"].append((name, o, n))
            _add("pool", lambda e: e.dma_start(out=dbg_d[:, o:o + n], in_=ap2d), list(bufs), [], dma="dbg%d" % len(stage["names"]))

        def mark(name, dumps=()):
            if stage["stopped"]:
                return
            if stop is not None and name == stop:
                for nm, ap, bufs in dumps():
                    dump(nm, ap, bufs)
                stage["stopped"] = True

        rot = [0]

        def nb():
            i = rot[0]
            rot[0] = (i + 1) % 7
            return banks[i], bankB[i]

        def nel(ap):
            n = 1
            for d in ap.shape[1:]:
                n *= int(d)
            return n

        def ecost(eng, n):
            if eng == "act":
                return n * 0.83 + 300.0
            if eng == "dve":
                return n * 1.04 + 170.0
            return n * 1.7 + 350.0

        def act(out, in_, func, reads, writes, **kw):
            return S.add("act", lambda e: e.activation(out=out, in_=in_, func=func, **kw), reads, writes,
                         cost=ecost("act", nel(out)))

        def ts(eng, out, in0, s1, s2, op0, op1, reads, writes):
            c = ecost(eng, nel(out))
            if s2 is None:
                return S.add(eng, lambda e: e.tensor_scalar(out=out, in0=in0, scalar1=s1, scalar2=None, op0=op0),
                             reads, writes, cost=c)
            return S.add(eng, lambda e: e.tensor_scalar(out=out, in0=in0, scalar1=s1, scalar2=s2, op0=op0, op1=op1),
                         reads, writes, cost=c)

        def tt(eng, out, in0, in1, op, reads, writes):
            return S.add(eng, lambda e: e.tensor_tensor(out=out, in0=in0, in1=in1, op=op), reads, writes,
                         cost=ecost(eng, nel(out)))

        def stt(out, in0, sc, in1, op0, op1, reads, writes):
            return S.add("dve", lambda e: e.scalar_tensor_tensor(out=out, in0=in0, scalar=sc, in1=in1, op0=op0, op1=op1),
                         reads, writes, cost=ecost("dve", nel(out)))

        def cp(eng, out, in_, reads, writes):
            if eng == "act":
                return act(out, in_, AF.Copy, reads, writes)
            return S.add(eng, lambda e: e.tensor_copy(out=out, in_=in_), reads, writes, cost=ecost(eng, nel(out)))

        def dma(eng, out, in_, key, reads, writes):
            nbytes = 128 * nel(out) * 4
            if eng == "pool":
                return S.add(eng, lambda e: e.dma_start(out=out, in_=in_), reads, writes, dma=key,
                             cost=1500.0, xfer=nbytes / 130.0 + 2000.0)
            return S.add(eng, lambda e: e.dma_start(out=out, in_=in_), reads, writes, dma=key,
                         cost=250.0, xfer=nbytes / 200.0 + 2000.0)

        def mmcost(pairs):
            c = 0.0
            for l, r in pairs:
                c += max(64, nel(r)) / 2.2 + 35.0
            return c

        def mmg(out, pairs, reads, writes):
            pairs = list(pairs)

            def fn(e):
                n = len(pairs)
                ins = None
                for i, (l, r) in enumerate(pairs):
                    ins = e.matmul(out, lhsT=l, rhs=r, start=(i == 0), stop=(i == n - 1))
                return ins
            return S.add("pe", fn, reads, writes, cost=mmcost(pairs))

        def mm_multi(groups, reads, writes):
            groups = [(o, list(p)) for o, p in groups]

            def fn(e):
                ins = None
                for out, pairs in groups:
                    n = len(pairs)
                    for i, (l, r) in enumerate(pairs):
                        ins = e.matmul(out, lhsT=l, rhs=r, start=(i == 0), stop=(i == n - 1))
                return ins
            return S.add("pe", fn, reads, writes, cost=sum(mmcost(p) for _, p in groups))

        def transposes(items, reads, writes):
            items = list(items)

            def fn(e):
                ins = None
                for o, i_ in items:
                    ins = e.transpose(out=o, in_=i_, identity=identb)
                return ins
            return S.add("pe", fn, reads + [Bc], writes, cost=110.0 * len(items))

        identb = A.alloc([128, 128], BF16)
        pmb = A.alloc([128, 128], BF16)
        mfb = A.alloc([128, 128], BF16)
        mbb = A.alloc([128, 128], BF16)
        identf = A.alloc([128, 128], F32)
        onesf = A.alloc([128, 128], F32)
        cT = A.alloc([128, 24], F32)
        cs = A.alloc([128, 8, 3], BF16)
        bmodP = A.alloc([128, 6, 8], F32)
        n1g = A.alloc([128, 8], F32)
        n2g = A.alloc([128, 8], F32)
        modP = A.alloc([128, 6, 8, 3], F32)
        lbP = A.alloc([128, 2, 2, 4], F32)
        lbv = A.alloc([128, 2, 4], F32)
        oml = A.alloc([128, 2, 4], F32)
        rdt = A.alloc([128, 8], F32)
        lg = A.alloc([128, 8], F32)
        nlg = A.alloc([128, 8], F32)
        g64 = A.alloc([128, 8], F32)
        g128 = A.alloc([128, 8], F32)
        cwP = A.alloc([128, 22, 3], F32)
        cbP = A.alloc([128, 22], F32)
        sm = A.alloc([128, 64], F32)
        Bc = Buf("consts")
        Bmod = Buf("modP")
        Bmg = [Buf("modg%d" % g) for g in range(6)]
        Bsm = Buf("sm")
        base_off = A.off

        cosb = A.alloc([128, L], BF16)
        sinb = A.alloc([128, L], BF16)
        posf = A.alloc([128, 128], F32)
        posb = A.alloc([128, 128], F32)
        RM = A.alloc([128, NT, 128], BF16)
        RT = A.alloc([128, 4, 128], F32)
        hgn = A.alloc([128, 512], F32)
        rtn = A.alloc([128, 512], F32)
        hxT = A.alloc([128, 8, TOK], BF16)
        mixT = A.alloc([128, 8, L], BF16)
        wfm = [A.alloc([128, 8, 384], BF16) for _ in range(2)]
        wtm = [A.alloc([128, 8, 256], BF16) for _ in range(2)]
        qT = A.alloc([128, L], F32)
        ar1 = A.off
        FA = A.alloc([128, TOK], F32)
        FB = A.alloc([128, TOK], F32)
        FC = A.alloc([128, TOK], F32)
        ar2 = A.off
        KdT = A.alloc([128, TOK], BF16)
        Kd = A.alloc([128, NT, 128], BF16)
        US = A.alloc([128, NT, 128], F32)
        ar2e = A.off
        QdT = [A.alloc([128, L], BF16) for _ in range(2)]
        S16 = [A.alloc([128, NX, 128], BF16) for _ in range(2)]
        AT = [A.alloc([128, NX, 128], BF16) for _ in range(2)]
        V = A.alloc([128, NT, 128], BF16)
        SG = A.alloc([128, NX, 128], BF16)
        tmpo = A.alloc([128, 4, 128], F32)
        tmpq = A.alloc([128, 4, 128], F32)
        mtok = A.alloc([128, 4, 128], BF16)
        mtok2 = A.alloc([128, 4, 128], BF16)
        tab = A.alloc([128, 2, 6, NT], F32)
        mixer_hi = A.off
        A.off = ar1
        xt = [A.alloc([128, D], F32) for _ in range(2)]
        xs = [A.alloc([128, D], BF16) for _ in range(2)]
        x1t = A.alloc([128, D], F32)
        gB1 = A.alloc([128, D], F32)
        diag = A.alloc([128, 8, 128], F32)
        assert A.off <= ar2
        A.off = ar2
        woutb = A.alloc([128, 8, D], BF16)
        assert A.off <= ar2e
        A.off = mixer_hi

        BFA, BFB, BFC = Buf("FA"), Buf("FB"), Buf("FC")
        BKdT, BKd, BUS = Buf("KdT"), Buf("Kd"), Buf("US")
        BmixT, BqT = Buf("mixT"), Buf("qT")
        BhxTb = [Buf("hxT%d" % i) for i in range(5)]
        BhxT = BhxTb

        def hx_blk(tok):
            return BhxTb[0] if tok < 256 else BhxTb[1 + (tok - 256) // 512]
        Bwfm = [Buf("wfm0"), Buf("wfm1")]
        Bwtm = [Buf("wtm0"), Buf("wtm1")]
        BQd = [Buf("Qd0"), Buf("Qd1")]
        BS16 = [Buf("S160"), Buf("S161")]
        BAT = [Buf("AT0"), Buf("AT1")]
        BV, BSG = Buf("V"), Buf("SG")
        Btmpo, Btmpq, Bmtok, Btab = Buf("tmpo"), Buf("tmpq"), Buf("mtok"), Buf("tab")
        BRT = Buf("RT")
        Bxt = [BFA, BFA]
        Bxs = [BFB, BFB]
        Bx1t = BFB
        BgB1 = BFC
        Bdiag = BFC
        Bwout = [BKdT, BKd, BUS]
        mixer_bufs0 = [BFA, BFB, BFC, BKdT, BKd, BUS, BmixT, BqT] + BhxTb + Bwfm + Bwtm + BQd + BS16 + BAT + \
                     [BV, BSG, Btmpo, Btmpq, Bmtok, Btab, BRT, Bc]
        Bxt = [Buf("xt0"), Buf("xt1")]
        Bxs = [Buf("xs0"), Buf("xs1")]
        Bx1t = Buf("x1t")
        BgB1 = Buf("gB1")
        Bdiag = Buf("diag")
        allF = [BFA, BFB, BFC]
        scratchB = Bxt + Bxs + [Bx1t, BgB1, Bdiag]

        Bbar = Buf("bar")

        def phase_barrier():
            S.add("dve", lambda e: e.memset(sm[:, 61:62], 0.0), [], allF + scratchB + [Bbar])

        ckn = [0]

        def CKf(q):
            ckn[0] += 1
            return "c%s%d" % (q, ckn[0] % 5)
        for dst, src in ((identb[:], ident_d), (pmb[:], pm_d), (mfb[:], mf_d), (mbb[:], mb_d),
                         (cosb[:], cos_d), (sinb[:], sin_d)):
            dma("pool", dst, src, CKf("p"), [], [Bc])
        for dst, src in ((identf[:], ident_d), (cT[:], cT_d), (bmodP[:], bmodP_d.rearrange("p (g j) -> p g j", g=6)),
                         (n1g[:], n1g_d), (n2g[:], n2g_d),
                         (lbP[:], lbP_d.rearrange("p (d i h) -> p d i h", d=2, i=2)),
                         (rdt[:], rd_d.partition_broadcast(128)),
                         (cwP[:], cwP_d.rearrange("p (f j) -> p f j", j=3)), (cbP[:], cbP_d),
                         (posf[:], posf_d), (posb[:], posb_d),
                         (hgn[:], hgn_d.partition_broadcast(128)), (rtn[:], rtn_d.partition_broadcast(128))):
            dma("sp", dst, src, CKf("s"), [], [Bc])
        S.add("dve", lambda e: e.memset(onesf[:], 1.0), [], [Bc])
        S.add("dve", lambda e: e.memset(RM[:], 1.0), [], [Bc])
        S.add("dve", lambda e: e.memset(RM[:, :, 0:1], 0.0), [], [Bc])
        act(cs[:], cT[:].rearrange("p (k j) -> p k j", j=3), AF.Silu, [Bc], [Bmod])
        tt("dve", lbv[:], lbP[:, :, 0, :], lbP[:, :, 1, :], ALU.subtract, [Bc], [Bmod])
        act(lbv[:], lbv[:], AF.Sigmoid, [Bmod], [Bmod])
        ts("dve", oml[:], lbv[:], -1.0, 1.0, ALU.mult, ALU.add, [Bmod], [Bmod])
        act(lg[:], rdt[:], AF.Sigmoid, [Bc], [Bmod])
        act(lg[:], lg[:], AF.Ln, [Bmod], [Bmod])
        ts("dve", nlg[:], lg[:], -1.0, None, ALU.mult, None, [Bmod], [Bmod])
        act(g64[:], lg[:], AF.Exp, [Bmod], [Bmod], scale=64.0)
        act(g128[:], lg[:], AF.Exp, [Bmod], [Bmod], scale=128.0)

        A.off = ar1
        wmv = [A.alloc([128, 8, 1024], BF16)]
        A.off = ar2
        wmv.append(A.alloc([128, 8, 1024], BF16))
        A.off = mixer_hi
        Bwm = [allF, [BKdT, BKd, BUS]]
        wmod_v = wmod_d.rearrange("(k p) n -> p k n", p=128)
        def mod_group(g, wv, wB, key):
            dma("pool", wv[:], wmod_v[:, :, g * 1024:(g + 1) * 1024], key, [], wB)
            bk, bb = nb()
            groups = []
            for j in range(8):
                groups.append((bk[:, j * 4:j * 4 + 3],
                               [(wv[:, k, j * 128:(j + 1) * 128], cs[:, k, :]) for k in range(8)]))
            mm_multi(groups, wB + [Bmod], [bb])
            tt("dve", modP[:, g], bk[:, 0:32].rearrange("p (j c) -> p j c", c=4)[:, :, 0:3],
               bmodP[:, g, :].unsqueeze(2).to_broadcast([128, 8, 3]), ALU.add, [bb, Bc], [Bmg[g]])
            if g in (1, 4):
                ng = n1g if g == 1 else n2g
                ts("dve", modP[:, g], modP[:, g], 1.0, None, ALU.add, None, [Bmg[g]], [Bmg[g]])
                tt("dve", modP[:, g], modP[:, g], ng[:].unsqueeze(2).to_broadcast([128, 8, 3]), ALU.mult,
                   [Bmg[g], Bc], [Bmg[g]])

        mod_group(1, wmv[0], Bwm[0], "wm0")
        mod_group(0, wmv[1], Bwm[1], "wm1")
        mark("setup", lambda: [("modP", modP[:].rearrange("p a b c -> p (a b c)"), Bmg), ("lbv", lbv[:].rearrange("p a b -> p (a b)"), [Bmod]),
                               ("lg", lg[:], [Bmod]), ("g64", g64[:], [Bmod])])
        xt1 = [QdT[i][:].bitcast(F32) for i in range(2)]
        xs1 = [S16[i][:].rearrange("p a b -> p (a b)")[:, 0:D] for i in range(2)]
        tmp1 = AT[0][:].rearrange("p a b -> p (a b)").bitcast(F32).rearrange("p (j t) -> p j t", j=8)
        Bsm1 = [Buf("sm_p1_0"), Buf("sm_p1_1")]

        def norm_T(src_ap, src_key_bufs, slot, gA, gB_, col, dstT, dstB, c0, extra_reads=(), from_dram=True,
                   keep=None):
            bx, bs, bsm = BQd[slot], BS16[slot], Bsm1[slot]
            dma("sp", xt1[slot], src_ap, "p1x%d" % slot, list(src_key_bufs), [bx])
            S.add("dve", lambda e: e.memset(sm[:, slot:slot + 1], 0.0), [], [bsm])
            act(xs1[slot], xt1[slot], AF.Square, [bx], [bs, bsm], accum_out=sm[:, slot:slot + 1])
            ts("dve", sm[:, 2 + slot:3 + slot], sm[:, slot:slot + 1], 1.0 / D, EPS, ALU.mult, ALU.add, [bsm], [bsm])
            act(sm[:, 4 + slot:5 + slot], sm[:, 2 + slot:3 + slot], AF.Ln, [bsm], [bsm])
            act(sm[:, 6 + slot:7 + slot], sm[:, 4 + slot:5 + slot], AF.Exp, [bsm], [bsm], scale=-0.5)
            act(xs1[slot], xt1[slot], AF.Copy, [bx, bsm], [bs], scale=sm[:, 6 + slot:7 + slot])
            transposes([(PT[:, j * 128:(j + 1) * 128], xs1[slot][:, j * 128:(j + 1) * 128]) for j in range(8)],
                       [bs], [PTB])
            tt("dve", tmp1, PT.rearrange("p (j t) -> p j t", j=8),
               modP[:, gA, :, col:col + 1].to_broadcast([128, 8, 128]), ALU.mult, [PTB, Bmg[gA]], [BAT[0]])
            tt("pool", dstT[:, :, c0:c0 + 128], tmp1,
               modP[:, gB_, :, col:col + 1].to_broadcast([128, 8, 128]), ALU.add, [BAT[0], Bmg[gB_]] + list(extra_reads),
               [dstB])

        def fm_proj(wv, wB, c, t0, n):
            bk, bb = nb()
            mmg(bk[:, 0:n], [(wv[:, k, c * 128:(c + 1) * 128], hxT[:, k, t0:t0 + n]) for k in range(8)],
                [wB, hx_blk(t0)], [bb])
            return bk, bb

        def gla_dir(d):
            a1 = tab[:, d, 0, :]
            a2 = tab[:, d, 1, :]
            Dc = tab[:, d, 2, :]
            for c0 in range(0, NT, 8):
                n = min(8, NT - c0)
                transposes([(PT[:, a * 128:(a + 1) * 128], KdT[:, (c0 + a) * 128:(c0 + a + 1) * 128]) for a in range(n)],
                           [BKdT], [PTB])
                cp("act", Kd[:].rearrange("p a k -> p (a k)")[:, c0 * 128:(c0 + n) * 128], PT[:, 0:n * 128], [PTB], [BKd])
            skip = NT - 1 if d == 0 else 2
            for c0 in range(0, NT, 4):
                cl = [c for c in range(c0, min(c0 + 4, NT))]
                bk, bb = nb()
                mm_multi([(bk[:, (c - c0) * 128:(c - c0 + 1) * 128], [(Kd[:, c, :], V[:, c, :])]) for c in cl],
                         [BKd, BV], [bb])
                n = len(cl)
                tt("dve", US[:, c0:c0 + n, :], bk[:, 0:n * 128].rearrange("p (a v) -> p a v", a=n),
                   a1[:, c0:c0 + n].unsqueeze(2).to_broadcast([128, n, 128]), ALU.mult, [bb, Btab], [BUS])
            order = list(range(NT)) if d == 0 else [1, 0] + list(range(NT - 1, 1, -1))
            for j in range(1, NT - 1):
                c, pc = order[j], order[j - 1]
                stt(US[:, c, :], US[:, pc, :], Dc[:, c:c + 1], US[:, c, :], ALU.mult, ALU.add, [BUS, Btab], [BUS])
            if d == 0:
                tt("pool", S16[d][:], US[:, 1:NT - 1, :], a2[:, 2:NT].unsqueeze(2).to_broadcast([128, NX, 128]),
                   ALU.mult, [BUS, Btab], [BS16[d]])
            else:
                tt("pool", S16[d][:, 0:NX - 1, :], US[:, 3:NT, :],
                   a2[:, 2:NT - 1].unsqueeze(2).to_broadcast([128, NX - 1, 128]), ALU.mult, [BUS, Btab], [BS16[d]])
                tt("pool", S16[d][:, NX - 1, :], US[:, 0, :], a2[:, NT - 1:NT].to_broadcast([128, 128]),
                   ALU.mult, [BUS, Btab], [BS16[d]])
            mask = mfb if d == 0 else mbb
            for x0 in range(0, NX, 4):
                bk, bb = nb()
                mm_multi([(bk[:, a * 128:(a + 1) * 128],
                           [(KdT[:, (x0 + a + 2) * 128:(x0 + a + 3) * 128], QdT[d][:, (x0 + a) * 128:(x0 + a + 1) * 128])])
                          for a in range(4)], [BKdT, BQd[d]], [bb])
                tt("dve", AT[d][:, x0:x0 + 4, :], bk[:].rearrange("p (a t) -> p a t", a=4),
                   mask[:].unsqueeze(1).to_broadcast([128, 4, 128]), ALU.mult, [bb, Bc], [BAT[d]])

        Bsm_go = Buf("sm_go")
        Bsm_fa, Bsm_ff = Buf("sm_fa"), Buf("sm_ff")

        tmps = [tmpo, tmpq]
        Btmps = [Btmpo, Btmpq]
        mtoks = [mtok, mtok2]
        Bmtoks = [Bmtok, Buf("mtok2")]
        Bsm_gos = [Buf("sm_go0"), Buf("sm_go1")]

        def gla_out(hd, is_ret, h):
            gain = rtn if is_ret else hgn
            for x0 in range(0, NX, 4):
                g2 = (x0 // 4) % 2
                tb, Btb, mt, Bmt, Bsm = tmps[g2], Btmps[g2], mtoks[g2], Bmtoks[g2], Bsm_gos[g2]
                c0 = 8 if g2 == 0 else 44
                bk, bb = nb()
                groups = []
                for a in range(4):
                    xi = x0 + a
                    pairs = []
                    for d in range(2):
                        pairs.append((AT[d][:, xi, :], V[:, xi + 2, :]))
                        pairs.append((QdT[d][:, xi * 128:(xi + 1) * 128], S16[d][:, xi, :]))
                    groups.append((bk[:, a * 128:(a + 1) * 128], pairs))
                mm_multi(groups, BAT + BQd + BS16 + [BV], [bb])
                o3 = bk[:].rearrange("p (a v) -> p a v", a=4)
                if is_ret:
                    cp("act", tb[:].rearrange("p a v -> p (a v)"), bk[:], [bb], [Btb])
                else:
                    tt("dve", tb[:], o3, SG[:, x0:x0 + 4, :], ALU.mult, [bb, BSG], [Btb])
                S.add("dve", lambda e, c0=c0: e.memset(sm[:, c0:c0 + 4], 0.0), [], [Bsm], cost=200.0)
                for a in range(4):
                    act(mt[:, a, :], tb[:, a, :], AF.Square, [Btb], [Bmt, Bsm], accum_out=sm[:, c0 + a:c0 + a + 1])
                ts("dve", sm[:, c0 + 4:c0 + 8], sm[:, c0:c0 + 4], 1.0 / 128, EPS, ALU.mult, ALU.add, [Bsm], [Bsm])
                act(sm[:, c0 + 8:c0 + 12], sm[:, c0 + 4:c0 + 8], AF.Ln, [Bsm], [Bsm])
                act(sm[:, c0 + 12:c0 + 16], sm[:, c0 + 8:c0 + 12], AF.Exp, [Bsm], [Bsm], scale=-0.5)
                tt("dve", tb[:], tb[:], sm[:, c0 + 12:c0 + 16].unsqueeze(2).to_broadcast([128, 4, 128]), ALU.mult,
                   [Btb, Bsm], [Btb])
                gb = gain[:, h * 128:(h + 1) * 128].unsqueeze(1).to_broadcast([128, 4, 128])
                if is_ret:
                    tt("pool", tb[:], tb[:], gb, ALU.mult, [Btb, Bc], [Btb])
                    tt("pool", mt[:], tb[:], SG[:, x0:x0 + 4, :], ALU.mult, [Btb, BSG], [Bmt])
                else:
                    tt("pool", mt[:], tb[:], gb, ALU.mult, [Btb, Bc], [Bmt])
                transposes([(PT[:, a * 128:(a + 1) * 128], mt[:, a, :]) for a in range(4)], [Bmt], [PTB])
                cp("act", mixT[:, hd, x0 * 128:(x0 + 4) * 128], PT[:, 0:512], [PTB], [BmixT])

        def tm_proj(sl, is_ret):
            for i0 in range(0, NT, 2):
                bk, bb = nb()
                n = 128 if i0 < 2 else 256
                mm_multi([(bk[:, a * 256:a * 256 + n],
                           [(hxT[:, k, (i0 + a) * 128:(i0 + a + 1) * 128], wtm[sl][:, k, 0:n]) for k in range(8)])
                          for a in range(2)], [hx_blk(i0 * 128), Bwtm[sl]], [bb])
                b3 = bk[:].rearrange("p (a c) -> p a c", a=2)
                cp("dve", V[:, i0:i0 + 2, :], b3[:, :, 0:128], [bb], [BV])
                if i0 >= 2:
                    for a in range(2):
                        act(SG[:, i0 - 2 + a, :], bk[:, a * 256 + 128:a * 256 + 256], AF.Silu if is_ret else AF.Sigmoid,
                            [bb], [BSG])

        def load_head_weights(sl, fm_cols, tm_cols):
            since = len(S.ops)
            for i, c in enumerate(fm_cols):
                dma("pool", wfm[sl][:, :, i * 128:(i + 1) * 128], win_v[:, :, c:c + 128], "wf%d_%d" % (sl, i), [], [Bwfm[sl]])
            for i, c in enumerate(tm_cols):
                dma("pool", wtm[sl][:, :, i * 128:(i + 1) * 128], win_v[:, :, c:c + 128], "wt%d_%d" % (sl, i), [], [Bwtm[sl]])

        win_v = win_d.rearrange("(k p) n -> p k n", p=128)
        FA3 = FA[:].rearrange("p (c t) -> p c t", t=128)
        FB3 = FB[:].rearrange("p (c t) -> p c t", t=128)
        FC3 = FC[:].rearrange("p (c t) -> p c t", t=128)
        TOKBLK = [(0, 256)] + [(256 + i * 512, 512) for i in range(4)]
        XBLK = [(256 + i * 512, 512) for i in range(4)]

        def hgrn_head(b, h, sl):
            for (t0, n) in XBLK:
                bk, bb = fm_proj(wfm[sl], Bwfm[sl], 0, t0, n)
                cp("act", qT[:, t0 - 256:t0 - 256 + n], bk[:, 0:n], [bb], [BqT])
            tm_proj(sl, False)
            for d in range(2):
                a1, a2, Dc, mid, tot, tmp = (tab[:, d, i, :] for i in range(6))
                for (t0, n) in TOKBLK:
                    bk, bb = fm_proj(wfm[sl], Bwfm[sl], 1 + d, t0, n)
                    act(FA[:, t0:t0 + n], bk[:, 0:n], AF.Sigmoid, [bb], [BFA])
                ts("dve", FA[:], FA[:], oml[:, d, h:h + 1], lbv[:, d, h:h + 1], ALU.mult, ALU.add, [BFA, Bmod], [BFA])
                act(FB[:], FA[:], AF.Ln, [BFA], [BFB])
                ts("pool", FA[:], FA[:], -1.0, 1.0, ALU.mult, ALU.add, [BFA], [BFA])
                S.add("dve", lambda e: e.tensor_tensor_scan(out=FC[:], data0=RM[:].rearrange("p c t -> p (c t)"),
                                                              data1=FB[:], initial=0.0, op0=ALU.mult, op1=ALU.add),
                      [BFB, Bc], [BFC], cost=5000.0)
                cp("dve", tot, FC3[:, :, 127], [BFC], [Btab])
                act(Dc, tot, AF.Exp, [Btab], [Btab])
                if d == 0:
                    cp("dve", mid, FC3[:, :, 63], [BFC], [Btab])
                    act(a2, mid, AF.Exp, [Btab], [Btab])
                    tt("dve", tmp, tot, mid, ALU.subtract, [Btab], [Btab])
                    act(a1, tmp, AF.Exp, [Btab], [Btab])
                else:
                    tt("dve", FC[:], FC[:], FB[:], ALU.subtract, [BFC, BFB], [BFC])
                    cp("dve", mid, FC3[:, :, 64], [BFC], [Btab])
                    act(a1, mid, AF.Exp, [Btab], [Btab])
                    tt("dve", tmp, tot, mid, ALU.subtract, [Btab], [Btab])
                    act(a2, tmp, AF.Exp, [Btab], [Btab])
                tt("dve", FC3, FC3, mid.unsqueeze(2).to_broadcast([128, NT, 128]), ALU.subtract, [BFC, Btab], [BFC])
                sq, sk = (1.0, -1.0) if d == 0 else (-1.0, 1.0)
                act(FB[:], FC[:], AF.Exp, [BFC], [BFB], scale=sk)
                tt("pool", KdT[:], FA[:], FB[:], ALU.mult, [BFA, BFB], [BKdT])
                act(FC[:, 256:TOK], FC[:, 256:TOK], AF.Exp, [BFC], [BFC], scale=sq)
                tt("dve", QdT[d][:], qT[:], FC[:, 256:TOK], ALU.mult, [BqT, BFC], [BQd[d]])
                gla_dir(d)
            gla_out(h, False, h)

        def ret_head(b, h, sl):
            lnsc = float(np.log(128.0 ** -0.5))
            act(RT[:, 0, :], posf[:], AF.Exp, [Bc, Bmod], [BRT], scale=lg[:, h:h + 1])
            act(RT[:, 1, :], posf[:], AF.Exp, [Bc, Bmod], [BRT], scale=nlg[:, h:h + 1])
            act(RT[:, 2, :], posb[:], AF.Exp, [Bc, Bmod], [BRT], scale=lg[:, 4 + h:5 + h])
            act(RT[:, 3, :], posb[:], AF.Exp, [Bc, Bmod], [BRT], scale=nlg[:, 4 + h:5 + h])
            for kd in (1, 3):
                ts("dve", RT[:, kd, :], RT[:, kd, :], 128.0 ** -0.5, None, ALU.mult, None, [BRT], [BRT])
            for d in range(2):
                for i, src in ((0, g64), (1, g64), (2, g128)):
                    cp("dve", tab[:, d, i, :], src[:, d * 4 + h:d * 4 + h + 1].to_broadcast([128, NT]), [Bmod], [Btab])
            tm_proj(sl, True)
            for which in range(2):
                dst, dB = (qT, BqT) if which == 0 else (FA, BFA)
                blks = XBLK if which == 0 else TOKBLK
                for (t0, n) in blks:
                    o0 = t0 - 256 if which == 0 else t0
                    bk, bb = fm_proj(wfm[sl], Bwfm[sl], which, t0, n)
                    if t0 < 256:
                        cp("act", dst[:, o0:o0 + n], bk[:, 0:n], [bb], [dB])
                        continue
                    xo = t0 - 256
                    cp("act", FB[:, 0:n].bitcast(BF16)[:, 0:n], bk[:, 0:n], [bb], [BFB])
                    tt("dve", FC[:, 0:n], bk[:, 0:n], cosb[:, xo:xo + n], ALU.mult, [bb, Bc], [BFC])
                    bk2, bb2 = nb()
                    mmg(bk2[:, 0:n], [(pmb[:], FB[:, 0:n].bitcast(BF16)[:, 0:n])], [BFB, Bc], [bb2])
                    tt("dve", FC[:, 512:512 + n], bk2[:, 0:n], sinb[:, xo:xo + n], ALU.mult, [bb2, Bc], [BFC])
                    tt("pool", dst[:, o0:o0 + n], FC[:, 0:n], FC[:, 512:512 + n], ALU.add, [BFC], [dB])
            for d in range(2):
                tt("dve", QdT[d][:].rearrange("p (c t) -> p c t", t=128), qT[:].rearrange("p (c t) -> p c t", t=128),
                   RT[:, 2 * d, :].unsqueeze(1).to_broadcast([128, NX, 128]), ALU.mult, [BqT, BRT], [BQd[d]])
                tt("pool", KdT[:].rearrange("p (c t) -> p c t", t=128), FA3,
                   RT[:, 2 * d + 1, :].unsqueeze(1).to_broadcast([128, NT, 128]), ALU.mult, [BFA, BRT], [BKdT])
                gla_dir(d)
            gla_out(4 + h, True, h)

        HG_COLS = lambda h: ([h * 128, 1024 + h * 128, 1536 + h * 128], [512 + h * 128, 2048 + h * 128])
        RT_COLS = lambda h: ([2560 + h * 128, 3072 + h * 128], [3584 + h * 128, 4096 + h * 128])
        wout_v = wout_d.rearrange("(k p) n -> p k n", p=128)
        Bx1d = [[Buf("x1d%d_%d" % (b, i)) for i in range(NX)] for b in range(2)]

        heads = [(False, h) for h in range(4)] + [(True, h) for h in range(4)]
        load_head_weights(0, *HG_COLS(0))
        mixT2 = mixT[:].rearrange("p k n -> p (k n)")
        wm2 = [mixT2[:, i * 8192:(i + 1) * 8192].rearrange("p (k n) -> p k n", k=8) for i in range(2)]
        for gi, g in enumerate((2, 3, 4, 5)):
            mod_group(g, wm2[gi % 2], [BmixT], "wm2_%d" % (gi % 2))
        for b in range(2):
            S.tag = "b%d_P1" % b
            for i in range(NT):
                slot = i % 2
                if i < 2:
                    src = ctx_d[b, i * 128:(i + 1) * 128, :]
                    col = 2
                else:
                    src = x_d[b, (i - 2) * 128:(i - 1) * 128, :]
                    col = b
                norm_T(src, [], slot, 1, 0, col, hxT, hx_blk(i * 128), i * 128)
            mark("P1b%d" % b, lambda: [("hxT0", hxT[:, 0, :], BhxTb), ("hxT7", hxT[:, 7, :], BhxTb)])
            phase_barrier()
            for hi, (is_ret, h) in enumerate(heads):
                sl = hi % 2
                S.tag = "b%d_h%d" % (b, hi)
                fm_cols, tm_cols = RT_COLS(h) if is_ret else HG_COLS(h)
                if not (b == 0 and hi == 0):
                    load_head_weights(sl, fm_cols, tm_cols)
                if is_ret:
                    ret_head(b, h, sl)
                else:
                    hgrn_head(b, h, sl)
                mark("head%d_%d" % (b, hi), lambda: [("mixT", mixT[:, hi, :], [BmixT]), ("QdT0", QdT[0][:], [BQd[0]]),
                                                     ("QdT1", QdT[1][:], [BQd[1]]), ("KdT", KdT[:], [BKdT]),
                                                     ("V", V[:].rearrange("p a b -> p (a b)"), [BV]),
                                                     ("tab", tab[:].rearrange("p a b c -> p (a b c)"), [Btab])])
            S.tag = "b%d_P3" % b
            phase_barrier()
            dma("pool", woutb[:], wout_v, "wout", [], Bwout)
            for j in range(8):
                ts("dve", diag[:, j, :], identf[:], modP[:, 2, j, b:b + 1], None, ALU.mult, None, [Bc, Bmg[2]], [Bdiag])
            for n in range(2):
                bk, bb = nb()
                mmg(bk[:], [(onesf[:], diag[:, 4 * n:4 * n + 4, :].rearrange("p j q -> p (j q)"))], [Bdiag, Bc], [bb])
                cp("act", gB1[:, n * 512:(n + 1) * 512], bk[:], [bb], [BgB1])
            x1ts = [x1t, diag[:].rearrange("p a b -> p (a b)")]
            Bx1ts = [Bx1t, Bdiag]
            for xi in range(NX):
                slot = xi % 2
                xo, Bxo = x1ts[slot], Bx1ts[slot]
                dma("sp", xt[slot][:], x_d[b, xi * 128:(xi + 1) * 128, :], "xt%d" % slot, [], [Bxt[slot]])
                for n in range(2):
                    bk, bb = nb()
                    mmg(bk[:], [(mixT[:, hd, xi * 128:(xi + 1) * 128], woutb[:, hd, n * 512:(n + 1) * 512])
                                for hd in range(8)], [BmixT] + Bwout, [bb])
                    tt("dve", xo[:, n * 512:(n + 1) * 512], bk[:], gB1[:, n * 512:(n + 1) * 512], ALU.mult,
                       [bb, BgB1], [Bxo])
                tt("pool", xo[:], xo[:], xt[slot][:], ALU.add, [Bxo, Bxt[slot]], [Bxo])
                dma("sp", x1_d[b, xi * 128:(xi + 1) * 128, :], xo[:], "x1st%d" % slot, [Bxo], [Bx1d[b][xi]])

        S.tag = "ffn"
        A.off = base_off
        fgn = A.alloc([128, D], F32)
        gB2 = A.alloc([128, D], F32)
        wupb = A.alloc([128, 8, 2 * DFF], BF16)
        wdnb = A.alloc([128, NFC, D], BF16)
        fxt = [A.alloc([128, D], F32) for _ in range(2)]
        fxs = A.alloc([128, D], BF16)
        h2T = [A.alloc([128, 8, BLK + 2], BF16) for _ in range(3)]
        hT = A.alloc([128, NFC, BLK], BF16)
        NEW = 3
        gbufs = [A.alloc([128, BLK + 2], F32) for _ in range(NEW)]
        t1s = [A.alloc([128, BLK], F32) for _ in range(NEW)]
        t2s = [A.alloc([128, BLK], F32) for _ in range(NEW)]
        hbs = [A.alloc([128, BLK], F32) for _ in range(NEW)]
        fdiag = A.alloc([128, 8, 128], F32)
        x2t = [A.alloc([128, D], F32) for _ in range(2)]
        outt = A.alloc([128, D], F32)
        Bfgn, BgB2, Bwup, Bwdn = Buf("fgn"), Buf("gB2"), Buf("wup"), Buf("wdn")
        Bfxt = [Buf("fxt0"), Buf("fxt1")]
        Bfxs = Buf("fxs")
        Bh2T = [Buf("h2T%d" % i) for i in range(3)]
        BhT, Bfdiag = Buf("hT"), Buf("fdiag")
        Bgbufs = [Buf("gbuf%d" % i) for i in range(NEW)]
        Bt1s = [Buf("t1%d" % i) for i in range(NEW)]
        Bt2s = [Buf("t2%d" % i) for i in range(NEW)]
        Bhbs = [Buf("hb%d" % i) for i in range(NEW)]
        build_program.ffn_end = None
        Bx2t = [Buf("x2t0"), Buf("x2t1")]
        Boutt = Buf("outt")
        ffn_bufs = [Bfgn, BgB2, Bwup, Bwdn, Bfxs, BhT, Bfdiag, Boutt] + Bfxt + Bh2T + Bx2t + Bgbufs + Bt1s + Bt2s + Bhbs
        build_program.ffn_end = A.off
        S.add("dve", lambda e: e.memset(sm[:, 60:61], 0.0), [], mixer_bufs0 + scratchB + ffn_bufs + [Bsm, Bsm_go, Bsm_fa, Bsm_ff] + Bsm1 + Bsm_gos + Bmtoks)

        wup_v = wup_d.rearrange("(k p) n -> p k n", p=128)
        wdn_v = wdn_d.rearrange("(f p) n -> p f n", p=128)
        UG = [(g * 512, min(512, DFF - g * 512)) for g in range(6)]
        Bwup_g = [Buf("wupg%d" % g) for g in range(6)]
        Bwdn_g = [Buf("wdng%d" % g) for g in range(4)]
        for g, (c0, n) in enumerate(UG):
            dma("pool", wupb[:, :, c0:c0 + n], wup_v[:, :, c0:c0 + n], "wupa%d" % g, [Bwup], [Bwup_g[g]])
            dma("pool", wupb[:, :, DFF + c0:DFF + c0 + n], wup_v[:, :, DFF + c0:DFF + c0 + n], "wupb%d" % g,
                [Bwup], [Bwup_g[g]])
        for gi, f0 in enumerate(range(0, NFC, 6)):
            f1 = min(NFC, f0 + 6)
            dma("pool", wdnb[:, f0:f1, :], wdn_v[:, f0:f1, :], "wdn%d" % f0, [Bwdn], [Bwdn_g[gi]])
        dma("sp", fgn[:], fgn_d.partition_broadcast(128), "fgn", [], [Bfgn])

        NB = L // BLK
        out_ops = []

        def ffn_A(b, j):
            Bsm = Bsm_fa
            g = b * NB + j
            sl = g % 3
            for a in range(BLK // 128):
                xi = j * (BLK // 128) + a
                s2 = (g * 2 + a) % 2
                dma("sp", fxt[s2][:], x1_d[b, xi * 128:(xi + 1) * 128, :], "fxt%d" % s2, [Bx1d[b][xi]], [Bfxt[s2]])
                S.add("dve", lambda e: e.memset(sm[:, 30:31], 0.0), [], [Bsm])
                act(fxs[:], fxt[s2][:], AF.Square, [Bfxt[s2]], [Bfxs, Bsm], accum_out=sm[:, 30:31])
                ts("dve", sm[:, 31:32], sm[:, 30:31], 1.0 / D, EPS, ALU.mult, ALU.add, [Bsm], [Bsm])
                act(sm[:, 32:33], sm[:, 31:32], AF.Ln, [Bsm], [Bsm])
                act(sm[:, 33:34], sm[:, 32:33], AF.Exp, [Bsm], [Bsm], scale=-0.5)
                ts("dve", fxs[:], fxt[s2][:], sm[:, 33:34], None, ALU.mult, None, [Bfxt[s2], Bsm], [Bfxs])
                transposes([(PT[:, jj * 128:(jj + 1) * 128], fxs[:, jj * 128:(jj + 1) * 128]) for jj in range(8)],
                           [Bfxs], [PTB])
                tt("dve", fdiag[:], PT.rearrange("p (j t) -> p j t", j=8),
                   modP[:, 4, :, b:b + 1].to_broadcast([128, 8, 128]), ALU.mult, [PTB, Bmg[4]], [Bfdiag])
                tt("pool", h2T[sl][:, :, 1 + a * 128:1 + (a + 1) * 128], fdiag[:],
                   modP[:, 3, :, b:b + 1].to_broadcast([128, 8, 128]), ALU.add, [Bfdiag, Bmg[3]], [Bh2T[sl]])
            if j == 0:
                S.add("pool", lambda e: e.memset(h2T[sl][:, :, 0:1], 0.0), [], [Bh2T[sl]])
            else:
                sp_ = (g - 1) % 3
                cp("pool", h2T[sl][:, :, 0:1], h2T[sp_][:, :, BLK:BLK + 1], [Bh2T[sp_]], [Bh2T[sl]])
                cp("pool", h2T[sp_][:, :, BLK + 1:BLK + 2], h2T[sl][:, :, 1:2], [Bh2T[sl]], [Bh2T[sp_]])
            if j == NB - 1:
                S.add("pool", lambda e: e.memset(h2T[sl][:, :, BLK + 1:BLK + 2], 0.0), [], [Bh2T[sl]])

        def ffn_F(b, j):
            Bsm = Bsm_ff
            g = b * NB + j
            sl = g % 3
            if j == 0:
                for jj in range(8):
                    ts("dve", fdiag[:, jj, :], identf[:], modP[:, 5, jj, b:b + 1], None, ALU.mult, None,
                       [Bc, Bmg[5]], [Bfdiag])
                for n in range(2):
                    bk, bb = nb()
                    mmg(bk[:], [(onesf[:], fdiag[:, 4 * n:4 * n + 4, :].rearrange("p j q -> p (j q)"))],
                        [Bfdiag, Bc], [bb])
                    cp("act", gB2[:, n * 512:(n + 1) * 512], bk[:], [bb], [BgB2])
            for fc in range(NFC):
                ei = fc % NEW
                gbuf, t1, t2, hb = gbufs[ei], t1s[ei], t2s[ei], hbs[ei]
                Bgbuf, Bt1, Bt2, Bhb = Bgbufs[ei], Bt1s[ei], Bt2s[ei], Bhbs[ei]
                bg, bbg = nb()
                mmg(bg[:, 0:BLK + 2], [(wupb[:, k, fc * 128:(fc + 1) * 128], h2T[sl][:, k, :]) for k in range(8)],
                    [Bwup_g[fc // 4], Bh2T[sl]], [bbg])
                bu, bbu = nb()
                mmg(bu[:, 0:BLK], [(wupb[:, k, DFF + fc * 128:DFF + (fc + 1) * 128], h2T[sl][:, k, 1:BLK + 1])
                                   for k in range(8)], [Bwup_g[fc // 4], Bh2T[sl]], [bbu])
                cp("act", gbuf[:], bg[:, 0:BLK + 2], [bbg], [Bgbuf])
                ts("pool", t1[:], gbuf[:, 1:BLK + 1], cwP[:, fc, 1:2], cbP[:, fc:fc + 1], ALU.mult, ALU.add,
                   [Bgbuf, Bc], [Bt1])
                stt(t2[:], gbuf[:, 0:BLK], cwP[:, fc, 0:1], t1[:], ALU.mult, ALU.add, [Bgbuf, Bt1, Bc], [Bt2])
                stt(t1[:], gbuf[:, 2:BLK + 2], cwP[:, fc, 2:3], t2[:], ALU.mult, ALU.add, [Bgbuf, Bt2, Bc], [Bt1])
                act(hb[:], t1[:], AF.Silu, [Bt1], [Bhb])
                tt("dve", hT[:, fc, :], hb[:], bu[:, 0:BLK], ALU.mult, [Bhb, bbu], [BhT])
            for a in range(BLK // 128):
                xi = j * (BLK // 128) + a
                s2 = (g * 2 + a) % 2
                dma("sp", x2t[s2][:], x1_d[b, xi * 128:(xi + 1) * 128, :], "x2t%d" % s2, [Bx1d[b][xi]], [Bx2t[s2]])
                for n in range(2):
                    bk, bb = nb()
                    mmg(bk[:], [(hT[:, fc, a * 128:(a + 1) * 128], wdnb[:, fc, n * 512:(n + 1) * 512])
                                for fc in range(NFC)], [BhT] + Bwdn_g, [bb])
                    tt("dve", outt[:, n * 512:(n + 1) * 512], bk[:], gB2[:, n * 512:(n + 1) * 512], ALU.mult,
                       [bb, BgB2], [Boutt])
                tt("pool", x2t[s2][:], x2t[s2][:], outt[:], ALU.add, [Bx2t[s2], Boutt], [Bx2t[s2]])
                S.add("dve", lambda e: e.memset(sm[:, 40:41], 0.0), [], [Bsm])
                act(outt[:], x2t[s2][:], AF.Square, [Bx2t[s2]], [Boutt, Bsm], accum_out=sm[:, 40:41])
                ts("dve", sm[:, 41:42], sm[:, 40:41], 1.0 / D, EPS, ALU.mult, ALU.add, [Bsm], [Bsm])
                act(sm[:, 42:43], sm[:, 41:42], AF.Ln, [Bsm], [Bsm])
                act(sm[:, 43:44], sm[:, 42:43], AF.Exp, [Bsm], [Bsm], scale=-0.5)
                stt(outt[:], x2t[s2][:], sm[:, 43:44], fgn[:], ALU.mult, ALU.mult, [Bx2t[s2], Bsm, Bfgn], [Boutt])
                out_ops.append(dma("sp", out_d[b, xi * 128:(xi + 1) * 128, :], outt[:], "outst", [Boutt], []))

        seq = [(b, j) for b in range(2) for j in range(NB)]
        ffn_A(*seq[0])
        for i, (b, j) in enumerate(seq):
            if i + 1 < len(seq):
                ffn_A(*seq[i + 1])
            ffn_F(b, j)

        fw = out_ops[-4:] + out_ops[:1]
        if stage["stopped"]:
            last = {}
            for i, o in enumerate(S.ops):
                if o.is_dma:
                    last[o.key] = i
            fw = list(last.values())
        if REORDER:
            remap = S.reorder()
            fw = [remap[i] for i in fw]
        S.emit(final_wait_ops=fw)
        build_program.stats = (len(S.ops), A.hi, getattr(S, "est_total", None))
        busy = {}
        for o in S.ops:
            busy[o.eng] = busy.get(o.eng, 0.0) + o.cost
        build_program.busy = busy
        build_program.ops = S.ops
        tags = {}
        for o in S.ops:
            if getattr(o, "t1", None) is not None:
                a = tags.setdefault(o.tag, [1e18, 0.0])
                a[0] = min(a[0], o.t0)
                a[1] = max(a[1], o.t1)
        build_program.tags = tags
        build_program.dbg_names = stage["names"]
    return nc


def _consts():
    k = np.arange(128)
    ident = np.eye(128, dtype=np.float32)
    swap = np.where((k % 64) < 32, k + 32, k - 32)
    pm = np.zeros((128, 128), np.float32)
    pm[swap, k] = 1.0
    s = np.arange(128)[:, None]
    t = np.arange(128)[None, :]
    mf = (s <= t).astype(np.float32)
    mb = (s >= t).astype(np.float32)
    tt_ = np.arange(L, dtype=np.float32)
    rows = np.floor(tt_ / 64.0)
    cols = tt_ - rows * 64.0
    quarter = 32
    freqs = (10000.0 ** (-np.arange(quarter, dtype=np.float32) / quarter)).astype(np.float32)
    cos = np.zeros((128, L), np.float32)
    sin = np.zeros((128, L), np.float32)
    for kk in range(128):
        pos = rows if kk < 64 else cols
        i = kk % 32
        ang = (pos * freqs[i]).astype(np.float32)
        cos[kk] = np.cos(ang)
        sin[kk] = -np.sin(ang) if (kk % 64) < 32 else np.sin(ang)
    posf = np.tile((np.arange(128, dtype=np.float32) - 63.0)[None, :], (128, 1))
    posb = np.tile((64.0 - np.arange(128, dtype=np.float32))[None, :], (128, 1))
    return dict(ident=ident, pm=pm, mf=mf, mb=mb, cos=cos, sin=sin, posf=posf, posb=posb)


def make_in_maps(x, c, ctx, c_ctx, w_mod, b_mod, norm1_g, w_in, hgrn_lb, hgrn_norm_g, ret_decay, ret_norm_g,
                 w_out, norm2_g, w_up, conv_w, conv_b, w_down, final_g):
    f = lambda a: np.ascontiguousarray(np.asarray(a, dtype=np.float32))
    cst = _consts()
    pl = lambda v: f(np.asarray(v).reshape(-1, 128).T)
    shared = dict(
        w_mod=f(w_mod[0]), bmodP=pl(b_mod[0]), n1g=pl(norm1_g[0]), n2g=pl(norm2_g[0]), w_in=f(w_in[0]),
        lbP=f(np.asarray(hgrn_lb).reshape(2, 2, 4, 128).transpose(3, 0, 1, 2).reshape(128, 16)),
        hgn=f(hgrn_norm_g[0]).reshape(1, 512), rd=f(ret_decay[0]).reshape(1, 8), rtn=f(ret_norm_g[0]).reshape(1, 512),
        w_out=f(w_out[0]), w_up=f(w_up[0]),
        cwP=f(np.asarray(conv_w[0]).reshape(3, NFC, 128).transpose(2, 1, 0).reshape(128, 66)),
        cbP=pl(conv_b[0]), w_down=f(w_down[0]), fgn=f(final_g).reshape(1, D), **cst)
    maps = []
    for core in range(NCORES):
        cc = np.stack([np.asarray(c[2 * core]), np.asarray(c[2 * core + 1]), np.asarray(c_ctx)], axis=0)
        cT = f(cc.reshape(3, 8, 128).transpose(2, 1, 0).reshape(128, 24))
        m = dict(shared)
        m.update(x=f(x[2 * core:2 * core + 2]), ctx=f(ctx[2 * core:2 * core + 2]), cT=cT)
        maps.append(m)
    return maps


_NC_CACHE = {}


def kernel(**inputs):
    if "nc" not in _NC_CACHE:
        _NC_CACHE["nc"] = build_program()
    nc = _NC_CACHE["nc"]
    in_maps = make_in_maps(**inputs)
    res = run_bass_kernel_spmd(nc, in_maps, core_ids=list(range(NCORES)))
    out = np.concatenate([np.asarray(r["out"]) for r in res.results], axis=0)
    return out.astype(np.float32)
```
